# Optimizing a Trainium2 kernel written in Bass

```python
import math
import jax, jax.numpy as jnp
from jax import lax
import numpy as np

D_MODEL = 1024
BATCH = 4
SEQ = 8192
DEPTH = 2

GRID_W = 64
CTX_LEN = 256
EPS = 1e-6
CHUNK = 64
N_MOD = 9
MASK_VALUE = -1e30
TINY = 1e-20

D_MIX = D_MODEL
D_HG = D_MIX // 4
HG_HEAD_DIM = 64
HG_HEADS = D_HG // HG_HEAD_DIM
D_NA = D_MIX // 4
NA_HEAD_DIM = 64
NA_HEADS = D_NA // NA_HEAD_DIM
WIN_R = 8
WIN_C = 16
D_SSM = D_MIX // 2
SSM_HEAD_DIM = 64
SSM_HEADS = D_SSM // SSM_HEAD_DIM
SSM_STATE = 128
SSM_GROUPS = 2
CONV_W = 5
CONV_DIM = D_SSM + 2 * SSM_GROUPS * SSM_STATE
D_IN_PROJ = 5 * D_HG + 3 * D_NA + D_SSM + CONV_DIM + 2 * SSM_HEADS
D_FF = ((8 * D_MODEL // 3 + 255) // 256) * 256

kernel_name = 'hybrid_hgrn2_natten_mamba2_macaron_dit'


def rms_norm(x, w):
    xf = x.astype(jnp.float32)
    y = xf * lax.rsqrt(jnp.mean(xf * xf, axis=-1, keepdims=True) + EPS)
    return (y * w.astype(jnp.float32)).astype(x.dtype)


def modulate(x, w, shift, scale):
    return rms_norm(x, w) * (1 + scale[..., None, :]) + shift[..., None, :]


def swiglu(h, w13, w2):
    up, gate = jnp.split(h @ w13, 2, axis=-1)
    return (jax.nn.silu(gate) * up) @ w2


def flip(t):
    return jnp.flip(t, axis=1)


def split_columns(p):
    sizes = [D_HG] * 5 + [D_NA] * 3 + [D_SSM, CONV_DIM, SSM_HEADS, SSM_HEADS]
    return jnp.split(p, np.cumsum(sizes)[:-1].tolist(), axis=-1)


def hgrn2_forget(f_raw, lb):
    r = f_raw.astype(jnp.float32)
    f = lb + (1.0 - lb) * jax.nn.sigmoid(r)
    log_f = jnp.log(jnp.maximum(f, TINY))
    return log_f, (1.0 - lb) * jax.nn.sigmoid(-r)


def gla_chunk_scan(q, k, v, log_f, s0):
    out_dtype = v.dtype
    q, k, v, log_f = (t.astype(jnp.float32) for t in (q, k, v, log_f))
    b, L, H, _ = q.shape
    nc = L // CHUNK

    def to_chunks(t):
        return jnp.moveaxis(t.reshape(b, nc, CHUNK, H, t.shape[-1]), 1, 0)

    causal = jnp.tril(jnp.ones((CHUNK, CHUNK), dtype=bool))

    def step(s, inp):
        qc, kc, vc, gc = inp
        bcum = jnp.cumsum(gc, axis=1)
        diff = bcum[:, :, None] - bcum[:, None, :]
        decay = jnp.where(causal[None, :, :, None, None], jnp.exp(jnp.minimum(diff, 0.0)), 0.0)
        scores = jnp.einsum('bthd,bshd,btshd->bhts', qc, kc, decay)
        o = (jnp.einsum('bhts,bshv->bthv', scores, vc)
             + jnp.einsum('bthd,bhdv->bthv', qc * jnp.exp(bcum), s))
        last = bcum[:, -1]
        s_new = (jnp.exp(last)[..., None] * s
                 + jnp.einsum('bshd,bshv->bhdv', kc * jnp.exp(jnp.minimum(last[:, None] - bcum, 0.0)), vc))
        return s_new, o

    s_fin, o = lax.scan(step, s0.astype(jnp.float32),
                        (to_chunks(q), to_chunks(k), to_chunks(v), to_chunks(log_f)))
    o = jnp.moveaxis(o, 0, 1).reshape(b, L, H, v.shape[-1])
    return o.astype(out_dtype), s_fin


def hgrn2_prepare(q_raw, f_fwd, f_bwd, i_raw, lb):
    shp = q_raw.shape[:2] + (HG_HEADS, HG_HEAD_DIM)
    q = jax.nn.silu(q_raw).reshape(shp)
    v = i_raw.reshape(shp)
    lf_f, k_f = hgrn2_forget(f_fwd, lb[0])
    lf_b, k_b = hgrn2_forget(f_bwd, lb[1])
    return q, v, lf_f.reshape(shp), k_f.reshape(shp), lf_b.reshape(shp), k_b.reshape(shp)


def gla_bidir(q, v, lf_f, k_f, lf_b, k_b, s0_f, s0_b):
    o_f, s_f = gla_chunk_scan(q, k_f, v, lf_f, s0_f)
    o_b, s_b = gla_chunk_scan(flip(q), flip(k_b), flip(v), flip(lf_b), s0_b)
    return o_f + flip(o_b), s_f, s_b


def hgrn2_output(o, g, w):
    b, L = o.shape[:2]
    y = rms_norm(o, w.reshape(HG_HEADS, HG_HEAD_DIM))
    return y.reshape(b, L, D_HG) * jax.nn.silu(g)


def neighbourhood_attention(q, k, v, k_ctx, v_ctx, rpb):
    b, L, H, Dh = q.shape
    rows = L // GRID_W
    wr = min(WIN_R, rows)
    scale = Dh ** -0.5
    qg = q.reshape(b, rows, GRID_W, H, Dh)
    kg = k.reshape(b, rows, GRID_W, H, Dh)
    vg = v.reshape(b, rows, GRID_W, H, Dh)
    r = jnp.arange(rows)
    r0 = jnp.clip(r - wr // 2, 0, rows - wr)
    row_idx = r0[:, None] + jnp.arange(wr)[None, :]
    k_rows = kg[:, row_idx]
    v_rows = vg[:, row_idx]
    col = jnp.arange(GRID_W)
    c0 = jnp.clip(col - WIN_C // 2, 0, GRID_W - WIN_C)
    col_in = (col[None, :] >= c0[:, None]) & (col[None, :] < c0[:, None] + WIN_C)
    dr_idx = row_idx - r[:, None] + (WIN_R - 1)
    dc_idx = jnp.clip(col[None, :] - col[:, None], -(WIN_C - 1), WIN_C - 1) + (WIN_C - 1)
    bias = rpb[:, dr_idx[:, None, :, None], dc_idx[None, :, None, :]]
    s_loc = jnp.einsum('brqhd,brjkhd->bhrqjk', qg, k_rows).astype(jnp.float32) * scale
    s_loc = jnp.where(col_in[None, None, None, :, None, :],
                      s_loc + bias[None].astype(jnp.float32), MASK_VALUE)
    n_loc = wr * GRID_W
    s_loc = s_loc.reshape(b, H, rows, GRID_W, n_loc)
    s_ctx = jnp.einsum('brqhd,bnhd->bhrqn', qg, k_ctx).astype(jnp.float32) * scale
    p = jax.nn.softmax(jnp.concatenate([s_loc, s_ctx], axis=-1), axis=-1).astype(v.dtype)
    p_loc = p[..., :n_loc].reshape(b, H, rows, GRID_W, wr, GRID_W)
    p_ctx = p[..., n_loc:]
    out = (jnp.einsum('bhrqjk,brjkhd->brqhd', p_loc, v_rows)
           + jnp.einsum('bhrqn,bnhd->brqhd', p_ctx, v_ctx))
    return out.reshape(b, L, H * Dh)


def context_attention(q, k, v):
    b, Lq, H, Dh = q.shape
    s = jnp.einsum('bqhd,bkhd->bhqk', q, k).astype(jnp.float32) * (Dh ** -0.5)
    p = jax.nn.softmax(s, axis=-1).astype(v.dtype)
    return jnp.einsum('bhqk,bkhd->bqhd', p, v).reshape(b, Lq, H * Dh)


def depthwise_conv(u, w, bias):
    out = lax.conv_general_dilated(u, w[:, None, :].astype(u.dtype), window_strides=(1,),
                                   padding=[(CONV_W // 2, CONV_W // 2)],
                                   dimension_numbers=('NWC', 'WIO', 'NWC'),
                                   feature_group_count=u.shape[-1])
    return out + bias


def ssd_chunk_scan(xs, dt, A, Bm, Cm, h0):
    out_dtype = xs.dtype
    xs, dt, A, Bm, Cm = (t.astype(jnp.float32) for t in (xs, dt, A, Bm, Cm))
    b, L, H, P = xs.shape
    G, N = Bm.shape[2], Bm.shape[3]
    R = H // G
    nc = L // CHUNK
    xr = xs.reshape(b, nc, CHUNK, G, R, P)
    dtr = dt.reshape(b, nc, CHUNK, G, R)
    Br = Bm.reshape(b, nc, CHUNK, G, N)
    Cr = Cm.reshape(b, nc, CHUNK, G, N)
    cum = jnp.cumsum(dtr * A.reshape(G, R), axis=2)
    causal = jnp.tril(jnp.ones((CHUNK, CHUNK), dtype=bool))
    seg = cum[:, :, :, None] - cum[:, :, None, :]
    Lm = jnp.where(causal[None, None, :, :, None, None], jnp.exp(jnp.minimum(seg, 0.0)), 0.0)
    cb = jnp.einsum('bctgn,bcsgn->bctsg', Cr, Br)
    y_diag = jnp.einsum('bctsg,bctsgr,bcsgr,bcsgrp->bctgrp', cb, Lm, dtr, xr)
    decay_end = jnp.exp(jnp.minimum(cum[:, :, -1:] - cum, 0.0))
    states = jnp.einsum('bcsgn,bcsgr,bcsgrp->bcgrpn', Br, decay_end * dtr, xr)
    chunk_decay = jnp.exp(cum[:, :, -1])

    def step(h, inp):
        st, dec = inp
        return dec[..., None, None] * h + st, h

    h_fin, h_in = lax.scan(step, h0.astype(jnp.float32),
                           (jnp.moveaxis(states, 1, 0), jnp.moveaxis(chunk_decay, 1, 0)))
    h_in = jnp.moveaxis(h_in, 0, 1)
    y_off = jnp.einsum('bctgn,bcgrpn,bctgr->bctgrp', Cr, h_in, jnp.exp(cum))
    y = (y_diag + y_off).reshape(b, L, H, P)
    return y.astype(out_dtype), h_fin


def mamba2_prepare(z, xbc, dt_f, dt_b, conv_w, conv_b, dt_bias):
    b, L = z.shape[:2]
    xbc = jax.nn.silu(depthwise_conv(xbc, conv_w, conv_b))
    xs, bm, cm = jnp.split(xbc, [D_SSM, D_SSM + SSM_GROUPS * SSM_STATE], axis=-1)
    xs = xs.reshape(b, L, SSM_HEADS, SSM_HEAD_DIM)
    bm = bm.reshape(b, L, SSM_GROUPS, SSM_STATE)
    cm = cm.reshape(b, L, SSM_GROUPS, SSM_STATE)
    dtf = jax.nn.softplus(dt_f.astype(jnp.float32) + dt_bias[0])
    dtb = jax.nn.softplus(dt_b.astype(jnp.float32) + dt_bias[1])
    return z, xs, bm, cm, dtf, dtb


def ssd_bidir(xs, bm, cm, dtf, dtb, A, h0_f, h0_b):
    y_f, h_f = ssd_chunk_scan(xs, dtf, A[0], bm, cm, h0_f)
    y_b, h_b = ssd_chunk_scan(flip(xs), flip(dtb), A[1], flip(bm), flip(cm), h0_b)
    return y_f + flip(y_b), h_f, h_b


def mamba2_output(y, xs, z, d_skip, w):
    b, L = y.shape[:2]
    y = (y + d_skip[:, None] * xs).reshape(b, L, D_SSM)
    return rms_norm(y * jax.nn.silu(z), w)


def token_mixer(h, hc, w_in, w_out, lb, hg_norm_w, rpb, na_norm_w,
                conv_w, conv_b, a_log, dt_bias, d_skip, ssm_norm_w, ctx_out):
    b = h.shape[0]
    lat = split_columns(h @ w_in)
    cpt = split_columns(hc @ w_in)

    q, v, lf_f, k_f, lf_b, k_b = hgrn2_prepare(lat[0], lat[1], lat[2], lat[3], lb)
    qc, vc, lfc_f, kc_f, lfc_b, kc_b = hgrn2_prepare(cpt[0], cpt[1], cpt[2], cpt[3], lb)
    s_zero = jnp.zeros((b, HG_HEADS, HG_HEAD_DIM, HG_HEAD_DIM), jnp.float32)
    oc_hg, s_f, s_b = gla_bidir(qc, vc, lfc_f, kc_f, lfc_b, kc_b, s_zero, s_zero)
    o_hg, _, _ = gla_bidir(q, v, lf_f, k_f, lf_b, k_b, s_f, s_b)
    hg_lat = hgrn2_output(o_hg, lat[4], hg_norm_w)

    def na_heads(t):
        return t.reshape(t.shape[:2] + (NA_HEADS, NA_HEAD_DIM))
    qa, ka, va = na_heads(lat[5]), na_heads(lat[6]), na_heads(lat[7])
    qca, kca, vca = na_heads(cpt[5]), na_heads(cpt[6]), na_heads(cpt[7])
    na_lat = rms_norm(neighbourhood_attention(qa, ka, va, kca, vca, rpb), na_norm_w)

    A = -jnp.exp(a_log.astype(jnp.float32))
    z, xs, bm, cm, dtf, dtb = mamba2_prepare(lat[8], lat[9], lat[10], lat[11], conv_w, conv_b, dt_bias)
    zc, xsc, bmc, cmc, dtfc, dtbc = mamba2_prepare(cpt[8], cpt[9], cpt[10], cpt[11], conv_w, conv_b, dt_bias)
    h_zero = jnp.zeros((b, SSM_GROUPS, SSM_HEADS // SSM_GROUPS, SSM_HEAD_DIM, SSM_STATE), jnp.float32)
    yc_ssm, h_f, h_b = ssd_bidir(xsc, bmc, cmc, dtfc, dtbc, A, h_zero, h_zero)
    y_ssm, _, _ = ssd_bidir(xs, bm, cm, dtf, dtb, A, h_f, h_b)
    ssm_lat = mamba2_output(y_ssm, xs, z, d_skip, ssm_norm_w)

    out_lat = jnp.concatenate([hg_lat, na_lat, ssm_lat], axis=-1) @ w_out
    if not ctx_out:
        return out_lat, None
    hg_c = hgrn2_output(oc_hg, cpt[4], hg_norm_w)
    na_c = rms_norm(context_attention(qca, kca, vca), na_norm_w)
    ssm_c = mamba2_output(yc_ssm, xsc, zc, d_skip, ssm_norm_w)
    out_ctx = jnp.concatenate([hg_c, na_c, ssm_c], axis=-1) @ w_out
    return out_lat, out_ctx


def setup_inputs(seed: int = 0) -> dict:
    key = jax.random.key(seed)
    ks = jax.random.split(key, 32)
    D = D_MODEL

    def nrm(k, shape, scale):
        return jax.random.normal(k, shape, jnp.float32) * scale

    def gain(k, shape):
        return 1.0 + 0.05 * jax.random.normal(k, shape, jnp.float32)

    dt0 = jnp.exp(jax.random.uniform(ks[20], (DEPTH, 2, SSM_HEADS), jnp.float32,
                                     minval=math.log(1e-3), maxval=math.log(1e-1)))
    dt_bias = dt0 + jnp.log(-jnp.expm1(-dt0))
    a_log = jnp.log(jax.random.uniform(ks[21], (DEPTH, 2, SSM_HEADS), jnp.float32, minval=1.0, maxval=16.0))
    return {
        'x': nrm(ks[0], (BATCH, SEQ, D), 1.0),
        'c': nrm(ks[1], (BATCH, D), 1.0),
        'ctx': nrm(ks[2], (BATCH, CTX_LEN, D), 1.0),
        'c_ctx': nrm(ks[3], (D,), 1.0),
        'w_mod': nrm(ks[4], (DEPTH, D, N_MOD * D), D ** -0.5),
        'b_mod': nrm(ks[5], (DEPTH, N_MOD * D), 0.02),
        'norm_ffn1': gain(ks[6], (DEPTH, D)),
        'ffn1_w13': nrm(ks[7], (DEPTH, D, 2 * D_FF), D ** -0.5),
        'ffn1_w2': nrm(ks[8], (DEPTH, D_FF, D), D_FF ** -0.5),
        'norm_mix': gain(ks[9], (DEPTH, D)),
        'w_in': nrm(ks[10], (DEPTH, D, D_IN_PROJ), D ** -0.5),
        'hg_lower_bounds': nrm(ks[11], (2, DEPTH, D_HG), 0.5),
        'hg_norm': gain(ks[12], (DEPTH, D_HG)),
        'na_rpb': nrm(ks[13], (DEPTH, NA_HEADS, 2 * WIN_R - 1, 2 * WIN_C - 1), 0.3),
        'na_norm': gain(ks[14], (DEPTH, D_NA)),
        'ssm_conv_w': nrm(ks[15], (DEPTH, CONV_W, CONV_DIM), CONV_W ** -0.5),
        'ssm_conv_b': nrm(ks[16], (DEPTH, CONV_DIM), 0.02),
        'ssm_a_log': a_log,
        'ssm_dt_bias': dt_bias,
        'ssm_d': gain(ks[17], (DEPTH, SSM_HEADS)),
        'ssm_norm': gain(ks[18], (DEPTH, D_SSM)),
        'w_out': nrm(ks[19], (DEPTH, D_MIX, D), D_MIX ** -0.5),
        'norm_ffn2': gain(ks[22], (DEPTH, D)),
        'ffn2_w13': nrm(ks[23], (DEPTH, D, 2 * D_FF), D ** -0.5),
        'ffn2_w2': nrm(ks[24], (DEPTH, D_FF, D), D_FF ** -0.5),
        'final_norm': gain(ks[25], (D,)),
    }


def reference(x, c, ctx, c_ctx, w_mod, b_mod, norm_ffn1, ffn1_w13, ffn1_w2, norm_mix, w_in,
              hg_lower_bounds, hg_norm, na_rpb, na_norm, ssm_conv_w, ssm_conv_b, ssm_a_log,
              ssm_dt_bias, ssm_d, ssm_norm, w_out, norm_ffn2, ffn2_w13, ffn2_w2, final_norm):
    lb_soft = jax.nn.softmax(hg_lower_bounds.astype(jnp.float32), axis=1)
    lower_bounds = jnp.cumsum(lb_soft, axis=1) - lb_soft[:, :1]
    sc = jax.nn.silu(c)
    scc = jax.nn.silu(c_ctx)
    xc = ctx
    for layer in range(DEPTH):
        last = layer == DEPTH - 1
        mod = jnp.split(sc @ w_mod[layer] + b_mod[layer], N_MOD, axis=-1)
        modc = jnp.split(scc @ w_mod[layer] + b_mod[layer], N_MOD, axis=-1)

        x = x + 0.5 * mod[2][:, None, :] * swiglu(
            modulate(x, norm_ffn1[layer], mod[0], mod[1]), ffn1_w13[layer], ffn1_w2[layer])
        xc = xc + 0.5 * modc[2] * swiglu(
            modulate(xc, norm_ffn1[layer], modc[0], modc[1]), ffn1_w13[layer], ffn1_w2[layer])

        h = modulate(x, norm_mix[layer], mod[3], mod[4])
        hc = modulate(xc, norm_mix[layer], modc[3], modc[4])
        mix, mix_c = token_mixer(h, hc, w_in[layer], w_out[layer], lower_bounds[:, layer],
                                 hg_norm[layer], na_rpb[layer], na_norm[layer],
                                 ssm_conv_w[layer], ssm_conv_b[layer], ssm_a_log[layer],
                                 ssm_dt_bias[layer], ssm_d[layer], ssm_norm[layer],
                                 ctx_out=not last)
        x = x + mod[5][:, None, :] * mix
        if not last:
            xc = xc + modc[5] * mix_c
            xc = xc + 0.5 * modc[8] * swiglu(
                modulate(xc, norm_ffn2[layer], modc[6], modc[7]), ffn2_w13[layer], ffn2_w2[layer])

        x = x + 0.5 * mod[8][:, None, :] * swiglu(
            modulate(x, norm_ffn2[layer], mod[6], mod[7]), ffn2_w13[layer], ffn2_w2[layer])
    return rms_norm(x, final_norm)
```

```python
import contextlib
import numpy as np
import concourse.bass as bass
import concourse.mybir as mybir
from concourse.bass_utils import run_bass_kernel_spmd

F32 = mybir.dt.float32
BF16 = mybir.dt.bfloat16
AF = mybir.ActivationFunctionType
ALU = mybir.AluOpType
AX = mybir.AxisListType

D = 1024
DEPTH = 2
DFF = 2816
CTX = 256
GW = 64
EPS = 1e-6
ENGS = ("pe", "act", "dve", "pool", "sp")
PH = "__ph"


class Op:
    __slots__ = ("eng", "fn", "deps", "dma", "semkey", "sig", "val", "idx")

    def __init__(self, eng, fn, dma, semkey):
        self.eng = eng
        self.fn = fn
        self.deps = {}
        self.dma = dma
        self.semkey = semkey
        self.sig = False
        self.val = 0
        self.idx = 0


class Sched:
    def __init__(self, nc):
        self.nc = nc
        self.ops = []
        self.last_w = {}
        self.readers = {}

    def add(self, eng, fn, reads=(), writes=(), dma=False, semkey=None, barrier=False):
        reads = list(reads)
        writes = list(writes)
        if barrier:
            writes.append(PH)
        else:
            reads.append(PH)
        op = Op(eng, fn, dma, semkey if semkey is not None else (writes[0] if (dma and writes) else None))
        op.idx = len(self.ops)
        cand = {}
        for k in reads + writes:
            w = self.last_w.get(k)
            if w is not None:
                cand[w.idx] = w
        for k in writes:
            for r in self.readers.get(k, ()):
                cand[r.idx] = r
        for d in cand.values():
            if (not d.dma) and (not op.dma) and d.eng == "pe" and op.eng == "pe":
                continue
            if d.dma and op.dma and d.semkey == op.semkey:
                pure_waw = all(self.last_w.get(k) is not d for k in reads) and \
                    all(d not in self.readers.get(k, ()) for k in writes)
                if pure_waw:
                    continue
            key = ("dma", d.semkey) if d.dma else ("eng", d.eng)
            old = op.deps.get(key)
            if old is None or old.idx < d.idx:
                op.deps[key] = d
            d.sig = True
        for k in writes:
            self.last_w[k] = op
            self.readers[k] = []
        for k in reads:
            self.readers.setdefault(k, []).append(op)
        self.ops.append(op)
        return op

    def emit(self):
        nc = self.nc
        import os as _os
        _mx = int(_os.environ.get("MAXOPS", "0"))
        if _mx:
            self.ops = self.ops[:_mx]
        cnt = {}
        dma_keys = []
        for op in self.ops:
            if op.dma:
                k = ("dma", op.semkey)
                if k not in cnt:
                    cnt[k] = 0
                    dma_keys.append(k)
                cnt[k] += 16
                op.val = cnt[k]
            elif op.sig:
                k = ("eng", op.eng)
                cnt[k] = cnt.get(k, 0) + 1
                op.val = cnt[k]
        self.maxvals = dict(cnt)
        sems = {}
        with contextlib.ExitStack() as es:
            for e in ENGS:
                sems[("eng", e)] = es.enter_context(nc.semaphore("s_" + e))
            for i, k in enumerate(dma_keys):
                sems[k] = es.enter_context(nc.semaphore("d%d" % i))
            self.nsems = len(sems)
            block = es.enter_context(nc.Block())
            ops = self.ops

            def run(engname, eng):
                waited = {}
                for op in ops:
                    if op.eng != engname:
                        continue
                    for k, d in op.deps.items():
                        if waited.get(k, 0) >= d.val:
                            continue
                        eng.wait_ge(sems[k], d.val)
                        waited[k] = d.val
                    ins = op.fn(eng)
                    if op.dma:
                        ins.then_inc(sems[("dma", op.semkey)], 16)
                    elif op.sig:
                        ins.then_inc(sems[("eng", op.eng)], 1)
                last = {}
                for op in ops:
                    if op.eng == engname and op.dma:
                        last[("dma", op.semkey)] = op.val
                for k, v in last.items():
                    if waited.get(k, 0) < v:
                        eng.wait_ge(sems[k], v)

            @block.tensor
            def _(e):
                run("pe", e)

            @block.scalar
            def _(e):
                run("act", e)

            @block.vector
            def _(e):
                run("dve", e)

            @block.gpsimd
            def _(e):
                run("pool", e)

            @block.sync
            def _(e):
                run("sp", e)


class T:
    def __init__(self, h, key):
        self.h = h
        self.key = key

    def __getitem__(self, idx):
        return self.h[idx]


def _dsize(dt):
    return 2 if dt == BF16 else 4


PF_COLS = ([0, 128] + [256, 384] + [512, 640] + [1024, 1152] + [1280, 1408] + [1536, 1664]
           + [2048, 2176, 2304, 2432] + [2560 + 128 * i for i in range(8)])
PF_Q, PF_FF, PF_FB, PF_G, PF_QA, PF_KA, PF_Z, PF_XBC = 0, 2, 4, 6, 8, 10, 12, 16
PF_NB = 24
PF_TR = ["silu"] * 2 + ["copy"] * 4 + ["silu"] * 2 + ["s8"] * 2 + ["copy"] * 2 + ["silu"] * 4 + ["copy"] * 8
PT_GROUPS = [(256, 768, 0), (768, 1024, 512), (1792, 2048, 768), (2048, 2560, 1024), (3584, 3600, 1536)]
PT_FF, PT_FB, PT_I, PT_VA, PT_Z, PT_DT = 0, 256, 512, 768, 1024, 1536
PT_W = 1552

FV_L = 72 + 8 + 8 + 8 + 2 + 4 + 40 + 8
FV_BMOD, FV_NF1, FV_NMX, FV_NF2, FV_HGN, FV_SSN, FV_CW, FV_CB = 0, 72, 80, 88, 96, 98, 102, 142
FV_G = DEPTH * FV_L
FV_C, FV_FN, FV_LB = FV_G, FV_G + 16, FV_G + 24
FV_N = FV_G + 24 + 8
RV_L = 256 + 512 + 16 + 16 + 512
RV_NAN, RV_DSK, RV_ALOG, RV_DTB, RV_SSN = 0, 256, 768, 784, 800
RV_G = DEPTH * RV_L
RV_LBR = RV_G
RV_N = RV_G + 1024


class KB:
    def __init__(self, nlat=64, dbg=(), layers=DEPTH, stop_after=None, parts=("hg", "ssd", "na")):
        self.parts = set(parts)
        self.nlat = nlat
        self.NT = nlat + 2
        self.NTOK = 128 * self.NT
        self.NLTOK = 128 * nlat
        self.dbg = set(dbg)
        self.layers = layers
        self.stop_after = stop_after
        self.nc = nc = bass.Bass("TRN2", target_bir_lowering=False)
        self.S = Sched(nc)
        self.uid = 0
        self.sb_lo = 16512
        self.sb_hi = 229344
        self.off = self.sb_lo
        self.NB = 256
        self.nblk = self.NTOK // self.NB
        self.declare_io()

    def sb(self, name, shape, dt):
        size = int(np.prod(shape[1:])) * _dsize(dt)
        size = (size + 31) // 32 * 32
        assert self.off + size <= self.sb_hi, (name, self.off, size)
        self.uid += 1
        h = self.nc.alloc_sbuf_tensor_at("%s_%d" % (name, self.uid), list(shape), dt, offset=self.off)
        self.off += size
        return T(h, "%s_%d" % (name, self.uid))

    def mark(self):
        return self.off

    def reset(self, m):
        self.off = m

    def barrier(self):
        scr = self.scr
        self.S.add("dve", lambda e: e.memset(scr[:, 0:1], 0.0), writes=[scr.key], barrier=True)

    def declare_io(self):
        nc = self.nc
        ein = lambda n, s, dt=F32: nc.dram_tensor(n, list(s), dt, kind="ExternalInput").ap()
        self.xT = ein("xT", [D, self.NTOK])
        self.fvec = ein("fvec", [128, FV_N])
        self.rvec = ein("rvec", [128, RV_N])
        self.w_mod = ein("w_mod", [DEPTH, D, 9 * D])
        self.w13 = [ein("ffn1_w13", [DEPTH, D, 2 * DFF]), ein("ffn2_w13", [DEPTH, D, 2 * DFF])]
        self.w2 = [ein("ffn1_w2", [DEPTH, DFF, D]), ein("ffn2_w2", [DEPTH, DFF, D])]
        self.w_in = ein("w_in", [DEPTH, D, 3600])
        self.w_out = ein("w_out", [DEPTH, D, D])
        self.outT = nc.dram_tensor("outT", [D, self.NLTOK], F32, kind="ExternalOutput").ap()
        self.XT = nc.dram_tensor("XTs", [D, self.NTOK], F32).ap()
        self.PF = nc.dram_tensor("PFs", [PF_NB * 128, self.NTOK], F32).ap()
        self.PT = nc.dram_tensor("PTs", [self.NTOK, PT_W], F32).ap()
        self.MIXT = nc.dram_tensor("MIXTs", [D, self.NTOK], BF16).ap()
        self.dbg_out = {}
        _mixer_io(self)

    def dbg_tensor(self, name, shape, dt=F32):
        ap = self.nc.dram_tensor("dbg_" + name, list(shape), dt, kind="ExternalOutput").ap()
        self.dbg_out[name] = ap
        return ap

    def setup_persist(self):
        S = self.S
        self.scr = self.sb("scr", [128, 8], F32)
        self.fv = self.sb("fv", [128, FV_N], F32)
        self.ones_bf = self.sb("ones", [128, 128], BF16)
        self.MOD = self.sb("MOD", [128, 2, 72], F32)
        self.scb = self.sb("scb", [128, 16], F32)
        fv, ones = self.fv, self.ones_bf
        S.add("sp", lambda e: e.dma_start(out=fv[:], in_=self.fvec), writes=[fv.key], dma=True)
        S.add("dve", lambda e: e.memset(ones[:], 1.0), writes=[ones.key])
        scb = self.scb
        S.add("act", lambda e: e.activation(out=scb[:], in_=fv[:, FV_C:FV_C + 16], func=AF.Silu),
              reads=[fv.key], writes=[scb.key])
        self.persist_end = self.mark()

    def phase_mod(self, l):
        S, nc = self.S, self.nc
        m = self.mark()
        HALF = 4608
        wm = [self.sb("wm", [128, HALF], F32) for _ in range(2)]
        psm = self.ps[0]
        scb, fv, MOD = self.scb, self.fv, self.MOD
        i = 0
        for k in range(8):
            for hf in range(2):
                w = wm[i % 2]
                i += 1
                src = self.w_mod[l, k * 128:(k + 1) * 128, hf * HALF:(hf + 1) * HALF]
                S.add("sp", lambda e, w=w, src=src: e.dma_start(out=w[:], in_=src), writes=[w.key], dma=True)
                for j in range(36):
                    fb = hf * 36 + j
                    S.add("pe", lambda e, w=w, j=j, fb=fb, k=k: e.matmul(
                        psm[:, fb * 2:fb * 2 + 2], lhsT=w[:, j * 128:(j + 1) * 128], rhs=scb[:, 2 * k:2 * k + 2],
                        start=(k == 0 and fb == 0), stop=(k == 7), skip_group_check=True),
                        reads=[w.key, scb.key], writes=[psm.key])
        bo = l * FV_L
        for s in range(2):
            S.add("dve", lambda e, s=s: e.tensor_tensor(out=MOD[:, s, :], in0=psm[:, s:144:2],
                                                       in1=fv[:, bo + FV_BMOD:bo + FV_BMOD + 72], op=ALU.add),
                  reads=[psm.key, fv.key], writes=[MOD.key])
        for s in range(2):
            for (js, nw) in ((1, FV_NF1), (4, FV_NMX), (7, FV_NF2)):
                S.add("dve", lambda e, s=s, js=js, nw=nw: e.scalar_tensor_tensor(
                    out=MOD[:, s, js * 8:js * 8 + 8], in0=MOD[:, s, js * 8:js * 8 + 8], scalar=1.0,
                    in1=fv[:, bo + nw:bo + nw + 8], op0=ALU.add, op1=ALU.mult),
                    reads=[MOD.key, fv.key], writes=[MOD.key])
                S.add("dve", lambda e, s=s, js=js: e.tensor_scalar_mul(
                    out=MOD[:, s, js * 8:js * 8 + 8], in0=MOD[:, s, js * 8:js * 8 + 8], scalar1=32.0),
                    reads=[MOD.key], writes=[MOD.key])
            for jg in (2, 8):
                S.add("dve", lambda e, s=s, jg=jg: e.tensor_scalar_mul(
                    out=MOD[:, s, jg * 8:jg * 8 + 8], in0=MOD[:, s, jg * 8:jg * 8 + 8], scalar1=0.5),
                    reads=[MOD.key], writes=[MOD.key])
        if "mod" in self.dbg:
            d = self.dbg_tensor("mod%d" % l, [128, 144])
            S.add("sp", lambda e: e.dma_start(out=d, in_=MOD[:].rearrange("p s c -> p (s c)")), reads=[MOD.key],
                  dma=True, semkey="dbg_mod%d" % l)
        self.barrier()
        self.reset(m)

    def load_w(self, name, src, K, N):
        kc = K // 128
        w = self.sb(name, [128, kc, N], BF16)
        for k in range(kc):
            s = src[k * 128:(k + 1) * 128, :]
            self.S.add("pool", lambda e, k=k, s=s: e.dma_start(out=w[:, k, :], in_=s), writes=[w.key], dma=True)
        return w

    def alloc_work(self):
        NB = self.NB
        self.xb = [self.sb("xb", [128, 8, NB], F32) for _ in range(2)]
        self.tmp = self.sb("tmp", [128, 8, NB], F32)
        self.hb = self.sb("hb", [128, 8, NB], BF16)
        self.sq = self.hb
        self.rstd = self.sb("rstd", [128, NB], F32)

    def blk(self, i):
        n0 = i * self.NB
        return n0, self.NB, (1 if n0 < CTX else 0)

    def load_x(self, xb, src, n0, N):
        v = src.rearrange("(k p) n -> p k n", p=128)[:, :, n0:n0 + N]
        self.S.add("sp", lambda e: e.dma_start(out=xb[:, :, :N], in_=v), writes=[xb.key], dma=True)

    def store_x(self, xb, dst, n0, N, key="dram_x"):
        v = dst.rearrange("(k p) n -> p k n", p=128)[:, :, n0:n0 + N]
        self.S.add("sp", lambda e: e.dma_start(out=v, in_=xb[:, :, :N]), reads=[xb.key], dma=True,
                   semkey="st_" + xb.key)

    def modulate(self, xb, N, A, SH, out, out_keyed):
        S = self.S
        sq, tmp, rstd, ones = self.sq, self.tmp, self.rstd, self.ones_bf
        pss = self.ps[0]
        S.add("act", lambda e: e.activation(out=sq[:, :, :N], in_=xb[:, :, :N], func=AF.Square),
              reads=[xb.key], writes=[sq.key])
        for k in range(8):
            S.add("pe", lambda e, k=k: e.matmul(pss[:, :N], lhsT=ones[:], rhs=sq[:, k, :N], start=(k == 0), stop=(k == 7)),
                  reads=[sq.key, ones.key], writes=[pss.key])
        S.add("dve", lambda e: e.tensor_scalar_add(out=rstd[:, :N], in0=pss[:, :N], scalar1=float(D * EPS)),
              reads=[pss.key], writes=[rstd.key])
        S.add("act", lambda e: e.activation(out=rstd[:, :N], in_=rstd[:, :N], func=AF.Ln),
              reads=[rstd.key], writes=[rstd.key])
        S.add("act", lambda e: e.activation(out=rstd[:, :N], in_=rstd[:, :N], func=AF.Exp, scale=-0.5),
              reads=[rstd.key], writes=[rstd.key])
        S.add("dve", lambda e: e.tensor_tensor(out=tmp[:, :, :N], in0=xb[:, :, :N],
                                               in1=rstd[:, :N].unsqueeze(1).to_broadcast([128, 8, N]), op=ALU.mult),
              reads=[xb.key, rstd.key], writes=[tmp.key])
        for k in range(8):
            if SH is None:
                S.add("act", lambda e, k=k: e.activation(out=out[:, k, :N], in_=tmp[:, k, :N], func=AF.Identity,
                                                         scale=A[:, k:k + 1]),
                      reads=[tmp.key, self.MOD.key, self.fv.key], writes=[out_keyed])
            else:
                S.add("act", lambda e, k=k: e.activation(out=out[:, k, :N], in_=tmp[:, k, :N], func=AF.Identity,
                                                         scale=A[:, k:k + 1], bias=SH[:, k:k + 1]),
                      reads=[tmp.key, self.MOD.key], writes=[out_keyed])

    def ffn(self, xb, N, s, jbase, w13b, w2b):
        S = self.S
        MOD = self.MOD
        A = MOD[:, s, (jbase + 1) * 8:(jbase + 2) * 8]
        SH = MOD[:, s, jbase * 8:(jbase + 1) * 8]
        G = MOD[:, s, (jbase + 2) * 8:(jbase + 3) * 8]
        hb, ab, sg = self.hb, self.ab, self.sg
        self.modulate(xb, N, A, SH, hb, hb.key)
        for j in range(22):
            pu, pg = self.ps[1 + j % 2], self.ps[3 + j % 2]
            for k in range(8):
                S.add("pe", lambda e, j=j, k=k, pu=pu: e.matmul(pu[:, :N], lhsT=w13b[:, k, j * 128:(j + 1) * 128],
                                                                rhs=hb[:, k, :N], start=(k == 0), stop=(k == 7)),
                      reads=[w13b.key, hb.key], writes=[pu.key])
            for k in range(8):
                S.add("pe", lambda e, j=j, k=k, pg=pg: e.matmul(pg[:, :N], lhsT=w13b[:, k, DFF + j * 128:DFF + (j + 1) * 128],
                                                                rhs=hb[:, k, :N], start=(k == 0), stop=(k == 7)),
                      reads=[w13b.key, hb.key], writes=[pg.key])
            sgj = sg[j % 2]
            S.add("act", lambda e, pg=pg, sgj=sgj: e.activation(out=sgj[:, :N], in_=pg[:, :N], func=AF.Silu),
                  reads=[pg.key], writes=[sgj.key])
            S.add("dve", lambda e, j=j, pu=pu, sgj=sgj: e.tensor_tensor(out=ab[:, j, :N], in0=sgj[:, :N], in1=pu[:, :N],
                                                                          op=ALU.mult),
                  reads=[pu.key, sgj.key], writes=[ab.key])
        for fb in range(8):
            po = self.ps[5 + fb % 2]
            for j in range(22):
                S.add("pe", lambda e, j=j, fb=fb, po=po: e.matmul(po[:, :N], lhsT=w2b[:, j, fb * 128:(fb + 1) * 128],
                                                                  rhs=ab[:, j, :N], start=(j == 0), stop=(j == 21)),
                      reads=[w2b.key, ab.key], writes=[po.key])
            S.add("dve", lambda e, fb=fb, po=po: e.scalar_tensor_tensor(out=xb[:, fb, :N], in0=po[:, :N],
                                                                        scalar=G[:, fb:fb + 1], in1=xb[:, fb, :N],
                                                                        op0=ALU.mult, op1=ALU.add),
                  reads=[po.key, xb.key, MOD.key], writes=[xb.key])

    def phase_ffn1(self, l):
        m = self.mark()
        w13b = self.load_w("w13b", self.w13[0][l], D, 2 * DFF)
        w2b = self.load_w("w2b", self.w2[0][l], DFF, D)
        self.alloc_work()
        self.ab = self.sb("ab", [128, 22, self.NB], BF16)
        self.sg = [self.sb("sg", [128, self.NB], F32) for _ in range(2)]
        src = self.xT if l == 0 else self.XT
        for i in range(self.nblk):
            n0, N, s = self.blk(i)
            xb = self.xb[i % 2]
            self.load_x(xb, src, n0, N)
            self.ffn(xb, N, s, 0, w13b, w2b)
            self.store_x(xb, self.XT, n0, N, key="dram_x1")
        self.barrier()
        self.reset(m)
        if "x1" in self.dbg:
            self.dump_dram("x1_%d" % l, self.XT, [D, self.NTOK], "dram_x1")

    def dump_dram(self, name, src, shape, key, dt=F32):
        d = self.dbg_tensor(name, shape, dt)
        self.S.add("sp", lambda e: e.dma_start(out=d, in_=src), dma=True, semkey="dbg_" + name)
        self.barrier()

    def phase_inproj(self, l):
        S = self.S
        m = self.mark()
        winb = self.load_w("winb", self.w_in[l], D, 3600)
        self.alloc_work()
        NB = self.NB
        pfst = self.sb("pfst", [128, PF_NB, NB], F32)
        ptsts = [self.sb("ptst", [128, PT_W], F32) for _ in range(NB // 128)]
        MOD, hb = self.MOD, self.hb
        for i in range(self.nblk):
            n0, N, s = self.blk(i)
            xb = self.xb[i % 2]
            self.load_x(xb, self.XT, n0, N)
            self.modulate(xb, N, MOD[:, s, 32:40], MOD[:, s, 24:32], hb, hb.key)
            for bi, c0 in enumerate(PF_COLS):
                p = self.ps[1 + bi % 4]
                for k in range(8):
                    S.add("pe", lambda e, k=k, c0=c0, p=p: e.matmul(p[:, :N], lhsT=winb[:, k, c0:c0 + 128], rhs=hb[:, k, :N],
                                                                    start=(k == 0), stop=(k == 7)),
                          reads=[winb.key, hb.key], writes=[p.key])
                tr = PF_TR[bi]
                if tr == "silu":
                    S.add("act", lambda e, bi=bi, p=p: e.activation(out=pfst[:, bi, :N], in_=p[:, :N], func=AF.Silu),
                          reads=[p.key], writes=[pfst.key])
                elif tr == "s8":
                    S.add("dve", lambda e, bi=bi, p=p: e.tensor_scalar_mul(out=pfst[:, bi, :N], in0=p[:, :N], scalar1=0.125),
                          reads=[p.key], writes=[pfst.key])
                else:
                    S.add("dve", lambda e, bi=bi, p=p: e.tensor_copy(out=pfst[:, bi, :N], in_=p[:, :N]),
                          reads=[p.key], writes=[pfst.key])
            dst = self.PF.rearrange("(f p) n -> p f n", p=128)[:, :, n0:n0 + N]
            S.add("sp", lambda e, dst=dst: e.dma_start(out=dst, in_=pfst[:, :, :N]), reads=[pfst.key],
                  dma=True, semkey="dram_pf")
            for t in range(N // 128):
                ptst = ptsts[t]
                for gi, (c0, c1, d0) in enumerate(PT_GROUPS):
                    p = self.ps[5 + gi % 2]
                    wdt = c1 - c0
                    for k in range(8):
                        S.add("pe", lambda e, k=k, t=t, c0=c0, c1=c1, p=p, wdt=wdt: e.matmul(
                            p[:, :wdt], lhsT=hb[:, k, t * 128:(t + 1) * 128], rhs=winb[:, k, c0:c1],
                            start=(k == 0), stop=(k == 7)), reads=[winb.key, hb.key], writes=[p.key])
                    S.add("act", lambda e, ptst=ptst, d0=d0, wdt=wdt, p=p: e.copy(out=ptst[:, d0:d0 + wdt], in_=p[:, :wdt]),
                          reads=[p.key], writes=[ptst.key])
                dstt = self.PT[n0 + t * 128:n0 + (t + 1) * 128, :]
                S.add("sp", lambda e, ptst=ptst, dstt=dstt: e.dma_start(out=dstt, in_=ptst[:]), reads=[ptst.key],
                      dma=True, semkey="st_" + ptst.key)
        self.barrier()
        self.reset(m)
        if "proj" in self.dbg:
            self.dump_dram("pf_%d" % l, self.PF, [PF_NB * 128, self.NTOK], "dram_pf")
            self.dump_dram("pt_%d" % l, self.PT, [self.NTOK, PT_W], "dram_pt")

    def phase_out(self, l):
        S = self.S
        last = (l == DEPTH - 1)
        m = self.mark()
        w13b = self.load_w("w13b", self.w13[1][l], D, 2 * DFF)
        w2b = self.load_w("w2b", self.w2[1][l], DFF, D)
        woutb = self.load_w("woutb", self.w_out[l], D, D)
        self.alloc_work()
        NB = self.NB
        self.ab = self.sb("ab", [128, 22, NB], BF16)
        self.sg = [self.sb("sg", [128, NB], F32) for _ in range(2)]
        mixb = [self.sb("mixb", [128, 8, NB], BF16)] * 2
        MOD = self.MOD
        for i in range(self.nblk):
            n0, N, s = self.blk(i)
            if last and s == 1:
                continue
            xb = self.xb[i % 2]
            mb = mixb[i % 2]
            self.load_x(xb, self.XT, n0, N)
            v = self.MIXT.rearrange("(k p) n -> p k n", p=128)[:, :, n0:n0 + N]
            S.add("sp", lambda e, mb=mb, v=v: e.dma_start(out=mb[:, :, :N], in_=v), writes=[mb.key], dma=True)
            G = MOD[:, s, 40:48]
            for fb in range(8):
                p = self.ps[5 + fb % 2]
                for k in range(8):
                    S.add("pe", lambda e, k=k, fb=fb, p=p, mb=mb: e.matmul(p[:, :N], lhsT=woutb[:, k, fb * 128:(fb + 1) * 128],
                                                                          rhs=mb[:, k, :N], start=(k == 0), stop=(k == 7)),
                          reads=[woutb.key, mb.key], writes=[p.key])
                S.add("dve", lambda e, fb=fb, p=p, xb=xb, G=G: e.scalar_tensor_tensor(out=xb[:, fb, :N], in0=p[:, :N],
                                                                                  scalar=G[:, fb:fb + 1], in1=xb[:, fb, :N],
                                                                                  op0=ALU.mult, op1=ALU.add),
                      reads=[p.key, xb.key, MOD.key], writes=[xb.key])
            self.ffn(xb, N, s, 6, w13b, w2b)
            if not last:
                self.store_x(xb, self.XT, n0, N, key="dram_x3")
            else:
                ob = xb
                fn = self.fn32
                self.modulate(xb, N, fn, None, ob, ob.key)
                v = self.outT.rearrange("(k p) n -> p k n", p=128)[:, :, n0 - CTX:n0 - CTX + N]
                S.add("sp", lambda e, v=v, ob=ob: e.dma_start(out=v, in_=ob[:, :, :N]), reads=[ob.key],
                      dma=True, semkey="st_" + ob.key)
        self.barrier()
        self.reset(m)
        if "x3" in self.dbg and not last:
            self.dump_dram("x3_%d" % l, self.XT, [D, self.NTOK], "dram_x3")

    def build(self):
        S = self.S
        big = [self.nc.alloc_psum_tensor("psb%d" % i, [128, 1024], F32) for i in range(4)]
        self.ps = [T(big[i // 2][:, (i % 2) * 512:(i % 2) * 512 + 512], "ps%d" % i) for i in range(8)]
        self.psbig = [T(big[i], "psB%d" % i) for i in range(4)]
        self.setup_persist()
        self.fn32 = None
        for l in range(self.layers):
            if not getattr(self, "only_mixer", False):
                self.phase_mod(l)
                if self.stop_after == ("mod", l):
                    break
                self.phase_ffn1(l)
                if self.stop_after == ("ffn1", l):
                    break
                self.phase_inproj(l)
                if self.stop_after == ("inproj", l):
                    break
            self.phase_mixer(l)
            if self.stop_after == ("mixer", l):
                break
            self.phase_out_wrap(l)
        S.emit()
        return self.nc

    def phase_out_wrap(self, l):
        last = (l == DEPTH - 1)
        if last:
            m = self.mark()
            fn32 = self.sb("fn32", [128, 8], F32)
            fv = self.fv
            self.S.add("dve", lambda e: e.tensor_scalar_mul(out=fn32[:], in0=fv[:, FV_FN:FV_FN + 8], scalar1=32.0),
                       reads=[fv.key], writes=[fn32.key])
            self.fn32 = fn32
            self.persist_tmp = self.mark()
            self.phase_out(l)
            self.reset(m)
        else:
            self.phase_out(l)

    def phase_mixer(self, l):
        m = self.mark()
        _setup_mixer_consts(self)
        if "ohg" in self.dbg:
            self.dbg_tensor("ohg%d" % l, [256, self.NTOK])
        if "yssm" in self.dbg:
            self.dbg_tensor("yssm%d" % l, [self.NTOK, 512])
        if "naraw" in self.dbg:
            self.dbg_tensor("naraw%d" % l, [self.NTOK, 256])
        if "hg" in self.parts:
            _phase_hgrn2(self, l)
        if "ssd" in self.parts:
            _phase_ssd(self, l)
        if "na" in self.parts:
            _phase_na(self, l)
        if "mix" in self.dbg:
            self.dump_dram("mix_%d" % l, self.MIXT, [D, self.NTOK], "x", BF16)
        self.reset(m)


def fm(v):
    v = np.asarray(v, np.float32)
    return v.reshape(-1, 128).T


def prep_shared(inp):
    fv = np.zeros((128, FV_N), np.float32)
    rv = np.zeros((128, RV_N), np.float32)
    for l in range(DEPTH):
        o = l * FV_L
        fv[:, o + FV_BMOD:o + FV_BMOD + 72] = fm(inp["b_mod"][l])
        fv[:, o + FV_NF1:o + FV_NF1 + 8] = fm(inp["norm_ffn1"][l])
        fv[:, o + FV_NMX:o + FV_NMX + 8] = fm(inp["norm_mix"][l])
        fv[:, o + FV_NF2:o + FV_NF2 + 8] = fm(inp["norm_ffn2"][l])
        fv[:, o + FV_HGN:o + FV_HGN + 2] = fm(inp["hg_norm"][l])
        fv[:, o + FV_SSN:o + FV_SSN + 4] = fm(inp["ssm_norm"][l])
        for j in range(5):
            fv[:, o + FV_CW + j * 8:o + FV_CW + j * 8 + 8] = fm(inp["ssm_conv_w"][l, j])
        fv[:, o + FV_CB:o + FV_CB + 8] = fm(inp["ssm_conv_b"][l])
        r = l * RV_L
        rv[:, r + RV_NAN:r + RV_NAN + 256] = inp["na_norm"][l][None, :]
        rv[:, r + RV_DSK:r + RV_DSK + 512] = np.repeat(inp["ssm_d"][l], 64)[None, :]
        rv[:, r + RV_ALOG:r + RV_ALOG + 16] = inp["ssm_a_log"][l].reshape(-1)[None, :]
        rv[:, r + RV_DTB:r + RV_DTB + 16] = inp["ssm_dt_bias"][l].reshape(-1)[None, :]
        rv[:, r + RV_SSN:r + RV_SSN + 512] = inp["ssm_norm"][l][None, :]
    fv[:, FV_FN:FV_FN + 8] = fm(inp["final_norm"])
    for dr in range(2):
        for l in range(DEPTH):
            rv[:, RV_LBR + (dr * 2 + l) * 256:RV_LBR + (dr * 2 + l) * 256 + 256] = inp["hg_lower_bounds"][dr, l][None, :]
    for dr in range(2):
        for l in range(DEPTH):
            fv[:, FV_LB + dr * 4 + l * 2:FV_LB + dr * 4 + l * 2 + 2] = fm(inp["hg_lower_bounds"][dr, l])
    return fv, rv


def prep_core(inp, b, nlat, fv_shared):
    fv = fv_shared.copy()
    cc = np.stack([fm(inp["c"][b]), fm(inp["c_ctx"])], axis=2)
    fv[:, FV_C:FV_C + 16] = cc.reshape(128, 16)
    xT = np.ascontiguousarray(np.concatenate([inp["ctx"][b], inp["x"][b][:128 * nlat]], axis=0).T)
    return fv, xT


def make_in_maps(inp, nlat, batches):
    fvs, rv = prep_shared(inp)
    shared = {k: np.ascontiguousarray(inp[k], np.float32) for k in
              ("w_mod", "ffn1_w13", "ffn2_w13", "ffn1_w2", "ffn2_w2", "w_in", "w_out")}
    cmat = make_cmat()
    nab = np.stack([make_nabias(np.asarray(inp["na_rpb"][l], np.float32), nlat) for l in range(DEPTH)])
    maps = []
    for b in batches:
        fv, xT = prep_core(inp, b, nlat, fvs)
        m = dict(shared)
        m.update({"xT": xT, "fvec": fv, "rvec": rv, "cmat": cmat, "nabias": nab})
        maps.append(m)
    return maps


CM_ID, CM_M1F, CM_M2F, CM_M3F, CM_M1B, CM_M2B, CM_M3B = 0, 128, 256, 384, 512, 640, 768
CM_M4F, CM_M4B, CM_HM, CM_BLK, CM_TRIF, CM_TRIB, CM_NEGF, CM_NEGB, CM_ONES = 896, 900, 904, 1160, 1288, 1416, 1544, 1672, 1800
CM_N = 1928
NEG = -30000.0


def make_cmat():
    c = np.zeros((128, CM_N), np.float32)
    u = np.arange(128)[:, None]
    t = np.arange(128)[None, :]
    same = (u // 32) == (t // 32)
    c[:, CM_ID:CM_ID + 128] = (u == t)
    mf = (t // 32) * 32 + 15
    c[:, CM_M1F:CM_M1F + 128] = same * (((u > mf) & (u <= t)) * 1.0 - ((u > t) & (u <= mf)) * 1.0)
    c[:, CM_M2F:CM_M2F + 128] = same & (u <= t)
    c[:, CM_M3F:CM_M3F + 128] = same & (u > t)
    mb = (t // 32) * 32 + 16
    c[:, CM_M1B:CM_M1B + 128] = same * (((u >= t) & (u < mb)) * 1.0 - ((u >= mb) & (u < t)) * 1.0)
    c[:, CM_M2B:CM_M2B + 128] = same & (u >= t)
    c[:, CM_M3B:CM_M3B + 128] = same & (u < t)
    j = np.arange(4)[None, :]
    c[:, CM_M4F:CM_M4F + 4] = (u // 32) == j
    c[:, CM_M4B:CM_M4B + 4] = (u // 32) == (3 - j)
    col = np.arange(128)[None, :]
    c[:, CM_HM:CM_HM + 128] = (col // 64 == 0)
    c[:, CM_HM + 128:CM_HM + 256] = (col // 64 == 1)
    c[:, CM_BLK:CM_BLK + 128] = (u // 64) == (t // 64)
    c[:, CM_TRIF:CM_TRIF + 128] = (u <= t)
    c[:, CM_TRIB:CM_TRIB + 128] = (u >= t)
    c[:, CM_NEGF:CM_NEGF + 128] = NEG * (u > t)
    c[:, CM_NEGB:CM_NEGB + 128] = NEG * (u < t)
    c[:, CM_ONES:CM_ONES + 128] = 1.0
    return c


def _mixer_io(self):
    nc = self.nc
    self.cmat = nc.dram_tensor("cmat", [128, CM_N], F32, kind="ExternalInput").ap()
    self.OHG = nc.dram_tensor("OHGs", [256, self.NTOK], F32).ap()
    _ssd_io(self)
    _na_io(self)


def _setup_mixer_consts(self):
    S = self.S
    self.cm = cm = self.sb("cm", [128, CM_N], F32)
    S.add("sp", lambda e: e.dma_start(out=cm[:], in_=self.cmat), writes=[cm.key], dma=True)
    self.rv = rv = self.sb("rv", [128, RV_N], F32)
    S.add("sp", lambda e: e.dma_start(out=rv[:], in_=self.rvec), writes=[rv.key], dma=True)
    self.blk_bf = blk = self.sb("blkbf", [128, 128], BF16)
    S.add("dve", lambda e: e.tensor_copy(out=blk[:], in_=cm[:, CM_BLK:CM_BLK + 128]), reads=[cm.key], writes=[blk.key])


def _hg_tiles(self, d):
    NT = self.NT
    chain = list(range(NT)) if d == 0 else [1, 0] + list(range(NT - 1, 1, -1))
    return [chain[0:2]] + [chain[i:i + 8] for i in range(2, NT, 8)]


def _phase_hgrn2(self, l):
    S = self.S
    m = self.mark()
    cm, fv, rv = self.cm, self.fv, self.rv
    ps = self.ps
    LBt = self.sb("LBt", [128, 2, 256], F32)
    OMLt = self.sb("OMLt", [128, 2, 256], F32)
    omlf = self.sb("omlf", [128, 2, 2], F32)
    if l == 0:
        S.add("dve", lambda e: e.memset(LBt[:], 0.0), writes=[LBt.key])
        S.add("dve", lambda e: e.memset(OMLt[:], 1.0), writes=[OMLt.key])
        S.add("dve", lambda e: e.memset(omlf[:], 1.0), writes=[omlf.key])
    else:
        for d in range(2):
            a0 = rv[:, RV_LBR + (d * 2 + 0) * 256:RV_LBR + (d * 2 + 0) * 256 + 256]
            a1 = rv[:, RV_LBR + (d * 2 + 1) * 256:RV_LBR + (d * 2 + 1) * 256 + 256]
            S.add("dve", lambda e, d=d, a0=a0, a1=a1: e.tensor_tensor(out=LBt[:, d, :], in0=a1, in1=a0, op=ALU.subtract),
                  reads=[rv.key], writes=[LBt.key])
            f0 = fv[:, FV_LB + d * 4:FV_LB + d * 4 + 2]
            f1 = fv[:, FV_LB + d * 4 + 2:FV_LB + d * 4 + 4]
            S.add("dve", lambda e, d=d, f0=f0, f1=f1: e.tensor_tensor(out=omlf[:, d, :], in0=f1, in1=f0, op=ALU.subtract),
                  reads=[fv.key], writes=[omlf.key])
        S.add("act", lambda e: e.activation(out=LBt[:], in_=LBt[:], func=AF.Sigmoid), reads=[LBt.key], writes=[LBt.key])
        S.add("act", lambda e: e.activation(out=omlf[:], in_=omlf[:], func=AF.Sigmoid), reads=[omlf.key], writes=[omlf.key])
        S.add("dve", lambda e: e.tensor_scalar(out=OMLt[:], in0=LBt[:], scalar1=-1.0, scalar2=1.0, op0=ALU.mult, op1=ALU.add),
              reads=[LBt.key], writes=[OMLt.key])
        S.add("dve", lambda e: e.tensor_scalar(out=omlf[:], in0=omlf[:], scalar1=-1.0, scalar2=1.0, op0=ALU.mult, op1=ALU.add),
              reads=[omlf.key], writes=[omlf.key])
    omlfh = self.sb("omlfh", [128, 2, 2, 2], F32)
    for d_ in range(2):
        for pr_ in range(2):
            for h2_ in range(2):
                S.add("dve", lambda e, d_=d_, pr_=pr_, h2_=h2_: e.tensor_tensor(
                    out=omlfh[:, d_, pr_, h2_:h2_ + 1], in0=omlf[:, d_, pr_:pr_ + 1],
                    in1=cm[:, CM_BLK + h2_ * 64:CM_BLK + h2_ * 64 + 1], op=ALU.mult),
                    reads=[omlf.key, cm.key], writes=[omlfh.key])
    D1 = [self.sb("D1", [128, 64, 33], F32) for _ in range(2)]
    SO = [self.sb("SO", [128, 64, 33], F32) for _ in range(2)]
    D0 = [self.sb("D0", [128, 64, 33], F32) for _ in range(2)]
    Sblk = [self.sb("Sblk", [128, 32, 128], BF16) for _ in range(2)]
    DEC = self.sb("DEC", [128, 2, 32], F32)
    ATs = [self.sb("ATs", [128, 4, 128], BF16) for _ in range(8)]
    QHs = [self.sb("QHs", [128, 2, 128], BF16) for _ in range(8)]
    VZs = [self.sb("VZs", [128, 2, 2, 128], BF16) for _ in range(8)]
    rtok = self.sb("rtok", [128, 256], F32)
    vtok = self.sb("vtok", [128, 256], F32)
    vtok_bf = self.sb("vtok_bf", [128, 256], BF16)
    qfm = self.sb("qfm", [128, 2, 128], F32)
    rfm = self.sb("rfm", [128, 2, 128], F32)
    sig = self.sb("sig", [128, 256], F32)
    tmpk = self.sb("tmpk", [128, 256], F32)
    lf = self.sb("lf", [128, 256], F32)
    ktok = self.sb("ktok", [128, 256], F32)
    sneg = self.sb("sneg", [128, 2, 128], F32)
    P1c = self.sb("P1c", [128, 256], F32)
    Ep = self.sb("Ep", [128, 256], F32)
    En = self.sb("En", [128, 256], F32)
    E2 = self.sb("E2", [128, 256], F32)
    E3 = self.sb("E3", [128, 256], F32)
    qt = self.sb("qt", [128, 2, 128], BF16)
    kt = self.sb("kt", [128, 2, 2, 128], BF16)
    khat = self.sb("khat", [128, 4, 256], BF16)
    of_ld = self.sb("of_ld", [128, 2, 128], F32)
    osum = self.sb("osum", [128, 2, 128], F32)
    osq = self.sb("osq", [128, 2, 128], BF16)
    orstd = self.sb("orstd", [128, 256], F32)
    sgl = self.sb("sgl", [128, 2, 128], F32)
    hgo = self.sb("hgo", [128, 2, 128], BF16)
    for pr in range(2):
        S.add("dve", lambda e, pr=pr: e.memset(Sblk[pr][:], 0.0), writes=[Sblk[pr].key])
        S.add("dve", lambda e, pr=pr: e.memset(D0[pr][:], 0.0), writes=[D0[pr].key])
        S.add("dve", lambda e, pr=pr: e.memset(D1[pr][:], 0.0), writes=[D1[pr].key])
    PFr = self.PF.rearrange("(f p) n -> p f n", p=128)
    OHGr = self.OHG.rearrange("(f p) n -> p f n", p=128)
    MIXr = self.MIXT.rearrange("(f p) n -> p f n", p=128)
    pA, pB, pC, pU, pO, pN = ps[1], ps[2], ps[3], ps[4], ps[5], ps[6]
    def do_dir(d):
        M1 = cm[:, (CM_M1F, CM_M1B)[d]:(CM_M1F, CM_M1B)[d] + 128]
        M2 = cm[:, (CM_M2F, CM_M2B)[d]:(CM_M2F, CM_M2B)[d] + 128]
        M3 = cm[:, (CM_M3F, CM_M3B)[d]:(CM_M3F, CM_M3B)[d] + 128]
        M4 = cm[:, (CM_M4F, CM_M4B)[d]:(CM_M4F, CM_M4B)[d] + 4]
        M4n = cm[:, CM_M4F:CM_M4F + 4]
        for pr in range(2):
            S.add("dve", lambda e, pr=pr: e.memset(D1[pr][:, :, 0:1], 0.0), writes=[D1[pr].key])
        def do_seg(seg):
            nt = len(seg)
            nch = 4 * nt
            def p1(ti, tile):
                n0 = tile * 128
                AT, QH, VZ = ATs[ti], QHs[ti], VZs[ti]
                S.add("sp", lambda e, n0=n0: e.dma_start(out=rtok[:], in_=self.PT[n0:n0 + 128, (PT_FF, PT_FB)[d]:(PT_FF, PT_FB)[d] + 256]),
                      writes=[rtok.key], dma=True)
                S.add("sp", lambda e, n0=n0: e.dma_start(out=vtok[:], in_=self.PT[n0:n0 + 128, PT_I:PT_I + 256]),
                      writes=[vtok.key], dma=True)
                S.add("sp", lambda e, n0=n0: e.dma_start(out=qfm[:], in_=PFr[:, PF_Q:PF_Q + 2, n0:n0 + 128]),
                      writes=[qfm.key], dma=True)
                fbk = (PF_FF, PF_FB)[d]
                S.add("sp", lambda e, n0=n0, fbk=fbk: e.dma_start(out=rfm[:], in_=PFr[:, fbk:fbk + 2, n0:n0 + 128]),
                      writes=[rfm.key], dma=True)
                S.add("act", lambda e: e.activation(out=sig[:], in_=rtok[:], func=AF.Sigmoid), reads=[rtok.key], writes=[sig.key])
                S.add("act", lambda e: e.copy(out=vtok_bf[:], in_=vtok[:]), reads=[vtok.key], writes=[vtok_bf.key])
                S.add("act", lambda e: e.activation(out=sneg[:], in_=rfm[:], func=AF.Sigmoid, scale=-1.0),
                      reads=[rfm.key], writes=[sneg.key])
                S.add("dve", lambda e: e.tensor_tensor(out=tmpk[:], in0=sig[:], in1=OMLt[:, d, :], op=ALU.mult),
                      reads=[sig.key, OMLt.key], writes=[tmpk.key])
                S.add("dve", lambda e: e.scalar_tensor_tensor(out=lf[:], in0=tmpk[:], scalar=1e-20, in1=LBt[:, d, :],
                                                              op0=ALU.max, op1=ALU.add),
                      reads=[tmpk.key, LBt.key], writes=[lf.key])
                S.add("dve", lambda e: e.tensor_tensor(out=ktok[:], in0=OMLt[:, d, :], in1=tmpk[:], op=ALU.subtract),
                      reads=[tmpk.key, OMLt.key], writes=[ktok.key])
                S.add("act", lambda e: e.activation(out=lf[:], in_=lf[:], func=AF.Ln), reads=[lf.key], writes=[lf.key])
                for pr in range(2):
                    S.add("pe", lambda e, pr=pr: e.matmul(pA[:, pr * 128:(pr + 1) * 128], lhsT=lf[:, pr * 128:(pr + 1) * 128], rhs=M1,
                                                          start=True, stop=True), reads=[lf.key, cm.key], writes=[pA.key])
                for pr in range(2):
                    S.add("pe", lambda e, pr=pr: e.matmul(pA[:, 256 + pr * 128:256 + (pr + 1) * 128], lhsT=lf[:, pr * 128:(pr + 1) * 128],
                                                          rhs=M2, start=True, stop=True), reads=[lf.key, cm.key], writes=[pA.key])
                S.add("pe", lambda e: e.matmul(pB[:, 0:256], lhsT=M3, rhs=lf[:], start=True, stop=True),
                      reads=[lf.key, cm.key], writes=[pB.key])
                for pr in range(2):
                    S.add("pe", lambda e, pr=pr: e.matmul(pB[:, 256 + pr * 4:260 + pr * 4], lhsT=lf[:, pr * 128:(pr + 1) * 128], rhs=M4,
                                                          start=True, stop=True), reads=[lf.key, cm.key], writes=[pB.key])
                S.add("dve", lambda e: e.tensor_scalar(out=P1c[:], in0=pA[:, 0:256], scalar1=40.0, scalar2=-40.0, op0=ALU.min, op1=ALU.max),
                      reads=[pA.key], writes=[P1c.key])
                S.add("act", lambda e: e.activation(out=Ep[:], in_=P1c[:], func=AF.Exp), reads=[P1c.key], writes=[Ep.key])
                S.add("act", lambda e: e.activation(out=En[:], in_=P1c[:], func=AF.Exp, scale=-1.0), reads=[P1c.key], writes=[En.key])
                S.add("act", lambda e: e.activation(out=E2[:], in_=pA[:, 256:512], func=AF.Exp), reads=[pA.key], writes=[E2.key])
                S.add("act", lambda e: e.activation(out=E3[:], in_=pB[:, 0:256], func=AF.Exp), reads=[pB.key], writes=[E3.key])
                c0 = ti * 4
                for pr in range(2):
                    S.add("act", lambda e, c0=c0, pr=pr: e.activation(out=DEC[:, pr, c0:c0 + 4], in_=pB[:, 256 + pr * 4:260 + pr * 4],
                                                                      func=AF.Exp), reads=[pB.key], writes=[DEC.key])
                qf2 = qfm[:].rearrange("p a b -> p (a b)")
                S.add("dve", lambda e: e.tensor_tensor(out=qt[:].rearrange("p a b -> p (a b)"), in0=qf2, in1=Ep[:], op=ALU.mult),
                      reads=[qfm.key, Ep.key], writes=[qt.key])
                S.add("dve", lambda e, QH=QH: e.tensor_tensor(out=QH[:].rearrange("p a b -> p (a b)"), in0=qf2, in1=E2[:], op=ALU.mult),
                      reads=[qfm.key, E2.key], writes=[QH.key])
                for pr in range(2):
                    for h2 in range(2):
                        S.add("dve", lambda e, pr=pr, h2=h2: e.scalar_tensor_tensor(
                            out=kt[:, pr, h2, :], in0=sneg[:, pr, :], scalar=omlfh[:, d, pr, h2:h2 + 1],
                            in1=En[:, pr * 128:(pr + 1) * 128], op0=ALU.mult, op1=ALU.mult),
                            reads=[sneg.key, omlfh.key, En.key], writes=[kt.key])
                for j in range(4):
                    S.add("dve", lambda e, j=j: e.scalar_tensor_tensor(out=khat[:, j, :], in0=ktok[:], scalar=M4n[:, j:j + 1], in1=E3[:],
                                                                       op0=ALU.mult, op1=ALU.mult),
                          reads=[ktok.key, cm.key, E3.key], writes=[khat.key])
                if d == 0 or True:
                    hm = cm[:, CM_HM:CM_HM + 256].rearrange("p (a b) -> p a b", a=2)
                    S.add("dve", lambda e, VZ=VZ, hm=hm: e.tensor_tensor(
                        out=VZ[:], in0=vtok[:].rearrange("p (a b) -> p a b", a=2).unsqueeze(2).to_broadcast([128, 2, 2, 128]),
                        in1=hm.unsqueeze(1).to_broadcast([128, 2, 2, 128]), op=ALU.mult),
                        reads=[vtok.key, cm.key], writes=[VZ.key])
                for h in range(4):
                    pr, h2 = h // 2, h % 2
                    S.add("pe", lambda e, h=h, pr=pr, h2=h2: e.matmul(pC[:, h * 128:(h + 1) * 128], lhsT=kt[:, pr, h2, :],
                                                                      rhs=qt[:, pr, :], start=True, stop=True),
                          reads=[kt.key, qt.key], writes=[pC.key])
                msk = cm[:, (CM_M2F, CM_M2B)[d]:(CM_M2F, CM_M2B)[d] + 128]
                S.add("dve", lambda e, AT=AT, msk=msk: e.tensor_tensor(out=AT[:], in0=pC[:].rearrange("p (a b) -> p a b", a=4),
                                                                       in1=msk.unsqueeze(1).to_broadcast([128, 4, 128]), op=ALU.mult),
                      reads=[pC.key, cm.key], writes=[AT.key])
                for pr in range(2):
                    for jj in range(4):
                        j = jj if d == 0 else 3 - jj
                        S.add("pe", lambda e, pr=pr, jj=jj, j=j: e.matmul(pU[:, jj * 128:(jj + 1) * 128], lhsT=khat[:, j, pr * 128:(pr + 1) * 128],
                                                                          rhs=vtok_bf[:, pr * 128:(pr + 1) * 128], start=True, stop=True),
                              reads=[khat.key, vtok_bf.key], writes=[pU.key])
                    for h2 in range(2):
                        src = pU[h2 * 64:(h2 + 1) * 64, :].rearrange("p (a b) -> p a b", a=4)[:, :, h2 * 64:(h2 + 1) * 64]
                        dstv = D1[pr][h2 * 64:(h2 + 1) * 64, :, 1 + c0:1 + c0 + 4].rearrange("p v c -> p c v")
                        if h2 == 0:
                            S.add("act", lambda e, pr=pr, src=src, dstv=dstv: e.copy(out=dstv, in_=src),
                                  reads=[pU.key], writes=[D1[pr].key])
                        else:
                            S.add("dve", lambda e, pr=pr, src=src, dstv=dstv: e.tensor_copy(out=dstv, in_=src),
                                  reads=[pU.key], writes=[D1[pr].key])
            for ti, tile in enumerate(seg):
                p1(ti, tile)
            for pr in range(2):
                SB = Sblk[pr]
                S.add("dve", lambda e, pr=pr: e.tensor_copy(out=D0[pr][:, :, 1:1 + nch],
                                                            in_=DEC[:, pr, 0:nch].unsqueeze(1).to_broadcast([128, 64, nch])),
                      reads=[DEC.key], writes=[D0[pr].key])
                S.add("dve", lambda e, pr=pr: e.tensor_tensor_scan(
                    out=SO[pr][:].rearrange("p v c -> p (v c)"), data0=D0[pr][:].rearrange("p v c -> p (v c)"),
                    data1=D1[pr][:].rearrange("p v c -> p (v c)"), initial=0.0, op0=ALU.mult, op1=ALU.add),
                    reads=[D0[pr].key, D1[pr].key], writes=[SO[pr].key])
                S.add("act", lambda e, pr=pr, SB=SB: e.copy(out=SB[0:64, 0:nch, 0:64], in_=SO[pr][0:64, :, 0:nch].rearrange("p v c -> p c v")),
                      reads=[SO[pr].key], writes=[SB.key])
                S.add("act", lambda e, pr=pr, SB=SB: e.copy(out=SB[64:128, 0:nch, 64:128], in_=SO[pr][64:128, :, 0:nch].rearrange("p v c -> p c v")),
                      reads=[SO[pr].key], writes=[SB.key])
                S.add("dve", lambda e, pr=pr: e.tensor_copy(out=D1[pr][:, :, 0:1], in_=SO[pr][:, :, nch:nch + 1]),
                      reads=[SO[pr].key], writes=[D1[pr].key])
            def p2(ti, tile):
                n0 = tile * 128
                AT, QH, VZ = ATs[ti], QHs[ti], VZs[ti]
                for pr in range(2):
                    for h2 in range(2):
                        h = pr * 2 + h2
                        S.add("pe", lambda e, pr=pr, h2=h2, h=h, AT=AT, VZ=VZ: e.matmul(
                            pO[:, pr * 128:(pr + 1) * 128], lhsT=VZ[:, pr, h2, :], rhs=AT[:, h, :],
                            start=(pr == 0 and h2 == 0), stop=False, skip_group_check=True),
                            reads=[VZ.key, AT.key], writes=[pO.key])
                for pr in range(2):
                    for j in range(4):
                        jj = j if d == 0 else 3 - j
                        c = ti * 4 + jj
                        S.add("pe", lambda e, pr=pr, j=j, c=c, QH=QH: e.matmul(
                            pO[:, pr * 128 + j * 32:pr * 128 + (j + 1) * 32], lhsT=Sblk[pr][:, c, :], rhs=QH[:, pr, j * 32:(j + 1) * 32],
                            start=False, stop=(pr == 1 and j == 3), skip_group_check=True),
                            reads=[Sblk[pr].key, QH.key], writes=[pO.key])
                if d == 0:
                    S.add("act", lambda e: e.copy(out=osum[:].rearrange("p a b -> p (a b)"), in_=pO[:, 0:256]), reads=[pO.key], writes=[osum.key])
                    S.add("sp", lambda e, n0=n0: e.dma_start(out=OHGr[:, :, n0:n0 + 128], in_=osum[:]), reads=[osum.key], dma=True,
                          semkey="st_ohg")
                else:
                    S.add("sp", lambda e, n0=n0: e.dma_start(out=of_ld[:], in_=OHGr[:, :, n0:n0 + 128]), writes=[of_ld.key], dma=True)
                    S.add("sp", lambda e, n0=n0: e.dma_start(out=sgl[:], in_=PFr[:, PF_G:PF_G + 2, n0:n0 + 128]), writes=[sgl.key], dma=True)
                    S.add("dve", lambda e: e.tensor_tensor(out=osum[:].rearrange("p a b -> p (a b)"), in0=of_ld[:].rearrange("p a b -> p (a b)"),
                                                           in1=pO[:, 0:256], op=ALU.add), reads=[of_ld.key, pO.key], writes=[osum.key])
                    if "ohg" in self.dbg:
                        S.add("sp", lambda e, n0=n0: e.dma_start(out=self.dbg_out["ohg%d" % l].rearrange("(f p) n -> p f n", p=128)[:, :, n0:n0 + 128],
                                                                 in_=osum[:]), reads=[osum.key], dma=True, semkey="dbg_ohg")
                    S.add("act", lambda e: e.activation(out=osq[:], in_=osum[:], func=AF.Square), reads=[osum.key], writes=[osq.key])
                    S.add("pe", lambda e: e.matmul(pN[:, 0:256], lhsT=self.blk_bf[:], rhs=osq[:].rearrange("p a b -> p (a b)"),
                                                   start=True, stop=True), reads=[osq.key, self.blk_bf.key], writes=[pN.key])
                    S.add("dve", lambda e: e.tensor_scalar(out=orstd[:], in0=pN[:, 0:256], scalar1=1.0 / 64, scalar2=EPS,
                                                           op0=ALU.mult, op1=ALU.add), reads=[pN.key], writes=[orstd.key])
                    S.add("act", lambda e: e.activation(out=orstd[:], in_=orstd[:], func=AF.Ln), reads=[orstd.key], writes=[orstd.key])
                    S.add("act", lambda e: e.activation(out=orstd[:], in_=orstd[:], func=AF.Exp, scale=-0.5), reads=[orstd.key], writes=[orstd.key])
                    S.add("dve", lambda e: e.tensor_tensor(out=osum[:].rearrange("p a b -> p (a b)"), in0=osum[:].rearrange("p a b -> p (a b)"),
                                                           in1=orstd[:], op=ALU.mult), reads=[osum.key, orstd.key], writes=[osum.key])
                    wo = l * FV_L + FV_HGN
                    for pr in range(2):
                        S.add("dve", lambda e, pr=pr: e.scalar_tensor_tensor(out=hgo[:, pr, :], in0=osum[:, pr, :], scalar=fv[:, wo + pr:wo + pr + 1],
                                                                             in1=sgl[:, pr, :], op0=ALU.mult, op1=ALU.mult),
                              reads=[osum.key, fv.key, sgl.key], writes=[hgo.key])
                    S.add("sp", lambda e, n0=n0: e.dma_start(out=MIXr[:, 0:2, n0:n0 + 128], in_=hgo[:]), reads=[hgo.key], dma=True,
                          semkey="st_mixhg")
            for ti, tile in enumerate(seg):
                p2(ti, tile)
        for seg in _hg_tiles(self, d):
            do_seg(seg)
        self.barrier()
    for d in range(2):
        do_dir(d)
    self.reset(m)


def _ssd_io(self):
    nc = self.nc
    self.XSs = nc.dram_tensor("XSs", [self.NTOK, 512], F32).ap()
    self.BCf = nc.dram_tensor("BCfs", [512, self.NTOK], BF16).ap()
    self.Bts = nc.dram_tensor("Bts", [self.NTOK, 256], BF16).ap()
    self.YS = nc.dram_tensor("YSs", [self.NTOK, 512], F32).ap()


def _phase_ssd(self, l):
    S = self.S
    m = self.mark()
    cm, fv, rv, ps = self.cm, self.fv, self.rv, self.ps
    NT = self.NT
    PFr = self.PF.rearrange("(f p) n -> p f n", p=128)
    BCr = self.BCf.rearrange("(f p) n -> p f n", p=128)
    MIXr = self.MIXT.rearrange("(f p) n -> p f n", p=128)
    IDENT = cm[:, CM_ID:CM_ID + 128]
    ONESF = cm[:, CM_ONES:CM_ONES + 128]
    fo = l * FV_L
    ro = l * RV_L
    xin = self.sb("xin", [128, 8, 132], F32)
    acc = self.sb("acc", [128, 8, 128], F32)
    ctmp = self.sb("ctmp", [128, 8, 128], F32)
    bcb = self.sb("bcb", [128, 4, 128], BF16)
    xs_st = self.sb("xs_st", [128, 512], F32)
    bt_st = self.sb("bt_st", [128, 256], BF16)
    pX, pBt = ps[1], ps[2]
    CW = fv[:, fo + FV_CW:fo + FV_CW + 40].rearrange("p (j k) -> p j k", j=5)
    CB = fv[:, fo + FV_CB:fo + FV_CB + 8]

    def conv_tile(tile):
        n0 = tile * 128
        s_lo, s_hi = (0, CTX) if tile < 2 else (CTX, self.NTOK)
        lo, hi = max(n0 - 2, s_lo), min(n0 + 130, s_hi)
        S.add("dve", lambda e: e.memset(xin[:, :, 0:2], 0.0), writes=[xin.key])
        S.add("dve", lambda e: e.memset(xin[:, :, 130:132], 0.0), writes=[xin.key])
        S.add("sp", lambda e: e.dma_start(out=xin[:, :, lo - (n0 - 2):hi - (n0 - 2)], in_=PFr[:, PF_XBC:PF_XBC + 8, lo:hi]),
              writes=[xin.key], dma=True)
        S.add("dve", lambda e: e.tensor_tensor(out=acc[:], in0=xin[:, :, 0:128], in1=CW[:, 0, :].unsqueeze(2).to_broadcast([128, 8, 128]),
                                               op=ALU.mult), reads=[xin.key, fv.key], writes=[acc.key])
        for j in range(1, 5):
            S.add("pool", lambda e, j=j: e.tensor_tensor(out=ctmp[:], in0=xin[:, :, j:j + 128],
                                                         in1=CW[:, j, :].unsqueeze(2).to_broadcast([128, 8, 128]), op=ALU.mult),
                  reads=[xin.key, fv.key], writes=[ctmp.key])
            S.add("dve", lambda e: e.tensor_tensor(out=acc[:], in0=acc[:], in1=ctmp[:], op=ALU.add),
                  reads=[acc.key, ctmp.key], writes=[acc.key])
        S.add("dve", lambda e: e.tensor_tensor(out=acc[:], in0=acc[:], in1=CB.unsqueeze(2).to_broadcast([128, 8, 128]), op=ALU.add),
              reads=[acc.key, fv.key], writes=[acc.key])
        S.add("act", lambda e: e.activation(out=acc[:], in_=acc[:], func=AF.Silu), reads=[acc.key], writes=[acc.key])
        S.add("dve", lambda e: e.tensor_copy(out=bcb[:], in_=acc[:, 4:8, :]), reads=[acc.key], writes=[bcb.key])
        S.add("sp", lambda e: e.dma_start(out=BCr[:, :, n0:n0 + 128], in_=bcb[:]), reads=[bcb.key], dma=True, semkey="st_bcf")
        for k in range(4):
            S.add("pe", lambda e, k=k: e.transpose(out=pX[:, k * 128:(k + 1) * 128], in_=acc[:, k, :], identity=IDENT),
                  reads=[acc.key, cm.key], writes=[pX.key])
        S.add("act", lambda e: e.copy(out=xs_st[:], in_=pX[:]), reads=[pX.key], writes=[xs_st.key])
        S.add("sp", lambda e: e.dma_start(out=self.XSs[n0:n0 + 128, :], in_=xs_st[:]), reads=[xs_st.key], dma=True, semkey="st_xs")
        for k in range(2):
            S.add("pe", lambda e, k=k: e.transpose(out=pBt[:, k * 128:(k + 1) * 128], in_=acc[:, 4 + k, :], identity=IDENT),
                  reads=[acc.key, cm.key], writes=[pBt.key])
        S.add("dve", lambda e: e.tensor_copy(out=bt_st[:], in_=pBt[:, 0:256]), reads=[pBt.key], writes=[bt_st.key])
        S.add("sp", lambda e: e.dma_start(out=self.Bts[n0:n0 + 128, :], in_=bt_st[:]), reads=[bt_st.key], dma=True, semkey="st_bt")

    for tile in range(NT):
        conv_tile(tile)
    self.barrier()
    self.reset(m)
    m = self.mark()
    Arow = self.sb("Arow", [128, 16], F32)
    S.add("act", lambda e: e.activation(out=Arow[:], in_=rv[:, ro + RV_ALOG:ro + RV_ALOG + 16], func=AF.Exp),
          reads=[rv.key], writes=[Arow.key])
    S.add("dve", lambda e: e.tensor_scalar_mul(out=Arow[:], in0=Arow[:], scalar1=-1.0), reads=[Arow.key], writes=[Arow.key])
    DTB = rv[:, ro + RV_DTB:ro + RV_DTB + 16]
    D0 = [self.sb("sD0", [128, 256, 9], F32) for _ in range(2)]
    D1 = [self.sb("sD1", [128, 256, 9], F32) for _ in range(2)]
    SO = [self.sb("sSO", [128, 256, 9], F32) for _ in range(2)]
    Hbf = [self.sb("Hbf", [128, 8, 256], BF16) for _ in range(2)]
    YD = [self.sb("YD", [128, 512], F32) for _ in range(8)]
    CFs = [self.sb("CFs", [128, 2, 128], BF16) for _ in range(8)]
    ECs = [self.sb("ECs", [128, 8], F32) for _ in range(8)]
    dtr = self.sb("dtr", [128, 8], F32)
    dt = self.sb("dt", [128, 8], F32)
    av = self.sb("av", [128, 8], F32)
    ABC = self.sb("ABC", [128, 8, 128], F32)
    ncum = self.sb("ncum", [128, 8], F32)
    dend = self.sb("dend", [128, 8], F32)
    dect = self.sb("dect", [128, 8], F32)
    xst = self.sb("xst", [128, 512], F32)
    btl = self.sb("btl", [128, 256], BF16)
    bcl = self.sb("bcl", [128, 4, 128], BF16)
    Lsb = self.sb("Lsb", [128, 8, 128], F32)
    Wb = self.sb("Wb", [128, 8, 128], BF16)
    xdt = self.sb("xdt", [128, 8, 64], BF16)
    xw = self.sb("xw", [128, 8, 64], BF16)
    yo = self.sb("yo", [128, 512], F32)
    yf = self.sb("yf", [128, 512], F32)
    zt = self.sb("zt", [128, 512], F32)
    ssq = self.sb("ssq", [128, 2], F32)
    yT = self.sb("yT", [128, 4, 128], BF16)
    for g in range(2):
        S.add("dve", lambda e, g=g: e.memset(D0[g][:], 0.0), writes=[D0[g].key])
        S.add("dve", lambda e, g=g: e.memset(D1[g][:], 0.0), writes=[D1[g].key])
    pS, pLa, pLb, pG, pY, pH, pY2, pT = ps[0], ps[1], ps[2], ps[3], ps[4], ps[5], ps[6], ps[7]
    SSNrow = rv[:, ro + RV_SSN:ro + RV_SSN + 512]
    DSKrow = rv[:, ro + RV_DSK:ro + RV_DSK + 512]

    def do_dir(d):
        TRI = cm[:, (CM_TRIF, CM_TRIB)[d]:(CM_TRIF, CM_TRIB)[d] + 128]
        NEGM = cm[:, (CM_NEGF, CM_NEGB)[d]:(CM_NEGF, CM_NEGB)[d] + 128]
        for g in range(2):
            S.add("dve", lambda e, g=g: e.memset(D1[g][:, :, 0:1], 0.0), writes=[D1[g].key])

        def do_seg(seg):
            nt = len(seg)

            def p1(ti, tile):
                n0 = tile * 128
                S.add("sp", lambda e: e.dma_start(out=dtr[:], in_=self.PT[n0:n0 + 128, PT_DT + d * 8:PT_DT + d * 8 + 8]),
                      writes=[dtr.key], dma=True)
                S.add("sp", lambda e: e.dma_start(out=xst[:], in_=self.XSs[n0:n0 + 128, :]), writes=[xst.key], dma=True)
                S.add("sp", lambda e: e.dma_start(out=btl[:], in_=self.Bts[n0:n0 + 128, :]), writes=[btl.key], dma=True)
                S.add("sp", lambda e: e.dma_start(out=bcl[:], in_=BCr[:, :, n0:n0 + 128]), writes=[bcl.key], dma=True)
                S.add("dve", lambda e: e.tensor_tensor(out=dt[:], in0=dtr[:], in1=DTB[:, d * 8:d * 8 + 8], op=ALU.add),
                      reads=[dtr.key, rv.key], writes=[dt.key])
                S.add("act", lambda e: e.activation(out=dt[:], in_=dt[:], func=AF.Exp), reads=[dt.key], writes=[dt.key])
                S.add("dve", lambda e: e.tensor_scalar_add(out=dt[:], in0=dt[:], scalar1=1.0), reads=[dt.key], writes=[dt.key])
                S.add("act", lambda e: e.activation(out=dt[:], in_=dt[:], func=AF.Ln), reads=[dt.key], writes=[dt.key])
                S.add("dve", lambda e: e.tensor_tensor(out=av[:], in0=dt[:], in1=Arow[:, d * 8:d * 8 + 8], op=ALU.mult),
                      reads=[dt.key, Arow.key], writes=[av.key])
                S.add("dve", lambda e: e.tensor_copy(out=ABC[:], in_=av[:].unsqueeze(2).to_broadcast([128, 8, 128])),
                      reads=[av.key], writes=[ABC.key])
                S.add("pe", lambda e: e.matmul(pS[:, 0:8], lhsT=TRI, rhs=av[:], start=True, stop=True), reads=[av.key, cm.key], writes=[pS.key])
                S.add("pe", lambda e: e.matmul(pS[:, 8:16], lhsT=ONESF, rhs=av[:], start=True, stop=True), reads=[av.key, cm.key], writes=[pS.key])
                S.add("dve", lambda e: e.tensor_scalar_mul(out=ncum[:], in0=pS[:, 0:8], scalar1=-1.0), reads=[pS.key], writes=[ncum.key])
                EC = ECs[ti]
                S.add("act", lambda e: e.activation(out=EC[:], in_=pS[:, 0:8], func=AF.Exp), reads=[pS.key], writes=[EC.key])
                S.add("dve", lambda e: e.tensor_tensor(out=dend[:], in0=pS[:, 8:16], in1=ncum[:], op=ALU.add),
                      reads=[pS.key, ncum.key], writes=[dend.key])
                S.add("act", lambda e: e.activation(out=dend[:], in_=dend[:], func=AF.Exp), reads=[dend.key], writes=[dend.key])
                S.add("act", lambda e: e.activation(out=dect[:], in_=pS[:, 8:16], func=AF.Exp), reads=[pS.key], writes=[dect.key])
                for h in range(8):
                    pl = (pLa, pLb)[h // 4]
                    hc = (h % 4) * 128
                    S.add("pe", lambda e, h=h, pl=pl, hc=hc: e.matmul(pl[:, hc:hc + 128], lhsT=ABC[:, h, :], rhs=TRI, start=True, stop=False),
                          reads=[ABC.key, cm.key], writes=[pl.key])
                    S.add("pe", lambda e, h=h, pl=pl, hc=hc: e.matmul(pl[:, hc:hc + 128], lhsT=IDENT, rhs=NEGM, start=False, stop=True),
                          reads=[cm.key], writes=[pl.key])
                    S.add("act", lambda e, h=h, pl=pl, hc=hc: e.activation(out=Lsb[:, h, :], in_=pl[:, hc:hc + 128], func=AF.Exp,
                                                                           bias=ncum[:, h:h + 1]),
                          reads=[pl.key, ncum.key], writes=[Lsb.key])
                for g in range(2):
                    S.add("pe", lambda e, g=g: e.matmul(pG[:, g * 128:(g + 1) * 128], lhsT=bcl[:, g, :], rhs=bcl[:, 2 + g, :],
                                                        start=True, stop=True), reads=[bcl.key], writes=[pG.key])
                for g in range(2):
                    S.add("dve", lambda e, g=g: e.tensor_tensor(
                        out=Wb[:, g * 4:(g + 1) * 4, :], in0=Lsb[:, g * 4:(g + 1) * 4, :],
                        in1=pG[:, g * 128:(g + 1) * 128].unsqueeze(1).to_broadcast([128, 4, 128]), op=ALU.mult),
                        reads=[Lsb.key, pG.key], writes=[Wb.key])
                S.add("dve", lambda e: e.tensor_tensor(out=xdt[:], in0=xst[:].rearrange("p (h q) -> p h q", h=8),
                                                       in1=dt[:].unsqueeze(2).to_broadcast([128, 8, 64]), op=ALU.mult),
                      reads=[xst.key, dt.key], writes=[xdt.key])
                for h in range(8):
                    S.add("pe", lambda e, h=h: e.matmul(pY[:, h * 64:(h + 1) * 64], lhsT=Wb[:, h, :], rhs=xdt[:, h, :], start=True, stop=True),
                          reads=[Wb.key, xdt.key], writes=[pY.key])
                Y = YD[ti]
                S.add("act", lambda e: e.copy(out=Y[:], in_=pY[:]), reads=[pY.key], writes=[Y.key])
                CF = CFs[ti]
                S.add("dve", lambda e: e.tensor_copy(out=CF[:], in_=bcl[:, 2:4, :]), reads=[bcl.key], writes=[CF.key])
                S.add("dve", lambda e: e.tensor_tensor(out=xw[:], in0=xdt[:], in1=dend[:].unsqueeze(2).to_broadcast([128, 8, 64]), op=ALU.mult),
                      reads=[xdt.key, dend.key], writes=[xw.key])
                for g in range(2):
                    S.add("pe", lambda e, g=g: e.matmul(pH[:, g * 256:(g + 1) * 256], lhsT=btl[:, g * 128:(g + 1) * 128],
                                                        rhs=xw[:, g * 4:(g + 1) * 4, :].rearrange("p h q -> p (h q)"), start=True, stop=True),
                          reads=[btl.key, xw.key], writes=[pH.key])
                for g in range(2):
                    S.add("act", lambda e, g=g: e.copy(out=D1[g][:, :, 1 + ti:2 + ti], in_=pH[:, g * 256:(g + 1) * 256].unsqueeze(2)),
                          reads=[pH.key], writes=[D1[g].key])
                    S.add("dve", lambda e, g=g: e.tensor_copy(
                        out=D0[g][:, :, 1 + ti:2 + ti].rearrange("p (h q) o -> p h (q o)", h=4),
                        in_=dect[:, g * 4:(g + 1) * 4].unsqueeze(2).to_broadcast([128, 4, 64])),
                        reads=[dect.key], writes=[D0[g].key])

            for ti, tile in enumerate(seg):
                p1(ti, tile)
            for g in range(2):
                S.add("dve", lambda e, g=g: e.tensor_tensor_scan(
                    out=SO[g][:].rearrange("p v c -> p (v c)"), data0=D0[g][:].rearrange("p v c -> p (v c)"),
                    data1=D1[g][:].rearrange("p v c -> p (v c)"), initial=0.0, op0=ALU.mult, op1=ALU.add),
                    reads=[D0[g].key, D1[g].key], writes=[SO[g].key])
                S.add("act", lambda e, g=g: e.copy(out=Hbf[g][:, 0:nt, :], in_=SO[g][:, :, 0:nt].rearrange("p v c -> p c v")),
                      reads=[SO[g].key], writes=[Hbf[g].key])
                S.add("dve", lambda e, g=g: e.tensor_copy(out=D1[g][:, :, 0:1], in_=SO[g][:, :, nt:nt + 1]),
                      reads=[SO[g].key], writes=[D1[g].key])

            def p2(ti, tile):
                n0 = tile * 128
                CF, EC, Y = CFs[ti], ECs[ti], YD[ti]
                for g in range(2):
                    S.add("pe", lambda e, g=g: e.matmul(pY2[:, g * 256:(g + 1) * 256], lhsT=CF[:, g, :], rhs=Hbf[g][:, ti, :], start=True, stop=True),
                          reads=[CF.key, Hbf[g].key], writes=[pY2.key])
                S.add("dve", lambda e: e.tensor_tensor(out=yo[:].rearrange("p (h q) -> p h q", h=8), in0=pY2[:].rearrange("p (h q) -> p h q", h=8),
                                                       in1=EC[:].unsqueeze(2).to_broadcast([128, 8, 64]), op=ALU.mult),
                      reads=[pY2.key, EC.key], writes=[yo.key])
                S.add("dve", lambda e: e.tensor_tensor(out=yo[:], in0=yo[:], in1=Y[:], op=ALU.add), reads=[yo.key, Y.key], writes=[yo.key])
                if d == 0:
                    S.add("sp", lambda e: e.dma_start(out=self.YS[n0:n0 + 128, :], in_=yo[:]), reads=[yo.key], dma=True, semkey="st_ys")
                    return
                S.add("sp", lambda e: e.dma_start(out=yf[:], in_=self.YS[n0:n0 + 128, :]), writes=[yf.key], dma=True)
                S.add("sp", lambda e: e.dma_start(out=xst[:], in_=self.XSs[n0:n0 + 128, :]), writes=[xst.key], dma=True)
                S.add("sp", lambda e: e.dma_start(out=zt[:], in_=self.PT[n0:n0 + 128, PT_Z:PT_Z + 512]), writes=[zt.key], dma=True)
                S.add("dve", lambda e: e.tensor_tensor(out=yo[:], in0=yo[:], in1=yf[:], op=ALU.add), reads=[yo.key, yf.key], writes=[yo.key])
                S.add("dve", lambda e: e.tensor_tensor(out=xst[:], in0=xst[:], in1=DSKrow, op=ALU.mult), reads=[xst.key, rv.key], writes=[xst.key])
                S.add("dve", lambda e: e.tensor_tensor(out=yo[:], in0=yo[:], in1=xst[:], op=ALU.add), reads=[yo.key, xst.key], writes=[yo.key])
                if "yssm" in self.dbg:
                    S.add("sp", lambda e: e.dma_start(out=self.dbg_out["yssm%d" % l][n0:n0 + 128, :], in_=yo[:]), reads=[yo.key], dma=True,
                          semkey="dbg_yssm")
                S.add("act", lambda e: e.activation(out=zt[:], in_=zt[:], func=AF.Silu), reads=[zt.key], writes=[zt.key])
                S.add("dve", lambda e: e.tensor_tensor(out=yo[:], in0=yo[:], in1=zt[:], op=ALU.mult), reads=[yo.key, zt.key], writes=[yo.key])
                S.add("act", lambda e: e.activation(out=zt[:], in_=yo[:], func=AF.Square, accum_out=ssq[:, 0:1]),
                      reads=[yo.key], writes=[zt.key, ssq.key])
                S.add("dve", lambda e: e.tensor_scalar(out=ssq[:, 1:2], in0=ssq[:, 0:1], scalar1=1.0 / 512, scalar2=EPS, op0=ALU.mult, op1=ALU.add),
                      reads=[ssq.key], writes=[ssq.key])
                S.add("act", lambda e: e.activation(out=ssq[:, 1:2], in_=ssq[:, 1:2], func=AF.Ln), reads=[ssq.key], writes=[ssq.key])
                S.add("act", lambda e: e.activation(out=ssq[:, 1:2], in_=ssq[:, 1:2], func=AF.Exp, scale=-0.5), reads=[ssq.key], writes=[ssq.key])
                S.add("dve", lambda e: e.scalar_tensor_tensor(out=yo[:], in0=yo[:], scalar=ssq[:, 1:2], in1=SSNrow, op0=ALU.mult, op1=ALU.mult),
                      reads=[yo.key, ssq.key, rv.key], writes=[yo.key])
                for k in range(4):
                    S.add("pe", lambda e, k=k: e.transpose(out=pT[:, k * 128:(k + 1) * 128], in_=yo[:, k * 128:(k + 1) * 128], identity=IDENT),
                          reads=[yo.key, cm.key], writes=[pT.key])
                S.add("act", lambda e: e.copy(out=yT[:].rearrange("p a b -> p (a b)"), in_=pT[:]), reads=[pT.key], writes=[yT.key])
                S.add("sp", lambda e: e.dma_start(out=MIXr[:, 4:8, n0:n0 + 128], in_=yT[:]), reads=[yT.key], dma=True, semkey="st_mixssd")

            for ti, tile in enumerate(seg):
                p2(ti, tile)

        for seg in _hg_tiles(self, d):
            do_seg(seg)
        self.barrier()

    for d in range(2):
        do_dir(d)
    self.reset(m)


NAB_N = 5 * 4 * 5 * 128


def make_nabias(rpb, nlat):
    rows = 2 * nlat
    out = np.full((128, 5, 4, 5, 128), NEG, np.float32)
    its = [0, 1, 2, nlat - 2, nlat - 1]
    p = np.arange(128)
    q = np.arange(128)
    for v, it in enumerate(its):
        kt0 = min(max(it - 2, 0), nlat - 5)
        r = 2 * it + q // 64
        cq = q % 64
        r0 = np.clip(r - 4, 0, rows - 8)
        c0 = np.clip(cq - 8, 0, 48)
        for kt in range(5):
            rk = 2 * (kt0 + kt) + p // 64
            ck = p % 64
            inw = ((rk[:, None] >= r0[None, :]) & (rk[:, None] < r0[None, :] + 8)
                   & (ck[:, None] >= c0[None, :]) & (ck[:, None] < c0[None, :] + 16))
            dr = np.clip(rk[:, None] - r[None, :] + 7, 0, 14)
            dc = np.clip(ck[:, None] - cq[None, :], -15, 15) + 15
            for h in range(4):
                b = rpb[h][dr, dc]
                out[:, v, h, kt, :] = np.where(inw, b, NEG)
    return out.reshape(128, NAB_N)


def _na_io(self):
    nc = self.nc
    self.nabias = nc.dram_tensor("nabias", [DEPTH, 128, NAB_N], F32, kind="ExternalInput").ap()


def _phase_na(self, l):
    S = self.S
    m = self.mark()
    cm, fv, rv, ps = self.cm, self.fv, self.rv, self.ps
    NT, nlat, NTOK = self.NT, self.nlat, self.NTOK
    last = (l == DEPTH - 1)
    PFr = self.PF.rearrange("(f p) n -> p f n", p=128)
    MIXr = self.MIXT.rearrange("(f p) n -> p f n", p=128)
    IDENT = cm[:, CM_ID:CM_ID + 128]
    ro = l * RV_L
    NANrow = rv[:, ro + RV_NAN:ro + RV_NAN + 256]
    KT = self.sb("KT", [128, 2, NTOK], BF16)
    Vaug = self.sb("Vaug", [128, NT, 4, 65], BF16)
    BT = self.sb("BT", [128, 5, 4, 5, 128], F32)
    qf = self.sb("qf", [128, 2, 128], F32)
    QZ = self.sb("QZ", [128, 2, 2, 128], BF16)
    mx = self.sb("mx", [128, 2], F32)
    DG = self.sb("DG", [128, 128], BF16)
    PTs = self.sb("PTs", [128, 896], BF16)
    rc = self.sb("rc", [128, 4], F32)
    onat = self.sb("onat", [128, 4, 64], F32)
    junk = self.sb("junk", [128, 256], F32)
    ssq = self.sb("nssq", [128, 2], F32)
    oT = self.sb("oT", [128, 2, 128], BF16)
    hrm = self.sb("hrm", [128, 2], F32)
    SB, ST = self.psbig[0], self.psbig[1]
    pOV, pT = ps[4], ps[5]
    for pr in range(2):
        S.add("pool", lambda e, pr=pr: e.dma_start(out=KT[:, pr, :], in_=PFr[:, PF_KA + pr, :]), writes=[KT.key], dma=True)
    S.add("sp", lambda e: e.dma_start(out=BT[:].rearrange("p a b c d -> p (a b c d)"), in_=self.nabias[l]), writes=[BT.key], dma=True)
    S.add("dve", lambda e: e.memset(Vaug[:, :, :, 64:65], 1.0), writes=[Vaug.key])
    for t in range(NT):
        S.add("pool", lambda e, t=t: e.dma_start(out=Vaug[:, t, :, 0:64],
                                                  in_=self.PT[t * 128:(t + 1) * 128, PT_VA:PT_VA + 256].rearrange("p (h q) -> p h q", h=4)),
              writes=[Vaug.key], dma=True)
    for h2 in range(2):
        S.add("dve", lambda e, h2=h2: e.tensor_copy(out=hrm[:, h2:h2 + 1], in_=cm[:, CM_BLK + h2 * 64:CM_BLK + h2 * 64 + 1]),
              reads=[cm.key], writes=[hrm.key])

    def q_tile(tile, keytiles, var):
        n0 = tile * 128
        nk = len(keytiles)
        nloc = 5 if var is not None else 0
        S.add("sp", lambda e: e.dma_start(out=qf[:], in_=PFr[:, PF_QA:PF_QA + 2, n0:n0 + 128]), writes=[qf.key], dma=True)
        S.add("dve", lambda e: e.tensor_tensor(out=QZ[:], in0=qf[:].unsqueeze(2).to_broadcast([128, 2, 2, 128]),
                                               in1=hrm[:].unsqueeze(1).unsqueeze(3).to_broadcast([128, 2, 2, 128]), op=ALU.mult),
              reads=[qf.key, hrm.key], writes=[QZ.key])
        for h in range(4):
            pr, h2 = h // 2, h % 2
            col = 0
            runs = []
            i = 0
            while i < nk:
                j = i
                while j + 1 < nk and keytiles[j + 1] == keytiles[j] + 1 and (j + 1 - i) < 4 and ((col + (j + 1 - i) * 128) % 512 != 0):
                    j += 1
                runs.append((keytiles[i], j - i + 1, col))
                col += (j - i + 1) * 128
                i = j + 1
            for (kt_, cnt, c_) in runs:
                S.add("pe", lambda e, pr=pr, h2=h2, kt_=kt_, cnt=cnt, c_=c_: e.matmul(
                    SB[:, c_:c_ + cnt * 128], lhsT=QZ[:, pr, h2, :], rhs=KT[:, pr, kt_ * 128:(kt_ + cnt) * 128], start=True, stop=True),
                    reads=[QZ.key, KT.key], writes=[SB.key])
            S.add("dve", lambda e: e.reduce_max(out=mx[:, 0:1], in_=SB[:, 0:nk * 128], axis=AX.X), reads=[SB.key], writes=[mx.key])
            S.add("dve", lambda e: e.tensor_scalar_mul(out=mx[:, 1:2], in0=mx[:, 0:1], scalar1=-1.0), reads=[mx.key], writes=[mx.key])
            S.add("dve", lambda e: e.tensor_scalar_mul(out=DG[:], in0=IDENT, scalar1=mx[:, 1:2]), reads=[mx.key, cm.key], writes=[DG.key])
            for kk, kt_ in enumerate(keytiles):
                dst = ST[:, kk * 128:(kk + 1) * 128]
                hasb = kk < nloc
                S.add("pe", lambda e, pr=pr, h2=h2, kt_=kt_, dst=dst: e.matmul(dst, lhsT=KT[:, pr, kt_ * 128:(kt_ + 1) * 128], rhs=QZ[:, pr, h2, :],
                                                                               start=True, stop=False),
                      reads=[QZ.key, KT.key], writes=[ST.key])
                S.add("pe", lambda e, dst=dst, hasb=hasb: e.matmul(dst, lhsT=self.ones_bf[:], rhs=DG[:], start=False, stop=(not hasb)),
                      reads=[DG.key, self.ones_bf.key], writes=[ST.key])
                if hasb:
                    S.add("pe", lambda e, dst=dst, h=h, kk=kk: e.matmul(dst, lhsT=IDENT, rhs=BT[:, var, h, kk, :], start=False, stop=True),
                          reads=[BT.key, cm.key], writes=[ST.key])
            S.add("act", lambda e: e.activation(out=PTs[:, 0:nk * 128], in_=ST[:, 0:nk * 128], func=AF.Exp), reads=[ST.key], writes=[PTs.key])
            for kk, kt_ in enumerate(keytiles):
                S.add("pe", lambda e, kk=kk, kt_=kt_, h=h: e.matmul(pOV[:, h * 65:(h + 1) * 65], lhsT=PTs[:, kk * 128:(kk + 1) * 128],
                                                                    rhs=Vaug[:, kt_, h, :], start=(kk == 0), stop=(kk == nk - 1)),
                      reads=[PTs.key, Vaug.key], writes=[pOV.key])
        OVv = pOV[:, 0:260].rearrange("p (h c) -> p h c", h=4)
        S.add("dve", lambda e: e.reciprocal(out=rc[:], in_=OVv[:, :, 64]), reads=[pOV.key], writes=[rc.key])
        S.add("dve", lambda e: e.tensor_tensor(out=onat[:], in0=OVv[:, :, 0:64], in1=rc[:].unsqueeze(2).to_broadcast([128, 4, 64]), op=ALU.mult),
              reads=[pOV.key, rc.key], writes=[onat.key])
        o2 = onat[:].rearrange("p h c -> p (h c)")
        if "naraw" in self.dbg:
            S.add("sp", lambda e: e.dma_start(out=self.dbg_out["naraw%d" % l][n0:n0 + 128, :], in_=o2), reads=[onat.key], dma=True,
                  semkey="dbg_naraw")
        S.add("act", lambda e: e.activation(out=junk[:], in_=o2, func=AF.Square, accum_out=ssq[:, 0:1]), reads=[onat.key],
              writes=[junk.key, ssq.key])
        S.add("dve", lambda e: e.tensor_scalar(out=ssq[:, 1:2], in0=ssq[:, 0:1], scalar1=1.0 / 256, scalar2=EPS, op0=ALU.mult, op1=ALU.add),
              reads=[ssq.key], writes=[ssq.key])
        S.add("act", lambda e: e.activation(out=ssq[:, 1:2], in_=ssq[:, 1:2], func=AF.Ln), reads=[ssq.key], writes=[ssq.key])
        S.add("act", lambda e: e.activation(out=ssq[:, 1:2], in_=ssq[:, 1:2], func=AF.Exp, scale=-0.5), reads=[ssq.key], writes=[ssq.key])
        S.add("dve", lambda e: e.scalar_tensor_tensor(out=junk[:], in0=o2, scalar=ssq[:, 1:2], in1=NANrow, op0=ALU.mult, op1=ALU.mult),
              reads=[onat.key, ssq.key, rv.key, junk.key], writes=[junk.key])
        for k in range(2):
            S.add("pe", lambda e, k=k: e.transpose(out=pT[:, k * 128:(k + 1) * 128], in_=junk[:, k * 128:(k + 1) * 128], identity=IDENT),
                  reads=[junk.key, cm.key], writes=[pT.key])
        S.add("act", lambda e: e.copy(out=oT[:].rearrange("p a b -> p (a b)"), in_=pT[:, 0:256]), reads=[pT.key], writes=[oT.key])
        S.add("sp", lambda e: e.dma_start(out=MIXr[:, 2:4, n0:n0 + 128], in_=oT[:]), reads=[oT.key], dma=True, semkey="st_mixna")

    if not last:
        for tile in range(2):
            q_tile(tile, [0, 1], None)
    for it in range(nlat):
        var = 0 if it == 0 else 1 if it == 1 else 3 if it == nlat - 2 else 4 if it == nlat - 1 else 2
        kt0 = min(max(it - 2, 0), nlat - 5)
        q_tile(it + 2, [kt0 + 2 + k for k in range(5)] + [0, 1], var)
    self.barrier()
    self.reset(m)


NLAT_FULL = 64
_CACHE = {}


def kernel(**inputs):
    inp = {k: np.asarray(v) for k, v in inputs.items()}
    nlat = NLAT_FULL
    B = inp["x"].shape[0]
    kb = KB(nlat=nlat)
    nc = kb.build()
    maps = make_in_maps(inp, nlat, list(range(B)))
    res = run_bass_kernel_spmd(nc, maps, core_ids=list(range(B)))
    out = np.stack([np.asarray(res.results[b]["outT"]).T for b in range(B)])
    return np.ascontiguousarray(out.astype(np.float32))
```

```python
import contextlib
import numpy as np
import concourse.bass as bass
import concourse.mybir as mybir
from concourse.bass_utils import run_bass_kernel_spmd

F32 = mybir.dt.float32
BF16 = mybir.dt.bfloat16
AF = mybir.ActivationFunctionType
ALU = mybir.AluOpType
AX = mybir.AxisListType

D = 1024
DEPTH = 2
DFF = 2816
CTX = 256
GW = 64
EPS = 1e-6
ENGS = ("pe", "act", "dve", "pool", "sp")
PH = "__ph"


class Op:
    __slots__ = ("eng", "fn", "deps", "dma", "semkey", "sig", "val", "idx", "slot")

    def __init__(self, eng, fn, dma, semkey):
        self.eng = eng
        self.fn = fn
        self.deps = {}
        self.dma = dma
        self.semkey = semkey
        self.sig = False
        self.val = 0
        self.idx = 0
        self.slot = None


class Sched:
    def __init__(self, nc):
        self.nc = nc
        self.ops = []
        self.last_w = {}
        self.readers = {}
        self.slotmap = {}

    def begin_record(self):
        self.rec = []
        self.rec_stage = 0

    def next_stage(self):
        if getattr(self, "rec", None) is not None:
            self.rec_stage += 1

    def end_record(self):
        r = self.rec
        self.rec = None
        return r

    def replay_staged(self, recs):
        nst = 1 + max((st for r in recs for (st, a, k) in r), default=0)
        for st in range(nst):
            for r in recs:
                for (s_, a, k) in r:
                    if s_ == st:
                        self.add(*a, **k)

    def add(self, eng, fn, reads=(), writes=(), dma=False, semkey=None, barrier=False):
        if getattr(self, "rec", None) is not None:
            self.rec.append((self.rec_stage, (eng, fn), dict(reads=reads, writes=writes, dma=dma, semkey=semkey, barrier=barrier)))
            return None
        reads = list(reads)
        writes = list(writes)
        if barrier:
            writes.append(PH)
        else:
            reads.append(PH)
        op = Op(eng, fn, dma, semkey if semkey is not None else (writes[0] if (dma and writes) else None))
        op.idx = len(self.ops)
        if barrier:
            self.slotmap = {}
        if dma:
            if eng == "pool":
                op.slot = ("p", op.semkey)
            else:
                if op.semkey not in self.slotmap:
                    self.slotmap[op.semkey] = len(self.slotmap)
                op.slot = self.slotmap[op.semkey]
        cand = {}
        for k in reads + writes:
            w = self.last_w.get(k)
            if w is not None:
                cand[w.idx] = w
        for k in writes:
            for r in self.readers.get(k, ()):
                cand[r.idx] = r
        for d in cand.values():
            if (not d.dma) and (not op.dma) and d.eng == "pe" and op.eng == "pe":
                continue
            if d.dma and op.dma and d.semkey == op.semkey:
                pure_waw = all(self.last_w.get(k) is not d for k in reads) and \
                    all(d not in self.readers.get(k, ()) for k in writes)
                if pure_waw:
                    continue
            key = ("dma", d.slot) if d.dma else ("eng", d.eng)
            old = op.deps.get(key)
            if old is None or old.idx < d.idx:
                op.deps[key] = d
            d.sig = True
        for k in writes:
            self.last_w[k] = op
            self.readers[k] = []
        for k in reads:
            self.readers.setdefault(k, []).append(op)
        self.ops.append(op)
        return op

    def emit(self):
        nc = self.nc
        import os as _os
        _mx = int(_os.environ.get("MAXOPS", "0"))
        if _mx:
            self.ops = self.ops[:_mx]
        cnt = {}
        dma_keys = []
        for op in self.ops:
            if op.dma:
                k = ("dma", op.slot)
                if k not in cnt:
                    cnt[k] = 0
                    dma_keys.append(k)
                cnt[k] += 16
                op.val = cnt[k]
            elif op.sig:
                k = ("eng", op.eng)
                cnt[k] = cnt.get(k, 0) + 1
                op.val = cnt[k]
        self.maxvals = dict(cnt)
        sems = {}
        with contextlib.ExitStack() as es:
            for e in ENGS:
                sems[("eng", e)] = es.enter_context(nc.semaphore("s_" + e))
            for i, k in enumerate(dma_keys):
                sems[k] = es.enter_context(nc.semaphore("d%d" % i))
            self.nsems = len(sems)
            block = es.enter_context(nc.Block())
            ops = self.ops

            def run(engname, eng):
                waited = {}
                for op in ops:
                    if op.eng != engname:
                        continue
                    for k, d in op.deps.items():
                        if waited.get(k, 0) >= d.val:
                            continue
                        eng.wait_ge(sems[k], d.val)
                        waited[k] = d.val
                    ins = op.fn(eng)
                    if op.dma:
                        ins.then_inc(sems[("dma", op.slot)], 16)
                    elif op.sig:
                        ins.then_inc(sems[("eng", op.eng)], 1)
                last = {}
                for op in ops:
                    if op.eng == engname and op.dma:
                        last[("dma", op.slot)] = max(op.val, last.get(("dma", op.slot), 0))
                for k, v in last.items():
                    if waited.get(k, 0) < v:
                        eng.wait_ge(sems[k], v)

            @block.tensor
            def _(e):
                run("pe", e)

            @block.scalar
            def _(e):
                run("act", e)

            @block.vector
            def _(e):
                run("dve", e)

            @block.gpsimd
            def _(e):
                run("pool", e)

            @block.sync
            def _(e):
                run("sp", e)


def run_staged(S, fn, seg, group=4):
    for g0 in range(0, len(seg), group):
        recs = []
        for ti in range(g0, min(g0 + group, len(seg))):
            S.begin_record()
            fn(ti, seg[ti])
            recs.append(S.end_record())
        S.replay_staged(recs)


class T:
    def __init__(self, h, key):
        self.h = h
        self.key = key

    def __getitem__(self, idx):
        return self.h[idx]


def _dsize(dt):
    return 2 if dt == BF16 else 4


PF_COLS = ([0, 128] + [256, 384] + [512, 640] + [1024, 1152] + [1280, 1408] + [1536, 1664]
           + [2048, 2176, 2304, 2432] + [2560 + 128 * i for i in range(8)])
PF_Q, PF_FF, PF_FB, PF_G, PF_QA, PF_KA, PF_Z, PF_XBC = 0, 2, 4, 6, 8, 10, 12, 16
PF_NB = 24
PF_TR = ["silu"] * 2 + ["copy"] * 4 + ["silu"] * 2 + ["s8"] * 2 + ["copy"] * 2 + ["silu"] * 4 + ["copy"] * 8
PT_GROUPS = [(256, 768, 0), (768, 1024, 512), (1792, 2048, 768), (2048, 2560, 1024), (3584, 3600, 1536)]
PT_FF, PT_FB, PT_I, PT_VA, PT_Z, PT_DT = 0, 256, 512, 768, 1024, 1536
PT_W = 1552

FV_L = 72 + 8 + 8 + 8 + 2 + 4 + 40 + 8
FV_BMOD, FV_NF1, FV_NMX, FV_NF2, FV_HGN, FV_SSN, FV_CW, FV_CB = 0, 72, 80, 88, 96, 98, 102, 142
FV_G = DEPTH * FV_L
FV_C, FV_FN, FV_LB = FV_G, FV_G + 16, FV_G + 24
FV_N = FV_G + 24 + 8
RV_L = 256 + 512 + 16 + 16 + 512
RV_NAN, RV_DSK, RV_ALOG, RV_DTB, RV_SSN = 0, 256, 768, 784, 800
RV_G = DEPTH * RV_L
RV_LBR = RV_G
RV_N = RV_G + 1024


class KB:
    def __init__(self, nlat=64, dbg=(), layers=DEPTH, stop_after=None, parts=("hg", "ssd", "na")):
        self.parts = set(parts)
        self.nlat = nlat
        self.NT = nlat + 2
        self.NTOK = 128 * self.NT
        self.NLTOK = 128 * nlat
        self.dbg = set(dbg)
        self.layers = layers
        self.stop_after = stop_after
        self.nc = nc = bass.Bass("TRN2", target_bir_lowering=False)
        self.S = Sched(nc)
        self.uid = 0
        self.sb_lo = 16512
        self.sb_hi = 229344
        self.off = self.sb_lo
        self.NB = 512
        assert nlat % 4 == 0
        self.nblk = 1 + self.NLTOK // self.NB
        self.declare_io()

    def sb(self, name, shape, dt):
        size = int(np.prod(shape[1:])) * _dsize(dt)
        size = (size + 31) // 32 * 32
        assert self.off + size <= self.sb_hi, (name, self.off, size)
        self.uid += 1
        h = self.nc.alloc_sbuf_tensor_at("%s_%d" % (name, self.uid), list(shape), dt, offset=self.off)
        self.off += size
        return T(h, "%s_%d" % (name, self.uid))

    def mark(self):
        return self.off

    def reset(self, m):
        self.off = m

    def barrier(self):
        scr = self.scr
        self.S.add("dve", lambda e: e.memset(scr[:, 0:1], 0.0), writes=[scr.key], barrier=True)

    def declare_io(self):
        nc = self.nc
        ein = lambda n, s, dt=F32: nc.dram_tensor(n, list(s), dt, kind="ExternalInput").ap()
        self.xT = ein("xT", [D, self.NTOK])
        self.fvec = ein("fvec", [128, FV_N])
        self.rvec = ein("rvec", [128, RV_N])
        self.w_mod = ein("w_mod", [DEPTH, D, 9 * D])
        self.w13 = [ein("ffn1_w13", [DEPTH, D, 2 * DFF]), ein("ffn2_w13", [DEPTH, D, 2 * DFF])]
        self.w2 = [ein("ffn1_w2", [DEPTH, DFF, D]), ein("ffn2_w2", [DEPTH, DFF, D])]
        self.w_in = ein("w_in", [DEPTH, D, 3600])
        self.w_out = ein("w_out", [DEPTH, D, D])
        self.outT = nc.dram_tensor("outT", [D, self.NLTOK], F32, kind="ExternalOutput").ap()
        self.XT = nc.dram_tensor("XTs", [D, self.NTOK], F32).ap()
        self.PF = nc.dram_tensor("PFs", [PF_NB * 128, self.NTOK], F32).ap()
        self.PT = nc.dram_tensor("PTs", [self.NTOK, PT_W], F32).ap()
        self.MIXT = nc.dram_tensor("MIXTs", [D, self.NTOK], BF16).ap()
        self.dbg_out = {}
        _mixer_io(self)

    def dbg_tensor(self, name, shape, dt=F32):
        ap = self.nc.dram_tensor("dbg_" + name, list(shape), dt, kind="ExternalOutput").ap()
        self.dbg_out[name] = ap
        return ap

    def setup_persist(self):
        S = self.S
        self.scr = self.sb("scr", [128, 8], F32)
        self.fv = self.sb("fv", [128, FV_N], F32)
        self.ones_bf = self.sb("ones", [128, 128], BF16)
        self.MOD = self.sb("MOD", [128, 2, 72], F32)
        self.scb = self.sb("scb", [128, 16], F32)
        fv, ones = self.fv, self.ones_bf
        S.add("sp", lambda e: e.dma_start(out=fv[:], in_=self.fvec), writes=[fv.key], dma=True)
        S.add("dve", lambda e: e.memset(ones[:], 1.0), writes=[ones.key])
        scb = self.scb
        S.add("act", lambda e: e.activation(out=scb[:], in_=fv[:, FV_C:FV_C + 16], func=AF.Silu),
              reads=[fv.key], writes=[scb.key])
        self.persist_end = self.mark()

    def phase_mod(self, l):
        S, nc = self.S, self.nc
        m = self.mark()
        HALF = 4608
        wm = [self.sb("wm", [128, HALF], F32) for _ in range(2)]
        psm = self.ps[0]
        scb, fv, MOD = self.scb, self.fv, self.MOD
        i = 0
        for k in range(8):
            for hf in range(2):
                w = wm[i % 2]
                i += 1
                src = self.w_mod[l, k * 128:(k + 1) * 128, hf * HALF:(hf + 1) * HALF]
                S.add("sp", lambda e, w=w, src=src: e.dma_start(out=w[:], in_=src), writes=[w.key], dma=True)
                for j in range(36):
                    fb = hf * 36 + j
                    S.add("pe", lambda e, w=w, j=j, fb=fb, k=k: e.matmul(
                        psm[:, fb * 2:fb * 2 + 2], lhsT=w[:, j * 128:(j + 1) * 128], rhs=scb[:, 2 * k:2 * k + 2],
                        start=(k == 0 and fb == 0), stop=(k == 7), skip_group_check=True),
                        reads=[w.key, scb.key], writes=[psm.key])
        bo = l * FV_L
        for s in range(2):
            S.add("dve", lambda e, s=s: e.tensor_tensor(out=MOD[:, s, :], in0=psm[:, s:144:2],
                                                       in1=fv[:, bo + FV_BMOD:bo + FV_BMOD + 72], op=ALU.add),
                  reads=[psm.key, fv.key], writes=[MOD.key])
        for s in range(2):
            for (js, nw) in ((1, FV_NF1), (4, FV_NMX), (7, FV_NF2)):
                S.add("dve", lambda e, s=s, js=js, nw=nw: e.scalar_tensor_tensor(
                    out=MOD[:, s, js * 8:js * 8 + 8], in0=MOD[:, s, js * 8:js * 8 + 8], scalar=1.0,
                    in1=fv[:, bo + nw:bo + nw + 8], op0=ALU.add, op1=ALU.mult),
                    reads=[MOD.key, fv.key], writes=[MOD.key])
                S.add("dve", lambda e, s=s, js=js: e.tensor_scalar_mul(
                    out=MOD[:, s, js * 8:js * 8 + 8], in0=MOD[:, s, js * 8:js * 8 + 8], scalar1=32.0),
                    reads=[MOD.key], writes=[MOD.key])
            for jg in (2, 8):
                S.add("dve", lambda e, s=s, jg=jg: e.tensor_scalar_mul(
                    out=MOD[:, s, jg * 8:jg * 8 + 8], in0=MOD[:, s, jg * 8:jg * 8 + 8], scalar1=0.5),
                    reads=[MOD.key], writes=[MOD.key])
        if "mod" in self.dbg:
            d = self.dbg_tensor("mod%d" % l, [128, 144])
            S.add("sp", lambda e: e.dma_start(out=d, in_=MOD[:].rearrange("p s c -> p (s c)")), reads=[MOD.key],
                  dma=True, semkey="dbg_mod%d" % l)
        self.barrier()
        self.reset(m)

    def load_w(self, name, src, K, N):
        kc = K // 128
        w = self.sb(name, [128, kc, N], BF16)
        for k in range(kc):
            s = src[k * 128:(k + 1) * 128, :]
            self.S.add("pool", lambda e, k=k, s=s: e.dma_start(out=w[:, k, :], in_=s), writes=[w.key], dma=True)
        return w

    def alloc_work(self, nxb=2):
        NB = self.NB
        self.xb = [self.sb("xb", [128, 8, NB], F32) for _ in range(nxb)] * (2 // nxb)
        self.tk = [self.sb("tk", [128, NB], F32) for _ in range(2)]
        self.hb = self.sb("hb", [128, 8, NB], BF16)
        self.sq = self.hb
        self.rstd = self.sb("rstd", [128, NB], F32)

    def blk(self, i):
        if i == 0:
            return 0, CTX, 1
        return CTX + (i - 1) * self.NB, self.NB, 0

    def load_x(self, xb, src, n0, N):
        v = src.rearrange("(k p) n -> p k n", p=128)[:, :, n0:n0 + N]
        self.S.add("sp", lambda e: e.dma_start(out=xb[:, :, :N], in_=v), writes=[xb.key], dma=True)

    def store_x(self, xb, dst, n0, N, key="dram_x"):
        v = dst.rearrange("(k p) n -> p k n", p=128)[:, :, n0:n0 + N]
        self.S.add("sp", lambda e: e.dma_start(out=v, in_=xb[:, :, :N]), reads=[xb.key], dma=True,
                   semkey="st_" + xb.key)

    def modulate(self, xb, N, A, SH, out, out_keyed):
        S = self.S
        sq, rstd, ones = self.sq, self.rstd, self.ones_bf
        pss = self.ps[0]
        S.add("act", lambda e: e.activation(out=sq[:, :, :N], in_=xb[:, :, :N], func=AF.Square),
              reads=[xb.key], writes=[sq.key])
        for k in range(8):
            S.add("pe", lambda e, k=k: e.matmul(pss[:, :N], lhsT=ones[:], rhs=sq[:, k, :N], start=(k == 0), stop=(k == 7)),
                  reads=[sq.key, ones.key], writes=[pss.key])
        S.add("dve", lambda e: e.tensor_scalar_add(out=rstd[:, :N], in0=pss[:, :N], scalar1=float(D * EPS)),
              reads=[pss.key], writes=[rstd.key])
        S.add("act", lambda e: e.activation(out=rstd[:, :N], in_=rstd[:, :N], func=AF.Ln),
              reads=[rstd.key], writes=[rstd.key])
        S.add("act", lambda e: e.activation(out=rstd[:, :N], in_=rstd[:, :N], func=AF.Exp, scale=-0.5),
              reads=[rstd.key], writes=[rstd.key])
        for k in range(8):
            if SH is None:
                S.add("dve", lambda e, k=k: e.scalar_tensor_tensor(out=out[:, k, :N], in0=xb[:, k, :N], scalar=A[:, k:k + 1],
                                                                   in1=rstd[:, :N], op0=ALU.mult, op1=ALU.mult),
                      reads=[xb.key, rstd.key, self.fv.key], writes=[out_keyed])
            else:
                tk = self.tk[k % 2]
                S.add("dve", lambda e, k=k, tk=tk: e.scalar_tensor_tensor(out=tk[:, :N], in0=xb[:, k, :N], scalar=A[:, k:k + 1],
                                                                          in1=rstd[:, :N], op0=ALU.mult, op1=ALU.mult),
                      reads=[xb.key, rstd.key, self.MOD.key], writes=[tk.key])
                S.add("act", lambda e, k=k, tk=tk: e.activation(out=out[:, k, :N], in_=tk[:, :N], func=AF.Identity,
                                                                bias=SH[:, k:k + 1]),
                      reads=[tk.key, self.MOD.key], writes=[out_keyed])

    def ffn(self, xb, N, s, jbase, w13b, w2b):
        S = self.S
        MOD = self.MOD
        A = MOD[:, s, (jbase + 1) * 8:(jbase + 2) * 8]
        SH = MOD[:, s, jbase * 8:(jbase + 1) * 8]
        G = MOD[:, s, (jbase + 2) * 8:(jbase + 3) * 8]
        hb, ab, sg = self.hb, self.ab, self.sg
        self.modulate(xb, N, A, SH, hb, hb.key)
        for j in range(22):
            pu, pg = self.ps[1 + j % 2], self.ps[3 + j % 2]
            for k in range(8):
                S.add("pe", lambda e, j=j, k=k, pu=pu: e.matmul(pu[:, :N], lhsT=w13b[:, k, j * 128:(j + 1) * 128],
                                                                rhs=hb[:, k, :N], start=(k == 0), stop=(k == 7)),
                      reads=[w13b.key, hb.key], writes=[pu.key])
            for k in range(8):
                S.add("pe", lambda e, j=j, k=k, pg=pg: e.matmul(pg[:, :N], lhsT=w13b[:, k, DFF + j * 128:DFF + (j + 1) * 128],
                                                                rhs=hb[:, k, :N], start=(k == 0), stop=(k == 7)),
                      reads=[w13b.key, hb.key], writes=[pg.key])
            sgj = sg[j % 2]
            S.add("act", lambda e, pg=pg, sgj=sgj: e.activation(out=sgj[:, :N], in_=pg[:, :N], func=AF.Silu),
                  reads=[pg.key], writes=[sgj.key])
            S.add("dve", lambda e, j=j, pu=pu, sgj=sgj: e.tensor_tensor(out=ab[:, j, :N], in0=sgj[:, :N], in1=pu[:, :N],
                                                                          op=ALU.mult),
                  reads=[pu.key, sgj.key], writes=[ab.key])
        for fb in range(8):
            po = self.ps[5 + fb % 2]
            for j in range(22):
                S.add("pe", lambda e, j=j, fb=fb, po=po: e.matmul(po[:, :N], lhsT=w2b[:, j, fb * 128:(fb + 1) * 128],
                                                                  rhs=ab[:, j, :N], start=(j == 0), stop=(j == 21)),
                      reads=[w2b.key, ab.key], writes=[po.key])
            S.add("dve", lambda e, fb=fb, po=po: e.scalar_tensor_tensor(out=xb[:, fb, :N], in0=po[:, :N],
                                                                        scalar=G[:, fb:fb + 1], in1=xb[:, fb, :N],
                                                                        op0=ALU.mult, op1=ALU.add),
                  reads=[po.key, xb.key, MOD.key], writes=[xb.key])

    def phase_ffn1(self, l):
        m = self.mark()
        w13b = self.load_w("w13b", self.w13[0][l], D, 2 * DFF)
        w2b = self.load_w("w2b", self.w2[0][l], DFF, D)
        self.alloc_work()
        self.ab = self.sb("ab", [128, 22, self.NB], BF16)
        self.sg = [self.sb("sg", [128, self.NB], F32) for _ in range(2)]
        src = self.xT if l == 0 else self.XT
        for i in range(self.nblk):
            n0, N, s = self.blk(i)
            xb = self.xb[i % 2]
            self.load_x(xb, src, n0, N)
            self.ffn(xb, N, s, 0, w13b, w2b)
            self.store_x(xb, self.XT, n0, N, key="dram_x1")
        self.barrier()
        self.reset(m)
        if "x1" in self.dbg:
            self.dump_dram("x1_%d" % l, self.XT, [D, self.NTOK], "dram_x1")

    def dump_dram(self, name, src, shape, key, dt=F32):
        d = self.dbg_tensor(name, shape, dt)
        self.S.add("sp", lambda e: e.dma_start(out=d, in_=src), dma=True, semkey="dbg_" + name)
        self.barrier()

    def phase_inproj(self, l):
        S = self.S
        m = self.mark()
        winb = self.load_w("winb", self.w_in[l], D, 3600)
        self.alloc_work()
        NB = self.NB
        pfst = self.sb("pfst", [128, PF_NB, NB], F32)
        ptsts = [self.sb("ptst", [128, PT_W], F32) for _ in range(NB // 128)]
        MOD, hb = self.MOD, self.hb
        def do_blk(i):
            n0, N, s = self.blk(i)
            xb = self.xb[i % 2]
            self.load_x(xb, self.XT, n0, N)
            self.modulate(xb, N, MOD[:, s, 32:40], MOD[:, s, 24:32], hb, hb.key)
            for bi, c0 in enumerate(PF_COLS):
                p = self.ps[1 + bi % 4]
                for k in range(8):
                    S.add("pe", lambda e, k=k, c0=c0, p=p: e.matmul(p[:, :N], lhsT=winb[:, k, c0:c0 + 128], rhs=hb[:, k, :N],
                                                                    start=(k == 0), stop=(k == 7)),
                          reads=[winb.key, hb.key], writes=[p.key])
                tr = PF_TR[bi]
                if tr == "silu":
                    S.add("act", lambda e, bi=bi, p=p: e.activation(out=pfst[:, bi, :N], in_=p[:, :N], func=AF.Silu),
                          reads=[p.key], writes=[pfst.key])
                elif tr == "s8":
                    S.add("dve", lambda e, bi=bi, p=p: e.tensor_scalar_mul(out=pfst[:, bi, :N], in0=p[:, :N], scalar1=0.125),
                          reads=[p.key], writes=[pfst.key])
                else:
                    S.add("dve", lambda e, bi=bi, p=p: e.tensor_copy(out=pfst[:, bi, :N], in_=p[:, :N]),
                          reads=[p.key], writes=[pfst.key])
            dst = self.PF.rearrange("(f p) n -> p f n", p=128)[:, :, n0:n0 + N]
            S.add("sp", lambda e, dst=dst: e.dma_start(out=dst, in_=pfst[:, :, :N]), reads=[pfst.key],
                  dma=True, semkey="dram_pf")
            for t in range(N // 128):
                ptst = ptsts[t]
                for gi, (c0, c1, d0) in enumerate(PT_GROUPS):
                    p = self.ps[5 + gi % 2]
                    wdt = c1 - c0
                    for k in range(8):
                        S.add("pe", lambda e, k=k, t=t, c0=c0, c1=c1, p=p, wdt=wdt: e.matmul(
                            p[:, :wdt], lhsT=hb[:, k, t * 128:(t + 1) * 128], rhs=winb[:, k, c0:c1],
                            start=(k == 0), stop=(k == 7)), reads=[winb.key, hb.key], writes=[p.key])
                    S.add("act", lambda e, ptst=ptst, d0=d0, wdt=wdt, p=p: e.copy(out=ptst[:, d0:d0 + wdt], in_=p[:, :wdt]),
                          reads=[p.key], writes=[ptst.key])
                dstt = self.PT[n0 + t * 128:n0 + (t + 1) * 128, :]
                S.add("sp", lambda e, ptst=ptst, dstt=dstt: e.dma_start(out=dstt, in_=ptst[:]), reads=[ptst.key],
                      dma=True, semkey="st_" + ptst.key)
        for i in range(self.nblk):
            do_blk(i)
        self.barrier()
        self.reset(m)
        if "proj" in self.dbg:
            self.dump_dram("pf_%d" % l, self.PF, [PF_NB * 128, self.NTOK], "dram_pf")
            self.dump_dram("pt_%d" % l, self.PT, [self.NTOK, PT_W], "dram_pt")

    def phase_out(self, l):
        S = self.S
        last = (l == DEPTH - 1)
        m = self.mark()
        w13b = self.load_w("w13b", self.w13[1][l], D, 2 * DFF)
        w2b = self.load_w("w2b", self.w2[1][l], DFF, D)
        woutb = self.load_w("woutb", self.w_out[l], D, D)
        self.alloc_work(1)
        NB = self.NB
        self.ab = self.sb("ab", [128, 22, NB], BF16)
        self.sg = [self.sb("sg", [128, NB], F32) for _ in range(2)]
        mixb = [T(self.ab.h, self.ab.key)] * 2
        MOD = self.MOD
        def do_blk(i):
            n0, N, s = self.blk(i)
            if last and s == 1:
                return
            xb = self.xb[i % 2]
            mb = mixb[i % 2]
            self.load_x(xb, self.XT, n0, N)
            v = self.MIXT.rearrange("(k p) n -> p k n", p=128)[:, :, n0:n0 + N]
            S.add("sp", lambda e, mb=mb, v=v: e.dma_start(out=mb[:, 0:8, :N], in_=v), writes=[mb.key], dma=True)
            G = MOD[:, s, 40:48]
            for fb in range(8):
                p = self.ps[5 + fb % 2]
                for k in range(8):
                    S.add("pe", lambda e, k=k, fb=fb, p=p, mb=mb: e.matmul(p[:, :N], lhsT=woutb[:, k, fb * 128:(fb + 1) * 128],
                                                                          rhs=mb[:, k, :N], start=(k == 0), stop=(k == 7)),
                          reads=[woutb.key, mb.key], writes=[p.key])
                S.add("dve", lambda e, fb=fb, p=p, xb=xb, G=G: e.scalar_tensor_tensor(out=xb[:, fb, :N], in0=p[:, :N],
                                                                                  scalar=G[:, fb:fb + 1], in1=xb[:, fb, :N],
                                                                                  op0=ALU.mult, op1=ALU.add),
                      reads=[p.key, xb.key, MOD.key], writes=[xb.key])
            self.ffn(xb, N, s, 6, w13b, w2b)
            if not last:
                self.store_x(xb, self.XT, n0, N, key="dram_x3")
            else:
                ob = xb
                fn = self.fn32
                self.modulate(xb, N, fn, None, ob, ob.key)
                v = self.outT.rearrange("(k p) n -> p k n", p=128)[:, :, n0 - CTX:n0 - CTX + N]
                S.add("sp", lambda e, v=v, ob=ob: e.dma_start(out=v, in_=ob[:, :, :N]), reads=[ob.key],
                      dma=True, semkey="st_" + ob.key)
        for i in range(self.nblk):
            do_blk(i)
        self.barrier()
        self.reset(m)
        if "x3" in self.dbg and not last:
            self.dump_dram("x3_%d" % l, self.XT, [D, self.NTOK], "dram_x3")

    def build(self):
        S = self.S
        big = [self.nc.alloc_psum_tensor("psb%d" % i, [128, 1024], F32) for i in range(4)]
        self.ps = [T(big[i // 2][:, (i % 2) * 512:(i % 2) * 512 + 512], "ps%d" % i) for i in range(8)]
        self.psbig = [T(big[i], "psB%d" % i) for i in range(4)]
        self.setup_persist()
        self.fn32 = None
        for l in range(self.layers):
            if not getattr(self, "only_mixer", False):
                self.phase_mod(l)
                if self.stop_after == ("mod", l):
                    break
                self.phase_ffn1(l)
                if self.stop_after == ("ffn1", l):
                    break
                self.phase_inproj(l)
                if self.stop_after == ("inproj", l):
                    break
            self.phase_mixer(l)
            if self.stop_after == ("mixer", l):
                break
            self.phase_out_wrap(l)
        S.emit()
        return self.nc

    def phase_out_wrap(self, l):
        last = (l == DEPTH - 1)
        if last:
            m = self.mark()
            fn32 = self.sb("fn32", [128, 8], F32)
            fv = self.fv
            self.S.add("dve", lambda e: e.tensor_scalar_mul(out=fn32[:], in0=fv[:, FV_FN:FV_FN + 8], scalar1=32.0),
                       reads=[fv.key], writes=[fn32.key])
            self.fn32 = fn32
            self.persist_tmp = self.mark()
            self.phase_out(l)
            self.reset(m)
        else:
            self.phase_out(l)

    def phase_mixer(self, l):
        m = self.mark()
        _setup_mixer_consts(self)
        if "ohg" in self.dbg:
            self.dbg_tensor("ohg%d" % l, [256, self.NTOK])
        if "yssm" in self.dbg:
            self.dbg_tensor("yssm%d" % l, [self.NTOK, 512])
        if "naraw" in self.dbg:
            self.dbg_tensor("naraw%d" % l, [self.NTOK, 256])
        if "hg" in self.parts:
            _phase_hgrn2(self, l)
        if "ssd" in self.parts:
            _phase_ssd(self, l)
        if "na" in self.parts:
            _phase_na(self, l)
        if "mix" in self.dbg:
            self.dump_dram("mix_%d" % l, self.MIXT, [D, self.NTOK], "x", BF16)
        self.reset(m)


def fm(v):
    v = np.asarray(v, np.float32)
    return v.reshape(-1, 128).T


def prep_shared(inp):
    fv = np.zeros((128, FV_N), np.float32)
    rv = np.zeros((128, RV_N), np.float32)
    for l in range(DEPTH):
        o = l * FV_L
        fv[:, o + FV_BMOD:o + FV_BMOD + 72] = fm(inp["b_mod"][l])
        fv[:, o + FV_NF1:o + FV_NF1 + 8] = fm(inp["norm_ffn1"][l])
        fv[:, o + FV_NMX:o + FV_NMX + 8] = fm(inp["norm_mix"][l])
        fv[:, o + FV_NF2:o + FV_NF2 + 8] = fm(inp["norm_ffn2"][l])
        fv[:, o + FV_HGN:o + FV_HGN + 2] = fm(inp["hg_norm"][l])
        fv[:, o + FV_SSN:o + FV_SSN + 4] = fm(inp["ssm_norm"][l])
        for j in range(5):
            fv[:, o + FV_CW + j * 8:o + FV_CW + j * 8 + 8] = fm(inp["ssm_conv_w"][l, j])
        fv[:, o + FV_CB:o + FV_CB + 8] = fm(inp["ssm_conv_b"][l])
        r = l * RV_L
        rv[:, r + RV_NAN:r + RV_NAN + 256] = inp["na_norm"][l][None, :]
        rv[:, r + RV_DSK:r + RV_DSK + 512] = np.repeat(inp["ssm_d"][l], 64)[None, :]
        rv[:, r + RV_ALOG:r + RV_ALOG + 16] = inp["ssm_a_log"][l].reshape(-1)[None, :]
        rv[:, r + RV_DTB:r + RV_DTB + 16] = inp["ssm_dt_bias"][l].reshape(-1)[None, :]
        rv[:, r + RV_SSN:r + RV_SSN + 512] = inp["ssm_norm"][l][None, :]
    fv[:, FV_FN:FV_FN + 8] = fm(inp["final_norm"])
    for dr in range(2):
        for l in range(DEPTH):
            rv[:, RV_LBR + (dr * 2 + l) * 256:RV_LBR + (dr * 2 + l) * 256 + 256] = inp["hg_lower_bounds"][dr, l][None, :]
    for dr in range(2):
        for l in range(DEPTH):
            fv[:, FV_LB + dr * 4 + l * 2:FV_LB + dr * 4 + l * 2 + 2] = fm(inp["hg_lower_bounds"][dr, l])
    return fv, rv


def prep_core(inp, b, nlat, fv_shared):
    fv = fv_shared.copy()
    cc = np.stack([fm(inp["c"][b]), fm(inp["c_ctx"])], axis=2)
    fv[:, FV_C:FV_C + 16] = cc.reshape(128, 16)
    xT = np.ascontiguousarray(np.concatenate([inp["ctx"][b], inp["x"][b][:128 * nlat]], axis=0).T)
    return fv, xT


def make_in_maps(inp, nlat, batches):
    fvs, rv = prep_shared(inp)
    shared = {k: np.ascontiguousarray(inp[k], np.float32) for k in
              ("w_mod", "ffn1_w13", "ffn2_w13", "ffn1_w2", "ffn2_w2", "w_in", "w_out")}
    cmat = make_cmat()
    nab = np.stack([make_nabias(np.asarray(inp["na_rpb"][l], np.float32), nlat) for l in range(DEPTH)])
    maps = []
    for b in batches:
        fv, xT = prep_core(inp, b, nlat, fvs)
        m = dict(shared)
        m.update({"xT": xT, "fvec": fv, "rvec": rv, "cmat": cmat, "nabias": nab})
        maps.append(m)
    return maps


CM_ID, CM_M1F, CM_M2F, CM_M3F, CM_M1B, CM_M2B, CM_M3B = 0, 128, 256, 384, 512, 640, 768
CM_M4F, CM_M4B, CM_HM, CM_BLK, CM_TRIF, CM_TRIB, CM_NEGF, CM_NEGB, CM_ONES = 896, 900, 904, 1160, 1288, 1416, 1544, 1672, 1800
CM_N = 1928
NEG = -30000.0


def make_cmat():
    c = np.zeros((128, CM_N), np.float32)
    u = np.arange(128)[:, None]
    t = np.arange(128)[None, :]
    same = (u // 32) == (t // 32)
    c[:, CM_ID:CM_ID + 128] = (u == t)
    mf = (t // 32) * 32 + 15
    c[:, CM_M1F:CM_M1F + 128] = same * (((u > mf) & (u <= t)) * 1.0 - ((u > t) & (u <= mf)) * 1.0)
    c[:, CM_M2F:CM_M2F + 128] = same & (u <= t)
    c[:, CM_M3F:CM_M3F + 128] = same & (u > t)
    mb = (t // 32) * 32 + 16
    c[:, CM_M1B:CM_M1B + 128] = same * (((u >= t) & (u < mb)) * 1.0 - ((u >= mb) & (u < t)) * 1.0)
    c[:, CM_M2B:CM_M2B + 128] = same & (u >= t)
    c[:, CM_M3B:CM_M3B + 128] = same & (u < t)
    j = np.arange(4)[None, :]
    c[:, CM_M4F:CM_M4F + 4] = (u // 32) == j
    c[:, CM_M4B:CM_M4B + 4] = (u // 32) == (3 - j)
    col = np.arange(128)[None, :]
    c[:, CM_HM:CM_HM + 128] = (col // 64 == 0)
    c[:, CM_HM + 128:CM_HM + 256] = (col // 64 == 1)
    c[:, CM_BLK:CM_BLK + 128] = (u // 64) == (t // 64)
    c[:, CM_TRIF:CM_TRIF + 128] = (u <= t)
    c[:, CM_TRIB:CM_TRIB + 128] = (u >= t)
    c[:, CM_NEGF:CM_NEGF + 128] = NEG * (u > t)
    c[:, CM_NEGB:CM_NEGB + 128] = NEG * (u < t)
    c[:, CM_ONES:CM_ONES + 128] = 1.0
    return c


def _mixer_io(self):
    nc = self.nc
    self.cmat = nc.dram_tensor("cmat", [128, CM_N], F32, kind="ExternalInput").ap()
    self.OHG = nc.dram_tensor("OHGs", [256, self.NTOK], F32).ap()
    _ssd_io(self)
    _na_io(self)


def _setup_mixer_consts(self):
    S = self.S
    self.cm = cm = self.sb("cm", [128, CM_N], F32)
    S.add("sp", lambda e: e.dma_start(out=cm[:], in_=self.cmat), writes=[cm.key], dma=True)
    self.rv = rv = self.sb("rv", [128, RV_N], F32)
    S.add("sp", lambda e: e.dma_start(out=rv[:], in_=self.rvec), writes=[rv.key], dma=True)
    self.blk_bf = blk = self.sb("blkbf", [128, 128], BF16)
    S.add("dve", lambda e: e.tensor_copy(out=blk[:], in_=cm[:, CM_BLK:CM_BLK + 128]), reads=[cm.key], writes=[blk.key])


def _hg_tiles(self, d):
    NT = self.NT
    chain = list(range(NT)) if d == 0 else [1, 0] + list(range(NT - 1, 1, -1))
    return [chain[0:2]] + [chain[i:i + 8] for i in range(2, NT, 8)]


def _phase_hgrn2(self, l):
    S = self.S
    blkbf_l = self.blk_bf
    m = self.mark()
    cm, fv, rv = self.cm, self.fv, self.rv
    ps = self.ps
    LBt = self.sb("LBt", [128, 2, 256], F32)
    OMLt = self.sb("OMLt", [128, 2, 256], F32)
    omlf = self.sb("omlf", [128, 2, 2], F32)
    if l == 0:
        S.add("dve", lambda e: e.memset(LBt[:], 0.0), writes=[LBt.key])
        S.add("dve", lambda e: e.memset(OMLt[:], 1.0), writes=[OMLt.key])
        S.add("dve", lambda e: e.memset(omlf[:], 1.0), writes=[omlf.key])
    else:
        for d in range(2):
            a0 = rv[:, RV_LBR + (d * 2 + 0) * 256:RV_LBR + (d * 2 + 0) * 256 + 256]
            a1 = rv[:, RV_LBR + (d * 2 + 1) * 256:RV_LBR + (d * 2 + 1) * 256 + 256]
            S.add("dve", lambda e, d=d, a0=a0, a1=a1: e.tensor_tensor(out=LBt[:, d, :], in0=a1, in1=a0, op=ALU.subtract),
                  reads=[rv.key], writes=[LBt.key])
            f0 = fv[:, FV_LB + d * 4:FV_LB + d * 4 + 2]
            f1 = fv[:, FV_LB + d * 4 + 2:FV_LB + d * 4 + 4]
            S.add("dve", lambda e, d=d, f0=f0, f1=f1: e.tensor_tensor(out=omlf[:, d, :], in0=f1, in1=f0, op=ALU.subtract),
                  reads=[fv.key], writes=[omlf.key])
        S.add("act", lambda e: e.activation(out=LBt[:], in_=LBt[:], func=AF.Sigmoid), reads=[LBt.key], writes=[LBt.key])
        S.add("act", lambda e: e.activation(out=omlf[:], in_=omlf[:], func=AF.Sigmoid), reads=[omlf.key], writes=[omlf.key])
        S.add("dve", lambda e: e.tensor_scalar(out=OMLt[:], in0=LBt[:], scalar1=-1.0, scalar2=1.0, op0=ALU.mult, op1=ALU.add),
              reads=[LBt.key], writes=[OMLt.key])
        S.add("dve", lambda e: e.tensor_scalar(out=omlf[:], in0=omlf[:], scalar1=-1.0, scalar2=1.0, op0=ALU.mult, op1=ALU.add),
              reads=[omlf.key], writes=[omlf.key])
    omlfh = self.sb("omlfh", [128, 2, 2, 2], F32)
    for d_ in range(2):
        for pr_ in range(2):
            for h2_ in range(2):
                S.add("dve", lambda e, d_=d_, pr_=pr_, h2_=h2_: e.tensor_tensor(
                    out=omlfh[:, d_, pr_, h2_:h2_ + 1], in0=omlf[:, d_, pr_:pr_ + 1],
                    in1=cm[:, CM_BLK + h2_ * 64:CM_BLK + h2_ * 64 + 1], op=ALU.mult),
                    reads=[omlf.key, cm.key], writes=[omlfh.key])
    D1 = [self.sb("D1", [128, 64, 33], F32) for _ in range(2)]
    SO = [self.sb("SO", [128, 64, 33], F32) for _ in range(2)]
    D0 = [self.sb("D0", [128, 64, 33], F32) for _ in range(2)]
    Sblk = [self.sb("Sblk", [128, 32, 128], BF16) for _ in range(2)]
    DEC = self.sb("DEC", [128, 2, 32], F32)
    ATs = [self.sb("ATs", [128, 4, 128], BF16) for _ in range(8)]
    QHs = [self.sb("QHs", [128, 2, 128], BF16) for _ in range(8)]
    VZs = [self.sb("VZs", [128, 2, 2, 128], BF16) for _ in range(8)]
    rtok_R = [self.sb("rtok", [128, 256], F32) for _ in range(4)]
    vtok_R = [self.sb("vtok", [128, 256], F32) for _ in range(4)]
    vtok_bf_R = [self.sb("vtok_bf", [128, 256], BF16) for _ in range(4)]
    qfm_R = [self.sb("qfm", [128, 2, 128], F32) for _ in range(4)]
    rfm_R = [self.sb("rfm", [128, 2, 128], F32) for _ in range(4)]
    sig_R = [self.sb("sig", [128, 256], F32) for _ in range(4)]
    tmpk_R = [self.sb("tmpk", [128, 256], F32) for _ in range(4)]
    lf_R = [self.sb("lf", [128, 256], F32) for _ in range(4)]
    ktok_R = [self.sb("ktok", [128, 256], F32) for _ in range(4)]
    sneg_R = [self.sb("sneg", [128, 2, 128], F32) for _ in range(4)]
    P1c_R = [self.sb("P1c", [128, 256], F32) for _ in range(4)]
    Ep_R = [self.sb("Ep", [128, 256], F32) for _ in range(4)]
    En_R = [self.sb("En", [128, 256], F32) for _ in range(4)]
    E2_R = [self.sb("E2", [128, 256], F32) for _ in range(4)]
    E3_R = [self.sb("E3", [128, 256], F32) for _ in range(4)]
    qt_R = [self.sb("qt", [128, 2, 128], BF16) for _ in range(4)]
    kt_R = [self.sb("kt", [128, 2, 2, 128], BF16) for _ in range(4)]
    khat_R = [self.sb("khat", [128, 4, 256], BF16) for _ in range(4)]
    of_ld_R = [self.sb("of_ld", [128, 2, 128], F32) for _ in range(4)]
    osum_R = [self.sb("osum", [128, 2, 128], F32) for _ in range(4)]
    osq_R = [self.sb("osq", [128, 2, 128], BF16) for _ in range(4)]
    orstd_R = [self.sb("orstd", [128, 256], F32) for _ in range(4)]
    sgl_R = [self.sb("sgl", [128, 2, 128], F32) for _ in range(4)]
    hgo_R = [self.sb("hgo", [128, 2, 128], BF16) for _ in range(4)]
    for pr in range(2):
        S.add("dve", lambda e, pr=pr: e.memset(Sblk[pr][:], 0.0), writes=[Sblk[pr].key])
        S.add("dve", lambda e, pr=pr: e.memset(D0[pr][:], 0.0), writes=[D0[pr].key])
        S.add("dve", lambda e, pr=pr: e.memset(D1[pr][:], 0.0), writes=[D1[pr].key])
    PFr = self.PF.rearrange("(f p) n -> p f n", p=128)
    OHGr = self.OHG.rearrange("(f p) n -> p f n", p=128)
    MIXr = self.MIXT.rearrange("(f p) n -> p f n", p=128)
    pA, pB, pC, pU, pO, pN = ps[1], ps[2], ps[3], ps[4], ps[5], ps[6]
    def do_dir(d):
        M1 = cm[:, (CM_M1F, CM_M1B)[d]:(CM_M1F, CM_M1B)[d] + 128]
        M2 = cm[:, (CM_M2F, CM_M2B)[d]:(CM_M2F, CM_M2B)[d] + 128]
        M3 = cm[:, (CM_M3F, CM_M3B)[d]:(CM_M3F, CM_M3B)[d] + 128]
        M4 = cm[:, (CM_M4F, CM_M4B)[d]:(CM_M4F, CM_M4B)[d] + 4]
        M4n = cm[:, CM_M4F:CM_M4F + 4]
        for pr in range(2):
            S.add("dve", lambda e, pr=pr: e.memset(D1[pr][:, :, 0:1], 0.0), writes=[D1[pr].key])
        def do_seg(seg):
            nt = len(seg)
            nch = 4 * nt
            def p1(ti, tile):
                n0 = tile * 128
                rtok, vtok, vtok_bf, qfm, rfm, sig, tmpk, lf, ktok, sneg, P1c, Ep, En, E2, E3, qt, kt, khat = rtok_R[ti % 4], vtok_R[ti % 4], vtok_bf_R[ti % 4], qfm_R[ti % 4], rfm_R[ti % 4], sig_R[ti % 4], tmpk_R[ti % 4], lf_R[ti % 4], ktok_R[ti % 4], sneg_R[ti % 4], P1c_R[ti % 4], Ep_R[ti % 4], En_R[ti % 4], E2_R[ti % 4], E3_R[ti % 4], qt_R[ti % 4], kt_R[ti % 4], khat_R[ti % 4]
                pA, pB = (ps[1], ps[2]) if ti % 2 == 0 else (ps[0], ps[7])
                AT, QH, VZ = ATs[ti], QHs[ti], VZs[ti]
                S.add("sp", lambda e, n0=n0: e.dma_start(out=rtok[:], in_=self.PT[n0:n0 + 128, (PT_FF, PT_FB)[d]:(PT_FF, PT_FB)[d] + 256]),
                      writes=[rtok.key], dma=True)
                S.add("sp", lambda e, n0=n0: e.dma_start(out=vtok[:], in_=self.PT[n0:n0 + 128, PT_I:PT_I + 256]),
                      writes=[vtok.key], dma=True)
                S.add("sp", lambda e, n0=n0: e.dma_start(out=qfm[:], in_=PFr[:, PF_Q:PF_Q + 2, n0:n0 + 128]),
                      writes=[qfm.key], dma=True)
                fbk = (PF_FF, PF_FB)[d]
                S.add("sp", lambda e, n0=n0, fbk=fbk: e.dma_start(out=rfm[:], in_=PFr[:, fbk:fbk + 2, n0:n0 + 128]),
                      writes=[rfm.key], dma=True)
                S.add("act", lambda e: e.activation(out=sig[:], in_=rtok[:], func=AF.Sigmoid), reads=[rtok.key], writes=[sig.key])
                S.add("act", lambda e: e.copy(out=vtok_bf[:], in_=vtok[:]), reads=[vtok.key], writes=[vtok_bf.key])
                S.add("act", lambda e: e.activation(out=sneg[:], in_=rfm[:], func=AF.Sigmoid, scale=-1.0),
                      reads=[rfm.key], writes=[sneg.key])
                S.add("dve", lambda e: e.tensor_tensor(out=tmpk[:], in0=sig[:], in1=OMLt[:, d, :], op=ALU.mult),
                      reads=[sig.key, OMLt.key], writes=[tmpk.key])
                S.add("dve", lambda e: e.scalar_tensor_tensor(out=lf[:], in0=tmpk[:], scalar=1e-20, in1=LBt[:, d, :],
                                                              op0=ALU.max, op1=ALU.add),
                      reads=[tmpk.key, LBt.key], writes=[lf.key])
                S.add("dve", lambda e: e.tensor_tensor(out=ktok[:], in0=OMLt[:, d, :], in1=tmpk[:], op=ALU.subtract),
                      reads=[tmpk.key, OMLt.key], writes=[ktok.key])
                S.add("act", lambda e: e.activation(out=lf[:], in_=lf[:], func=AF.Ln), reads=[lf.key], writes=[lf.key])
                S.next_stage()
                for pr in range(2):
                    S.add("pe", lambda e, pr=pr: e.matmul(pA[:, pr * 128:(pr + 1) * 128], lhsT=lf[:, pr * 128:(pr + 1) * 128], rhs=M1,
                                                          start=True, stop=True), reads=[lf.key, cm.key], writes=[pA.key])
                for pr in range(2):
                    S.add("pe", lambda e, pr=pr: e.matmul(pA[:, 256 + pr * 128:256 + (pr + 1) * 128], lhsT=lf[:, pr * 128:(pr + 1) * 128],
                                                          rhs=M2, start=True, stop=True), reads=[lf.key, cm.key], writes=[pA.key])
                S.add("pe", lambda e: e.matmul(pB[:, 0:256], lhsT=M3, rhs=lf[:], start=True, stop=True),
                      reads=[lf.key, cm.key], writes=[pB.key])
                for pr in range(2):
                    S.add("pe", lambda e, pr=pr: e.matmul(pB[:, 256 + pr * 4:260 + pr * 4], lhsT=lf[:, pr * 128:(pr + 1) * 128], rhs=M4,
                                                          start=True, stop=True), reads=[lf.key, cm.key], writes=[pB.key])
                S.add("dve", lambda e: e.tensor_scalar(out=P1c[:], in0=pA[:, 0:256], scalar1=40.0, scalar2=-40.0, op0=ALU.min, op1=ALU.max),
                      reads=[pA.key], writes=[P1c.key])
                S.add("act", lambda e: e.activation(out=Ep[:], in_=P1c[:], func=AF.Exp), reads=[P1c.key], writes=[Ep.key])
                S.add("act", lambda e: e.activation(out=En[:], in_=P1c[:], func=AF.Exp, scale=-1.0), reads=[P1c.key], writes=[En.key])
                S.add("act", lambda e: e.activation(out=E2[:], in_=pA[:, 256:512], func=AF.Exp), reads=[pA.key], writes=[E2.key])
                S.add("act", lambda e: e.activation(out=E3[:], in_=pB[:, 0:256], func=AF.Exp), reads=[pB.key], writes=[E3.key])
                c0 = ti * 4
                for pr in range(2):
                    S.add("act", lambda e, c0=c0, pr=pr: e.activation(out=DEC[:, pr, c0:c0 + 4], in_=pB[:, 256 + pr * 4:260 + pr * 4],
                                                                      func=AF.Exp), reads=[pB.key], writes=[DEC.key])
                qf2 = qfm[:].rearrange("p a b -> p (a b)")
                S.add("dve", lambda e: e.tensor_tensor(out=qt[:].rearrange("p a b -> p (a b)"), in0=qf2, in1=Ep[:], op=ALU.mult),
                      reads=[qfm.key, Ep.key], writes=[qt.key])
                S.add("dve", lambda e, QH=QH: e.tensor_tensor(out=QH[:].rearrange("p a b -> p (a b)"), in0=qf2, in1=E2[:], op=ALU.mult),
                      reads=[qfm.key, E2.key], writes=[QH.key])
                for pr in range(2):
                    for h2 in range(2):
                        S.add("dve", lambda e, pr=pr, h2=h2: e.scalar_tensor_tensor(
                            out=kt[:, pr, h2, :], in0=sneg[:, pr, :], scalar=omlfh[:, d, pr, h2:h2 + 1],
                            in1=En[:, pr * 128:(pr + 1) * 128], op0=ALU.mult, op1=ALU.mult),
                            reads=[sneg.key, omlfh.key, En.key], writes=[kt.key])
                for j in range(4):
                    S.add("dve", lambda e, j=j: e.scalar_tensor_tensor(out=khat[:, j, :], in0=ktok[:], scalar=M4n[:, j:j + 1], in1=E3[:],
                                                                       op0=ALU.mult, op1=ALU.mult),
                          reads=[ktok.key, cm.key, E3.key], writes=[khat.key])
                if d == 0 or True:
                    hm = cm[:, CM_HM:CM_HM + 256].rearrange("p (a b) -> p a b", a=2)
                    S.add("dve", lambda e, VZ=VZ, hm=hm: e.tensor_tensor(
                        out=VZ[:], in0=vtok[:].rearrange("p (a b) -> p a b", a=2).unsqueeze(2).to_broadcast([128, 2, 2, 128]),
                        in1=hm.unsqueeze(1).to_broadcast([128, 2, 2, 128]), op=ALU.mult),
                        reads=[vtok.key, cm.key], writes=[VZ.key])
                S.next_stage()
                for h in range(4):
                    pr, h2 = h // 2, h % 2
                    S.add("pe", lambda e, h=h, pr=pr, h2=h2: e.matmul(pC[:, h * 128:(h + 1) * 128], lhsT=kt[:, pr, h2, :],
                                                                      rhs=qt[:, pr, :], start=True, stop=True),
                          reads=[kt.key, qt.key], writes=[pC.key])
                msk = cm[:, (CM_M2F, CM_M2B)[d]:(CM_M2F, CM_M2B)[d] + 128]
                S.add("dve", lambda e, AT=AT, msk=msk: e.tensor_tensor(out=AT[:], in0=pC[:].rearrange("p (a b) -> p a b", a=4),
                                                                       in1=msk.unsqueeze(1).to_broadcast([128, 4, 128]), op=ALU.mult),
                      reads=[pC.key, cm.key], writes=[AT.key])
                S.next_stage()
                for pr in range(2):
                    for jj in range(4):
                        j = jj if d == 0 else 3 - jj
                        S.add("pe", lambda e, pr=pr, jj=jj, j=j: e.matmul(pU[:, jj * 128:(jj + 1) * 128], lhsT=khat[:, j, pr * 128:(pr + 1) * 128],
                                                                          rhs=vtok_bf[:, pr * 128:(pr + 1) * 128], start=True, stop=True),
                              reads=[khat.key, vtok_bf.key], writes=[pU.key])
                    for h2 in range(2):
                        src = pU[h2 * 64:(h2 + 1) * 64, :].rearrange("p (a b) -> p a b", a=4)[:, :, h2 * 64:(h2 + 1) * 64]
                        dstv = D1[pr][h2 * 64:(h2 + 1) * 64, :, 1 + c0:1 + c0 + 4].rearrange("p v c -> p c v")
                        if h2 == 0:
                            S.add("act", lambda e, pr=pr, src=src, dstv=dstv: e.copy(out=dstv, in_=src),
                                  reads=[pU.key], writes=[D1[pr].key])
                        else:
                            S.add("dve", lambda e, pr=pr, src=src, dstv=dstv: e.tensor_copy(out=dstv, in_=src),
                                  reads=[pU.key], writes=[D1[pr].key])
            run_staged(S, p1, seg)
            for pr in range(2):
                SB = Sblk[pr]
                S.add("dve", lambda e, pr=pr: e.tensor_copy(out=D0[pr][:, :, 1:1 + nch],
                                                            in_=DEC[:, pr, 0:nch].unsqueeze(1).to_broadcast([128, 64, nch])),
                      reads=[DEC.key], writes=[D0[pr].key])
                S.add("dve", lambda e, pr=pr: e.tensor_tensor_scan(
                    out=SO[pr][:].rearrange("p v c -> p (v c)"), data0=D0[pr][:].rearrange("p v c -> p (v c)"),
                    data1=D1[pr][:].rearrange("p v c -> p (v c)"), initial=0.0, op0=ALU.mult, op1=ALU.add),
                    reads=[D0[pr].key, D1[pr].key], writes=[SO[pr].key])
                S.add("act", lambda e, pr=pr, SB=SB: e.copy(out=SB[0:64, 0:nch, 0:64], in_=SO[pr][0:64, :, 0:nch].rearrange("p v c -> p c v")),
                      reads=[SO[pr].key], writes=[SB.key])
                S.add("act", lambda e, pr=pr, SB=SB: e.copy(out=SB[64:128, 0:nch, 64:128], in_=SO[pr][64:128, :, 0:nch].rearrange("p v c -> p c v")),
                      reads=[SO[pr].key], writes=[SB.key])
                S.add("dve", lambda e, pr=pr: e.tensor_copy(out=D1[pr][:, :, 0:1], in_=SO[pr][:, :, nch:nch + 1]),
                      reads=[SO[pr].key], writes=[D1[pr].key])
            def p2(ti, tile):
                n0 = tile * 128
                of_ld, osum, osq, orstd, sgl, hgo = of_ld_R[ti % 4], osum_R[ti % 4], osq_R[ti % 4], orstd_R[ti % 4], sgl_R[ti % 4], hgo_R[ti % 4]
                AT, QH, VZ = ATs[ti], QHs[ti], VZs[ti]
                for pr in range(2):
                    for h2 in range(2):
                        h = pr * 2 + h2
                        S.add("pe", lambda e, pr=pr, h2=h2, h=h, AT=AT, VZ=VZ: e.matmul(
                            pO[:, pr * 128:(pr + 1) * 128], lhsT=VZ[:, pr, h2, :], rhs=AT[:, h, :],
                            start=(pr == 0 and h2 == 0), stop=False, skip_group_check=True),
                            reads=[VZ.key, AT.key], writes=[pO.key])
                for pr in range(2):
                    for j in range(4):
                        jj = j if d == 0 else 3 - j
                        c = ti * 4 + jj
                        S.add("pe", lambda e, pr=pr, j=j, c=c, QH=QH: e.matmul(
                            pO[:, pr * 128 + j * 32:pr * 128 + (j + 1) * 32], lhsT=Sblk[pr][:, c, :], rhs=QH[:, pr, j * 32:(j + 1) * 32],
                            start=False, stop=(pr == 1 and j == 3), skip_group_check=True),
                            reads=[Sblk[pr].key, QH.key], writes=[pO.key])
                if d == 0:
                    S.add("act", lambda e: e.copy(out=osum[:].rearrange("p a b -> p (a b)"), in_=pO[:, 0:256]), reads=[pO.key], writes=[osum.key])
                    S.add("sp", lambda e, n0=n0: e.dma_start(out=OHGr[:, :, n0:n0 + 128], in_=osum[:]), reads=[osum.key], dma=True,
                          semkey="st_" + osum.key)
                else:
                    S.add("sp", lambda e, n0=n0: e.dma_start(out=of_ld[:], in_=OHGr[:, :, n0:n0 + 128]), writes=[of_ld.key], dma=True)
                    S.add("sp", lambda e, n0=n0: e.dma_start(out=sgl[:], in_=PFr[:, PF_G:PF_G + 2, n0:n0 + 128]), writes=[sgl.key], dma=True)
                    S.add("dve", lambda e: e.tensor_tensor(out=osum[:].rearrange("p a b -> p (a b)"), in0=of_ld[:].rearrange("p a b -> p (a b)"),
                                                           in1=pO[:, 0:256], op=ALU.add), reads=[of_ld.key, pO.key], writes=[osum.key])
                    if "ohg" in self.dbg:
                        S.add("sp", lambda e, n0=n0: e.dma_start(out=self.dbg_out["ohg%d" % l].rearrange("(f p) n -> p f n", p=128)[:, :, n0:n0 + 128],
                                                                 in_=osum[:]), reads=[osum.key], dma=True, semkey="dbg_ohg")
                    S.next_stage()
                    S.add("act", lambda e: e.activation(out=osq[:], in_=osum[:], func=AF.Square), reads=[osum.key], writes=[osq.key])
                    S.add("pe", lambda e: e.matmul(pN[:, 0:256], lhsT=blkbf_l[:], rhs=osq[:].rearrange("p a b -> p (a b)"),
                                                   start=True, stop=True), reads=[osq.key, blkbf_l.key], writes=[pN.key])
                    S.add("dve", lambda e: e.tensor_scalar(out=orstd[:], in0=pN[:, 0:256], scalar1=1.0 / 64, scalar2=EPS,
                                                           op0=ALU.mult, op1=ALU.add), reads=[pN.key], writes=[orstd.key])
                    S.add("act", lambda e: e.activation(out=orstd[:], in_=orstd[:], func=AF.Ln), reads=[orstd.key], writes=[orstd.key])
                    S.add("act", lambda e: e.activation(out=orstd[:], in_=orstd[:], func=AF.Exp, scale=-0.5), reads=[orstd.key], writes=[orstd.key])
                    S.add("dve", lambda e: e.tensor_tensor(out=osum[:].rearrange("p a b -> p (a b)"), in0=osum[:].rearrange("p a b -> p (a b)"),
                                                           in1=orstd[:], op=ALU.mult), reads=[osum.key, orstd.key], writes=[osum.key])
                    wo = l * FV_L + FV_HGN
                    for pr in range(2):
                        S.add("dve", lambda e, pr=pr: e.scalar_tensor_tensor(out=hgo[:, pr, :], in0=osum[:, pr, :], scalar=fv[:, wo + pr:wo + pr + 1],
                                                                             in1=sgl[:, pr, :], op0=ALU.mult, op1=ALU.mult),
                              reads=[osum.key, fv.key, sgl.key], writes=[hgo.key])
                    S.add("sp", lambda e, n0=n0: e.dma_start(out=MIXr[:, 0:2, n0:n0 + 128], in_=hgo[:]), reads=[hgo.key], dma=True,
                          semkey="st_" + hgo.key)
            run_staged(S, p2, seg)
        for seg in _hg_tiles(self, d):
            do_seg(seg)
        self.barrier()
    for d in range(2):
        do_dir(d)
    self.reset(m)


def _ssd_io(self):
    nc = self.nc
    self.XSs = nc.dram_tensor("XSs", [self.NTOK, 512], F32).ap()
    self.BCf = nc.dram_tensor("BCfs", [512, self.NTOK], BF16).ap()
    self.Bts = nc.dram_tensor("Bts", [self.NTOK, 256], BF16).ap()
    self.YS = nc.dram_tensor("YSs", [self.NTOK, 512], F32).ap()


def _phase_ssd(self, l):
    S = self.S
    m = self.mark()
    cm, fv, rv, ps = self.cm, self.fv, self.rv, self.ps
    NT = self.NT
    PFr = self.PF.rearrange("(f p) n -> p f n", p=128)
    BCr = self.BCf.rearrange("(f p) n -> p f n", p=128)
    MIXr = self.MIXT.rearrange("(f p) n -> p f n", p=128)
    IDENT = cm[:, CM_ID:CM_ID + 128]
    ONESF = cm[:, CM_ONES:CM_ONES + 128]
    fo = l * FV_L
    ro = l * RV_L
    xin_R = [self.sb("xin", [128, 8, 132], F32) for _ in range(2)]
    acc_R = [self.sb("acc", [128, 8, 128], F32) for _ in range(2)]
    ctmp_R = [self.sb("ctmp", [128, 8, 128], F32) for _ in range(2)]
    bcb_R = [self.sb("bcb", [128, 4, 128], BF16) for _ in range(2)]
    xs_st_R = [self.sb("xs_st", [128, 512], F32) for _ in range(2)]
    bt_st_R = [self.sb("bt_st", [128, 256], BF16) for _ in range(2)]
    pX, pBt = ps[1], ps[2]
    CW = fv[:, fo + FV_CW:fo + FV_CW + 40].rearrange("p (j k) -> p j k", j=5)
    CB = fv[:, fo + FV_CB:fo + FV_CB + 8]

    def conv_tile(tile):
        n0 = tile * 128
        xin, acc, ctmp, bcb, xs_st, bt_st = xin_R[tile % 2], acc_R[tile % 2], ctmp_R[tile % 2], bcb_R[tile % 2], xs_st_R[tile % 2], bt_st_R[tile % 2]
        s_lo, s_hi = (0, CTX) if tile < 2 else (CTX, self.NTOK)
        lo, hi = max(n0 - 2, s_lo), min(n0 + 130, s_hi)
        S.add("dve", lambda e: e.memset(xin[:, :, 0:2], 0.0), writes=[xin.key])
        S.add("dve", lambda e: e.memset(xin[:, :, 130:132], 0.0), writes=[xin.key])
        S.add("sp", lambda e: e.dma_start(out=xin[:, :, lo - (n0 - 2):hi - (n0 - 2)], in_=PFr[:, PF_XBC:PF_XBC + 8, lo:hi]),
              writes=[xin.key], dma=True)
        S.add("dve", lambda e: e.tensor_tensor(out=acc[:], in0=xin[:, :, 0:128], in1=CW[:, 0, :].unsqueeze(2).to_broadcast([128, 8, 128]),
                                               op=ALU.mult), reads=[xin.key, fv.key], writes=[acc.key])
        for j in range(1, 5):
            S.add("pool", lambda e, j=j: e.tensor_tensor(out=ctmp[:], in0=xin[:, :, j:j + 128],
                                                         in1=CW[:, j, :].unsqueeze(2).to_broadcast([128, 8, 128]), op=ALU.mult),
                  reads=[xin.key, fv.key], writes=[ctmp.key])
            S.add("dve", lambda e: e.tensor_tensor(out=acc[:], in0=acc[:], in1=ctmp[:], op=ALU.add),
                  reads=[acc.key, ctmp.key], writes=[acc.key])
        S.add("dve", lambda e: e.tensor_tensor(out=acc[:], in0=acc[:], in1=CB.unsqueeze(2).to_broadcast([128, 8, 128]), op=ALU.add),
              reads=[acc.key, fv.key], writes=[acc.key])
        S.add("act", lambda e: e.activation(out=acc[:], in_=acc[:], func=AF.Silu), reads=[acc.key], writes=[acc.key])
        S.add("dve", lambda e: e.tensor_copy(out=bcb[:], in_=acc[:, 4:8, :]), reads=[acc.key], writes=[bcb.key])
        S.add("sp", lambda e: e.dma_start(out=BCr[:, :, n0:n0 + 128], in_=bcb[:]), reads=[bcb.key], dma=True, semkey="st_" + bcb.key)
        for k in range(4):
            S.add("pe", lambda e, k=k: e.transpose(out=pX[:, k * 128:(k + 1) * 128], in_=acc[:, k, :], identity=IDENT),
                  reads=[acc.key, cm.key], writes=[pX.key])
        S.add("act", lambda e: e.copy(out=xs_st[:], in_=pX[:]), reads=[pX.key], writes=[xs_st.key])
        S.add("sp", lambda e: e.dma_start(out=self.XSs[n0:n0 + 128, :], in_=xs_st[:]), reads=[xs_st.key], dma=True, semkey="st_" + xs_st.key)
        for k in range(2):
            S.add("pe", lambda e, k=k: e.transpose(out=pBt[:, k * 128:(k + 1) * 128], in_=acc[:, 4 + k, :], identity=IDENT),
                  reads=[acc.key, cm.key], writes=[pBt.key])
        S.add("dve", lambda e: e.tensor_copy(out=bt_st[:], in_=pBt[:, 0:256]), reads=[pBt.key], writes=[bt_st.key])
        S.add("sp", lambda e: e.dma_start(out=self.Bts[n0:n0 + 128, :], in_=bt_st[:]), reads=[bt_st.key], dma=True, semkey="st_" + bt_st.key)

    for tile in range(NT):
        conv_tile(tile)
    self.barrier()
    self.reset(m)
    m = self.mark()
    Arow = self.sb("Arow", [128, 16], F32)
    S.add("act", lambda e: e.activation(out=Arow[:], in_=rv[:, ro + RV_ALOG:ro + RV_ALOG + 16], func=AF.Exp),
          reads=[rv.key], writes=[Arow.key])
    S.add("dve", lambda e: e.tensor_scalar_mul(out=Arow[:], in0=Arow[:], scalar1=-1.0), reads=[Arow.key], writes=[Arow.key])
    DTB = rv[:, ro + RV_DTB:ro + RV_DTB + 16]
    D0 = [self.sb("sD0", [128, 256, 9], F32) for _ in range(2)]
    D1 = [self.sb("sD1", [128, 256, 9], F32) for _ in range(2)]
    SO = [self.sb("sSO", [128, 256, 9], F32) for _ in range(2)]
    Hbf = [self.sb("Hbf", [128, 8, 256], BF16) for _ in range(2)]
    YD = [self.sb("YD", [128, 512], F32) for _ in range(8)]
    CFs = [self.sb("CFs", [128, 2, 128], BF16) for _ in range(8)]
    ECs = [self.sb("ECs", [128, 8], F32) for _ in range(8)]
    dtr_R4 = [self.sb("dtr", [128, 8], F32) for _ in range(4)]
    dt_R4 = [self.sb("dt", [128, 8], F32) for _ in range(4)]
    av_R4 = [self.sb("av", [128, 8], F32) for _ in range(4)]
    ABC_R4 = [self.sb("ABC", [128, 8, 128], F32) for _ in range(4)]
    ncum_R4 = [self.sb("ncum", [128, 8], F32) for _ in range(4)]
    dend_R4 = [self.sb("dend", [128, 8], F32) for _ in range(4)]
    dect_R4 = [self.sb("dect", [128, 8], F32) for _ in range(4)]
    xst_R4 = [self.sb("xst", [128, 512], F32) for _ in range(4)]
    btl_R4 = [self.sb("btl", [128, 256], BF16) for _ in range(4)]
    bcl_R4 = [self.sb("bcl", [128, 4, 128], BF16) for _ in range(4)]
    Lsb_R4 = [self.sb("Lsb", [128, 8, 128], F32) for _ in range(4)]
    Wb_R4 = [self.sb("Wb", [128, 8, 128], BF16) for _ in range(4)]
    xdt_R4 = [self.sb("xdt", [128, 8, 64], BF16) for _ in range(4)]
    xw_R4 = [self.sb("xw", [128, 8, 64], BF16) for _ in range(4)]
    xst2_R = [self.sb("xst2", [128, 512], F32) for _ in range(4)]
    yo_R = [self.sb("yo", [128, 512], F32) for _ in range(4)]
    yf_R = [self.sb("yf", [128, 512], F32) for _ in range(4)]
    zt_R = [self.sb("zt", [128, 512], F32) for _ in range(4)]
    ssq_R = [self.sb("ssq", [128, 2], F32) for _ in range(4)]
    yT_R = [self.sb("yT", [128, 4, 128], BF16) for _ in range(4)]
    for g in range(2):
        S.add("dve", lambda e, g=g: e.memset(D0[g][:], 0.0), writes=[D0[g].key])
        S.add("dve", lambda e, g=g: e.memset(D1[g][:], 0.0), writes=[D1[g].key])
    pS, pLa, pLb, pG, pY, pH, pY2, pT = ps[0], ps[1], ps[2], ps[3], ps[4], ps[5], ps[6], ps[7]
    SSNrow = rv[:, ro + RV_SSN:ro + RV_SSN + 512]
    DSKrow = rv[:, ro + RV_DSK:ro + RV_DSK + 512]

    def do_dir(d):
        TRI = cm[:, (CM_TRIF, CM_TRIB)[d]:(CM_TRIF, CM_TRIB)[d] + 128]
        NEGM = cm[:, (CM_NEGF, CM_NEGB)[d]:(CM_NEGF, CM_NEGB)[d] + 128]
        for g in range(2):
            S.add("dve", lambda e, g=g: e.memset(D1[g][:, :, 0:1], 0.0), writes=[D1[g].key])

        def do_seg(seg):
            nt = len(seg)

            def p1(ti, tile):
                n0 = tile * 128
                dtr, dt, av, ABC, ncum, dend, dect, xst, btl, bcl, Lsb, Wb, xdt, xw = dtr_R4[ti % 4], dt_R4[ti % 4], av_R4[ti % 4], ABC_R4[ti % 4], ncum_R4[ti % 4], dend_R4[ti % 4], dect_R4[ti % 4], xst_R4[ti % 4], btl_R4[ti % 4], bcl_R4[ti % 4], Lsb_R4[ti % 4], Wb_R4[ti % 4], xdt_R4[ti % 4], xw_R4[ti % 4]
                S.add("sp", lambda e: e.dma_start(out=dtr[:], in_=self.PT[n0:n0 + 128, PT_DT + d * 8:PT_DT + d * 8 + 8]),
                      writes=[dtr.key], dma=True)
                S.add("sp", lambda e: e.dma_start(out=xst[:], in_=self.XSs[n0:n0 + 128, :]), writes=[xst.key], dma=True)
                S.add("sp", lambda e: e.dma_start(out=btl[:], in_=self.Bts[n0:n0 + 128, :]), writes=[btl.key], dma=True)
                S.add("sp", lambda e: e.dma_start(out=bcl[:], in_=BCr[:, :, n0:n0 + 128]), writes=[bcl.key], dma=True)
                S.add("dve", lambda e: e.tensor_tensor(out=dt[:], in0=dtr[:], in1=DTB[:, d * 8:d * 8 + 8], op=ALU.add),
                      reads=[dtr.key, rv.key], writes=[dt.key])
                S.add("act", lambda e: e.activation(out=dt[:], in_=dt[:], func=AF.Exp), reads=[dt.key], writes=[dt.key])
                S.add("dve", lambda e: e.tensor_scalar_add(out=dt[:], in0=dt[:], scalar1=1.0), reads=[dt.key], writes=[dt.key])
                S.add("act", lambda e: e.activation(out=dt[:], in_=dt[:], func=AF.Ln), reads=[dt.key], writes=[dt.key])
                S.add("dve", lambda e: e.tensor_tensor(out=av[:], in0=dt[:], in1=Arow[:, d * 8:d * 8 + 8], op=ALU.mult),
                      reads=[dt.key, Arow.key], writes=[av.key])
                S.add("dve", lambda e: e.tensor_copy(out=ABC[:], in_=av[:].unsqueeze(2).to_broadcast([128, 8, 128])),
                      reads=[av.key], writes=[ABC.key])
                S.next_stage()
                S.add("pe", lambda e: e.matmul(pS[:, 0:8], lhsT=TRI, rhs=av[:], start=True, stop=True), reads=[av.key, cm.key], writes=[pS.key])
                S.add("pe", lambda e: e.matmul(pS[:, 8:16], lhsT=ONESF, rhs=av[:], start=True, stop=True), reads=[av.key, cm.key], writes=[pS.key])
                S.add("dve", lambda e: e.tensor_scalar_mul(out=ncum[:], in0=pS[:, 0:8], scalar1=-1.0), reads=[pS.key], writes=[ncum.key])
                EC = ECs[ti]
                S.add("act", lambda e: e.activation(out=EC[:], in_=pS[:, 0:8], func=AF.Exp), reads=[pS.key], writes=[EC.key])
                S.add("dve", lambda e: e.tensor_tensor(out=dend[:], in0=pS[:, 8:16], in1=ncum[:], op=ALU.add),
                      reads=[pS.key, ncum.key], writes=[dend.key])
                S.add("act", lambda e: e.activation(out=dend[:], in_=dend[:], func=AF.Exp), reads=[dend.key], writes=[dend.key])
                S.add("act", lambda e: e.activation(out=dect[:], in_=pS[:, 8:16], func=AF.Exp), reads=[pS.key], writes=[dect.key])
                S.next_stage()
                for h in range(8):
                    pl = (pLa, pLb)[h // 4]
                    hc = (h % 4) * 128
                    S.add("pe", lambda e, h=h, pl=pl, hc=hc: e.matmul(pl[:, hc:hc + 128], lhsT=ABC[:, h, :], rhs=TRI, start=True, stop=False),
                          reads=[ABC.key, cm.key], writes=[pl.key])
                    S.add("pe", lambda e, h=h, pl=pl, hc=hc: e.matmul(pl[:, hc:hc + 128], lhsT=IDENT, rhs=NEGM, start=False, stop=True),
                          reads=[cm.key], writes=[pl.key])
                    S.add("act", lambda e, h=h, pl=pl, hc=hc: e.activation(out=Lsb[:, h, :], in_=pl[:, hc:hc + 128], func=AF.Exp,
                                                                           bias=ncum[:, h:h + 1]),
                          reads=[pl.key, ncum.key], writes=[Lsb.key])
                S.next_stage()
                for g in range(2):
                    S.add("pe", lambda e, g=g: e.matmul(pG[:, g * 128:(g + 1) * 128], lhsT=bcl[:, g, :], rhs=bcl[:, 2 + g, :],
                                                        start=True, stop=True), reads=[bcl.key], writes=[pG.key])
                for g in range(2):
                    S.add("dve", lambda e, g=g: e.tensor_tensor(
                        out=Wb[:, g * 4:(g + 1) * 4, :], in0=Lsb[:, g * 4:(g + 1) * 4, :],
                        in1=pG[:, g * 128:(g + 1) * 128].unsqueeze(1).to_broadcast([128, 4, 128]), op=ALU.mult),
                        reads=[Lsb.key, pG.key], writes=[Wb.key])
                S.add("dve", lambda e: e.tensor_tensor(out=xdt[:], in0=xst[:].rearrange("p (h q) -> p h q", h=8),
                                                       in1=dt[:].unsqueeze(2).to_broadcast([128, 8, 64]), op=ALU.mult),
                      reads=[xst.key, dt.key], writes=[xdt.key])
                S.next_stage()
                for h in range(8):
                    S.add("pe", lambda e, h=h: e.matmul(pY[:, h * 64:(h + 1) * 64], lhsT=Wb[:, h, :], rhs=xdt[:, h, :], start=True, stop=True),
                          reads=[Wb.key, xdt.key], writes=[pY.key])
                Y = YD[ti]
                S.add("act", lambda e: e.copy(out=Y[:], in_=pY[:]), reads=[pY.key], writes=[Y.key])
                CF = CFs[ti]
                S.add("dve", lambda e: e.tensor_copy(out=CF[:], in_=bcl[:, 2:4, :]), reads=[bcl.key], writes=[CF.key])
                S.add("dve", lambda e: e.tensor_tensor(out=xw[:], in0=xdt[:], in1=dend[:].unsqueeze(2).to_broadcast([128, 8, 64]), op=ALU.mult),
                      reads=[xdt.key, dend.key], writes=[xw.key])
                S.next_stage()
                for g in range(2):
                    S.add("pe", lambda e, g=g: e.matmul(pH[:, g * 256:(g + 1) * 256], lhsT=btl[:, g * 128:(g + 1) * 128],
                                                        rhs=xw[:, g * 4:(g + 1) * 4, :].rearrange("p h q -> p (h q)"), start=True, stop=True),
                          reads=[btl.key, xw.key], writes=[pH.key])
                for g in range(2):
                    S.add("act", lambda e, g=g: e.copy(out=D1[g][:, :, 1 + ti:2 + ti], in_=pH[:, g * 256:(g + 1) * 256].unsqueeze(2)),
                          reads=[pH.key], writes=[D1[g].key])
                    S.add("dve", lambda e, g=g: e.tensor_copy(
                        out=D0[g][:, :, 1 + ti:2 + ti].rearrange("p (h q) o -> p h (q o)", h=4),
                        in_=dect[:, g * 4:(g + 1) * 4].unsqueeze(2).to_broadcast([128, 4, 64])),
                        reads=[dect.key], writes=[D0[g].key])

            for g0 in range(0, nt, 4):
                recs = []
                for ti in range(g0, min(g0 + 4, nt)):
                    S.begin_record()
                    p1(ti, seg[ti])
                    recs.append(S.end_record())
                S.replay_staged(recs)
            for g in range(2):
                S.add("dve", lambda e, g=g: e.tensor_tensor_scan(
                    out=SO[g][:].rearrange("p v c -> p (v c)"), data0=D0[g][:].rearrange("p v c -> p (v c)"),
                    data1=D1[g][:].rearrange("p v c -> p (v c)"), initial=0.0, op0=ALU.mult, op1=ALU.add),
                    reads=[D0[g].key, D1[g].key], writes=[SO[g].key])
                S.add("act", lambda e, g=g: e.copy(out=Hbf[g][:, 0:nt, :], in_=SO[g][:, :, 0:nt].rearrange("p v c -> p c v")),
                      reads=[SO[g].key], writes=[Hbf[g].key])
                S.add("dve", lambda e, g=g: e.tensor_copy(out=D1[g][:, :, 0:1], in_=SO[g][:, :, nt:nt + 1]),
                      reads=[SO[g].key], writes=[D1[g].key])

            def p2(ti, tile):
                n0 = tile * 128
                yo, yf, zt, ssq, yT, xst2 = yo_R[ti % 4], yf_R[ti % 4], zt_R[ti % 4], ssq_R[ti % 4], yT_R[ti % 4], xst2_R[ti % 4]
                xst = xst2
                CF, EC, Y = CFs[ti], ECs[ti], YD[ti]
                for g in range(2):
                    S.add("pe", lambda e, g=g: e.matmul(pY2[:, g * 256:(g + 1) * 256], lhsT=CF[:, g, :], rhs=Hbf[g][:, ti, :], start=True, stop=True),
                          reads=[CF.key, Hbf[g].key], writes=[pY2.key])
                S.add("dve", lambda e: e.tensor_tensor(out=yo[:].rearrange("p (h q) -> p h q", h=8), in0=pY2[:].rearrange("p (h q) -> p h q", h=8),
                                                       in1=EC[:].unsqueeze(2).to_broadcast([128, 8, 64]), op=ALU.mult),
                      reads=[pY2.key, EC.key], writes=[yo.key])
                S.add("dve", lambda e: e.tensor_tensor(out=yo[:], in0=yo[:], in1=Y[:], op=ALU.add), reads=[yo.key, Y.key], writes=[yo.key])
                if d == 0:
                    S.add("sp", lambda e: e.dma_start(out=self.YS[n0:n0 + 128, :], in_=yo[:]), reads=[yo.key], dma=True, semkey="st_" + yo.key)
                    return
                S.add("sp", lambda e: e.dma_start(out=yf[:], in_=self.YS[n0:n0 + 128, :]), writes=[yf.key], dma=True)
                S.add("sp", lambda e: e.dma_start(out=xst[:], in_=self.XSs[n0:n0 + 128, :]), writes=[xst.key], dma=True)
                S.add("sp", lambda e: e.dma_start(out=zt[:], in_=self.PT[n0:n0 + 128, PT_Z:PT_Z + 512]), writes=[zt.key], dma=True)
                S.next_stage()
                S.add("dve", lambda e: e.tensor_tensor(out=yo[:], in0=yo[:], in1=yf[:], op=ALU.add), reads=[yo.key, yf.key], writes=[yo.key])
                S.add("dve", lambda e: e.tensor_tensor(out=xst[:], in0=xst[:], in1=DSKrow, op=ALU.mult), reads=[xst.key, rv.key], writes=[xst.key])
                S.add("dve", lambda e: e.tensor_tensor(out=yo[:], in0=yo[:], in1=xst[:], op=ALU.add), reads=[yo.key, xst.key], writes=[yo.key])
                if "yssm" in self.dbg:
                    S.add("sp", lambda e: e.dma_start(out=self.dbg_out["yssm%d" % l][n0:n0 + 128, :], in_=yo[:]), reads=[yo.key], dma=True,
                          semkey="dbg_yssm")
                S.add("act", lambda e: e.activation(out=zt[:], in_=zt[:], func=AF.Silu), reads=[zt.key], writes=[zt.key])
                S.add("dve", lambda e: e.tensor_tensor(out=yo[:], in0=yo[:], in1=zt[:], op=ALU.mult), reads=[yo.key, zt.key], writes=[yo.key])
                S.add("act", lambda e: e.activation(out=zt[:], in_=yo[:], func=AF.Square, accum_out=ssq[:, 0:1]),
                      reads=[yo.key], writes=[zt.key, ssq.key])
                S.add("dve", lambda e: e.tensor_scalar(out=ssq[:, 1:2], in0=ssq[:, 0:1], scalar1=1.0 / 512, scalar2=EPS, op0=ALU.mult, op1=ALU.add),
                      reads=[ssq.key], writes=[ssq.key])
                S.add("act", lambda e: e.activation(out=ssq[:, 1:2], in_=ssq[:, 1:2], func=AF.Ln), reads=[ssq.key], writes=[ssq.key])
                S.add("act", lambda e: e.activation(out=ssq[:, 1:2], in_=ssq[:, 1:2], func=AF.Exp, scale=-0.5), reads=[ssq.key], writes=[ssq.key])
                S.add("dve", lambda e: e.scalar_tensor_tensor(out=yo[:], in0=yo[:], scalar=ssq[:, 1:2], in1=SSNrow, op0=ALU.mult, op1=ALU.mult),
                      reads=[yo.key, ssq.key, rv.key], writes=[yo.key])
                S.next_stage()
                for k in range(4):
                    S.add("pe", lambda e, k=k: e.transpose(out=pT[:, k * 128:(k + 1) * 128], in_=yo[:, k * 128:(k + 1) * 128], identity=IDENT),
                          reads=[yo.key, cm.key], writes=[pT.key])
                S.add("act", lambda e: e.copy(out=yT[:].rearrange("p a b -> p (a b)"), in_=pT[:]), reads=[pT.key], writes=[yT.key])
                S.add("sp", lambda e: e.dma_start(out=MIXr[:, 4:8, n0:n0 + 128], in_=yT[:]), reads=[yT.key], dma=True, semkey="st_" + yT.key)

            run_staged(S, p2, seg)

        for seg in _hg_tiles(self, d):
            do_seg(seg)
        self.barrier()

    for d in range(2):
        do_dir(d)
    self.reset(m)


NAB_N = 5 * 4 * 5 * 128


def make_nabias(rpb, nlat):
    rows = 2 * nlat
    out = np.full((128, 5, 4, 5, 128), NEG, np.float32)
    its = [0, 1, 2, nlat - 2, nlat - 1]
    p = np.arange(128)
    q = np.arange(128)
    for v, it in enumerate(its):
        kt0 = min(max(it - 2, 0), nlat - 5)
        r = 2 * it + q // 64
        cq = q % 64
        r0 = np.clip(r - 4, 0, rows - 8)
        c0 = np.clip(cq - 8, 0, 48)
        for kt in range(5):
            rk = 2 * (kt0 + kt) + p // 64
            ck = p % 64
            inw = ((rk[:, None] >= r0[None, :]) & (rk[:, None] < r0[None, :] + 8)
                   & (ck[:, None] >= c0[None, :]) & (ck[:, None] < c0[None, :] + 16))
            dr = np.clip(rk[:, None] - r[None, :] + 7, 0, 14)
            dc = np.clip(ck[:, None] - cq[None, :], -15, 15) + 15
            for h in range(4):
                b = rpb[h][dr, dc]
                out[:, v, h, kt, :] = np.where(inw, b, NEG)
    return out.reshape(128, NAB_N)


def _na_io(self):
    nc = self.nc
    self.nabias = nc.dram_tensor("nabias", [DEPTH, 128, NAB_N], F32, kind="ExternalInput").ap()


def _phase_na(self, l):
    S = self.S
    m = self.mark()
    cm, fv, rv, ps = self.cm, self.fv, self.rv, self.ps
    NT, nlat, NTOK = self.NT, self.nlat, self.NTOK
    last = (l == DEPTH - 1)
    PFr = self.PF.rearrange("(f p) n -> p f n", p=128)
    MIXr = self.MIXT.rearrange("(f p) n -> p f n", p=128)
    IDENT = cm[:, CM_ID:CM_ID + 128]
    ro = l * RV_L
    NANrow = rv[:, ro + RV_NAN:ro + RV_NAN + 256]
    KT = self.sb("KT", [128, 2, NTOK], BF16)
    Vaug = self.sb("Vaug", [128, NT, 4, 65], BF16)
    BT = self.sb("BT", [128, 5, 4, 5, 128], F32)
    qf_R = [self.sb("qf", [128, 2, 128], F32) for _ in range(2)]
    QZ_R = [self.sb("QZ", [128, 2, 2, 128], BF16) for _ in range(2)]
    mx_R = [self.sb("mx", [128, 2], F32) for _ in range(2)]
    DG_R = [self.sb("DG", [128, 128], BF16) for _ in range(2)]
    PTs_R = [self.sb("PTs", [128, 896], BF16) for _ in range(2)]
    rc_R = [self.sb("rc", [128, 4], F32) for _ in range(2)]
    onat_R = [self.sb("onat", [128, 4, 64], F32) for _ in range(2)]
    junk_R = [self.sb("junk", [128, 256], F32) for _ in range(2)]
    ssq_R = [self.sb("nssq", [128, 2], F32) for _ in range(2)]
    oT_R = [self.sb("oT", [128, 2, 128], BF16) for _ in range(2)]
    hrm = self.sb("hrm", [128, 2], F32)
    SB, ST = self.psbig[0], self.psbig[1]
    pOV, pT = ps[4], ps[5]
    for pr in range(2):
        S.add("pool", lambda e, pr=pr: e.dma_start(out=KT[:, pr, :], in_=PFr[:, PF_KA + pr, :]), writes=[KT.key], dma=True)
    S.add("sp", lambda e: e.dma_start(out=BT[:].rearrange("p a b c d -> p (a b c d)"), in_=self.nabias[l]), writes=[BT.key], dma=True)
    S.add("dve", lambda e: e.memset(Vaug[:, :, :, 64:65], 1.0), writes=[Vaug.key])
    for t in range(NT):
        S.add("pool", lambda e, t=t: e.dma_start(out=Vaug[:, t, :, 0:64],
                                                  in_=self.PT[t * 128:(t + 1) * 128, PT_VA:PT_VA + 256].rearrange("p (h q) -> p h q", h=4)),
              writes=[Vaug.key], dma=True)
    for h2 in range(2):
        S.add("dve", lambda e, h2=h2: e.tensor_copy(out=hrm[:, h2:h2 + 1], in_=cm[:, CM_BLK + h2 * 64:CM_BLK + h2 * 64 + 1]),
              reads=[cm.key], writes=[hrm.key])

    def q_tile(tile, keytiles, var):
        n0 = tile * 128
        nk = len(keytiles)
        qf, QZ, rc, onat, junk, ssq, oT = qf_R[tile % 2], QZ_R[tile % 2], rc_R[tile % 2], onat_R[tile % 2], junk_R[tile % 2], ssq_R[tile % 2], oT_R[tile % 2]
        nloc = 5 if var is not None else 0
        S.add("sp", lambda e: e.dma_start(out=qf[:], in_=PFr[:, PF_QA:PF_QA + 2, n0:n0 + 128]), writes=[qf.key], dma=True)
        S.add("dve", lambda e: e.tensor_tensor(out=QZ[:], in0=qf[:].unsqueeze(2).to_broadcast([128, 2, 2, 128]),
                                               in1=hrm[:].unsqueeze(1).unsqueeze(3).to_broadcast([128, 2, 2, 128]), op=ALU.mult),
              reads=[qf.key, hrm.key], writes=[QZ.key])
        for h in range(4):
            pr, h2 = h // 2, h % 2
            mx, DG, PTs = mx_R[h % 2], DG_R[h % 2], PTs_R[h % 2]
            SB, ST = (self.psbig[0], self.psbig[1]) if h % 2 == 0 else (self.psbig[3], self.psbig[1])
            col = 0
            runs = []
            i = 0
            while i < nk:
                j = i
                while j + 1 < nk and keytiles[j + 1] == keytiles[j] + 1 and (j + 1 - i) < 4 and ((col + (j + 1 - i) * 128) % 512 != 0):
                    j += 1
                runs.append((keytiles[i], j - i + 1, col))
                col += (j - i + 1) * 128
                i = j + 1
            for (kt_, cnt, c_) in runs:
                S.add("pe", lambda e, pr=pr, h2=h2, kt_=kt_, cnt=cnt, c_=c_: e.matmul(
                    SB[:, c_:c_ + cnt * 128], lhsT=QZ[:, pr, h2, :], rhs=KT[:, pr, kt_ * 128:(kt_ + cnt) * 128], start=True, stop=True),
                    reads=[QZ.key, KT.key], writes=[SB.key])
            S.add("dve", lambda e: e.reduce_max(out=mx[:, 0:1], in_=SB[:, 0:nk * 128], axis=AX.X), reads=[SB.key], writes=[mx.key])
            S.add("dve", lambda e: e.tensor_scalar_mul(out=mx[:, 1:2], in0=mx[:, 0:1], scalar1=-1.0), reads=[mx.key], writes=[mx.key])
            S.add("dve", lambda e: e.tensor_scalar_mul(out=DG[:], in0=IDENT, scalar1=mx[:, 1:2]), reads=[mx.key, cm.key], writes=[DG.key])
            for kk, kt_ in enumerate(keytiles):
                dst = ST[:, kk * 128:(kk + 1) * 128]
                hasb = kk < nloc
                S.add("pe", lambda e, pr=pr, h2=h2, kt_=kt_, dst=dst: e.matmul(dst, lhsT=KT[:, pr, kt_ * 128:(kt_ + 1) * 128], rhs=QZ[:, pr, h2, :],
                                                                               start=True, stop=False),
                      reads=[QZ.key, KT.key], writes=[ST.key])
                S.add("pe", lambda e, dst=dst, hasb=hasb: e.matmul(dst, lhsT=self.ones_bf[:], rhs=DG[:], start=False, stop=(not hasb)),
                      reads=[DG.key, self.ones_bf.key], writes=[ST.key])
                if hasb:
                    S.add("pe", lambda e, dst=dst, h=h, kk=kk: e.matmul(dst, lhsT=IDENT, rhs=BT[:, var, h, kk, :], start=False, stop=True),
                          reads=[BT.key, cm.key], writes=[ST.key])
            S.add("act", lambda e: e.activation(out=PTs[:, 0:nk * 128], in_=ST[:, 0:nk * 128], func=AF.Exp), reads=[ST.key], writes=[PTs.key])
            for kk, kt_ in enumerate(keytiles):
                S.add("pe", lambda e, kk=kk, kt_=kt_, h=h: e.matmul(pOV[:, h * 65:(h + 1) * 65], lhsT=PTs[:, kk * 128:(kk + 1) * 128],
                                                                    rhs=Vaug[:, kt_, h, :], start=(kk == 0), stop=(kk == nk - 1)),
                      reads=[PTs.key, Vaug.key], writes=[pOV.key])
        OVv = pOV[:, 0:260].rearrange("p (h c) -> p h c", h=4)
        S.add("dve", lambda e: e.reciprocal(out=rc[:], in_=OVv[:, :, 64]), reads=[pOV.key], writes=[rc.key])
        S.add("dve", lambda e: e.tensor_tensor(out=onat[:], in0=OVv[:, :, 0:64], in1=rc[:].unsqueeze(2).to_broadcast([128, 4, 64]), op=ALU.mult),
              reads=[pOV.key, rc.key], writes=[onat.key])
        o2 = onat[:].rearrange("p h c -> p (h c)")
        if "naraw" in self.dbg:
            S.add("sp", lambda e: e.dma_start(out=self.dbg_out["naraw%d" % l][n0:n0 + 128, :], in_=o2), reads=[onat.key], dma=True,
                  semkey="dbg_naraw")
        S.add("act", lambda e: e.activation(out=junk[:], in_=o2, func=AF.Square, accum_out=ssq[:, 0:1]), reads=[onat.key],
              writes=[junk.key, ssq.key])
        S.add("dve", lambda e: e.tensor_scalar(out=ssq[:, 1:2], in0=ssq[:, 0:1], scalar1=1.0 / 256, scalar2=EPS, op0=ALU.mult, op1=ALU.add),
              reads=[ssq.key], writes=[ssq.key])
        S.add("act", lambda e: e.activation(out=ssq[:, 1:2], in_=ssq[:, 1:2], func=AF.Ln), reads=[ssq.key], writes=[ssq.key])
        S.add("act", lambda e: e.activation(out=ssq[:, 1:2], in_=ssq[:, 1:2], func=AF.Exp, scale=-0.5), reads=[ssq.key], writes=[ssq.key])
        S.add("dve", lambda e: e.scalar_tensor_tensor(out=junk[:], in0=o2, scalar=ssq[:, 1:2], in1=NANrow, op0=ALU.mult, op1=ALU.mult),
              reads=[onat.key, ssq.key, rv.key, junk.key], writes=[junk.key])
        for k in range(2):
            S.add("pe", lambda e, k=k: e.transpose(out=pT[:, k * 128:(k + 1) * 128], in_=junk[:, k * 128:(k + 1) * 128], identity=IDENT),
                  reads=[junk.key, cm.key], writes=[pT.key])
        S.add("act", lambda e: e.copy(out=oT[:].rearrange("p a b -> p (a b)"), in_=pT[:, 0:256]), reads=[pT.key], writes=[oT.key])
        S.add("sp", lambda e: e.dma_start(out=MIXr[:, 2:4, n0:n0 + 128], in_=oT[:]), reads=[oT.key], dma=True, semkey="st_" + oT.key)

    if not last:
        for tile in range(2):
            q_tile(tile, [0, 1], None)
    for it in range(nlat):
        var = 0 if it == 0 else 1 if it == 1 else 3 if it == nlat - 2 else 4 if it == nlat - 1 else 2
        kt0 = min(max(it - 2, 0), nlat - 5)
        q_tile(it + 2, [kt0 + 2 + k for k in range(5)] + [0, 1], var)
    self.barrier()
    self.reset(m)


NLAT_FULL = 64
_CACHE = {}


def kernel(**inputs):
    inp = {k: np.asarray(v) for k, v in inputs.items()}
    nlat = NLAT_FULL
    B = inp["x"].shape[0]
    kb = KB(nlat=nlat)
    nc = kb.build()
    maps = make_in_maps(inp, nlat, list(range(B)))
    res = run_bass_kernel_spmd(nc, maps, core_ids=list(range(B)))
    out = np.stack([np.asarray(res.results[b]["outT"]).T for b in range(B)])
    return np.ascontiguousarray(out.astype(np.float32))
```

```python
import contextlib
import numpy as np
import concourse.bass as bass
import concourse.mybir as mybir
from concourse.bass_utils import run_bass_kernel_spmd

F32 = mybir.dt.float32
BF16 = mybir.dt.bfloat16
AF = mybir.ActivationFunctionType
ALU = mybir.AluOpType
AX = mybir.AxisListType

D = 1024
DEPTH = 2
DFF = 2816
CTX = 256
GW = 64
EPS = 1e-6
ENGS = ("pe", "act", "dve", "pool", "sp")
PH = "__ph"


class Op:
    __slots__ = ("eng", "fn", "deps", "dma", "semkey", "sig", "val", "idx", "slot")

    def __init__(self, eng, fn, dma, semkey):
        self.eng = eng
        self.fn = fn
        self.deps = {}
        self.dma = dma
        self.semkey = semkey
        self.sig = False
        self.val = 0
        self.idx = 0
        self.slot = None


class Sched:
    def __init__(self, nc):
        self.nc = nc
        self.ops = []
        self.last_w = {}
        self.readers = {}
        self.slotmap = {}

    def begin_record(self):
        self.rec = []
        self.rec_stage = 0

    def next_stage(self):
        if getattr(self, "rec", None) is not None:
            self.rec_stage += 1

    def end_record(self):
        r = self.rec
        self.rec = None
        return r

    def replay_staged(self, recs):
        nst = 1 + max((st for r in recs for (st, a, k) in r), default=0)
        for st in range(nst):
            for r in recs:
                for (s_, a, k) in r:
                    if s_ == st:
                        self.add(*a, **k)

    def add(self, eng, fn, reads=(), writes=(), dma=False, semkey=None, barrier=False):
        if getattr(self, "rec", None) is not None:
            self.rec.append((self.rec_stage, (eng, fn), dict(reads=reads, writes=writes, dma=dma, semkey=semkey, barrier=barrier)))
            return None
        reads = list(reads)
        writes = list(writes)
        if barrier:
            writes.append(PH)
        else:
            reads.append(PH)
        op = Op(eng, fn, dma, semkey if semkey is not None else (writes[0] if (dma and writes) else None))
        op.idx = len(self.ops)
        if barrier:
            self.slotmap = {}
        if dma:
            if eng == "pool":
                op.slot = ("p", op.semkey)
            else:
                if op.semkey not in self.slotmap:
                    self.slotmap[op.semkey] = len(self.slotmap)
                op.slot = self.slotmap[op.semkey]
        cand = {}
        for k in reads + writes:
            w = self.last_w.get(k)
            if w is not None:
                cand[w.idx] = w
        for k in writes:
            for r in self.readers.get(k, ()):
                cand[r.idx] = r
        for d in cand.values():
            if (not d.dma) and (not op.dma) and d.eng == "pe" and op.eng == "pe":
                continue
            if d.dma and op.dma and d.semkey == op.semkey:
                pure_waw = all(self.last_w.get(k) is not d for k in reads) and \
                    all(d not in self.readers.get(k, ()) for k in writes)
                if pure_waw:
                    continue
            key = ("dma", d.slot) if d.dma else ("eng", d.eng)
            old = op.deps.get(key)
            if old is None or old.idx < d.idx:
                op.deps[key] = d
            d.sig = True
        for k in writes:
            self.last_w[k] = op
            self.readers[k] = []
        for k in reads:
            self.readers.setdefault(k, []).append(op)
        self.ops.append(op)
        return op

    def emit(self):
        nc = self.nc
        import os as _os
        _mx = int(_os.environ.get("MAXOPS", "0"))
        if _mx:
            self.ops = self.ops[:_mx]
        cnt = {}
        dma_keys = []
        for op in self.ops:
            if op.dma:
                k = ("dma", op.slot)
                if k not in cnt:
                    cnt[k] = 0
                    dma_keys.append(k)
                cnt[k] += 16
                op.val = cnt[k]
            elif op.sig:
                k = ("eng", op.eng)
                cnt[k] = cnt.get(k, 0) + 1
                op.val = cnt[k]
        self.maxvals = dict(cnt)
        sems = {}
        with contextlib.ExitStack() as es:
            for e in ENGS:
                sems[("eng", e)] = es.enter_context(nc.semaphore("s_" + e))
            for i, k in enumerate(dma_keys):
                sems[k] = es.enter_context(nc.semaphore("d%d" % i))
            self.nsems = len(sems)
            block = es.enter_context(nc.Block())
            ops = self.ops

            def run(engname, eng):
                waited = {}
                for op in ops:
                    if op.eng != engname:
                        continue
                    for k, d in op.deps.items():
                        if waited.get(k, 0) >= d.val:
                            continue
                        eng.wait_ge(sems[k], d.val)
                        waited[k] = d.val
                    ins = op.fn(eng)
                    if op.dma:
                        ins.then_inc(sems[("dma", op.slot)], 16)
                    elif op.sig:
                        ins.then_inc(sems[("eng", op.eng)], 1)
                last = {}
                for op in ops:
                    if op.eng == engname and op.dma:
                        last[("dma", op.slot)] = max(op.val, last.get(("dma", op.slot), 0))
                for k, v in last.items():
                    if waited.get(k, 0) < v:
                        eng.wait_ge(sems[k], v)

            @block.tensor
            def _(e):
                run("pe", e)

            @block.scalar
            def _(e):
                run("act", e)

            @block.vector
            def _(e):
                run("dve", e)

            @block.gpsimd
            def _(e):
                run("pool", e)

            @block.sync
            def _(e):
                run("sp", e)


def run_staged(S, fn, seg, group=4):
    for g0 in range(0, len(seg), group):
        recs = []
        for ti in range(g0, min(g0 + group, len(seg))):
            S.begin_record()
            fn(ti, seg[ti])
            recs.append(S.end_record())
        S.replay_staged(recs)


class T:
    def __init__(self, h, key):
        self.h = h
        self.key = key

    def __getitem__(self, idx):
        return self.h[idx]


def _dsize(dt):
    return 2 if dt == BF16 else 4


PF_COLS = ([0, 128] + [256, 384] + [512, 640] + [1024, 1152] + [1280, 1408] + [1536, 1664]
           + [2048, 2176, 2304, 2432] + [2560 + 128 * i for i in range(8)])
PF_Q, PF_FF, PF_FB, PF_G, PF_QA, PF_KA, PF_Z, PF_XBC = 0, 2, 4, 6, 8, 10, 12, 16
PF_NB = 24
PF_TR = ["silu"] * 2 + ["copy"] * 4 + ["silu"] * 2 + ["s8"] * 2 + ["copy"] * 2 + ["silu"] * 4 + ["copy"] * 8
PT_GROUPS = [(256, 768, 0), (768, 1024, 512), (1792, 2048, 768), (2048, 2560, 1024), (3584, 3600, 1536)]
PT_FF, PT_FB, PT_I, PT_VA, PT_Z, PT_DT = 0, 256, 512, 768, 1024, 1536
PT_W = 1552

FV_L = 72 + 8 + 8 + 8 + 2 + 4 + 40 + 8
FV_BMOD, FV_NF1, FV_NMX, FV_NF2, FV_HGN, FV_SSN, FV_CW, FV_CB = 0, 72, 80, 88, 96, 98, 102, 142
FV_G = DEPTH * FV_L
FV_C, FV_FN, FV_LB = FV_G, FV_G + 16, FV_G + 24
FV_N = FV_G + 24 + 8
RV_L = 256 + 512 + 16 + 16 + 512
RV_NAN, RV_DSK, RV_ALOG, RV_DTB, RV_SSN = 0, 256, 768, 784, 800
RV_G = DEPTH * RV_L
RV_LBR = RV_G
RV_N = RV_G + 1024


class KB:
    def __init__(self, nlat=64, dbg=(), layers=DEPTH, stop_after=None, parts=("hg", "ssd", "na")):
        self.parts = set(parts)
        self.nlat = nlat
        self.NT = nlat + 2
        self.NTOK = 128 * self.NT
        self.NLTOK = 128 * nlat
        self.dbg = set(dbg)
        self.layers = layers
        self.stop_after = stop_after
        self.nc = nc = bass.Bass("TRN2", target_bir_lowering=False)
        self.S = Sched(nc)
        self.uid = 0
        self.sb_lo = 16512
        self.sb_hi = 229344
        self.off = self.sb_lo
        self.NB = 512
        assert nlat % 4 == 0
        self.nblk = 1 + self.NLTOK // self.NB
        self.declare_io()

    def sb(self, name, shape, dt):
        size = int(np.prod(shape[1:])) * _dsize(dt)
        size = (size + 31) // 32 * 32
        assert self.off + size <= self.sb_hi, (name, self.off, size)
        self.uid += 1
        h = self.nc.alloc_sbuf_tensor_at("%s_%d" % (name, self.uid), list(shape), dt, offset=self.off)
        self.off += size
        return T(h, "%s_%d" % (name, self.uid))

    def mark(self):
        return self.off

    def reset(self, m):
        self.off = m

    def barrier(self):
        scr = self.scr
        self.S.add("dve", lambda e: e.memset(scr[:, 0:1], 0.0), writes=[scr.key], barrier=True)

    def declare_io(self):
        nc = self.nc
        ein = lambda n, s, dt=F32: nc.dram_tensor(n, list(s), dt, kind="ExternalInput").ap()
        self.xT = ein("xT", [D, self.NTOK])
        self.fvec = ein("fvec", [128, FV_N])
        self.rvec = ein("rvec", [128, RV_N])
        self.w_mod = ein("w_mod", [DEPTH, D, 9 * D])
        self.w13 = [ein("ffn1_w13", [DEPTH, D, 2 * DFF]), ein("ffn2_w13", [DEPTH, D, 2 * DFF])]
        self.w2 = [ein("ffn1_w2", [DEPTH, DFF, D]), ein("ffn2_w2", [DEPTH, DFF, D])]
        self.w_in = ein("w_in", [DEPTH, D, 3600])
        self.w_out = ein("w_out", [DEPTH, D, D])
        self.outT = nc.dram_tensor("outT", [D, self.NLTOK], F32, kind="ExternalOutput").ap()
        self.XT = nc.dram_tensor("XTs", [D, self.NTOK], F32).ap()
        self.PF = nc.dram_tensor("PFs", [PF_NB * 128, self.NTOK], F32).ap()
        self.PT = nc.dram_tensor("PTs", [self.NTOK, PT_W], F32).ap()
        self.MIXT = nc.dram_tensor("MIXTs", [D, self.NTOK], BF16).ap()
        self.dbg_out = {}
        _mixer_io(self)

    def dbg_tensor(self, name, shape, dt=F32):
        ap = self.nc.dram_tensor("dbg_" + name, list(shape), dt, kind="ExternalOutput").ap()
        self.dbg_out[name] = ap
        return ap

    def setup_persist(self):
        S = self.S
        self.scr = self.sb("scr", [128, 8], F32)
        self.fv = self.sb("fv", [128, FV_N], F32)
        self.ones_bf = self.sb("ones", [128, 128], BF16)
        self.MOD = self.sb("MOD", [128, 2, 72], F32)
        self.scb = self.sb("scb", [128, 16], F32)
        fv, ones = self.fv, self.ones_bf
        S.add("sp", lambda e: e.dma_start(out=fv[:], in_=self.fvec), writes=[fv.key], dma=True)
        S.add("dve", lambda e: e.memset(ones[:], 1.0), writes=[ones.key])
        scb = self.scb
        S.add("act", lambda e: e.activation(out=scb[:], in_=fv[:, FV_C:FV_C + 16], func=AF.Silu),
              reads=[fv.key], writes=[scb.key])
        self.persist_end = self.mark()

    def phase_mod(self, l):
        S, nc = self.S, self.nc
        m = self.mark()
        HALF = 4608
        wm = [self.sb("wm", [128, HALF], F32) for _ in range(2)]
        psm = self.ps[0]
        scb, fv, MOD = self.scb, self.fv, self.MOD
        i = 0
        for k in range(8):
            for hf in range(2):
                w = wm[i % 2]
                i += 1
                src = self.w_mod[l, k * 128:(k + 1) * 128, hf * HALF:(hf + 1) * HALF]
                S.add("sp", lambda e, w=w, src=src: e.dma_start(out=w[:], in_=src), writes=[w.key], dma=True)
                for j in range(36):
                    fb = hf * 36 + j
                    S.add("pe", lambda e, w=w, j=j, fb=fb, k=k: e.matmul(
                        psm[:, fb * 2:fb * 2 + 2], lhsT=w[:, j * 128:(j + 1) * 128], rhs=scb[:, 2 * k:2 * k + 2],
                        start=(k == 0 and fb == 0), stop=(k == 7), skip_group_check=True),
                        reads=[w.key, scb.key], writes=[psm.key])
        bo = l * FV_L
        for s in range(2):
            S.add("dve", lambda e, s=s: e.tensor_tensor(out=MOD[:, s, :], in0=psm[:, s:144:2],
                                                       in1=fv[:, bo + FV_BMOD:bo + FV_BMOD + 72], op=ALU.add),
                  reads=[psm.key, fv.key], writes=[MOD.key])
        for s in range(2):
            for (js, nw) in ((1, FV_NF1), (4, FV_NMX), (7, FV_NF2)):
                S.add("dve", lambda e, s=s, js=js, nw=nw: e.scalar_tensor_tensor(
                    out=MOD[:, s, js * 8:js * 8 + 8], in0=MOD[:, s, js * 8:js * 8 + 8], scalar=1.0,
                    in1=fv[:, bo + nw:bo + nw + 8], op0=ALU.add, op1=ALU.mult),
                    reads=[MOD.key, fv.key], writes=[MOD.key])
                S.add("dve", lambda e, s=s, js=js: e.tensor_scalar_mul(
                    out=MOD[:, s, js * 8:js * 8 + 8], in0=MOD[:, s, js * 8:js * 8 + 8], scalar1=32.0),
                    reads=[MOD.key], writes=[MOD.key])
            for jg in (2, 8):
                S.add("dve", lambda e, s=s, jg=jg: e.tensor_scalar_mul(
                    out=MOD[:, s, jg * 8:jg * 8 + 8], in0=MOD[:, s, jg * 8:jg * 8 + 8], scalar1=0.5),
                    reads=[MOD.key], writes=[MOD.key])
        if "mod" in self.dbg:
            d = self.dbg_tensor("mod%d" % l, [128, 144])
            S.add("sp", lambda e: e.dma_start(out=d, in_=MOD[:].rearrange("p s c -> p (s c)")), reads=[MOD.key],
                  dma=True, semkey="dbg_mod%d" % l)
        self.barrier()
        self.reset(m)

    def load_w(self, name, src, K, N):
        kc = K // 128
        w = self.sb(name, [128, kc, N], BF16)
        for k in range(kc):
            s = src[k * 128:(k + 1) * 128, :]
            self.S.add("pool", lambda e, k=k, s=s: e.dma_start(out=w[:, k, :], in_=s), writes=[w.key], dma=True)
        return w

    def alloc_work(self, nxb=2):
        NB = self.NB
        self.xb = [self.sb("xb", [128, 8, NB], F32) for _ in range(nxb)] * (2 // nxb)
        self.tk = [self.sb("tk", [128, NB], F32) for _ in range(2)]
        self.hb = self.sb("hb", [128, 8, NB], BF16)
        self.sq = self.hb
        self.rstd = self.sb("rstd", [128, NB], F32)

    def blk(self, i):
        if i == 0:
            return 0, CTX, 1
        return CTX + (i - 1) * self.NB, self.NB, 0

    def load_x(self, xb, src, n0, N):
        v = src.rearrange("(k p) n -> p k n", p=128)[:, :, n0:n0 + N]
        self.S.add("sp", lambda e: e.dma_start(out=xb[:, :, :N], in_=v), writes=[xb.key], dma=True)

    def store_x(self, xb, dst, n0, N, key="dram_x"):
        v = dst.rearrange("(k p) n -> p k n", p=128)[:, :, n0:n0 + N]
        self.S.add("sp", lambda e: e.dma_start(out=v, in_=xb[:, :, :N]), reads=[xb.key], dma=True,
                   semkey="st_" + xb.key)

    def modulate(self, xb, N, A, SH, out, out_keyed):
        S = self.S
        sq, rstd, ones = self.sq, self.rstd, self.ones_bf
        pss = self.ps[0]
        S.add("act", lambda e: e.activation(out=sq[:, :, :N], in_=xb[:, :, :N], func=AF.Square),
              reads=[xb.key], writes=[sq.key])
        for k in range(8):
            S.add("pe", lambda e, k=k: e.matmul(pss[:, :N], lhsT=ones[:], rhs=sq[:, k, :N], start=(k == 0), stop=(k == 7)),
                  reads=[sq.key, ones.key], writes=[pss.key])
        S.add("dve", lambda e: e.tensor_scalar_add(out=rstd[:, :N], in0=pss[:, :N], scalar1=float(D * EPS)),
              reads=[pss.key], writes=[rstd.key])
        S.add("act", lambda e: e.activation(out=rstd[:, :N], in_=rstd[:, :N], func=AF.Ln),
              reads=[rstd.key], writes=[rstd.key])
        S.add("act", lambda e: e.activation(out=rstd[:, :N], in_=rstd[:, :N], func=AF.Exp, scale=-0.5),
              reads=[rstd.key], writes=[rstd.key])
        for k in range(8):
            if SH is None:
                S.add("dve", lambda e, k=k: e.scalar_tensor_tensor(out=out[:, k, :N], in0=xb[:, k, :N], scalar=A[:, k:k + 1],
                                                                   in1=rstd[:, :N], op0=ALU.mult, op1=ALU.mult),
                      reads=[xb.key, rstd.key, self.fv.key], writes=[out_keyed])
            else:
                tk = self.tk[k % 2]
                S.add("dve", lambda e, k=k, tk=tk: e.scalar_tensor_tensor(out=tk[:, :N], in0=xb[:, k, :N], scalar=A[:, k:k + 1],
                                                                          in1=rstd[:, :N], op0=ALU.mult, op1=ALU.mult),
                      reads=[xb.key, rstd.key, self.MOD.key], writes=[tk.key])
                S.add("act", lambda e, k=k, tk=tk: e.activation(out=out[:, k, :N], in_=tk[:, :N], func=AF.Identity,
                                                                bias=SH[:, k:k + 1]),
                      reads=[tk.key, self.MOD.key], writes=[out_keyed])

    def ffn(self, xb, N, s, jbase, w13b, w2b):
        S = self.S
        MOD = self.MOD
        A = MOD[:, s, (jbase + 1) * 8:(jbase + 2) * 8]
        SH = MOD[:, s, jbase * 8:(jbase + 1) * 8]
        G = MOD[:, s, (jbase + 2) * 8:(jbase + 3) * 8]
        hb, ab, sg = self.hb, self.ab, self.sg
        self.modulate(xb, N, A, SH, hb, hb.key)
        for j in range(22):
            pu, pg = self.ps[1 + j % 2], self.ps[3 + j % 2]
            for k in range(8):
                S.add("pe", lambda e, j=j, k=k, pu=pu: e.matmul(pu[:, :N], lhsT=w13b[:, k, j * 128:(j + 1) * 128],
                                                                rhs=hb[:, k, :N], start=(k == 0), stop=(k == 7)),
                      reads=[w13b.key, hb.key], writes=[pu.key])
            for k in range(8):
                S.add("pe", lambda e, j=j, k=k, pg=pg: e.matmul(pg[:, :N], lhsT=w13b[:, k, DFF + j * 128:DFF + (j + 1) * 128],
                                                                rhs=hb[:, k, :N], start=(k == 0), stop=(k == 7)),
                      reads=[w13b.key, hb.key], writes=[pg.key])
            sgj = sg[j % 2]
            S.add("act", lambda e, pg=pg, sgj=sgj: e.activation(out=sgj[:, :N], in_=pg[:, :N], func=AF.Silu),
                  reads=[pg.key], writes=[sgj.key])
            S.add("dve", lambda e, j=j, pu=pu, sgj=sgj: e.tensor_tensor(out=ab[:, j, :N], in0=sgj[:, :N], in1=pu[:, :N],
                                                                          op=ALU.mult),
                  reads=[pu.key, sgj.key], writes=[ab.key])
        for fb in range(8):
            po = self.ps[5 + fb % 2]
            for j in range(22):
                S.add("pe", lambda e, j=j, fb=fb, po=po: e.matmul(po[:, :N], lhsT=w2b[:, j, fb * 128:(fb + 1) * 128],
                                                                  rhs=ab[:, j, :N], start=(j == 0), stop=(j == 21)),
                      reads=[w2b.key, ab.key], writes=[po.key])
            S.add("dve", lambda e, fb=fb, po=po: e.scalar_tensor_tensor(out=xb[:, fb, :N], in0=po[:, :N],
                                                                        scalar=G[:, fb:fb + 1], in1=xb[:, fb, :N],
                                                                        op0=ALU.mult, op1=ALU.add),
                  reads=[po.key, xb.key, MOD.key], writes=[xb.key])

    def phase_ffn1(self, l):
        m = self.mark()
        w13b = self.load_w("w13b", self.w13[0][l], D, 2 * DFF)
        w2b = self.load_w("w2b", self.w2[0][l], DFF, D)
        self.alloc_work()
        self.ab = self.sb("ab", [128, 22, self.NB], BF16)
        self.sg = [self.sb("sg", [128, self.NB], F32) for _ in range(2)]
        src = self.xT if l == 0 else self.XT
        for i in range(self.nblk):
            n0, N, s = self.blk(i)
            xb = self.xb[i % 2]
            self.load_x(xb, src, n0, N)
            self.ffn(xb, N, s, 0, w13b, w2b)
            self.store_x(xb, self.XT, n0, N, key="dram_x1")
        self.barrier()
        self.reset(m)
        if "x1" in self.dbg:
            self.dump_dram("x1_%d" % l, self.XT, [D, self.NTOK], "dram_x1")

    def dump_dram(self, name, src, shape, key, dt=F32):
        d = self.dbg_tensor(name, shape, dt)
        self.S.add("sp", lambda e: e.dma_start(out=d, in_=src), dma=True, semkey="dbg_" + name)
        self.barrier()

    def phase_inproj(self, l):
        S = self.S
        m = self.mark()
        winb = self.load_w("winb", self.w_in[l], D, 3600)
        self.alloc_work()
        NB = self.NB
        pfst = self.sb("pfst", [128, PF_NB, NB], F32)
        ptsts = [self.sb("ptst", [128, PT_W], F32) for _ in range(NB // 128)]
        MOD, hb = self.MOD, self.hb
        def do_blk(i):
            n0, N, s = self.blk(i)
            xb = self.xb[i % 2]
            self.load_x(xb, self.XT, n0, N)
            self.modulate(xb, N, MOD[:, s, 32:40], MOD[:, s, 24:32], hb, hb.key)
            for bi, c0 in enumerate(PF_COLS):
                p = self.ps[1 + bi % 4]
                for k in range(8):
                    S.add("pe", lambda e, k=k, c0=c0, p=p: e.matmul(p[:, :N], lhsT=winb[:, k, c0:c0 + 128], rhs=hb[:, k, :N],
                                                                    start=(k == 0), stop=(k == 7)),
                          reads=[winb.key, hb.key], writes=[p.key])
                tr = PF_TR[bi]
                if tr == "silu":
                    S.add("act", lambda e, bi=bi, p=p: e.activation(out=pfst[:, bi, :N], in_=p[:, :N], func=AF.Silu),
                          reads=[p.key], writes=[pfst.key])
                elif tr == "s8":
                    S.add("dve", lambda e, bi=bi, p=p: e.tensor_scalar_mul(out=pfst[:, bi, :N], in0=p[:, :N], scalar1=0.125),
                          reads=[p.key], writes=[pfst.key])
                else:
                    S.add("dve", lambda e, bi=bi, p=p: e.tensor_copy(out=pfst[:, bi, :N], in_=p[:, :N]),
                          reads=[p.key], writes=[pfst.key])
            dst = self.PF.rearrange("(f p) n -> p f n", p=128)[:, :, n0:n0 + N]
            S.add("sp", lambda e, dst=dst: e.dma_start(out=dst, in_=pfst[:, :, :N]), reads=[pfst.key],
                  dma=True, semkey="dram_pf")
            for t in range(N // 128):
                ptst = ptsts[t]
                for gi, (c0, c1, d0) in enumerate(PT_GROUPS):
                    p = self.ps[5 + gi % 2]
                    wdt = c1 - c0
                    for k in range(8):
                        S.add("pe", lambda e, k=k, t=t, c0=c0, c1=c1, p=p, wdt=wdt: e.matmul(
                            p[:, :wdt], lhsT=hb[:, k, t * 128:(t + 1) * 128], rhs=winb[:, k, c0:c1],
                            start=(k == 0), stop=(k == 7)), reads=[winb.key, hb.key], writes=[p.key])
                    S.add("act", lambda e, ptst=ptst, d0=d0, wdt=wdt, p=p: e.copy(out=ptst[:, d0:d0 + wdt], in_=p[:, :wdt]),
                          reads=[p.key], writes=[ptst.key])
                dstt = self.PT[n0 + t * 128:n0 + (t + 1) * 128, :]
                S.add("sp", lambda e, ptst=ptst, dstt=dstt: e.dma_start(out=dstt, in_=ptst[:]), reads=[ptst.key],
                      dma=True, semkey="st_" + ptst.key)
        for i in range(self.nblk):
            do_blk(i)
        self.barrier()
        self.reset(m)
        if "proj" in self.dbg:
            self.dump_dram("pf_%d" % l, self.PF, [PF_NB * 128, self.NTOK], "dram_pf")
            self.dump_dram("pt_%d" % l, self.PT, [self.NTOK, PT_W], "dram_pt")

    def phase_out(self, l):
        S = self.S
        last = (l == DEPTH - 1)
        m = self.mark()
        w13b = self.load_w("w13b", self.w13[1][l], D, 2 * DFF)
        w2b = self.load_w("w2b", self.w2[1][l], DFF, D)
        woutb = self.load_w("woutb", self.w_out[l], D, D)
        self.alloc_work(1)
        NB = self.NB
        self.ab = self.sb("ab", [128, 22, NB], BF16)
        self.sg = [self.sb("sg", [128, NB], F32) for _ in range(2)]
        mixb = [T(self.ab.h, self.ab.key)] * 2
        MOD = self.MOD
        def do_blk(i):
            n0, N, s = self.blk(i)
            if last and s == 1:
                return
            xb = self.xb[i % 2]
            mb = mixb[i % 2]
            self.load_x(xb, self.XT, n0, N)
            v = self.MIXT.rearrange("(k p) n -> p k n", p=128)[:, :, n0:n0 + N]
            S.add("sp", lambda e, mb=mb, v=v: e.dma_start(out=mb[:, 0:8, :N], in_=v), writes=[mb.key], dma=True)
            G = MOD[:, s, 40:48]
            for fb in range(8):
                p = self.ps[5 + fb % 2]
                for k in range(8):
                    S.add("pe", lambda e, k=k, fb=fb, p=p, mb=mb: e.matmul(p[:, :N], lhsT=woutb[:, k, fb * 128:(fb + 1) * 128],
                                                                          rhs=mb[:, k, :N], start=(k == 0), stop=(k == 7)),
                          reads=[woutb.key, mb.key], writes=[p.key])
                S.add("dve", lambda e, fb=fb, p=p, xb=xb, G=G: e.scalar_tensor_tensor(out=xb[:, fb, :N], in0=p[:, :N],
                                                                                  scalar=G[:, fb:fb + 1], in1=xb[:, fb, :N],
                                                                                  op0=ALU.mult, op1=ALU.add),
                      reads=[p.key, xb.key, MOD.key], writes=[xb.key])
            self.ffn(xb, N, s, 6, w13b, w2b)
            if not last:
                self.store_x(xb, self.XT, n0, N, key="dram_x3")
            else:
                ob = xb
                fn = self.fn32
                self.modulate(xb, N, fn, None, ob, ob.key)
                v = self.outT.rearrange("(k p) n -> p k n", p=128)[:, :, n0 - CTX:n0 - CTX + N]
                S.add("sp", lambda e, v=v, ob=ob: e.dma_start(out=v, in_=ob[:, :, :N]), reads=[ob.key],
                      dma=True, semkey="st_" + ob.key)
        for i in range(self.nblk):
            do_blk(i)
        self.barrier()
        self.reset(m)
        if "x3" in self.dbg and not last:
            self.dump_dram("x3_%d" % l, self.XT, [D, self.NTOK], "dram_x3")

    def build(self):
        S = self.S
        big = [self.nc.alloc_psum_tensor("psb%d" % i, [128, 1024], F32) for i in range(4)]
        self.ps = [T(big[i // 2][:, (i % 2) * 512:(i % 2) * 512 + 512], "ps%d" % i) for i in range(8)]
        self.psbig = [T(big[i], "psB%d" % i) for i in range(4)]
        self.setup_persist()
        self.fn32 = None
        for l in range(self.layers):
            if not getattr(self, "only_mixer", False):
                self.phase_mod(l)
                if self.stop_after == ("mod", l):
                    break
                self.phase_ffn1(l)
                if self.stop_after == ("ffn1", l):
                    break
                self.phase_inproj(l)
                if self.stop_after == ("inproj", l):
                    break
            self.phase_mixer(l)
            if self.stop_after == ("mixer", l):
                break
            self.phase_out_wrap(l)
        S.emit()
        return self.nc

    def phase_out_wrap(self, l):
        last = (l == DEPTH - 1)
        if last:
            m = self.mark()
            fn32 = self.sb("fn32", [128, 8], F32)
            fv = self.fv
            self.S.add("dve", lambda e: e.tensor_scalar_mul(out=fn32[:], in0=fv[:, FV_FN:FV_FN + 8], scalar1=32.0),
                       reads=[fv.key], writes=[fn32.key])
            self.fn32 = fn32
            self.persist_tmp = self.mark()
            self.phase_out(l)
            self.reset(m)
        else:
            self.phase_out(l)

    def phase_mixer(self, l):
        m = self.mark()
        _setup_mixer_consts(self)
        if "ohg" in self.dbg:
            self.dbg_tensor("ohg%d" % l, [256, self.NTOK])
        if "yssm" in self.dbg:
            self.dbg_tensor("yssm%d" % l, [self.NTOK, 512])
        if "naraw" in self.dbg:
            self.dbg_tensor("naraw%d" % l, [self.NTOK, 256])
        if "hg" in self.parts:
            _phase_hgrn2(self, l)
        if "ssd" in self.parts:
            _phase_ssd(self, l)
        if "na" in self.parts:
            _phase_na(self, l)
        if "mix" in self.dbg:
            self.dump_dram("mix_%d" % l, self.MIXT, [D, self.NTOK], "x", BF16)
        self.reset(m)


def fm(v):
    v = np.asarray(v, np.float32)
    return v.reshape(-1, 128).T


def prep_shared(inp):
    fv = np.zeros((128, FV_N), np.float32)
    rv = np.zeros((128, RV_N), np.float32)
    for l in range(DEPTH):
        o = l * FV_L
        fv[:, o + FV_BMOD:o + FV_BMOD + 72] = fm(inp["b_mod"][l])
        fv[:, o + FV_NF1:o + FV_NF1 + 8] = fm(inp["norm_ffn1"][l])
        fv[:, o + FV_NMX:o + FV_NMX + 8] = fm(inp["norm_mix"][l])
        fv[:, o + FV_NF2:o + FV_NF2 + 8] = fm(inp["norm_ffn2"][l])
        fv[:, o + FV_HGN:o + FV_HGN + 2] = fm(inp["hg_norm"][l])
        fv[:, o + FV_SSN:o + FV_SSN + 4] = fm(inp["ssm_norm"][l])
        for j in range(5):
            fv[:, o + FV_CW + j * 8:o + FV_CW + j * 8 + 8] = fm(inp["ssm_conv_w"][l, j])
        fv[:, o + FV_CB:o + FV_CB + 8] = fm(inp["ssm_conv_b"][l])
        r = l * RV_L
        rv[:, r + RV_NAN:r + RV_NAN + 256] = inp["na_norm"][l][None, :]
        rv[:, r + RV_DSK:r + RV_DSK + 512] = np.repeat(inp["ssm_d"][l], 64)[None, :]
        rv[:, r + RV_ALOG:r + RV_ALOG + 16] = inp["ssm_a_log"][l].reshape(-1)[None, :]
        rv[:, r + RV_DTB:r + RV_DTB + 16] = inp["ssm_dt_bias"][l].reshape(-1)[None, :]
        rv[:, r + RV_SSN:r + RV_SSN + 512] = inp["ssm_norm"][l][None, :]
    fv[:, FV_FN:FV_FN + 8] = fm(inp["final_norm"])
    for dr in range(2):
        for l in range(DEPTH):
            rv[:, RV_LBR + (dr * 2 + l) * 256:RV_LBR + (dr * 2 + l) * 256 + 256] = inp["hg_lower_bounds"][dr, l][None, :]
    for dr in range(2):
        for l in range(DEPTH):
            fv[:, FV_LB + dr * 4 + l * 2:FV_LB + dr * 4 + l * 2 + 2] = fm(inp["hg_lower_bounds"][dr, l])
    return fv, rv


def prep_core(inp, b, nlat, fv_shared):
    fv = fv_shared.copy()
    cc = np.stack([fm(inp["c"][b]), fm(inp["c_ctx"])], axis=2)
    fv[:, FV_C:FV_C + 16] = cc.reshape(128, 16)
    xT = np.ascontiguousarray(np.concatenate([inp["ctx"][b], inp["x"][b][:128 * nlat]], axis=0).T)
    return fv, xT


def make_in_maps(inp, nlat, batches):
    fvs, rv = prep_shared(inp)
    shared = {k: np.ascontiguousarray(inp[k], np.float32) for k in
              ("w_mod", "ffn1_w13", "ffn2_w13", "ffn1_w2", "ffn2_w2", "w_in", "w_out")}
    cmat = make_cmat()
    nab = np.stack([make_nabias(np.asarray(inp["na_rpb"][l], np.float32), nlat) for l in range(DEPTH)])
    maps = []
    for b in batches:
        fv, xT = prep_core(inp, b, nlat, fvs)
        m = dict(shared)
        m.update({"xT": xT, "fvec": fv, "rvec": rv, "cmat": cmat, "nabias": nab})
        maps.append(m)
    return maps


CM_ID, CM_M1F, CM_M2F, CM_M3F, CM_M1B, CM_M2B, CM_M3B = 0, 128, 256, 384, 512, 640, 768
CM_M4F, CM_M4B, CM_HM, CM_BLK, CM_TRIF, CM_TRIB, CM_NEGF, CM_NEGB, CM_ONES = 896, 900, 904, 1160, 1288, 1416, 1544, 1672, 1800
CM_N = 1928
NEG = -30000.0


def make_cmat():
    c = np.zeros((128, CM_N), np.float32)
    u = np.arange(128)[:, None]
    t = np.arange(128)[None, :]
    same = (u // 32) == (t // 32)
    c[:, CM_ID:CM_ID + 128] = (u == t)
    mf = (t // 32) * 32 + 15
    c[:, CM_M1F:CM_M1F + 128] = same * (((u > mf) & (u <= t)) * 1.0 - ((u > t) & (u <= mf)) * 1.0)
    c[:, CM_M2F:CM_M2F + 128] = same & (u <= t)
    c[:, CM_M3F:CM_M3F + 128] = same & (u > t)
    mb = (t // 32) * 32 + 16
    c[:, CM_M1B:CM_M1B + 128] = same * (((u >= t) & (u < mb)) * 1.0 - ((u >= mb) & (u < t)) * 1.0)
    c[:, CM_M2B:CM_M2B + 128] = same & (u >= t)
    c[:, CM_M3B:CM_M3B + 128] = same & (u < t)
    j = np.arange(4)[None, :]
    c[:, CM_M4F:CM_M4F + 4] = (u // 32) == j
    c[:, CM_M4B:CM_M4B + 4] = (u // 32) == (3 - j)
    col = np.arange(128)[None, :]
    c[:, CM_HM:CM_HM + 128] = (col // 64 == 0)
    c[:, CM_HM + 128:CM_HM + 256] = (col // 64 == 1)
    c[:, CM_BLK:CM_BLK + 128] = (u // 64) == (t // 64)
    c[:, CM_TRIF:CM_TRIF + 128] = (u <= t)
    c[:, CM_TRIB:CM_TRIB + 128] = (u >= t)
    c[:, CM_NEGF:CM_NEGF + 128] = NEG * (u > t)
    c[:, CM_NEGB:CM_NEGB + 128] = NEG * (u < t)
    c[:, CM_ONES:CM_ONES + 128] = 1.0
    return c


def _mixer_io(self):
    nc = self.nc
    self.cmat = nc.dram_tensor("cmat", [128, CM_N], F32, kind="ExternalInput").ap()
    self.OHG = nc.dram_tensor("OHGs", [256, self.NTOK], F32).ap()
    _ssd_io(self)
    _na_io(self)


def _setup_mixer_consts(self):
    S = self.S
    self.cm = cm = self.sb("cm", [128, CM_N], F32)
    S.add("sp", lambda e: e.dma_start(out=cm[:], in_=self.cmat), writes=[cm.key], dma=True)
    self.rv = rv = self.sb("rv", [128, RV_N], F32)
    S.add("sp", lambda e: e.dma_start(out=rv[:], in_=self.rvec), writes=[rv.key], dma=True)
    self.blk_bf = blk = self.sb("blkbf", [128, 128], BF16)
    S.add("dve", lambda e: e.tensor_copy(out=blk[:], in_=cm[:, CM_BLK:CM_BLK + 128]), reads=[cm.key], writes=[blk.key])


def _hg_tiles(self, d):
    NT = self.NT
    chain = list(range(NT)) if d == 0 else [1, 0] + list(range(NT - 1, 1, -1))
    return [chain[0:2]] + [chain[i:i + 8] for i in range(2, NT, 8)]


def _phase_hgrn2(self, l):
    S = self.S
    blkbf_l = self.blk_bf
    m = self.mark()
    cm, fv, rv = self.cm, self.fv, self.rv
    ps = self.ps
    LBt = self.sb("LBt", [128, 2, 256], F32)
    OMLt = self.sb("OMLt", [128, 2, 256], F32)
    omlf = self.sb("omlf", [128, 2, 2], F32)
    if l == 0:
        S.add("dve", lambda e: e.memset(LBt[:], 0.0), writes=[LBt.key])
        S.add("dve", lambda e: e.memset(OMLt[:], 1.0), writes=[OMLt.key])
        S.add("dve", lambda e: e.memset(omlf[:], 1.0), writes=[omlf.key])
    else:
        for d in range(2):
            a0 = rv[:, RV_LBR + (d * 2 + 0) * 256:RV_LBR + (d * 2 + 0) * 256 + 256]
            a1 = rv[:, RV_LBR + (d * 2 + 1) * 256:RV_LBR + (d * 2 + 1) * 256 + 256]
            S.add("dve", lambda e, d=d, a0=a0, a1=a1: e.tensor_tensor(out=LBt[:, d, :], in0=a1, in1=a0, op=ALU.subtract),
                  reads=[rv.key], writes=[LBt.key])
            f0 = fv[:, FV_LB + d * 4:FV_LB + d * 4 + 2]
            f1 = fv[:, FV_LB + d * 4 + 2:FV_LB + d * 4 + 4]
            S.add("dve", lambda e, d=d, f0=f0, f1=f1: e.tensor_tensor(out=omlf[:, d, :], in0=f1, in1=f0, op=ALU.subtract),
                  reads=[fv.key], writes=[omlf.key])
        S.add("act", lambda e: e.activation(out=LBt[:], in_=LBt[:], func=AF.Sigmoid), reads=[LBt.key], writes=[LBt.key])
        S.add("act", lambda e: e.activation(out=omlf[:], in_=omlf[:], func=AF.Sigmoid), reads=[omlf.key], writes=[omlf.key])
        S.add("dve", lambda e: e.tensor_scalar(out=OMLt[:], in0=LBt[:], scalar1=-1.0, scalar2=1.0, op0=ALU.mult, op1=ALU.add),
              reads=[LBt.key], writes=[OMLt.key])
        S.add("dve", lambda e: e.tensor_scalar(out=omlf[:], in0=omlf[:], scalar1=-1.0, scalar2=1.0, op0=ALU.mult, op1=ALU.add),
              reads=[omlf.key], writes=[omlf.key])
    omlfh = self.sb("omlfh", [128, 2, 2, 2], F32)
    for d_ in range(2):
        for pr_ in range(2):
            for h2_ in range(2):
                S.add("dve", lambda e, d_=d_, pr_=pr_, h2_=h2_: e.tensor_tensor(
                    out=omlfh[:, d_, pr_, h2_:h2_ + 1], in0=omlf[:, d_, pr_:pr_ + 1],
                    in1=cm[:, CM_BLK + h2_ * 64:CM_BLK + h2_ * 64 + 1], op=ALU.mult),
                    reads=[omlf.key, cm.key], writes=[omlfh.key])
    D1 = [self.sb("D1", [128, 64, 33], F32) for _ in range(2)]
    SO = [self.sb("SO", [128, 64, 33], F32) for _ in range(2)]
    D0 = [self.sb("D0", [128, 64, 33], F32) for _ in range(2)]
    Sblk = [self.sb("Sblk", [128, 32, 128], BF16) for _ in range(2)]
    DEC = self.sb("DEC", [128, 2, 32], F32)
    ATs = [self.sb("ATs", [128, 4, 128], BF16) for _ in range(8)]
    QHs = [self.sb("QHs", [128, 2, 128], BF16) for _ in range(8)]
    VZs = [self.sb("VZs", [128, 2, 2, 128], BF16) for _ in range(8)]
    rtok_R = [self.sb("rtok", [128, 256], F32) for _ in range(4)]
    vtok_R = [self.sb("vtok", [128, 256], F32) for _ in range(4)]
    vtok_bf_R = [self.sb("vtok_bf", [128, 256], BF16) for _ in range(4)]
    qfm_R = [self.sb("qfm", [128, 2, 128], F32) for _ in range(4)]
    rfm_R = [self.sb("rfm", [128, 2, 128], F32) for _ in range(4)]
    sig_R = [self.sb("sig", [128, 256], F32) for _ in range(4)]
    tmpk_R = [self.sb("tmpk", [128, 256], F32) for _ in range(4)]
    lf_R = [self.sb("lf", [128, 256], F32) for _ in range(4)]
    ktok_R = [self.sb("ktok", [128, 256], F32) for _ in range(4)]
    sneg_R = [self.sb("sneg", [128, 2, 128], F32) for _ in range(4)]
    P1c_R = [self.sb("P1c", [128, 256], F32) for _ in range(4)]
    Ep_R = [self.sb("Ep", [128, 256], F32) for _ in range(4)]
    En_R = [self.sb("En", [128, 256], F32) for _ in range(4)]
    E2_R = [self.sb("E2", [128, 256], F32) for _ in range(4)]
    E3_R = [self.sb("E3", [128, 256], F32) for _ in range(4)]
    qt_R = [self.sb("qt", [128, 2, 128], BF16) for _ in range(4)]
    kt_R = [self.sb("kt", [128, 2, 2, 128], BF16) for _ in range(4)]
    khat_R = [self.sb("khat", [128, 4, 256], BF16) for _ in range(4)]
    of_ld_R = [self.sb("of_ld", [128, 2, 128], F32) for _ in range(4)]
    osum_R = [self.sb("osum", [128, 2, 128], F32) for _ in range(4)]
    osq_R = [self.sb("osq", [128, 2, 128], BF16) for _ in range(4)]
    orstd_R = [self.sb("orstd", [128, 256], F32) for _ in range(4)]
    sgl_R = [self.sb("sgl", [128, 2, 128], F32) for _ in range(4)]
    hgo_R = [self.sb("hgo", [128, 2, 128], BF16) for _ in range(4)]
    for pr in range(2):
        S.add("dve", lambda e, pr=pr: e.memset(Sblk[pr][:], 0.0), writes=[Sblk[pr].key])
        S.add("dve", lambda e, pr=pr: e.memset(D0[pr][:], 0.0), writes=[D0[pr].key])
        S.add("dve", lambda e, pr=pr: e.memset(D1[pr][:], 0.0), writes=[D1[pr].key])
    PFr = self.PF.rearrange("(f p) n -> p f n", p=128)
    OHGr = self.OHG.rearrange("(f p) n -> p f n", p=128)
    MIXr = self.MIXT.rearrange("(f p) n -> p f n", p=128)
    pA, pB, pC, pU, pO, pN = ps[1], ps[2], ps[3], ps[4], ps[5], ps[6]
    def do_dir(d):
        M1 = cm[:, (CM_M1F, CM_M1B)[d]:(CM_M1F, CM_M1B)[d] + 128]
        M2 = cm[:, (CM_M2F, CM_M2B)[d]:(CM_M2F, CM_M2B)[d] + 128]
        M3 = cm[:, (CM_M3F, CM_M3B)[d]:(CM_M3F, CM_M3B)[d] + 128]
        M4 = cm[:, (CM_M4F, CM_M4B)[d]:(CM_M4F, CM_M4B)[d] + 4]
        M4n = cm[:, CM_M4F:CM_M4F + 4]
        for pr in range(2):
            S.add("dve", lambda e, pr=pr: e.memset(D1[pr][:, :, 0:1], 0.0), writes=[D1[pr].key])
        def do_seg(seg):
            nt = len(seg)
            nch = 4 * nt
            def p1(ti, tile):
                n0 = tile * 128
                rtok, vtok, vtok_bf, qfm, rfm, sig, tmpk, lf, ktok, sneg, P1c, Ep, En, E2, E3, qt, kt, khat = rtok_R[ti % 4], vtok_R[ti % 4], vtok_bf_R[ti % 4], qfm_R[ti % 4], rfm_R[ti % 4], sig_R[ti % 4], tmpk_R[ti % 4], lf_R[ti % 4], ktok_R[ti % 4], sneg_R[ti % 4], P1c_R[ti % 4], Ep_R[ti % 4], En_R[ti % 4], E2_R[ti % 4], E3_R[ti % 4], qt_R[ti % 4], kt_R[ti % 4], khat_R[ti % 4]
                pA, pB = (ps[1], ps[2]) if ti % 2 == 0 else (ps[0], ps[7])
                AT, QH, VZ = ATs[ti], QHs[ti], VZs[ti]
                S.add("sp", lambda e, n0=n0: e.dma_start(out=rtok[:], in_=self.PT[n0:n0 + 128, (PT_FF, PT_FB)[d]:(PT_FF, PT_FB)[d] + 256]),
                      writes=[rtok.key], dma=True)
                S.add("sp", lambda e, n0=n0: e.dma_start(out=vtok[:], in_=self.PT[n0:n0 + 128, PT_I:PT_I + 256]),
                      writes=[vtok.key], dma=True)
                S.add("sp", lambda e, n0=n0: e.dma_start(out=qfm[:], in_=PFr[:, PF_Q:PF_Q + 2, n0:n0 + 128]),
                      writes=[qfm.key], dma=True)
                fbk = (PF_FF, PF_FB)[d]
                S.add("sp", lambda e, n0=n0, fbk=fbk: e.dma_start(out=rfm[:], in_=PFr[:, fbk:fbk + 2, n0:n0 + 128]),
                      writes=[rfm.key], dma=True)
                S.add("act", lambda e: e.activation(out=sig[:], in_=rtok[:], func=AF.Sigmoid), reads=[rtok.key], writes=[sig.key])
                S.add("act", lambda e: e.copy(out=vtok_bf[:], in_=vtok[:]), reads=[vtok.key], writes=[vtok_bf.key])
                S.add("act", lambda e: e.activation(out=sneg[:], in_=rfm[:], func=AF.Sigmoid, scale=-1.0),
                      reads=[rfm.key], writes=[sneg.key])
                S.add("dve", lambda e: e.tensor_tensor(out=tmpk[:], in0=sig[:], in1=OMLt[:, d, :], op=ALU.mult),
                      reads=[sig.key, OMLt.key], writes=[tmpk.key])
                S.add("dve", lambda e: e.scalar_tensor_tensor(out=lf[:], in0=tmpk[:], scalar=1e-20, in1=LBt[:, d, :],
                                                              op0=ALU.max, op1=ALU.add),
                      reads=[tmpk.key, LBt.key], writes=[lf.key])
                S.add("dve", lambda e: e.tensor_tensor(out=ktok[:], in0=OMLt[:, d, :], in1=tmpk[:], op=ALU.subtract),
                      reads=[tmpk.key, OMLt.key], writes=[ktok.key])
                S.add("act", lambda e: e.activation(out=lf[:], in_=lf[:], func=AF.Ln), reads=[lf.key], writes=[lf.key])
                S.next_stage()
                for pr in range(2):
                    S.add("pe", lambda e, pr=pr: e.matmul(pA[:, pr * 128:(pr + 1) * 128], lhsT=lf[:, pr * 128:(pr + 1) * 128], rhs=M1,
                                                          start=True, stop=True), reads=[lf.key, cm.key], writes=[pA.key])
                for pr in range(2):
                    S.add("pe", lambda e, pr=pr: e.matmul(pA[:, 256 + pr * 128:256 + (pr + 1) * 128], lhsT=lf[:, pr * 128:(pr + 1) * 128],
                                                          rhs=M2, start=True, stop=True), reads=[lf.key, cm.key], writes=[pA.key])
                S.add("pe", lambda e: e.matmul(pB[:, 0:256], lhsT=M3, rhs=lf[:], start=True, stop=True),
                      reads=[lf.key, cm.key], writes=[pB.key])
                for pr in range(2):
                    S.add("pe", lambda e, pr=pr: e.matmul(pB[:, 256 + pr * 4:260 + pr * 4], lhsT=lf[:, pr * 128:(pr + 1) * 128], rhs=M4,
                                                          start=True, stop=True), reads=[lf.key, cm.key], writes=[pB.key])
                S.add("dve", lambda e: e.tensor_scalar(out=P1c[:], in0=pA[:, 0:256], scalar1=40.0, scalar2=-40.0, op0=ALU.min, op1=ALU.max),
                      reads=[pA.key], writes=[P1c.key])
                S.add("act", lambda e: e.activation(out=Ep[:], in_=P1c[:], func=AF.Exp), reads=[P1c.key], writes=[Ep.key])
                S.add("act", lambda e: e.activation(out=En[:], in_=P1c[:], func=AF.Exp, scale=-1.0), reads=[P1c.key], writes=[En.key])
                S.add("act", lambda e: e.activation(out=E2[:], in_=pA[:, 256:512], func=AF.Exp), reads=[pA.key], writes=[E2.key])
                S.add("act", lambda e: e.activation(out=E3[:], in_=pB[:, 0:256], func=AF.Exp), reads=[pB.key], writes=[E3.key])
                c0 = ti * 4
                for pr in range(2):
                    S.add("act", lambda e, c0=c0, pr=pr: e.activation(out=DEC[:, pr, c0:c0 + 4], in_=pB[:, 256 + pr * 4:260 + pr * 4],
                                                                      func=AF.Exp), reads=[pB.key], writes=[DEC.key])
                qf2 = qfm[:].rearrange("p a b -> p (a b)")
                S.add("dve", lambda e: e.tensor_tensor(out=qt[:].rearrange("p a b -> p (a b)"), in0=qf2, in1=Ep[:], op=ALU.mult),
                      reads=[qfm.key, Ep.key], writes=[qt.key])
                S.add("dve", lambda e, QH=QH: e.tensor_tensor(out=QH[:].rearrange("p a b -> p (a b)"), in0=qf2, in1=E2[:], op=ALU.mult),
                      reads=[qfm.key, E2.key], writes=[QH.key])
                for pr in range(2):
                    for h2 in range(2):
                        S.add("dve", lambda e, pr=pr, h2=h2: e.scalar_tensor_tensor(
                            out=kt[:, pr, h2, :], in0=sneg[:, pr, :], scalar=omlfh[:, d, pr, h2:h2 + 1],
                            in1=En[:, pr * 128:(pr + 1) * 128], op0=ALU.mult, op1=ALU.mult),
                            reads=[sneg.key, omlfh.key, En.key], writes=[kt.key])
                for j in range(4):
                    S.add("dve", lambda e, j=j: e.scalar_tensor_tensor(out=khat[:, j, :], in0=ktok[:], scalar=M4n[:, j:j + 1], in1=E3[:],
                                                                       op0=ALU.mult, op1=ALU.mult),
                          reads=[ktok.key, cm.key, E3.key], writes=[khat.key])
                if d == 0 or True:
                    hm = cm[:, CM_HM:CM_HM + 256].rearrange("p (a b) -> p a b", a=2)
                    S.add("dve", lambda e, VZ=VZ, hm=hm: e.tensor_tensor(
                        out=VZ[:], in0=vtok[:].rearrange("p (a b) -> p a b", a=2).unsqueeze(2).to_broadcast([128, 2, 2, 128]),
                        in1=hm.unsqueeze(1).to_broadcast([128, 2, 2, 128]), op=ALU.mult),
                        reads=[vtok.key, cm.key], writes=[VZ.key])
                S.next_stage()
                for h in range(4):
                    pr, h2 = h // 2, h % 2
                    S.add("pe", lambda e, h=h, pr=pr, h2=h2: e.matmul(pC[:, h * 128:(h + 1) * 128], lhsT=kt[:, pr, h2, :],
                                                                      rhs=qt[:, pr, :], start=True, stop=True),
                          reads=[kt.key, qt.key], writes=[pC.key])
                msk = cm[:, (CM_M2F, CM_M2B)[d]:(CM_M2F, CM_M2B)[d] + 128]
                S.add("dve", lambda e, AT=AT, msk=msk: e.tensor_tensor(out=AT[:], in0=pC[:].rearrange("p (a b) -> p a b", a=4),
                                                                       in1=msk.unsqueeze(1).to_broadcast([128, 4, 128]), op=ALU.mult),
                      reads=[pC.key, cm.key], writes=[AT.key])
                S.next_stage()
                for pr in range(2):
                    for jj in range(4):
                        j = jj if d == 0 else 3 - jj
                        S.add("pe", lambda e, pr=pr, jj=jj, j=j: e.matmul(pU[:, jj * 128:(jj + 1) * 128], lhsT=khat[:, j, pr * 128:(pr + 1) * 128],
                                                                          rhs=vtok_bf[:, pr * 128:(pr + 1) * 128], start=True, stop=True),
                              reads=[khat.key, vtok_bf.key], writes=[pU.key])
                    for h2 in range(2):
                        src = pU[h2 * 64:(h2 + 1) * 64, :].rearrange("p (a b) -> p a b", a=4)[:, :, h2 * 64:(h2 + 1) * 64]
                        dstv = D1[pr][h2 * 64:(h2 + 1) * 64, :, 1 + c0:1 + c0 + 4].rearrange("p v c -> p c v")
                        if h2 == 0:
                            S.add("act", lambda e, pr=pr, src=src, dstv=dstv: e.copy(out=dstv, in_=src),
                                  reads=[pU.key], writes=[D1[pr].key])
                        else:
                            S.add("dve", lambda e, pr=pr, src=src, dstv=dstv: e.tensor_copy(out=dstv, in_=src),
                                  reads=[pU.key], writes=[D1[pr].key])
            run_staged(S, p1, seg)
            for pr in range(2):
                SB = Sblk[pr]
                S.add("dve", lambda e, pr=pr: e.tensor_copy(out=D0[pr][:, :, 1:1 + nch],
                                                            in_=DEC[:, pr, 0:nch].unsqueeze(1).to_broadcast([128, 64, nch])),
                      reads=[DEC.key], writes=[D0[pr].key])
                S.add("dve", lambda e, pr=pr: e.tensor_tensor_scan(
                    out=SO[pr][:].rearrange("p v c -> p (v c)"), data0=D0[pr][:].rearrange("p v c -> p (v c)"),
                    data1=D1[pr][:].rearrange("p v c -> p (v c)"), initial=0.0, op0=ALU.mult, op1=ALU.add),
                    reads=[D0[pr].key, D1[pr].key], writes=[SO[pr].key])
                S.add("act", lambda e, pr=pr, SB=SB: e.copy(out=SB[0:64, 0:nch, 0:64], in_=SO[pr][0:64, :, 0:nch].rearrange("p v c -> p c v")),
                      reads=[SO[pr].key], writes=[SB.key])
                S.add("act", lambda e, pr=pr, SB=SB: e.copy(out=SB[64:128, 0:nch, 64:128], in_=SO[pr][64:128, :, 0:nch].rearrange("p v c -> p c v")),
                      reads=[SO[pr].key], writes=[SB.key])
                S.add("dve", lambda e, pr=pr: e.tensor_copy(out=D1[pr][:, :, 0:1], in_=SO[pr][:, :, nch:nch + 1]),
                      reads=[SO[pr].key], writes=[D1[pr].key])
            def p2(ti, tile):
                n0 = tile * 128
                of_ld, osum, osq, orstd, sgl, hgo = of_ld_R[ti % 4], osum_R[ti % 4], osq_R[ti % 4], orstd_R[ti % 4], sgl_R[ti % 4], hgo_R[ti % 4]
                AT, QH, VZ = ATs[ti], QHs[ti], VZs[ti]
                for pr in range(2):
                    for h2 in range(2):
                        h = pr * 2 + h2
                        S.add("pe", lambda e, pr=pr, h2=h2, h=h, AT=AT, VZ=VZ: e.matmul(
                            pO[:, pr * 128:(pr + 1) * 128], lhsT=VZ[:, pr, h2, :], rhs=AT[:, h, :],
                            start=(pr == 0 and h2 == 0), stop=False, skip_group_check=True),
                            reads=[VZ.key, AT.key], writes=[pO.key])
                for pr in range(2):
                    for j in range(4):
                        jj = j if d == 0 else 3 - j
                        c = ti * 4 + jj
                        S.add("pe", lambda e, pr=pr, j=j, c=c, QH=QH: e.matmul(
                            pO[:, pr * 128 + j * 32:pr * 128 + (j + 1) * 32], lhsT=Sblk[pr][:, c, :], rhs=QH[:, pr, j * 32:(j + 1) * 32],
                            start=False, stop=(pr == 1 and j == 3), skip_group_check=True),
                            reads=[Sblk[pr].key, QH.key], writes=[pO.key])
                if d == 0:
                    S.add("act", lambda e: e.copy(out=osum[:].rearrange("p a b -> p (a b)"), in_=pO[:, 0:256]), reads=[pO.key], writes=[osum.key])
                    S.add("sp", lambda e, n0=n0: e.dma_start(out=OHGr[:, :, n0:n0 + 128], in_=osum[:]), reads=[osum.key], dma=True,
                          semkey="st_" + osum.key)
                else:
                    S.add("sp", lambda e, n0=n0: e.dma_start(out=of_ld[:], in_=OHGr[:, :, n0:n0 + 128]), writes=[of_ld.key], dma=True)
                    S.add("sp", lambda e, n0=n0: e.dma_start(out=sgl[:], in_=PFr[:, PF_G:PF_G + 2, n0:n0 + 128]), writes=[sgl.key], dma=True)
                    S.add("dve", lambda e: e.tensor_tensor(out=osum[:].rearrange("p a b -> p (a b)"), in0=of_ld[:].rearrange("p a b -> p (a b)"),
                                                           in1=pO[:, 0:256], op=ALU.add), reads=[of_ld.key, pO.key], writes=[osum.key])
                    if "ohg" in self.dbg:
                        S.add("sp", lambda e, n0=n0: e.dma_start(out=self.dbg_out["ohg%d" % l].rearrange("(f p) n -> p f n", p=128)[:, :, n0:n0 + 128],
                                                                 in_=osum[:]), reads=[osum.key], dma=True, semkey="dbg_ohg")
                    S.next_stage()
                    S.add("act", lambda e: e.activation(out=osq[:], in_=osum[:], func=AF.Square), reads=[osum.key], writes=[osq.key])
                    S.add("pe", lambda e: e.matmul(pN[:, 0:256], lhsT=blkbf_l[:], rhs=osq[:].rearrange("p a b -> p (a b)"),
                                                   start=True, stop=True), reads=[osq.key, blkbf_l.key], writes=[pN.key])
                    S.add("dve", lambda e: e.tensor_scalar(out=orstd[:], in0=pN[:, 0:256], scalar1=1.0 / 64, scalar2=EPS,
                                                           op0=ALU.mult, op1=ALU.add), reads=[pN.key], writes=[orstd.key])
                    S.add("act", lambda e: e.activation(out=orstd[:], in_=orstd[:], func=AF.Ln), reads=[orstd.key], writes=[orstd.key])
                    S.add("act", lambda e: e.activation(out=orstd[:], in_=orstd[:], func=AF.Exp, scale=-0.5), reads=[orstd.key], writes=[orstd.key])
                    S.add("dve", lambda e: e.tensor_tensor(out=osum[:].rearrange("p a b -> p (a b)"), in0=osum[:].rearrange("p a b -> p (a b)"),
                                                           in1=orstd[:], op=ALU.mult), reads=[osum.key, orstd.key], writes=[osum.key])
                    wo = l * FV_L + FV_HGN
                    for pr in range(2):
                        S.add("dve", lambda e, pr=pr: e.scalar_tensor_tensor(out=hgo[:, pr, :], in0=osum[:, pr, :], scalar=fv[:, wo + pr:wo + pr + 1],
                                                                             in1=sgl[:, pr, :], op0=ALU.mult, op1=ALU.mult),
                              reads=[osum.key, fv.key, sgl.key], writes=[hgo.key])
                    S.add("sp", lambda e, n0=n0: e.dma_start(out=MIXr[:, 0:2, n0:n0 + 128], in_=hgo[:]), reads=[hgo.key], dma=True,
                          semkey="st_" + hgo.key)
            run_staged(S, p2, seg)
        for seg in _hg_tiles(self, d):
            do_seg(seg)
        self.barrier()
    for d in range(2):
        do_dir(d)
    self.reset(m)


def _ssd_io(self):
    nc = self.nc
    self.XSs = nc.dram_tensor("XSs", [self.NTOK, 512], F32).ap()
    self.BCf = nc.dram_tensor("BCfs", [512, self.NTOK], BF16).ap()
    self.Bts = nc.dram_tensor("Bts", [self.NTOK, 256], BF16).ap()
    self.YS = nc.dram_tensor("YSs", [self.NTOK, 512], F32).ap()


def _phase_ssd(self, l):
    S = self.S
    m = self.mark()
    cm, fv, rv, ps = self.cm, self.fv, self.rv, self.ps
    NT = self.NT
    PFr = self.PF.rearrange("(f p) n -> p f n", p=128)
    BCr = self.BCf.rearrange("(f p) n -> p f n", p=128)
    MIXr = self.MIXT.rearrange("(f p) n -> p f n", p=128)
    IDENT = cm[:, CM_ID:CM_ID + 128]
    ONESF = cm[:, CM_ONES:CM_ONES + 128]
    fo = l * FV_L
    ro = l * RV_L
    xin_R = [self.sb("xin", [128, 8, 132], F32) for _ in range(2)]
    acc_R = [self.sb("acc", [128, 8, 128], F32) for _ in range(2)]
    ctmp_R = [self.sb("ctmp", [128, 8, 128], F32) for _ in range(2)]
    acc2_R = [self.sb("acc2", [128, 8, 128], F32) for _ in range(2)]
    dtmp_R = [self.sb("dtmp", [128, 8, 128], F32) for _ in range(2)]
    bcb_R = [self.sb("bcb", [128, 4, 128], BF16) for _ in range(2)]
    xs_st_R = [self.sb("xs_st", [128, 512], F32) for _ in range(2)]
    bt_st_R = [self.sb("bt_st", [128, 256], BF16) for _ in range(2)]
    pX, pBt = ps[1], ps[2]
    CW = fv[:, fo + FV_CW:fo + FV_CW + 40].rearrange("p (j k) -> p j k", j=5)
    CB = fv[:, fo + FV_CB:fo + FV_CB + 8]

    def conv_tile(tile):
        n0 = tile * 128
        xin, acc, ctmp, bcb, xs_st, bt_st = xin_R[tile % 2], acc_R[tile % 2], ctmp_R[tile % 2], bcb_R[tile % 2], xs_st_R[tile % 2], bt_st_R[tile % 2]
        acc2, dtmp = acc2_R[tile % 2], dtmp_R[tile % 2]
        s_lo, s_hi = (0, CTX) if tile < 2 else (CTX, self.NTOK)
        lo, hi = max(n0 - 2, s_lo), min(n0 + 130, s_hi)
        S.add("dve", lambda e: e.memset(xin[:, :, 0:2], 0.0), writes=[xin.key])
        S.add("dve", lambda e: e.memset(xin[:, :, 130:132], 0.0), writes=[xin.key])
        S.add("sp", lambda e: e.dma_start(out=xin[:, :, lo - (n0 - 2):hi - (n0 - 2)], in_=PFr[:, PF_XBC:PF_XBC + 8, lo:hi]),
              writes=[xin.key], dma=True)
        def cwb(j):
            return CW[:, j, :].unsqueeze(2).to_broadcast([128, 8, 128])
        S.add("pool", lambda e: e.tensor_tensor(out=acc2[:], in0=xin[:, :, 1:129], in1=cwb(1), op=ALU.mult),
              reads=[xin.key, fv.key], writes=[acc2.key])
        S.add("pool", lambda e: e.tensor_tensor(out=ctmp[:], in0=xin[:, :, 2:130], in1=cwb(2), op=ALU.mult),
              reads=[xin.key, fv.key], writes=[ctmp.key])
        S.add("pool", lambda e: e.tensor_tensor(out=acc2[:], in0=acc2[:], in1=ctmp[:], op=ALU.add),
              reads=[acc2.key, ctmp.key], writes=[acc2.key])
        S.add("dve", lambda e: e.tensor_tensor(out=acc[:], in0=xin[:, :, 0:128], in1=cwb(0), op=ALU.mult),
              reads=[xin.key, fv.key], writes=[acc.key])
        for j in (3, 4):
            S.add("dve", lambda e, j=j: e.tensor_tensor(out=dtmp[:], in0=xin[:, :, j:j + 128], in1=cwb(j), op=ALU.mult),
                  reads=[xin.key, fv.key], writes=[dtmp.key])
            S.add("dve", lambda e: e.tensor_tensor(out=acc[:], in0=acc[:], in1=dtmp[:], op=ALU.add),
                  reads=[acc.key, dtmp.key], writes=[acc.key])
        S.add("dve", lambda e: e.tensor_tensor(out=acc[:], in0=acc[:], in1=acc2[:], op=ALU.add),
              reads=[acc.key, acc2.key], writes=[acc.key])
        S.add("dve", lambda e: e.tensor_tensor(out=acc[:], in0=acc[:], in1=CB.unsqueeze(2).to_broadcast([128, 8, 128]), op=ALU.add),
              reads=[acc.key, fv.key], writes=[acc.key])
        S.add("act", lambda e: e.activation(out=acc[:], in_=acc[:], func=AF.Silu), reads=[acc.key], writes=[acc.key])
        S.add("dve", lambda e: e.tensor_copy(out=bcb[:], in_=acc[:, 4:8, :]), reads=[acc.key], writes=[bcb.key])
        S.add("sp", lambda e: e.dma_start(out=BCr[:, :, n0:n0 + 128], in_=bcb[:]), reads=[bcb.key], dma=True, semkey="st_" + bcb.key)
        for k in range(4):
            S.add("pe", lambda e, k=k: e.transpose(out=pX[:, k * 128:(k + 1) * 128], in_=acc[:, k, :], identity=IDENT),
                  reads=[acc.key, cm.key], writes=[pX.key])
        S.add("act", lambda e: e.copy(out=xs_st[:], in_=pX[:]), reads=[pX.key], writes=[xs_st.key])
        S.add("sp", lambda e: e.dma_start(out=self.XSs[n0:n0 + 128, :], in_=xs_st[:]), reads=[xs_st.key], dma=True, semkey="st_" + xs_st.key)
        for k in range(2):
            S.add("pe", lambda e, k=k: e.transpose(out=pBt[:, k * 128:(k + 1) * 128], in_=acc[:, 4 + k, :], identity=IDENT),
                  reads=[acc.key, cm.key], writes=[pBt.key])
        S.add("dve", lambda e: e.tensor_copy(out=bt_st[:], in_=pBt[:, 0:256]), reads=[pBt.key], writes=[bt_st.key])
        S.add("sp", lambda e: e.dma_start(out=self.Bts[n0:n0 + 128, :], in_=bt_st[:]), reads=[bt_st.key], dma=True, semkey="st_" + bt_st.key)

    for tile in range(NT):
        conv_tile(tile)
    self.barrier()
    self.reset(m)
    m = self.mark()
    Arow = self.sb("Arow", [128, 16], F32)
    S.add("act", lambda e: e.activation(out=Arow[:], in_=rv[:, ro + RV_ALOG:ro + RV_ALOG + 16], func=AF.Exp),
          reads=[rv.key], writes=[Arow.key])
    S.add("dve", lambda e: e.tensor_scalar_mul(out=Arow[:], in0=Arow[:], scalar1=-1.0), reads=[Arow.key], writes=[Arow.key])
    DTB = rv[:, ro + RV_DTB:ro + RV_DTB + 16]
    D0 = [self.sb("sD0", [128, 256, 9], F32) for _ in range(2)]
    D1 = [self.sb("sD1", [128, 256, 9], F32) for _ in range(2)]
    SO = [self.sb("sSO", [128, 256, 9], F32) for _ in range(2)]
    Hbf = [self.sb("Hbf", [128, 8, 256], BF16) for _ in range(2)]
    YD = [self.sb("YD", [128, 512], F32) for _ in range(8)]
    CFs = [self.sb("CFs", [128, 2, 128], BF16) for _ in range(8)]
    ECs = [self.sb("ECs", [128, 8], F32) for _ in range(8)]
    dtr_R4 = [self.sb("dtr", [128, 8], F32) for _ in range(4)]
    dt_R4 = [self.sb("dt", [128, 8], F32) for _ in range(4)]
    av_R4 = [self.sb("av", [128, 8], F32) for _ in range(4)]
    ABC_R4 = [self.sb("ABC", [128, 8, 128], F32) for _ in range(4)]
    ncum_R4 = [self.sb("ncum", [128, 8], F32) for _ in range(4)]
    dend_R4 = [self.sb("dend", [128, 8], F32) for _ in range(4)]
    dect_R4 = [self.sb("dect", [128, 8], F32) for _ in range(4)]
    xst_R4 = [self.sb("xst", [128, 512], F32) for _ in range(4)]
    btl_R4 = [self.sb("btl", [128, 256], BF16) for _ in range(4)]
    bcl_R4 = [self.sb("bcl", [128, 4, 128], BF16) for _ in range(4)]
    Lsb_R4 = [self.sb("Lsb", [128, 8, 128], F32) for _ in range(4)]
    Wb_R4 = [self.sb("Wb", [128, 8, 128], BF16) for _ in range(4)]
    xdt_R4 = [self.sb("xdt", [128, 8, 64], BF16) for _ in range(4)]
    xw_R4 = [self.sb("xw", [128, 8, 64], BF16) for _ in range(4)]
    xst2_R = [self.sb("xst2", [128, 512], F32) for _ in range(4)]
    yo_R = [self.sb("yo", [128, 512], F32) for _ in range(4)]
    yf_R = [self.sb("yf", [128, 512], F32) for _ in range(4)]
    zt_R = [self.sb("zt", [128, 512], F32) for _ in range(4)]
    ssq_R = [self.sb("ssq", [128, 2], F32) for _ in range(4)]
    yT_R = [self.sb("yT", [128, 4, 128], BF16) for _ in range(4)]
    for g in range(2):
        S.add("dve", lambda e, g=g: e.memset(D0[g][:], 0.0), writes=[D0[g].key])
        S.add("dve", lambda e, g=g: e.memset(D1[g][:], 0.0), writes=[D1[g].key])
    pS, pLa, pLb, pG, pY, pH, pY2, pT = ps[0], ps[1], ps[2], ps[3], ps[4], ps[5], ps[6], ps[7]
    SSNrow = rv[:, ro + RV_SSN:ro + RV_SSN + 512]
    DSKrow = rv[:, ro + RV_DSK:ro + RV_DSK + 512]

    def do_dir(d):
        TRI = cm[:, (CM_TRIF, CM_TRIB)[d]:(CM_TRIF, CM_TRIB)[d] + 128]
        NEGM = cm[:, (CM_NEGF, CM_NEGB)[d]:(CM_NEGF, CM_NEGB)[d] + 128]
        for g in range(2):
            S.add("dve", lambda e, g=g: e.memset(D1[g][:, :, 0:1], 0.0), writes=[D1[g].key])

        def do_seg(seg):
            nt = len(seg)

            def p1(ti, tile):
                n0 = tile * 128
                dtr, dt, av, ABC, ncum, dend, dect, xst, btl, bcl, Lsb, Wb, xdt, xw = dtr_R4[ti % 4], dt_R4[ti % 4], av_R4[ti % 4], ABC_R4[ti % 4], ncum_R4[ti % 4], dend_R4[ti % 4], dect_R4[ti % 4], xst_R4[ti % 4], btl_R4[ti % 4], bcl_R4[ti % 4], Lsb_R4[ti % 4], Wb_R4[ti % 4], xdt_R4[ti % 4], xw_R4[ti % 4]
                S.add("sp", lambda e: e.dma_start(out=dtr[:], in_=self.PT[n0:n0 + 128, PT_DT + d * 8:PT_DT + d * 8 + 8]),
                      writes=[dtr.key], dma=True)
                S.add("sp", lambda e: e.dma_start(out=xst[:], in_=self.XSs[n0:n0 + 128, :]), writes=[xst.key], dma=True)
                S.add("sp", lambda e: e.dma_start(out=btl[:], in_=self.Bts[n0:n0 + 128, :]), writes=[btl.key], dma=True)
                S.add("sp", lambda e: e.dma_start(out=bcl[:], in_=BCr[:, :, n0:n0 + 128]), writes=[bcl.key], dma=True)
                S.add("dve", lambda e: e.tensor_tensor(out=dt[:], in0=dtr[:], in1=DTB[:, d * 8:d * 8 + 8], op=ALU.add),
                      reads=[dtr.key, rv.key], writes=[dt.key])
                S.add("act", lambda e: e.activation(out=dt[:], in_=dt[:], func=AF.Exp), reads=[dt.key], writes=[dt.key])
                S.add("dve", lambda e: e.tensor_scalar_add(out=dt[:], in0=dt[:], scalar1=1.0), reads=[dt.key], writes=[dt.key])
                S.add("act", lambda e: e.activation(out=dt[:], in_=dt[:], func=AF.Ln), reads=[dt.key], writes=[dt.key])
                S.add("dve", lambda e: e.tensor_tensor(out=av[:], in0=dt[:], in1=Arow[:, d * 8:d * 8 + 8], op=ALU.mult),
                      reads=[dt.key, Arow.key], writes=[av.key])
                S.add("dve", lambda e: e.tensor_copy(out=ABC[:], in_=av[:].unsqueeze(2).to_broadcast([128, 8, 128])),
                      reads=[av.key], writes=[ABC.key])
                S.next_stage()
                S.add("pe", lambda e: e.matmul(pS[:, 0:8], lhsT=TRI, rhs=av[:], start=True, stop=True), reads=[av.key, cm.key], writes=[pS.key])
                S.add("pe", lambda e: e.matmul(pS[:, 8:16], lhsT=ONESF, rhs=av[:], start=True, stop=True), reads=[av.key, cm.key], writes=[pS.key])
                S.add("dve", lambda e: e.tensor_scalar_mul(out=ncum[:], in0=pS[:, 0:8], scalar1=-1.0), reads=[pS.key], writes=[ncum.key])
                EC = ECs[ti]
                S.add("act", lambda e: e.activation(out=EC[:], in_=pS[:, 0:8], func=AF.Exp), reads=[pS.key], writes=[EC.key])
                S.add("dve", lambda e: e.tensor_tensor(out=dend[:], in0=pS[:, 8:16], in1=ncum[:], op=ALU.add),
                      reads=[pS.key, ncum.key], writes=[dend.key])
                S.add("act", lambda e: e.activation(out=dend[:], in_=dend[:], func=AF.Exp), reads=[dend.key], writes=[dend.key])
                S.add("act", lambda e: e.activation(out=dect[:], in_=pS[:, 8:16], func=AF.Exp), reads=[pS.key], writes=[dect.key])
                S.next_stage()
                for h in range(8):
                    pl = (pLa, pLb)[h // 4]
                    hc = (h % 4) * 128
                    S.add("pe", lambda e, h=h, pl=pl, hc=hc: e.matmul(pl[:, hc:hc + 128], lhsT=ABC[:, h, :], rhs=TRI, start=True, stop=False),
                          reads=[ABC.key, cm.key], writes=[pl.key])
                    S.add("pe", lambda e, h=h, pl=pl, hc=hc: e.matmul(pl[:, hc:hc + 128], lhsT=IDENT, rhs=NEGM, start=False, stop=True),
                          reads=[cm.key], writes=[pl.key])
                    S.add("act", lambda e, h=h, pl=pl, hc=hc: e.activation(out=Lsb[:, h, :], in_=pl[:, hc:hc + 128], func=AF.Exp,
                                                                           bias=ncum[:, h:h + 1]),
                          reads=[pl.key, ncum.key], writes=[Lsb.key])
                S.next_stage()
                for g in range(2):
                    S.add("pe", lambda e, g=g: e.matmul(pG[:, g * 128:(g + 1) * 128], lhsT=bcl[:, g, :], rhs=bcl[:, 2 + g, :],
                                                        start=True, stop=True), reads=[bcl.key], writes=[pG.key])
                for g in range(2):
                    S.add("dve", lambda e, g=g: e.tensor_tensor(
                        out=Wb[:, g * 4:(g + 1) * 4, :], in0=Lsb[:, g * 4:(g + 1) * 4, :],
                        in1=pG[:, g * 128:(g + 1) * 128].unsqueeze(1).to_broadcast([128, 4, 128]), op=ALU.mult),
                        reads=[Lsb.key, pG.key], writes=[Wb.key])
                S.add("dve", lambda e: e.tensor_tensor(out=xdt[:], in0=xst[:].rearrange("p (h q) -> p h q", h=8),
                                                       in1=dt[:].unsqueeze(2).to_broadcast([128, 8, 64]), op=ALU.mult),
                      reads=[xst.key, dt.key], writes=[xdt.key])
                S.next_stage()
                for h in range(8):
                    S.add("pe", lambda e, h=h: e.matmul(pY[:, h * 64:(h + 1) * 64], lhsT=Wb[:, h, :], rhs=xdt[:, h, :], start=True, stop=True),
                          reads=[Wb.key, xdt.key], writes=[pY.key])
                Y = YD[ti]
                S.add("act", lambda e: e.copy(out=Y[:], in_=pY[:]), reads=[pY.key], writes=[Y.key])
                CF = CFs[ti]
                S.add("dve", lambda e: e.tensor_copy(out=CF[:], in_=bcl[:, 2:4, :]), reads=[bcl.key], writes=[CF.key])
                S.add("dve", lambda e: e.tensor_tensor(out=xw[:], in0=xdt[:], in1=dend[:].unsqueeze(2).to_broadcast([128, 8, 64]), op=ALU.mult),
                      reads=[xdt.key, dend.key], writes=[xw.key])
                S.next_stage()
                for g in range(2):
                    S.add("pe", lambda e, g=g: e.matmul(pH[:, g * 256:(g + 1) * 256], lhsT=btl[:, g * 128:(g + 1) * 128],
                                                        rhs=xw[:, g * 4:(g + 1) * 4, :].rearrange("p h q -> p (h q)"), start=True, stop=True),
                          reads=[btl.key, xw.key], writes=[pH.key])
                for g in range(2):
                    S.add("act", lambda e, g=g: e.copy(out=D1[g][:, :, 1 + ti:2 + ti], in_=pH[:, g * 256:(g + 1) * 256].unsqueeze(2)),
                          reads=[pH.key], writes=[D1[g].key])
                    S.add("dve", lambda e, g=g: e.tensor_copy(
                        out=D0[g][:, :, 1 + ti:2 + ti].rearrange("p (h q) o -> p h (q o)", h=4),
                        in_=dect[:, g * 4:(g + 1) * 4].unsqueeze(2).to_broadcast([128, 4, 64])),
                        reads=[dect.key], writes=[D0[g].key])

            for g0 in range(0, nt, 4):
                recs = []
                for ti in range(g0, min(g0 + 4, nt)):
                    S.begin_record()
                    p1(ti, seg[ti])
                    recs.append(S.end_record())
                S.replay_staged(recs)
            for g in range(2):
                S.add("dve", lambda e, g=g: e.tensor_tensor_scan(
                    out=SO[g][:].rearrange("p v c -> p (v c)"), data0=D0[g][:].rearrange("p v c -> p (v c)"),
                    data1=D1[g][:].rearrange("p v c -> p (v c)"), initial=0.0, op0=ALU.mult, op1=ALU.add),
                    reads=[D0[g].key, D1[g].key], writes=[SO[g].key])
                S.add("act", lambda e, g=g: e.copy(out=Hbf[g][:, 0:nt, :], in_=SO[g][:, :, 0:nt].rearrange("p v c -> p c v")),
                      reads=[SO[g].key], writes=[Hbf[g].key])
                S.add("dve", lambda e, g=g: e.tensor_copy(out=D1[g][:, :, 0:1], in_=SO[g][:, :, nt:nt + 1]),
                      reads=[SO[g].key], writes=[D1[g].key])

            def p2(ti, tile):
                n0 = tile * 128
                yo, yf, zt, ssq, yT, xst2 = yo_R[ti % 4], yf_R[ti % 4], zt_R[ti % 4], ssq_R[ti % 4], yT_R[ti % 4], xst2_R[ti % 4]
                xst = xst2
                CF, EC, Y = CFs[ti], ECs[ti], YD[ti]
                for g in range(2):
                    S.add("pe", lambda e, g=g: e.matmul(pY2[:, g * 256:(g + 1) * 256], lhsT=CF[:, g, :], rhs=Hbf[g][:, ti, :], start=True, stop=True),
                          reads=[CF.key, Hbf[g].key], writes=[pY2.key])
                S.add("dve", lambda e: e.tensor_tensor(out=yo[:].rearrange("p (h q) -> p h q", h=8), in0=pY2[:].rearrange("p (h q) -> p h q", h=8),
                                                       in1=EC[:].unsqueeze(2).to_broadcast([128, 8, 64]), op=ALU.mult),
                      reads=[pY2.key, EC.key], writes=[yo.key])
                S.add("dve", lambda e: e.tensor_tensor(out=yo[:], in0=yo[:], in1=Y[:], op=ALU.add), reads=[yo.key, Y.key], writes=[yo.key])
                if d == 0:
                    S.add("sp", lambda e: e.dma_start(out=self.YS[n0:n0 + 128, :], in_=yo[:]), reads=[yo.key], dma=True, semkey="st_" + yo.key)
                    return
                S.add("sp", lambda e: e.dma_start(out=yf[:], in_=self.YS[n0:n0 + 128, :]), writes=[yf.key], dma=True)
                S.add("sp", lambda e: e.dma_start(out=xst[:], in_=self.XSs[n0:n0 + 128, :]), writes=[xst.key], dma=True)
                S.add("sp", lambda e: e.dma_start(out=zt[:], in_=self.PT[n0:n0 + 128, PT_Z:PT_Z + 512]), writes=[zt.key], dma=True)
                S.next_stage()
                S.add("dve", lambda e: e.tensor_tensor(out=yo[:], in0=yo[:], in1=yf[:], op=ALU.add), reads=[yo.key, yf.key], writes=[yo.key])
                S.add("dve", lambda e: e.tensor_tensor(out=xst[:], in0=xst[:], in1=DSKrow, op=ALU.mult), reads=[xst.key, rv.key], writes=[xst.key])
                S.add("dve", lambda e: e.tensor_tensor(out=yo[:], in0=yo[:], in1=xst[:], op=ALU.add), reads=[yo.key, xst.key], writes=[yo.key])
                if "yssm" in self.dbg:
                    S.add("sp", lambda e: e.dma_start(out=self.dbg_out["yssm%d" % l][n0:n0 + 128, :], in_=yo[:]), reads=[yo.key], dma=True,
                          semkey="dbg_yssm")
                S.add("act", lambda e: e.activation(out=zt[:], in_=zt[:], func=AF.Silu), reads=[zt.key], writes=[zt.key])
                S.add("dve", lambda e: e.tensor_tensor(out=yo[:], in0=yo[:], in1=zt[:], op=ALU.mult), reads=[yo.key, zt.key], writes=[yo.key])
                S.add("act", lambda e: e.activation(out=zt[:], in_=yo[:], func=AF.Square, accum_out=ssq[:, 0:1]),
                      reads=[yo.key], writes=[zt.key, ssq.key])
                S.add("dve", lambda e: e.tensor_scalar(out=ssq[:, 1:2], in0=ssq[:, 0:1], scalar1=1.0 / 512, scalar2=EPS, op0=ALU.mult, op1=ALU.add),
                      reads=[ssq.key], writes=[ssq.key])
                S.add("act", lambda e: e.activation(out=ssq[:, 1:2], in_=ssq[:, 1:2], func=AF.Ln), reads=[ssq.key], writes=[ssq.key])
                S.add("act", lambda e: e.activation(out=ssq[:, 1:2], in_=ssq[:, 1:2], func=AF.Exp, scale=-0.5), reads=[ssq.key], writes=[ssq.key])
                S.add("dve", lambda e: e.scalar_tensor_tensor(out=yo[:], in0=yo[:], scalar=ssq[:, 1:2], in1=SSNrow, op0=ALU.mult, op1=ALU.mult),
                      reads=[yo.key, ssq.key, rv.key], writes=[yo.key])
                S.next_stage()
                for k in range(4):
                    S.add("pe", lambda e, k=k: e.transpose(out=pT[:, k * 128:(k + 1) * 128], in_=yo[:, k * 128:(k + 1) * 128], identity=IDENT),
                          reads=[yo.key, cm.key], writes=[pT.key])
                S.add("act", lambda e: e.copy(out=yT[:].rearrange("p a b -> p (a b)"), in_=pT[:]), reads=[pT.key], writes=[yT.key])
                S.add("sp", lambda e: e.dma_start(out=MIXr[:, 4:8, n0:n0 + 128], in_=yT[:]), reads=[yT.key], dma=True, semkey="st_" + yT.key)

            run_staged(S, p2, seg)

        for seg in _hg_tiles(self, d):
            do_seg(seg)
        self.barrier()

    for d in range(2):
        do_dir(d)
    self.reset(m)


NAB_N = 5 * 4 * 5 * 128


def make_nabias(rpb, nlat):
    rows = 2 * nlat
    out = np.full((128, 5, 4, 5, 128), NEG, np.float32)
    its = [0, 1, 2, nlat - 2, nlat - 1]
    p = np.arange(128)
    q = np.arange(128)
    for v, it in enumerate(its):
        kt0 = min(max(it - 2, 0), nlat - 5)
        r = 2 * it + q // 64
        cq = q % 64
        r0 = np.clip(r - 4, 0, rows - 8)
        c0 = np.clip(cq - 8, 0, 48)
        for kt in range(5):
            rk = 2 * (kt0 + kt) + p // 64
            ck = p % 64
            inw = ((rk[:, None] >= r0[None, :]) & (rk[:, None] < r0[None, :] + 8)
                   & (ck[:, None] >= c0[None, :]) & (ck[:, None] < c0[None, :] + 16))
            dr = np.clip(rk[:, None] - r[None, :] + 7, 0, 14)
            dc = np.clip(ck[:, None] - cq[None, :], -15, 15) + 15
            for h in range(4):
                b = rpb[h][dr, dc]
                out[:, v, h, kt, :] = np.where(inw, b, NEG)
    return out.reshape(128, NAB_N)


def _na_io(self):
    nc = self.nc
    self.nabias = nc.dram_tensor("nabias", [DEPTH, 128, NAB_N], F32, kind="ExternalInput").ap()


def _phase_na(self, l):
    S = self.S
    m = self.mark()
    cm, fv, rv, ps = self.cm, self.fv, self.rv, self.ps
    NT, nlat, NTOK = self.NT, self.nlat, self.NTOK
    last = (l == DEPTH - 1)
    PFr = self.PF.rearrange("(f p) n -> p f n", p=128)
    MIXr = self.MIXT.rearrange("(f p) n -> p f n", p=128)
    IDENT = cm[:, CM_ID:CM_ID + 128]
    ro = l * RV_L
    NANrow = rv[:, ro + RV_NAN:ro + RV_NAN + 256]
    KT = self.sb("KT", [128, 2, NTOK], BF16)
    Vaug = self.sb("Vaug", [128, NT, 4, 65], BF16)
    BT = self.sb("BT", [128, 5, 4, 5, 128], BF16)
    IDb = self.sb("IDb", [128, 128], BF16)
    ID4 = self.sb("ID4", [128, 4, 128], F32)
    qf_R = [self.sb("qf", [128, 2, 128], F32) for _ in range(2)]
    QZ_R = [self.sb("QZ", [128, 2, 2, 128], BF16) for _ in range(2)]
    mx_R = [self.sb("mx", [128, 2], F32) for _ in range(4)]
    DG_R = [self.sb("DG", [128, 512], BF16) for _ in range(4)]
    PTs_R = [self.sb("PTs", [128, 896], BF16) for _ in range(4)]
    rc_R = [self.sb("rc", [128, 4], F32) for _ in range(2)]
    onat_R = [self.sb("onat", [128, 4, 64], F32) for _ in range(2)]
    junk_R = [self.sb("junk", [128, 256], F32) for _ in range(2)]
    ssq_R = [self.sb("nssq", [128, 2], F32) for _ in range(2)]
    oT_R = [self.sb("oT", [128, 2, 128], BF16) for _ in range(2)]
    hrm = self.sb("hrm", [128, 2], F32)
    SB, ST = self.psbig[0], self.psbig[1]
    pOV, pT = ps[4], ps[5]
    for pr in range(2):
        S.add("pool", lambda e, pr=pr: e.dma_start(out=KT[:, pr, :], in_=PFr[:, PF_KA + pr, :]), writes=[KT.key], dma=True)
    S.add("pool", lambda e: e.dma_start(out=BT[:].rearrange("p a b c d -> p (a b c d)"), in_=self.nabias[l]), writes=[BT.key], dma=True)
    S.add("dve", lambda e: e.tensor_copy(out=IDb[:], in_=IDENT), reads=[cm.key], writes=[IDb.key])
    S.add("dve", lambda e: e.tensor_copy(out=ID4[:], in_=IDENT.unsqueeze(1).to_broadcast([128, 4, 128])), reads=[cm.key], writes=[ID4.key])
    S.add("dve", lambda e: e.memset(Vaug[:, :, :, 64:65], 1.0), writes=[Vaug.key])
    for t in range(NT):
        S.add("pool", lambda e, t=t: e.dma_start(out=Vaug[:, t, :, 0:64],
                                                  in_=self.PT[t * 128:(t + 1) * 128, PT_VA:PT_VA + 256].rearrange("p (h q) -> p h q", h=4)),
              writes=[Vaug.key], dma=True)
    for h2 in range(2):
        S.add("dve", lambda e, h2=h2: e.tensor_copy(out=hrm[:, h2:h2 + 1], in_=cm[:, CM_BLK + h2 * 64:CM_BLK + h2 * 64 + 1]),
              reads=[cm.key], writes=[hrm.key])

    def q_tile(tile, keytiles, var):
        n0 = tile * 128
        nk = len(keytiles)
        qf, QZ, rc, onat, junk, ssq, oT = qf_R[tile % 2], QZ_R[tile % 2], rc_R[tile % 2], onat_R[tile % 2], junk_R[tile % 2], ssq_R[tile % 2], oT_R[tile % 2]
        nloc = 5 if var is not None else 0
        S.add("sp", lambda e: e.dma_start(out=qf[:], in_=PFr[:, PF_QA:PF_QA + 2, n0:n0 + 128]), writes=[qf.key], dma=True)
        S.add("dve", lambda e: e.tensor_tensor(out=QZ[:], in0=qf[:].unsqueeze(2).to_broadcast([128, 2, 2, 128]),
                                               in1=hrm[:].unsqueeze(1).unsqueeze(3).to_broadcast([128, 2, 2, 128]), op=ALU.mult),
              reads=[qf.key, hrm.key], writes=[QZ.key])
        def head_fn(hi, h):
            pr, h2 = h // 2, h % 2
            mx, DG, PTs = mx_R[h % 4], DG_R[h % 4], PTs_R[h % 4]
            SB, ST = (self.psbig[0], self.psbig[1]) if h % 2 == 0 else (self.psbig[3], self.psbig[1])
            col = 0
            runs = []
            i = 0
            while i < nk:
                j = i
                while j + 1 < nk and keytiles[j + 1] == keytiles[j] + 1 and (j + 1 - i) < 4 and ((col + (j + 1 - i) * 128) % 512 != 0):
                    j += 1
                runs.append((keytiles[i], j - i + 1, col))
                col += (j - i + 1) * 128
                i = j + 1
            for (kt_, cnt, c_) in runs:
                S.add("pe", lambda e, pr=pr, h2=h2, kt_=kt_, cnt=cnt, c_=c_: e.matmul(
                    SB[:, c_:c_ + cnt * 128], lhsT=QZ[:, pr, h2, :], rhs=KT[:, pr, kt_ * 128:(kt_ + cnt) * 128], start=True, stop=True),
                    reads=[QZ.key, KT.key], writes=[SB.key])
            S.add("dve", lambda e: e.reduce_max(out=mx[:, 0:1], in_=SB[:, 0:nk * 128], axis=AX.X), reads=[SB.key], writes=[mx.key])
            S.add("dve", lambda e: e.tensor_scalar_mul(out=mx[:, 1:2], in0=mx[:, 0:1], scalar1=-1.0), reads=[mx.key], writes=[mx.key])
            S.add("dve", lambda e: e.tensor_scalar_mul(out=DG[:], in0=ID4[:].rearrange("p a b -> p (a b)"), scalar1=mx[:, 1:2]),
                  reads=[mx.key, ID4.key], writes=[DG.key])
            S.next_stage()
            for b0 in (0, 4):
                kks = [kk for kk in range(nk) if b0 <= kk < b0 + 4]
                if not kks:
                    continue
                ncols = len(kks) * 128
                nbias = len([kk for kk in kks if kk < nloc])
                S.add("pe", lambda e, b0=b0, ncols=ncols: e.matmul(ST[:, b0 * 128:b0 * 128 + ncols], lhsT=self.ones_bf[:], rhs=DG[:, 0:ncols],
                                                                   start=True, stop=False, skip_group_check=True),
                      reads=[DG.key, self.ones_bf.key], writes=[ST.key])
                for kk in kks:
                    kt_ = keytiles[kk]
                    lastmm = (kk == kks[-1]) and nbias == 0
                    S.add("pe", lambda e, pr=pr, h2=h2, kt_=kt_, kk=kk, lastmm=lastmm: e.matmul(
                        ST[:, kk * 128:(kk + 1) * 128], lhsT=KT[:, pr, kt_ * 128:(kt_ + 1) * 128], rhs=QZ[:, pr, h2, :],
                        start=False, stop=lastmm, skip_group_check=True),
                        reads=[QZ.key, KT.key], writes=[ST.key])
                if nbias:
                    k0 = kks[0]
                    S.add("pe", lambda e, h=h, k0=k0, nbias=nbias: e.matmul(
                        ST[:, k0 * 128:(k0 + nbias) * 128], lhsT=IDb[:],
                        rhs=BT[:, var, h, k0:k0 + nbias, :].rearrange("p a b -> p (a b)"),
                        start=False, stop=True, skip_group_check=True),
                        reads=[BT.key, IDb.key], writes=[ST.key])
            S.add("act", lambda e: e.activation(out=PTs[:, 0:nk * 128], in_=ST[:, 0:nk * 128], func=AF.Exp), reads=[ST.key], writes=[PTs.key])
            S.next_stage()
            for kk, kt_ in enumerate(keytiles):
                S.add("pe", lambda e, kk=kk, kt_=kt_, h=h: e.matmul(pOV[:, h * 65:(h + 1) * 65], lhsT=PTs[:, kk * 128:(kk + 1) * 128],
                                                                    rhs=Vaug[:, kt_, h, :], start=(kk == 0), stop=(kk == nk - 1)),
                      reads=[PTs.key, Vaug.key], writes=[pOV.key])
        run_staged(S, head_fn, [0, 1, 2, 3], group=4)
        OVv = pOV[:, 0:260].rearrange("p (h c) -> p h c", h=4)
        S.add("dve", lambda e: e.reciprocal(out=rc[:], in_=OVv[:, :, 64]), reads=[pOV.key], writes=[rc.key])
        S.add("dve", lambda e: e.tensor_tensor(out=onat[:], in0=OVv[:, :, 0:64], in1=rc[:].unsqueeze(2).to_broadcast([128, 4, 64]), op=ALU.mult),
              reads=[pOV.key, rc.key], writes=[onat.key])
        o2 = onat[:].rearrange("p h c -> p (h c)")
        if "naraw" in self.dbg:
            S.add("sp", lambda e: e.dma_start(out=self.dbg_out["naraw%d" % l][n0:n0 + 128, :], in_=o2), reads=[onat.key], dma=True,
                  semkey="dbg_naraw")
        S.add("act", lambda e: e.activation(out=junk[:], in_=o2, func=AF.Square, accum_out=ssq[:, 0:1]), reads=[onat.key],
              writes=[junk.key, ssq.key])
        S.add("dve", lambda e: e.tensor_scalar(out=ssq[:, 1:2], in0=ssq[:, 0:1], scalar1=1.0 / 256, scalar2=EPS, op0=ALU.mult, op1=ALU.add),
              reads=[ssq.key], writes=[ssq.key])
        S.add("act", lambda e: e.activation(out=ssq[:, 1:2], in_=ssq[:, 1:2], func=AF.Ln), reads=[ssq.key], writes=[ssq.key])
        S.add("act", lambda e: e.activation(out=ssq[:, 1:2], in_=ssq[:, 1:2], func=AF.Exp, scale=-0.5), reads=[ssq.key], writes=[ssq.key])
        S.add("dve", lambda e: e.scalar_tensor_tensor(out=junk[:], in0=o2, scalar=ssq[:, 1:2], in1=NANrow, op0=ALU.mult, op1=ALU.mult),
              reads=[onat.key, ssq.key, rv.key, junk.key], writes=[junk.key])
        for k in range(2):
            S.add("pe", lambda e, k=k: e.transpose(out=pT[:, k * 128:(k + 1) * 128], in_=junk[:, k * 128:(k + 1) * 128], identity=IDENT),
                  reads=[junk.key, cm.key], writes=[pT.key])
        S.add("act", lambda e: e.copy(out=oT[:].rearrange("p a b -> p (a b)"), in_=pT[:, 0:256]), reads=[pT.key], writes=[oT.key])
        S.add("sp", lambda e: e.dma_start(out=MIXr[:, 2:4, n0:n0 + 128], in_=oT[:]), reads=[oT.key], dma=True, semkey="st_" + oT.key)

    if not last:
        for tile in range(2):
            q_tile(tile, [0, 1], None)
    for it in range(nlat):
        var = 0 if it == 0 else 1 if it == 1 else 3 if it == nlat - 2 else 4 if it == nlat - 1 else 2
        kt0 = min(max(it - 2, 0), nlat - 5)
        q_tile(it + 2, [kt0 + 2 + k for k in range(5)] + [0, 1], var)
    self.barrier()
    self.reset(m)


NLAT_FULL = 64
_CACHE = {}


def kernel(**inputs):
    inp = {k: np.asarray(v) for k, v in inputs.items()}
    nlat = NLAT_FULL
    B = inp["x"].shape[0]
    kb = KB(nlat=nlat)
    nc = kb.build()
    maps = make_in_maps(inp, nlat, list(range(B)))
    res = run_bass_kernel_spmd(nc, maps, core_ids=list(range(B)))
    out = np.stack([np.asarray(res.results[b]["outT"]).T for b in range(B)])
    return np.ascontiguousarray(out.astype(np.float32))
```

```python
import contextlib
import numpy as np
import concourse.bass as bass
import concourse.mybir as mybir
from concourse.bass_utils import run_bass_kernel_spmd

F32 = mybir.dt.float32
BF16 = mybir.dt.bfloat16
AF = mybir.ActivationFunctionType
ALU = mybir.AluOpType
AX = mybir.AxisListType

D = 1024
DEPTH = 2
DFF = 2816
CTX = 256
GW = 64
EPS = 1e-6
ENGS = ("pe", "act", "dve", "pool", "sp")
PH = "__ph"


class Op:
    __slots__ = ("eng", "fn", "deps", "dma", "semkey", "sig", "val", "idx", "slot")

    def __init__(self, eng, fn, dma, semkey):
        self.eng = eng
        self.fn = fn
        self.deps = {}
        self.dma = dma
        self.semkey = semkey
        self.sig = False
        self.val = 0
        self.idx = 0
        self.slot = None


class Sched:
    def __init__(self, nc):
        self.nc = nc
        self.ops = []
        self.last_w = {}
        self.readers = {}
        self.slotmap = {}

    def begin_record(self):
        self.rec = []
        self.rec_stage = 0

    def next_stage(self):
        if getattr(self, "rec", None) is not None:
            self.rec_stage += 1

    def end_record(self):
        r = self.rec
        self.rec = None
        return r

    def replay_staged(self, recs):
        nst = 1 + max((st for r in recs for (st, a, k) in r), default=0)
        for st in range(nst):
            for r in recs:
                for (s_, a, k) in r:
                    if s_ == st:
                        self.add(*a, **k)

    def add(self, eng, fn, reads=(), writes=(), dma=False, semkey=None, barrier=False):
        if getattr(self, "rec", None) is not None:
            self.rec.append((self.rec_stage, (eng, fn), dict(reads=reads, writes=writes, dma=dma, semkey=semkey, barrier=barrier)))
            return None
        reads = list(reads)
        writes = list(writes)
        if barrier:
            writes.append(PH)
        else:
            reads.append(PH)
        op = Op(eng, fn, dma, semkey if semkey is not None else (writes[0] if (dma and writes) else None))
        op.idx = len(self.ops)
        if barrier:
            self.slotmap = {}
        if dma:
            if eng == "pool":
                op.slot = ("p", op.semkey)
            else:
                if op.semkey not in self.slotmap:
                    self.slotmap[op.semkey] = len(self.slotmap)
                op.slot = self.slotmap[op.semkey]
        cand = {}
        for k in reads + writes:
            w = self.last_w.get(k)
            if w is not None:
                cand[w.idx] = w
        for k in writes:
            for r in self.readers.get(k, ()):
                cand[r.idx] = r
        for d in cand.values():
            if (not d.dma) and (not op.dma) and d.eng == "pe" and op.eng == "pe":
                continue
            if d.dma and op.dma and d.semkey == op.semkey:
                pure_waw = all(self.last_w.get(k) is not d for k in reads) and \
                    all(d not in self.readers.get(k, ()) for k in writes)
                if pure_waw:
                    continue
            key = ("dma", d.slot) if d.dma else ("eng", d.eng)
            old = op.deps.get(key)
            if old is None or old.idx < d.idx:
                op.deps[key] = d
            d.sig = True
        for k in writes:
            self.last_w[k] = op
            self.readers[k] = []
        for k in reads:
            self.readers.setdefault(k, []).append(op)
        self.ops.append(op)
        return op

    def emit(self):
        nc = self.nc
        import os as _os
        _mx = int(_os.environ.get("MAXOPS", "0"))
        if _mx:
            self.ops = self.ops[:_mx]
        cnt = {}
        dma_keys = []
        for op in self.ops:
            if op.dma:
                k = ("dma", op.slot)
                if k not in cnt:
                    cnt[k] = 0
                    dma_keys.append(k)
                cnt[k] += 16
                op.val = cnt[k]
            elif op.sig:
                k = ("eng", op.eng)
                cnt[k] = cnt.get(k, 0) + 1
                op.val = cnt[k]
        self.maxvals = dict(cnt)
        sems = {}
        with contextlib.ExitStack() as es:
            for e in ENGS:
                sems[("eng", e)] = es.enter_context(nc.semaphore("s_" + e))
            for i, k in enumerate(dma_keys):
                sems[k] = es.enter_context(nc.semaphore("d%d" % i))
            self.nsems = len(sems)
            block = es.enter_context(nc.Block())
            ops = self.ops

            def run(engname, eng):
                waited = {}
                for op in ops:
                    if op.eng != engname:
                        continue
                    for k, d in op.deps.items():
                        if waited.get(k, 0) >= d.val:
                            continue
                        eng.wait_ge(sems[k], d.val)
                        waited[k] = d.val
                    ins = op.fn(eng)
                    if op.dma:
                        ins.then_inc(sems[("dma", op.slot)], 16)
                    elif op.sig:
                        ins.then_inc(sems[("eng", op.eng)], 1)
                last = {}
                for op in ops:
                    if op.eng == engname and op.dma:
                        last[("dma", op.slot)] = max(op.val, last.get(("dma", op.slot), 0))
                for k, v in last.items():
                    if waited.get(k, 0) < v:
                        eng.wait_ge(sems[k], v)

            @block.tensor
            def _(e):
                run("pe", e)

            @block.scalar
            def _(e):
                run("act", e)

            @block.vector
            def _(e):
                run("dve", e)

            @block.gpsimd
            def _(e):
                run("pool", e)

            @block.sync
            def _(e):
                run("sp", e)


def run_staged(S, fn, seg, group=4):
    for g0 in range(0, len(seg), group):
        recs = []
        for ti in range(g0, min(g0 + group, len(seg))):
            S.begin_record()
            fn(ti, seg[ti])
            recs.append(S.end_record())
        S.replay_staged(recs)


class T:
    def __init__(self, h, key):
        self.h = h
        self.key = key

    def __getitem__(self, idx):
        return self.h[idx]


def _dsize(dt):
    return 2 if dt == BF16 else 4


PF_COLS = ([0, 128] + [256, 384] + [512, 640] + [1024, 1152] + [1280, 1408] + [1536, 1664]
           + [2048, 2176, 2304, 2432] + [2560 + 128 * i for i in range(8)])
PF_Q, PF_FF, PF_FB, PF_G, PF_QA, PF_KA, PF_Z, PF_XBC = 0, 2, 4, 6, 8, 10, 12, 16
PF_NB = 24
PF_TR = ["silu"] * 2 + ["copy"] * 4 + ["silu"] * 2 + ["s8"] * 2 + ["copy"] * 2 + ["silu"] * 4 + ["copy"] * 8
PT_GROUPS = [(256, 768, 0), (768, 1024, 512), (1792, 2048, 768), (2048, 2560, 1024), (3584, 3600, 1536)]
PT_FF, PT_FB, PT_I, PT_VA, PT_Z, PT_DT = 0, 256, 512, 768, 1024, 1536
PT_W = 1552

FV_L = 72 + 8 + 8 + 8 + 2 + 4 + 40 + 8
FV_BMOD, FV_NF1, FV_NMX, FV_NF2, FV_HGN, FV_SSN, FV_CW, FV_CB = 0, 72, 80, 88, 96, 98, 102, 142
FV_G = DEPTH * FV_L
FV_C, FV_FN, FV_LB = FV_G, FV_G + 16, FV_G + 24
FV_N = FV_G + 24 + 8
RV_L = 256 + 512 + 16 + 16 + 512
RV_NAN, RV_DSK, RV_ALOG, RV_DTB, RV_SSN = 0, 256, 768, 784, 800
RV_G = DEPTH * RV_L
RV_LBR = RV_G
RV_N = RV_G + 1024


class KB:
    def __init__(self, nlat=64, dbg=(), layers=DEPTH, stop_after=None, parts=("hg", "ssd", "na")):
        self.parts = set(parts)
        self.nlat = nlat
        self.NT = nlat + 2
        self.NTOK = 128 * self.NT
        self.NLTOK = 128 * nlat
        self.dbg = set(dbg)
        self.layers = layers
        self.stop_after = stop_after
        self.nc = nc = bass.Bass("TRN2", target_bir_lowering=False)
        self.S = Sched(nc)
        self.uid = 0
        self.sb_lo = 16512
        self.sb_hi = 229344
        self.off = self.sb_lo
        self.NB = 512
        assert nlat % 4 == 0
        self.nblk = 1 + self.NLTOK // self.NB
        self.declare_io()

    def sb(self, name, shape, dt):
        size = int(np.prod(shape[1:])) * _dsize(dt)
        size = (size + 31) // 32 * 32
        assert self.off + size <= self.sb_hi, (name, self.off, size)
        self.uid += 1
        h = self.nc.alloc_sbuf_tensor_at("%s_%d" % (name, self.uid), list(shape), dt, offset=self.off)
        self.off += size
        return T(h, "%s_%d" % (name, self.uid))

    def mark(self):
        return self.off

    def reset(self, m):
        self.off = m

    def barrier(self):
        scr = self.scr
        self.S.add("dve", lambda e: e.memset(scr[:, 0:1], 0.0), writes=[scr.key], barrier=True)

    def declare_io(self):
        nc = self.nc
        ein = lambda n, s, dt=F32: nc.dram_tensor(n, list(s), dt, kind="ExternalInput").ap()
        self.xT = ein("xT", [D, self.NTOK])
        self.fvec = ein("fvec", [128, FV_N])
        self.rvec = ein("rvec", [128, RV_N])
        self.w_mod = ein("w_mod", [DEPTH, D, 9 * D])
        self.w13 = [ein("ffn1_w13", [DEPTH, D, 2 * DFF]), ein("ffn2_w13", [DEPTH, D, 2 * DFF])]
        self.w2 = [ein("ffn1_w2", [DEPTH, DFF, D]), ein("ffn2_w2", [DEPTH, DFF, D])]
        self.w_in = ein("w_in", [DEPTH, D, 3600])
        self.w_out = ein("w_out", [DEPTH, D, D])
        self.outT = nc.dram_tensor("outT", [D, self.NLTOK], F32, kind="ExternalOutput").ap()
        self.XT = nc.dram_tensor("XTs", [D, self.NTOK], F32).ap()
        self.PF = nc.dram_tensor("PFs", [PF_NB * 128, self.NTOK], F32).ap()
        self.PT = nc.dram_tensor("PTs", [self.NTOK, PT_W], F32).ap()
        self.MIXT = nc.dram_tensor("MIXTs", [D, self.NTOK], BF16).ap()
        self.dbg_out = {}
        _mixer_io(self)

    def dbg_tensor(self, name, shape, dt=F32):
        ap = self.nc.dram_tensor("dbg_" + name, list(shape), dt, kind="ExternalOutput").ap()
        self.dbg_out[name] = ap
        return ap

    def setup_persist(self):
        S = self.S
        self.scr = self.sb("scr", [128, 8], F32)
        self.fv = self.sb("fv", [128, FV_N], F32)
        self.ones_bf = self.sb("ones", [128, 128], BF16)
        self.MOD = self.sb("MOD", [128, 2, 72], F32)
        self.scb = self.sb("scb", [128, 16], F32)
        fv, ones = self.fv, self.ones_bf
        S.add("sp", lambda e: e.dma_start(out=fv[:], in_=self.fvec), writes=[fv.key], dma=True)
        S.add("dve", lambda e: e.memset(ones[:], 1.0), writes=[ones.key])
        scb = self.scb
        S.add("act", lambda e: e.activation(out=scb[:], in_=fv[:, FV_C:FV_C + 16], func=AF.Silu),
              reads=[fv.key], writes=[scb.key])
        self.persist_end = self.mark()

    def phase_mod(self, l):
        S, nc = self.S, self.nc
        m = self.mark()
        HALF = 4608
        wm = [self.sb("wm", [128, HALF], F32) for _ in range(2)]
        psm = self.ps[0]
        scb, fv, MOD = self.scb, self.fv, self.MOD
        i = 0
        for k in range(8):
            for hf in range(2):
                w = wm[i % 2]
                i += 1
                src = self.w_mod[l, k * 128:(k + 1) * 128, hf * HALF:(hf + 1) * HALF]
                S.add("sp", lambda e, w=w, src=src: e.dma_start(out=w[:], in_=src), writes=[w.key], dma=True)
                for j in range(36):
                    fb = hf * 36 + j
                    S.add("pe", lambda e, w=w, j=j, fb=fb, k=k: e.matmul(
                        psm[:, fb * 2:fb * 2 + 2], lhsT=w[:, j * 128:(j + 1) * 128], rhs=scb[:, 2 * k:2 * k + 2],
                        start=(k == 0 and fb == 0), stop=(k == 7), skip_group_check=True),
                        reads=[w.key, scb.key], writes=[psm.key])
        bo = l * FV_L
        for s in range(2):
            S.add("dve", lambda e, s=s: e.tensor_tensor(out=MOD[:, s, :], in0=psm[:, s:144:2],
                                                       in1=fv[:, bo + FV_BMOD:bo + FV_BMOD + 72], op=ALU.add),
                  reads=[psm.key, fv.key], writes=[MOD.key])
        for s in range(2):
            for (js, nw) in ((1, FV_NF1), (4, FV_NMX), (7, FV_NF2)):
                S.add("dve", lambda e, s=s, js=js, nw=nw: e.scalar_tensor_tensor(
                    out=MOD[:, s, js * 8:js * 8 + 8], in0=MOD[:, s, js * 8:js * 8 + 8], scalar=1.0,
                    in1=fv[:, bo + nw:bo + nw + 8], op0=ALU.add, op1=ALU.mult),
                    reads=[MOD.key, fv.key], writes=[MOD.key])
                S.add("dve", lambda e, s=s, js=js: e.tensor_scalar_mul(
                    out=MOD[:, s, js * 8:js * 8 + 8], in0=MOD[:, s, js * 8:js * 8 + 8], scalar1=32.0),
                    reads=[MOD.key], writes=[MOD.key])
            for jg in (2, 8):
                S.add("dve", lambda e, s=s, jg=jg: e.tensor_scalar_mul(
                    out=MOD[:, s, jg * 8:jg * 8 + 8], in0=MOD[:, s, jg * 8:jg * 8 + 8], scalar1=0.5),
                    reads=[MOD.key], writes=[MOD.key])
        if "mod" in self.dbg:
            d = self.dbg_tensor("mod%d" % l, [128, 144])
            S.add("sp", lambda e: e.dma_start(out=d, in_=MOD[:].rearrange("p s c -> p (s c)")), reads=[MOD.key],
                  dma=True, semkey="dbg_mod%d" % l)
        self.barrier()
        self.reset(m)

    def load_w(self, name, src, K, N):
        kc = K // 128
        w = self.sb(name, [128, kc, N], BF16)
        for k in range(kc):
            s = src[k * 128:(k + 1) * 128, :]
            self.S.add("pool", lambda e, k=k, s=s: e.dma_start(out=w[:, k, :], in_=s), writes=[w.key], dma=True)
        return w

    def alloc_work(self, nxb=2):
        NB = self.NB
        self.xb = [self.sb("xb", [128, 8, NB], F32) for _ in range(nxb)] * (2 // nxb)
        self.tk = [self.sb("tk", [128, NB], F32) for _ in range(2)]
        self.hb = self.sb("hb", [128, 8, NB], BF16)
        self.sq = self.hb
        self.rstd = self.sb("rstd", [128, NB], F32)

    def blk(self, i):
        if i == 0:
            return 0, CTX, 1
        return CTX + (i - 1) * self.NB, self.NB, 0

    def load_x(self, xb, src, n0, N):
        v = src.rearrange("(k p) n -> p k n", p=128)[:, :, n0:n0 + N]
        self.S.add("sp", lambda e: e.dma_start(out=xb[:, :, :N], in_=v), writes=[xb.key], dma=True)

    def store_x(self, xb, dst, n0, N, key="dram_x"):
        v = dst.rearrange("(k p) n -> p k n", p=128)[:, :, n0:n0 + N]
        self.S.add("sp", lambda e: e.dma_start(out=v, in_=xb[:, :, :N]), reads=[xb.key], dma=True,
                   semkey="st_" + xb.key)

    def modulate(self, xb, N, A, SH, out, out_keyed):
        S = self.S
        sq, rstd, ones = self.sq, self.rstd, self.ones_bf
        pss = self.ps[0]
        S.add("act", lambda e: e.activation(out=sq[:, :, :N], in_=xb[:, :, :N], func=AF.Square),
              reads=[xb.key], writes=[sq.key])
        for k in range(8):
            S.add("pe", lambda e, k=k: e.matmul(pss[:, :N], lhsT=ones[:], rhs=sq[:, k, :N], start=(k == 0), stop=(k == 7)),
                  reads=[sq.key, ones.key], writes=[pss.key])
        S.add("dve", lambda e: e.tensor_scalar_add(out=rstd[:, :N], in0=pss[:, :N], scalar1=float(D * EPS)),
              reads=[pss.key], writes=[rstd.key])
        S.add("act", lambda e: e.activation(out=rstd[:, :N], in_=rstd[:, :N], func=AF.Ln),
              reads=[rstd.key], writes=[rstd.key])
        S.add("act", lambda e: e.activation(out=rstd[:, :N], in_=rstd[:, :N], func=AF.Exp, scale=-0.5),
              reads=[rstd.key], writes=[rstd.key])
        for k in range(8):
            if SH is None:
                S.add("dve", lambda e, k=k: e.scalar_tensor_tensor(out=out[:, k, :N], in0=xb[:, k, :N], scalar=A[:, k:k + 1],
                                                                   in1=rstd[:, :N], op0=ALU.mult, op1=ALU.mult),
                      reads=[xb.key, rstd.key, self.fv.key], writes=[out_keyed])
            else:
                tk = self.tk[k % 2]
                S.add("dve", lambda e, k=k, tk=tk: e.scalar_tensor_tensor(out=tk[:, :N], in0=xb[:, k, :N], scalar=A[:, k:k + 1],
                                                                          in1=rstd[:, :N], op0=ALU.mult, op1=ALU.mult),
                      reads=[xb.key, rstd.key, self.MOD.key], writes=[tk.key])
                S.add("act", lambda e, k=k, tk=tk: e.activation(out=out[:, k, :N], in_=tk[:, :N], func=AF.Identity,
                                                                bias=SH[:, k:k + 1]),
                      reads=[tk.key, self.MOD.key], writes=[out_keyed])

    def ffn(self, xb, N, s, jbase, w13b, w2b):
        S = self.S
        MOD = self.MOD
        A = MOD[:, s, (jbase + 1) * 8:(jbase + 2) * 8]
        SH = MOD[:, s, jbase * 8:(jbase + 1) * 8]
        G = MOD[:, s, (jbase + 2) * 8:(jbase + 3) * 8]
        hb, ab, sg = self.hb, self.ab, self.sg
        self.modulate(xb, N, A, SH, hb, hb.key)
        for j in range(22):
            pu, pg = self.ps[1 + j % 2], self.ps[3 + j % 2]
            for k in range(8):
                S.add("pe", lambda e, j=j, k=k, pu=pu: e.matmul(pu[:, :N], lhsT=w13b[:, k, j * 128:(j + 1) * 128],
                                                                rhs=hb[:, k, :N], start=(k == 0), stop=(k == 7)),
                      reads=[w13b.key, hb.key], writes=[pu.key])
            for k in range(8):
                S.add("pe", lambda e, j=j, k=k, pg=pg: e.matmul(pg[:, :N], lhsT=w13b[:, k, DFF + j * 128:DFF + (j + 1) * 128],
                                                                rhs=hb[:, k, :N], start=(k == 0), stop=(k == 7)),
                      reads=[w13b.key, hb.key], writes=[pg.key])
            sgj = sg[j % 2]
            S.add("act", lambda e, pg=pg, sgj=sgj: e.activation(out=sgj[:, :N], in_=pg[:, :N], func=AF.Silu),
                  reads=[pg.key], writes=[sgj.key])
            S.add("dve", lambda e, j=j, pu=pu, sgj=sgj: e.tensor_tensor(out=ab[:, j, :N], in0=sgj[:, :N], in1=pu[:, :N],
                                                                          op=ALU.mult),
                  reads=[pu.key, sgj.key], writes=[ab.key])
        for fb in range(8):
            po = self.ps[5 + fb % 2]
            for j in range(22):
                S.add("pe", lambda e, j=j, fb=fb, po=po: e.matmul(po[:, :N], lhsT=w2b[:, j, fb * 128:(fb + 1) * 128],
                                                                  rhs=ab[:, j, :N], start=(j == 0), stop=(j == 21)),
                      reads=[w2b.key, ab.key], writes=[po.key])
            S.add("dve", lambda e, fb=fb, po=po: e.scalar_tensor_tensor(out=xb[:, fb, :N], in0=po[:, :N],
                                                                        scalar=G[:, fb:fb + 1], in1=xb[:, fb, :N],
                                                                        op0=ALU.mult, op1=ALU.add),
                  reads=[po.key, xb.key, MOD.key], writes=[xb.key])

    def phase_ffn1(self, l):
        m = self.mark()
        w13b = self.load_w("w13b", self.w13[0][l], D, 2 * DFF)
        w2b = self.load_w("w2b", self.w2[0][l], DFF, D)
        self.alloc_work()
        self.ab = self.sb("ab", [128, 22, self.NB], BF16)
        self.sg = [self.sb("sg", [128, self.NB], F32) for _ in range(2)]
        src = self.xT if l == 0 else self.XT
        for i in range(self.nblk):
            n0, N, s = self.blk(i)
            xb = self.xb[i % 2]
            self.load_x(xb, src, n0, N)
            self.ffn(xb, N, s, 0, w13b, w2b)
            self.store_x(xb, self.XT, n0, N, key="dram_x1")
        self.barrier()
        self.reset(m)
        if "x1" in self.dbg:
            self.dump_dram("x1_%d" % l, self.XT, [D, self.NTOK], "dram_x1")

    def dump_dram(self, name, src, shape, key, dt=F32):
        d = self.dbg_tensor(name, shape, dt)
        self.S.add("sp", lambda e: e.dma_start(out=d, in_=src), dma=True, semkey="dbg_" + name)
        self.barrier()

    def phase_inproj(self, l):
        S = self.S
        m = self.mark()
        winb = self.load_w("winb", self.w_in[l], D, 3600)
        self.alloc_work()
        NB = self.NB
        pfst = self.sb("pfst", [128, PF_NB, NB], F32)
        ptsts = [self.sb("ptst", [128, PT_W], F32) for _ in range(NB // 128)]
        MOD, hb = self.MOD, self.hb
        def do_blk(i):
            n0, N, s = self.blk(i)
            xb = self.xb[i % 2]
            self.load_x(xb, self.XT, n0, N)
            self.modulate(xb, N, MOD[:, s, 32:40], MOD[:, s, 24:32], hb, hb.key)
            for bi, c0 in enumerate(PF_COLS):
                p = self.ps[1 + bi % 4]
                for k in range(8):
                    S.add("pe", lambda e, k=k, c0=c0, p=p: e.matmul(p[:, :N], lhsT=winb[:, k, c0:c0 + 128], rhs=hb[:, k, :N],
                                                                    start=(k == 0), stop=(k == 7)),
                          reads=[winb.key, hb.key], writes=[p.key])
                tr = PF_TR[bi]
                if tr == "silu":
                    S.add("act", lambda e, bi=bi, p=p: e.activation(out=pfst[:, bi, :N], in_=p[:, :N], func=AF.Silu),
                          reads=[p.key], writes=[pfst.key])
                elif tr == "s8":
                    S.add("dve", lambda e, bi=bi, p=p: e.tensor_scalar_mul(out=pfst[:, bi, :N], in0=p[:, :N], scalar1=0.125),
                          reads=[p.key], writes=[pfst.key])
                else:
                    S.add("dve", lambda e, bi=bi, p=p: e.tensor_copy(out=pfst[:, bi, :N], in_=p[:, :N]),
                          reads=[p.key], writes=[pfst.key])
            dst = self.PF.rearrange("(f p) n -> p f n", p=128)[:, :, n0:n0 + N]
            S.add("sp", lambda e, dst=dst: e.dma_start(out=dst, in_=pfst[:, :, :N]), reads=[pfst.key],
                  dma=True, semkey="dram_pf")
            for t in range(N // 128):
                ptst = ptsts[t]
                for gi, (c0, c1, d0) in enumerate(PT_GROUPS):
                    p = self.ps[5 + gi % 2]
                    wdt = c1 - c0
                    for k in range(8):
                        S.add("pe", lambda e, k=k, t=t, c0=c0, c1=c1, p=p, wdt=wdt: e.matmul(
                            p[:, :wdt], lhsT=hb[:, k, t * 128:(t + 1) * 128], rhs=winb[:, k, c0:c1],
                            start=(k == 0), stop=(k == 7)), reads=[winb.key, hb.key], writes=[p.key])
                    S.add("act", lambda e, ptst=ptst, d0=d0, wdt=wdt, p=p: e.copy(out=ptst[:, d0:d0 + wdt], in_=p[:, :wdt]),
                          reads=[p.key], writes=[ptst.key])
                dstt = self.PT[n0 + t * 128:n0 + (t + 1) * 128, :]
                S.add("sp", lambda e, ptst=ptst, dstt=dstt: e.dma_start(out=dstt, in_=ptst[:]), reads=[ptst.key],
                      dma=True, semkey="st_" + ptst.key)
        for i in range(self.nblk):
            do_blk(i)
        self.barrier()
        self.reset(m)
        if "proj" in self.dbg:
            self.dump_dram("pf_%d" % l, self.PF, [PF_NB * 128, self.NTOK], "dram_pf")
            self.dump_dram("pt_%d" % l, self.PT, [self.NTOK, PT_W], "dram_pt")

    def phase_out(self, l):
        S = self.S
        last = (l == DEPTH - 1)
        m = self.mark()
        w13b = self.load_w("w13b", self.w13[1][l], D, 2 * DFF)
        w2b = self.load_w("w2b", self.w2[1][l], DFF, D)
        woutb = self.load_w("woutb", self.w_out[l], D, D)
        self.alloc_work(1)
        NB = self.NB
        self.ab = self.sb("ab", [128, 22, NB], BF16)
        self.sg = [self.sb("sg", [128, NB], F32) for _ in range(2)]
        mixb = [T(self.ab.h, self.ab.key)] * 2
        MOD = self.MOD
        def do_blk(i):
            n0, N, s = self.blk(i)
            if last and s == 1:
                return
            xb = self.xb[i % 2]
            mb = mixb[i % 2]
            self.load_x(xb, self.XT, n0, N)
            v = self.MIXT.rearrange("(k p) n -> p k n", p=128)[:, :, n0:n0 + N]
            S.add("sp", lambda e, mb=mb, v=v: e.dma_start(out=mb[:, 0:8, :N], in_=v), writes=[mb.key], dma=True)
            G = MOD[:, s, 40:48]
            for fb in range(8):
                p = self.ps[5 + fb % 2]
                for k in range(8):
                    S.add("pe", lambda e, k=k, fb=fb, p=p, mb=mb: e.matmul(p[:, :N], lhsT=woutb[:, k, fb * 128:(fb + 1) * 128],
                                                                          rhs=mb[:, k, :N], start=(k == 0), stop=(k == 7)),
                          reads=[woutb.key, mb.key], writes=[p.key])
                S.add("dve", lambda e, fb=fb, p=p, xb=xb, G=G: e.scalar_tensor_tensor(out=xb[:, fb, :N], in0=p[:, :N],
                                                                                  scalar=G[:, fb:fb + 1], in1=xb[:, fb, :N],
                                                                                  op0=ALU.mult, op1=ALU.add),
                      reads=[p.key, xb.key, MOD.key], writes=[xb.key])
            self.ffn(xb, N, s, 6, w13b, w2b)
            if not last:
                self.store_x(xb, self.XT, n0, N, key="dram_x3")
            else:
                ob = xb
                fn = self.fn32
                self.modulate(xb, N, fn, None, ob, ob.key)
                v = self.outT.rearrange("(k p) n -> p k n", p=128)[:, :, n0 - CTX:n0 - CTX + N]
                S.add("sp", lambda e, v=v, ob=ob: e.dma_start(out=v, in_=ob[:, :, :N]), reads=[ob.key],
                      dma=True, semkey="st_" + ob.key)
        for i in range(self.nblk):
            do_blk(i)
        self.barrier()
        self.reset(m)
        if "x3" in self.dbg and not last:
            self.dump_dram("x3_%d" % l, self.XT, [D, self.NTOK], "dram_x3")

    def build(self):
        S = self.S
        big = [self.nc.alloc_psum_tensor("psb%d" % i, [128, 1024], F32) for i in range(4)]
        self.ps = [T(big[i // 2][:, (i % 2) * 512:(i % 2) * 512 + 512], "ps%d" % i) for i in range(8)]
        self.psbig = [T(big[i], "psB%d" % i) for i in range(4)]
        self.setup_persist()
        self.fn32 = None
        for l in range(self.layers):
            if not getattr(self, "only_mixer", False):
                self.phase_mod(l)
                if self.stop_after == ("mod", l):
                    break
                self.phase_ffn1(l)
                if self.stop_after == ("ffn1", l):
                    break
                self.phase_inproj(l)
                if self.stop_after == ("inproj", l):
                    break
            self.phase_mixer(l)
            if self.stop_after == ("mixer", l):
                break
            self.phase_out_wrap(l)
        S.emit()
        return self.nc

    def phase_out_wrap(self, l):
        last = (l == DEPTH - 1)
        if last:
            m = self.mark()
            fn32 = self.sb("fn32", [128, 8], F32)
            fv = self.fv
            self.S.add("dve", lambda e: e.tensor_scalar_mul(out=fn32[:], in0=fv[:, FV_FN:FV_FN + 8], scalar1=32.0),
                       reads=[fv.key], writes=[fn32.key])
            self.fn32 = fn32
            self.persist_tmp = self.mark()
            self.phase_out(l)
            self.reset(m)
        else:
            self.phase_out(l)

    def phase_mixer(self, l):
        m = self.mark()
        _setup_mixer_consts(self)
        if "ohg" in self.dbg:
            self.dbg_tensor("ohg%d" % l, [256, self.NTOK])
        if "yssm" in self.dbg:
            self.dbg_tensor("yssm%d" % l, [self.NTOK, 512])
        if "naraw" in self.dbg:
            self.dbg_tensor("naraw%d" % l, [self.NTOK, 256])
        if "hg" in self.parts:
            _phase_hgrn2(self, l)
        if "ssd" in self.parts:
            _phase_ssd(self, l)
        if "na" in self.parts:
            _phase_na(self, l)
        if "mix" in self.dbg:
            self.dump_dram("mix_%d" % l, self.MIXT, [D, self.NTOK], "x", BF16)
        self.reset(m)


def fm(v):
    v = np.asarray(v, np.float32)
    return v.reshape(-1, 128).T


def prep_shared(inp):
    fv = np.zeros((128, FV_N), np.float32)
    rv = np.zeros((128, RV_N), np.float32)
    for l in range(DEPTH):
        o = l * FV_L
        fv[:, o + FV_BMOD:o + FV_BMOD + 72] = fm(inp["b_mod"][l])
        fv[:, o + FV_NF1:o + FV_NF1 + 8] = fm(inp["norm_ffn1"][l])
        fv[:, o + FV_NMX:o + FV_NMX + 8] = fm(inp["norm_mix"][l])
        fv[:, o + FV_NF2:o + FV_NF2 + 8] = fm(inp["norm_ffn2"][l])
        fv[:, o + FV_HGN:o + FV_HGN + 2] = fm(inp["hg_norm"][l])
        fv[:, o + FV_SSN:o + FV_SSN + 4] = fm(inp["ssm_norm"][l])
        for j in range(5):
            fv[:, o + FV_CW + j * 8:o + FV_CW + j * 8 + 8] = fm(inp["ssm_conv_w"][l, j])
        fv[:, o + FV_CB:o + FV_CB + 8] = fm(inp["ssm_conv_b"][l])
        r = l * RV_L
        rv[:, r + RV_NAN:r + RV_NAN + 256] = inp["na_norm"][l][None, :]
        rv[:, r + RV_DSK:r + RV_DSK + 512] = np.repeat(inp["ssm_d"][l], 64)[None, :]
        rv[:, r + RV_ALOG:r + RV_ALOG + 16] = inp["ssm_a_log"][l].reshape(-1)[None, :]
        rv[:, r + RV_DTB:r + RV_DTB + 16] = inp["ssm_dt_bias"][l].reshape(-1)[None, :]
        rv[:, r + RV_SSN:r + RV_SSN + 512] = inp["ssm_norm"][l][None, :]
    fv[:, FV_FN:FV_FN + 8] = fm(inp["final_norm"])
    for dr in range(2):
        for l in range(DEPTH):
            rv[:, RV_LBR + (dr * 2 + l) * 256:RV_LBR + (dr * 2 + l) * 256 + 256] = inp["hg_lower_bounds"][dr, l][None, :]
    for dr in range(2):
        for l in range(DEPTH):
            fv[:, FV_LB + dr * 4 + l * 2:FV_LB + dr * 4 + l * 2 + 2] = fm(inp["hg_lower_bounds"][dr, l])
    return fv, rv


def prep_core(inp, b, nlat, fv_shared):
    fv = fv_shared.copy()
    cc = np.stack([fm(inp["c"][b]), fm(inp["c_ctx"])], axis=2)
    fv[:, FV_C:FV_C + 16] = cc.reshape(128, 16)
    xT = np.ascontiguousarray(np.concatenate([inp["ctx"][b], inp["x"][b][:128 * nlat]], axis=0).T)
    return fv, xT


def make_in_maps(inp, nlat, batches):
    fvs, rv = prep_shared(inp)
    shared = {k: np.ascontiguousarray(inp[k], np.float32) for k in
              ("w_mod", "ffn1_w13", "ffn2_w13", "ffn1_w2", "ffn2_w2", "w_in", "w_out")}
    cmat = make_cmat()
    nab = np.stack([make_nabias(np.asarray(inp["na_rpb"][l], np.float32), nlat) for l in range(DEPTH)])
    maps = []
    for b in batches:
        fv, xT = prep_core(inp, b, nlat, fvs)
        m = dict(shared)
        m.update({"xT": xT, "fvec": fv, "rvec": rv, "cmat": cmat, "nabias": nab})
        maps.append(m)
    return maps


CM_ID, CM_M1F, CM_M2F, CM_M3F, CM_M1B, CM_M2B, CM_M3B = 0, 128, 256, 384, 512, 640, 768
CM_M4F, CM_M4B, CM_HM, CM_BLK, CM_TRIF, CM_TRIB, CM_NEGF, CM_NEGB, CM_ONES = 896, 900, 904, 1160, 1288, 1416, 1544, 1672, 1800
CM_N = 1928
NEG = -30000.0


def make_cmat():
    c = np.zeros((128, CM_N), np.float32)
    u = np.arange(128)[:, None]
    t = np.arange(128)[None, :]
    same = (u // 32) == (t // 32)
    c[:, CM_ID:CM_ID + 128] = (u == t)
    mf = (t // 32) * 32 + 15
    c[:, CM_M1F:CM_M1F + 128] = same * (((u > mf) & (u <= t)) * 1.0 - ((u > t) & (u <= mf)) * 1.0)
    c[:, CM_M2F:CM_M2F + 128] = same & (u <= t)
    c[:, CM_M3F:CM_M3F + 128] = same & (u > t)
    mb = (t // 32) * 32 + 16
    c[:, CM_M1B:CM_M1B + 128] = same * (((u >= t) & (u < mb)) * 1.0 - ((u >= mb) & (u < t)) * 1.0)
    c[:, CM_M2B:CM_M2B + 128] = same & (u >= t)
    c[:, CM_M3B:CM_M3B + 128] = same & (u < t)
    j = np.arange(4)[None, :]
    c[:, CM_M4F:CM_M4F + 4] = (u // 32) == j
    c[:, CM_M4B:CM_M4B + 4] = (u // 32) == (3 - j)
    col = np.arange(128)[None, :]
    c[:, CM_HM:CM_HM + 128] = (col // 64 == 0)
    c[:, CM_HM + 128:CM_HM + 256] = (col // 64 == 1)
    c[:, CM_BLK:CM_BLK + 128] = (u // 64) == (t // 64)
    c[:, CM_TRIF:CM_TRIF + 128] = (u <= t)
    c[:, CM_TRIB:CM_TRIB + 128] = (u >= t)
    c[:, CM_NEGF:CM_NEGF + 128] = NEG * (u > t)
    c[:, CM_NEGB:CM_NEGB + 128] = NEG * (u < t)
    c[:, CM_ONES:CM_ONES + 128] = 1.0
    return c


def _mixer_io(self):
    nc = self.nc
    self.cmat = nc.dram_tensor("cmat", [128, CM_N], F32, kind="ExternalInput").ap()
    self.OHG = nc.dram_tensor("OHGs", [256, self.NTOK], F32).ap()
    _ssd_io(self)
    _na_io(self)


def _setup_mixer_consts(self):
    S = self.S
    self.cm = cm = self.sb("cm", [128, CM_N], F32)
    S.add("sp", lambda e: e.dma_start(out=cm[:], in_=self.cmat), writes=[cm.key], dma=True)
    self.rv = rv = self.sb("rv", [128, RV_N], F32)
    S.add("sp", lambda e: e.dma_start(out=rv[:], in_=self.rvec), writes=[rv.key], dma=True)
    self.blk_bf = blk = self.sb("blkbf", [128, 128], BF16)
    S.add("dve", lambda e: e.tensor_copy(out=blk[:], in_=cm[:, CM_BLK:CM_BLK + 128]), reads=[cm.key], writes=[blk.key])


def _hg_tiles(self, d):
    NT = self.NT
    chain = list(range(NT)) if d == 0 else [1, 0] + list(range(NT - 1, 1, -1))
    return [chain[0:2]] + [chain[i:i + 8] for i in range(2, NT, 8)]


def _phase_hgrn2(self, l):
    S = self.S
    blkbf_l = self.blk_bf
    m = self.mark()
    cm, fv, rv = self.cm, self.fv, self.rv
    ps = self.ps
    LBt = self.sb("LBt", [128, 2, 256], F32)
    OMLt = self.sb("OMLt", [128, 2, 256], F32)
    omlf = self.sb("omlf", [128, 2, 2], F32)
    if l == 0:
        S.add("dve", lambda e: e.memset(LBt[:], 0.0), writes=[LBt.key])
        S.add("dve", lambda e: e.memset(OMLt[:], 1.0), writes=[OMLt.key])
        S.add("dve", lambda e: e.memset(omlf[:], 1.0), writes=[omlf.key])
    else:
        for d in range(2):
            a0 = rv[:, RV_LBR + (d * 2 + 0) * 256:RV_LBR + (d * 2 + 0) * 256 + 256]
            a1 = rv[:, RV_LBR + (d * 2 + 1) * 256:RV_LBR + (d * 2 + 1) * 256 + 256]
            S.add("dve", lambda e, d=d, a0=a0, a1=a1: e.tensor_tensor(out=LBt[:, d, :], in0=a1, in1=a0, op=ALU.subtract),
                  reads=[rv.key], writes=[LBt.key])
            f0 = fv[:, FV_LB + d * 4:FV_LB + d * 4 + 2]
            f1 = fv[:, FV_LB + d * 4 + 2:FV_LB + d * 4 + 4]
            S.add("dve", lambda e, d=d, f0=f0, f1=f1: e.tensor_tensor(out=omlf[:, d, :], in0=f1, in1=f0, op=ALU.subtract),
                  reads=[fv.key], writes=[omlf.key])
        S.add("act", lambda e: e.activation(out=LBt[:], in_=LBt[:], func=AF.Sigmoid), reads=[LBt.key], writes=[LBt.key])
        S.add("act", lambda e: e.activation(out=omlf[:], in_=omlf[:], func=AF.Sigmoid), reads=[omlf.key], writes=[omlf.key])
        S.add("dve", lambda e: e.tensor_scalar(out=OMLt[:], in0=LBt[:], scalar1=-1.0, scalar2=1.0, op0=ALU.mult, op1=ALU.add),
              reads=[LBt.key], writes=[OMLt.key])
        S.add("dve", lambda e: e.tensor_scalar(out=omlf[:], in0=omlf[:], scalar1=-1.0, scalar2=1.0, op0=ALU.mult, op1=ALU.add),
              reads=[omlf.key], writes=[omlf.key])
    omlfh = self.sb("omlfh", [128, 2, 2, 2], F32)
    for d_ in range(2):
        for pr_ in range(2):
            for h2_ in range(2):
                S.add("dve", lambda e, d_=d_, pr_=pr_, h2_=h2_: e.tensor_tensor(
                    out=omlfh[:, d_, pr_, h2_:h2_ + 1], in0=omlf[:, d_, pr_:pr_ + 1],
                    in1=cm[:, CM_BLK + h2_ * 64:CM_BLK + h2_ * 64 + 1], op=ALU.mult),
                    reads=[omlf.key, cm.key], writes=[omlfh.key])
    D1 = [self.sb("D1", [128, 64, 33], F32) for _ in range(2)]
    SO = [self.sb("SO", [128, 64, 33], F32) for _ in range(2)]
    D0 = [self.sb("D0", [128, 64, 33], F32) for _ in range(2)]
    Sblk = [self.sb("Sblk", [128, 32, 128], BF16) for _ in range(2)]
    DEC = self.sb("DEC", [128, 2, 32], F32)
    ATs = [self.sb("ATs", [128, 4, 128], BF16) for _ in range(8)]
    QHs = [self.sb("QHs", [128, 2, 128], BF16) for _ in range(8)]
    VZs = [self.sb("VZs", [128, 2, 2, 128], BF16) for _ in range(8)]
    rtok_R = [self.sb("rtok", [128, 256], F32) for _ in range(4)]
    vtok_R = [self.sb("vtok", [128, 256], F32) for _ in range(4)]
    vtok_bf_R = [self.sb("vtok_bf", [128, 256], BF16) for _ in range(4)]
    qfm_R = [self.sb("qfm", [128, 2, 128], F32) for _ in range(4)]
    rfm_R = [self.sb("rfm", [128, 2, 128], F32) for _ in range(4)]
    sig_R = [self.sb("sig", [128, 256], F32) for _ in range(4)]
    tmpk_R = [self.sb("tmpk", [128, 256], F32) for _ in range(4)]
    lf_R = [self.sb("lf", [128, 256], F32) for _ in range(4)]
    ktok_R = [self.sb("ktok", [128, 256], F32) for _ in range(4)]
    sneg_R = [self.sb("sneg", [128, 2, 128], F32) for _ in range(4)]
    P1c_R = [self.sb("P1c", [128, 256], F32) for _ in range(4)]
    Ep_R = [self.sb("Ep", [128, 256], F32) for _ in range(4)]
    En_R = [self.sb("En", [128, 256], F32) for _ in range(4)]
    E2_R = [self.sb("E2", [128, 256], F32) for _ in range(4)]
    E3_R = [self.sb("E3", [128, 256], F32) for _ in range(4)]
    qt_R = [self.sb("qt", [128, 2, 128], BF16) for _ in range(4)]
    kt_R = [self.sb("kt", [128, 2, 2, 128], BF16) for _ in range(4)]
    khat_R = [self.sb("khat", [128, 4, 256], BF16) for _ in range(4)]
    of_ld_R = [self.sb("of_ld", [128, 2, 128], F32) for _ in range(4)]
    osum_R = [self.sb("osum", [128, 2, 128], F32) for _ in range(4)]
    osq_R = [self.sb("osq", [128, 2, 128], BF16) for _ in range(4)]
    orstd_R = [self.sb("orstd", [128, 256], F32) for _ in range(4)]
    sgl_R = [self.sb("sgl", [128, 2, 128], F32) for _ in range(4)]
    hgo_R = [self.sb("hgo", [128, 2, 128], BF16) for _ in range(4)]
    for pr in range(2):
        S.add("dve", lambda e, pr=pr: e.memset(Sblk[pr][:], 0.0), writes=[Sblk[pr].key])
        S.add("dve", lambda e, pr=pr: e.memset(D0[pr][:], 0.0), writes=[D0[pr].key])
        S.add("dve", lambda e, pr=pr: e.memset(D1[pr][:], 0.0), writes=[D1[pr].key])
    PFr = self.PF.rearrange("(f p) n -> p f n", p=128)
    OHGr = self.OHG.rearrange("(f p) n -> p f n", p=128)
    MIXr = self.MIXT.rearrange("(f p) n -> p f n", p=128)
    pA, pB, pC, pU, pO, pN = ps[1], ps[2], ps[3], ps[4], ps[5], ps[6]
    def do_dir(d):
        M1 = cm[:, (CM_M1F, CM_M1B)[d]:(CM_M1F, CM_M1B)[d] + 128]
        M2 = cm[:, (CM_M2F, CM_M2B)[d]:(CM_M2F, CM_M2B)[d] + 128]
        M3 = cm[:, (CM_M3F, CM_M3B)[d]:(CM_M3F, CM_M3B)[d] + 128]
        M4 = cm[:, (CM_M4F, CM_M4B)[d]:(CM_M4F, CM_M4B)[d] + 4]
        M4n = cm[:, CM_M4F:CM_M4F + 4]
        for pr in range(2):
            S.add("dve", lambda e, pr=pr: e.memset(D1[pr][:, :, 0:1], 0.0), writes=[D1[pr].key])
        def do_seg(seg):
            nt = len(seg)
            nch = 4 * nt
            def p1(ti, tile):
                n0 = tile * 128
                rtok, vtok, vtok_bf, qfm, rfm, sig, tmpk, lf, ktok, sneg, P1c, Ep, En, E2, E3, qt, kt, khat = rtok_R[ti % 4], vtok_R[ti % 4], vtok_bf_R[ti % 4], qfm_R[ti % 4], rfm_R[ti % 4], sig_R[ti % 4], tmpk_R[ti % 4], lf_R[ti % 4], ktok_R[ti % 4], sneg_R[ti % 4], P1c_R[ti % 4], Ep_R[ti % 4], En_R[ti % 4], E2_R[ti % 4], E3_R[ti % 4], qt_R[ti % 4], kt_R[ti % 4], khat_R[ti % 4]
                pA, pB = (ps[1], ps[2]) if ti % 2 == 0 else (ps[0], ps[7])
                AT, QH, VZ = ATs[ti], QHs[ti], VZs[ti]
                S.add("sp", lambda e, n0=n0: e.dma_start(out=rtok[:], in_=self.PT[n0:n0 + 128, (PT_FF, PT_FB)[d]:(PT_FF, PT_FB)[d] + 256]),
                      writes=[rtok.key], dma=True)
                S.add("sp", lambda e, n0=n0: e.dma_start(out=vtok[:], in_=self.PT[n0:n0 + 128, PT_I:PT_I + 256]),
                      writes=[vtok.key], dma=True)
                S.add("sp", lambda e, n0=n0: e.dma_start(out=qfm[:], in_=PFr[:, PF_Q:PF_Q + 2, n0:n0 + 128]),
                      writes=[qfm.key], dma=True)
                fbk = (PF_FF, PF_FB)[d]
                S.add("sp", lambda e, n0=n0, fbk=fbk: e.dma_start(out=rfm[:], in_=PFr[:, fbk:fbk + 2, n0:n0 + 128]),
                      writes=[rfm.key], dma=True)
                S.add("act", lambda e: e.activation(out=sig[:], in_=rtok[:], func=AF.Sigmoid), reads=[rtok.key], writes=[sig.key])
                S.add("act", lambda e: e.copy(out=vtok_bf[:], in_=vtok[:]), reads=[vtok.key], writes=[vtok_bf.key])
                S.add("act", lambda e: e.activation(out=sneg[:], in_=rfm[:], func=AF.Sigmoid, scale=-1.0),
                      reads=[rfm.key], writes=[sneg.key])
                S.add("dve", lambda e: e.tensor_tensor(out=tmpk[:], in0=sig[:], in1=OMLt[:, d, :], op=ALU.mult),
                      reads=[sig.key, OMLt.key], writes=[tmpk.key])
                S.add("dve", lambda e: e.scalar_tensor_tensor(out=lf[:], in0=tmpk[:], scalar=1e-20, in1=LBt[:, d, :],
                                                              op0=ALU.max, op1=ALU.add),
                      reads=[tmpk.key, LBt.key], writes=[lf.key])
                S.add("dve", lambda e: e.tensor_tensor(out=ktok[:], in0=OMLt[:, d, :], in1=tmpk[:], op=ALU.subtract),
                      reads=[tmpk.key, OMLt.key], writes=[ktok.key])
                S.add("act", lambda e: e.activation(out=lf[:], in_=lf[:], func=AF.Ln), reads=[lf.key], writes=[lf.key])
                S.next_stage()
                for pr in range(2):
                    S.add("pe", lambda e, pr=pr: e.matmul(pA[:, pr * 128:(pr + 1) * 128], lhsT=lf[:, pr * 128:(pr + 1) * 128], rhs=M1,
                                                          start=True, stop=True), reads=[lf.key, cm.key], writes=[pA.key])
                for pr in range(2):
                    S.add("pe", lambda e, pr=pr: e.matmul(pA[:, 256 + pr * 128:256 + (pr + 1) * 128], lhsT=lf[:, pr * 128:(pr + 1) * 128],
                                                          rhs=M2, start=True, stop=True), reads=[lf.key, cm.key], writes=[pA.key])
                S.add("pe", lambda e: e.matmul(pB[:, 0:256], lhsT=M3, rhs=lf[:], start=True, stop=True),
                      reads=[lf.key, cm.key], writes=[pB.key])
                for pr in range(2):
                    S.add("pe", lambda e, pr=pr: e.matmul(pB[:, 256 + pr * 4:260 + pr * 4], lhsT=lf[:, pr * 128:(pr + 1) * 128], rhs=M4,
                                                          start=True, stop=True), reads=[lf.key, cm.key], writes=[pB.key])
                S.add("dve", lambda e: e.tensor_scalar(out=P1c[:], in0=pA[:, 0:256], scalar1=40.0, scalar2=-40.0, op0=ALU.min, op1=ALU.max),
                      reads=[pA.key], writes=[P1c.key])
                S.add("act", lambda e: e.activation(out=Ep[:], in_=P1c[:], func=AF.Exp), reads=[P1c.key], writes=[Ep.key])
                S.add("act", lambda e: e.activation(out=En[:], in_=P1c[:], func=AF.Exp, scale=-1.0), reads=[P1c.key], writes=[En.key])
                S.add("act", lambda e: e.activation(out=E2[:], in_=pA[:, 256:512], func=AF.Exp), reads=[pA.key], writes=[E2.key])
                S.add("act", lambda e: e.activation(out=E3[:], in_=pB[:, 0:256], func=AF.Exp), reads=[pB.key], writes=[E3.key])
                c0 = ti * 4
                for pr in range(2):
                    S.add("act", lambda e, c0=c0, pr=pr: e.activation(out=DEC[:, pr, c0:c0 + 4], in_=pB[:, 256 + pr * 4:260 + pr * 4],
                                                                      func=AF.Exp), reads=[pB.key], writes=[DEC.key])
                qf2 = qfm[:].rearrange("p a b -> p (a b)")
                S.add("dve", lambda e: e.tensor_tensor(out=qt[:].rearrange("p a b -> p (a b)"), in0=qf2, in1=Ep[:], op=ALU.mult),
                      reads=[qfm.key, Ep.key], writes=[qt.key])
                S.add("dve", lambda e, QH=QH: e.tensor_tensor(out=QH[:].rearrange("p a b -> p (a b)"), in0=qf2, in1=E2[:], op=ALU.mult),
                      reads=[qfm.key, E2.key], writes=[QH.key])
                for pr in range(2):
                    for h2 in range(2):
                        S.add("dve", lambda e, pr=pr, h2=h2: e.scalar_tensor_tensor(
                            out=kt[:, pr, h2, :], in0=sneg[:, pr, :], scalar=omlfh[:, d, pr, h2:h2 + 1],
                            in1=En[:, pr * 128:(pr + 1) * 128], op0=ALU.mult, op1=ALU.mult),
                            reads=[sneg.key, omlfh.key, En.key], writes=[kt.key])
                for j in range(4):
                    S.add("dve", lambda e, j=j: e.scalar_tensor_tensor(out=khat[:, j, :], in0=ktok[:], scalar=M4n[:, j:j + 1], in1=E3[:],
                                                                       op0=ALU.mult, op1=ALU.mult),
                          reads=[ktok.key, cm.key, E3.key], writes=[khat.key])
                if d == 0 or True:
                    hm = cm[:, CM_HM:CM_HM + 256].rearrange("p (a b) -> p a b", a=2)
                    S.add("dve", lambda e, VZ=VZ, hm=hm: e.tensor_tensor(
                        out=VZ[:], in0=vtok[:].rearrange("p (a b) -> p a b", a=2).unsqueeze(2).to_broadcast([128, 2, 2, 128]),
                        in1=hm.unsqueeze(1).to_broadcast([128, 2, 2, 128]), op=ALU.mult),
                        reads=[vtok.key, cm.key], writes=[VZ.key])
                S.next_stage()
                for h in range(4):
                    pr, h2 = h // 2, h % 2
                    S.add("pe", lambda e, h=h, pr=pr, h2=h2: e.matmul(pC[:, h * 128:(h + 1) * 128], lhsT=kt[:, pr, h2, :],
                                                                      rhs=qt[:, pr, :], start=True, stop=True),
                          reads=[kt.key, qt.key], writes=[pC.key])
                msk = cm[:, (CM_M2F, CM_M2B)[d]:(CM_M2F, CM_M2B)[d] + 128]
                S.add("dve", lambda e, AT=AT, msk=msk: e.tensor_tensor(out=AT[:], in0=pC[:].rearrange("p (a b) -> p a b", a=4),
                                                                       in1=msk.unsqueeze(1).to_broadcast([128, 4, 128]), op=ALU.mult),
                      reads=[pC.key, cm.key], writes=[AT.key])
                S.next_stage()
                for pr in range(2):
                    for jj in range(4):
                        j = jj if d == 0 else 3 - jj
                        S.add("pe", lambda e, pr=pr, jj=jj, j=j: e.matmul(pU[:, jj * 128:(jj + 1) * 128], lhsT=khat[:, j, pr * 128:(pr + 1) * 128],
                                                                          rhs=vtok_bf[:, pr * 128:(pr + 1) * 128], start=True, stop=True),
                              reads=[khat.key, vtok_bf.key], writes=[pU.key])
                    for h2 in range(2):
                        src = pU[h2 * 64:(h2 + 1) * 64, :].rearrange("p (a b) -> p a b", a=4)[:, :, h2 * 64:(h2 + 1) * 64]
                        dstv = D1[pr][h2 * 64:(h2 + 1) * 64, :, 1 + c0:1 + c0 + 4].rearrange("p v c -> p c v")
                        if h2 == 0:
                            S.add("act", lambda e, pr=pr, src=src, dstv=dstv: e.copy(out=dstv, in_=src),
                                  reads=[pU.key], writes=[D1[pr].key])
                        else:
                            S.add("dve", lambda e, pr=pr, src=src, dstv=dstv: e.tensor_copy(out=dstv, in_=src),
                                  reads=[pU.key], writes=[D1[pr].key])
            run_staged(S, p1, seg)
            for pr in range(2):
                SB = Sblk[pr]
                S.add("dve", lambda e, pr=pr: e.tensor_copy(out=D0[pr][:, :, 1:1 + nch],
                                                            in_=DEC[:, pr, 0:nch].unsqueeze(1).to_broadcast([128, 64, nch])),
                      reads=[DEC.key], writes=[D0[pr].key])
                S.add("dve", lambda e, pr=pr: e.tensor_tensor_scan(
                    out=SO[pr][:].rearrange("p v c -> p (v c)"), data0=D0[pr][:].rearrange("p v c -> p (v c)"),
                    data1=D1[pr][:].rearrange("p v c -> p (v c)"), initial=0.0, op0=ALU.mult, op1=ALU.add),
                    reads=[D0[pr].key, D1[pr].key], writes=[SO[pr].key])
                S.add("act", lambda e, pr=pr, SB=SB: e.copy(out=SB[0:64, 0:nch, 0:64], in_=SO[pr][0:64, :, 0:nch].rearrange("p v c -> p c v")),
                      reads=[SO[pr].key], writes=[SB.key])
                S.add("act", lambda e, pr=pr, SB=SB: e.copy(out=SB[64:128, 0:nch, 64:128], in_=SO[pr][64:128, :, 0:nch].rearrange("p v c -> p c v")),
                      reads=[SO[pr].key], writes=[SB.key])
                S.add("dve", lambda e, pr=pr: e.tensor_copy(out=D1[pr][:, :, 0:1], in_=SO[pr][:, :, nch:nch + 1]),
                      reads=[SO[pr].key], writes=[D1[pr].key])
            def p2(ti, tile):
                n0 = tile * 128
                of_ld, osum, osq, orstd, sgl, hgo = of_ld_R[ti % 4], osum_R[ti % 4], osq_R[ti % 4], orstd_R[ti % 4], sgl_R[ti % 4], hgo_R[ti % 4]
                AT, QH, VZ = ATs[ti], QHs[ti], VZs[ti]
                for pr in range(2):
                    for h2 in range(2):
                        h = pr * 2 + h2
                        S.add("pe", lambda e, pr=pr, h2=h2, h=h, AT=AT, VZ=VZ: e.matmul(
                            pO[:, pr * 128:(pr + 1) * 128], lhsT=VZ[:, pr, h2, :], rhs=AT[:, h, :],
                            start=(pr == 0 and h2 == 0), stop=False, skip_group_check=True),
                            reads=[VZ.key, AT.key], writes=[pO.key])
                for pr in range(2):
                    for j in range(4):
                        jj = j if d == 0 else 3 - j
                        c = ti * 4 + jj
                        S.add("pe", lambda e, pr=pr, j=j, c=c, QH=QH: e.matmul(
                            pO[:, pr * 128 + j * 32:pr * 128 + (j + 1) * 32], lhsT=Sblk[pr][:, c, :], rhs=QH[:, pr, j * 32:(j + 1) * 32],
                            start=False, stop=(pr == 1 and j == 3), skip_group_check=True),
                            reads=[Sblk[pr].key, QH.key], writes=[pO.key])
                if d == 0:
                    S.add("act", lambda e: e.copy(out=osum[:].rearrange("p a b -> p (a b)"), in_=pO[:, 0:256]), reads=[pO.key], writes=[osum.key])
                    S.add("sp", lambda e, n0=n0: e.dma_start(out=OHGr[:, :, n0:n0 + 128], in_=osum[:]), reads=[osum.key], dma=True,
                          semkey="st_" + osum.key)
                else:
                    S.add("sp", lambda e, n0=n0: e.dma_start(out=of_ld[:], in_=OHGr[:, :, n0:n0 + 128]), writes=[of_ld.key], dma=True)
                    S.add("sp", lambda e, n0=n0: e.dma_start(out=sgl[:], in_=PFr[:, PF_G:PF_G + 2, n0:n0 + 128]), writes=[sgl.key], dma=True)
                    S.add("dve", lambda e: e.tensor_tensor(out=osum[:].rearrange("p a b -> p (a b)"), in0=of_ld[:].rearrange("p a b -> p (a b)"),
                                                           in1=pO[:, 0:256], op=ALU.add), reads=[of_ld.key, pO.key], writes=[osum.key])
                    if "ohg" in self.dbg:
                        S.add("sp", lambda e, n0=n0: e.dma_start(out=self.dbg_out["ohg%d" % l].rearrange("(f p) n -> p f n", p=128)[:, :, n0:n0 + 128],
                                                                 in_=osum[:]), reads=[osum.key], dma=True, semkey="dbg_ohg")
                    S.next_stage()
                    S.add("act", lambda e: e.activation(out=osq[:], in_=osum[:], func=AF.Square), reads=[osum.key], writes=[osq.key])
                    S.add("pe", lambda e: e.matmul(pN[:, 0:256], lhsT=blkbf_l[:], rhs=osq[:].rearrange("p a b -> p (a b)"),
                                                   start=True, stop=True), reads=[osq.key, blkbf_l.key], writes=[pN.key])
                    S.add("dve", lambda e: e.tensor_scalar(out=orstd[:], in0=pN[:, 0:256], scalar1=1.0 / 64, scalar2=EPS,
                                                           op0=ALU.mult, op1=ALU.add), reads=[pN.key], writes=[orstd.key])
                    S.add("act", lambda e: e.activation(out=orstd[:], in_=orstd[:], func=AF.Ln), reads=[orstd.key], writes=[orstd.key])
                    S.add("act", lambda e: e.activation(out=orstd[:], in_=orstd[:], func=AF.Exp, scale=-0.5), reads=[orstd.key], writes=[orstd.key])
                    S.add("dve", lambda e: e.tensor_tensor(out=osum[:].rearrange("p a b -> p (a b)"), in0=osum[:].rearrange("p a b -> p (a b)"),
                                                           in1=orstd[:], op=ALU.mult), reads=[osum.key, orstd.key], writes=[osum.key])
                    wo = l * FV_L + FV_HGN
                    for pr in range(2):
                        S.add("dve", lambda e, pr=pr: e.scalar_tensor_tensor(out=hgo[:, pr, :], in0=osum[:, pr, :], scalar=fv[:, wo + pr:wo + pr + 1],
                                                                             in1=sgl[:, pr, :], op0=ALU.mult, op1=ALU.mult),
                              reads=[osum.key, fv.key, sgl.key], writes=[hgo.key])
                    S.add("sp", lambda e, n0=n0: e.dma_start(out=MIXr[:, 0:2, n0:n0 + 128], in_=hgo[:]), reads=[hgo.key], dma=True,
                          semkey="st_" + hgo.key)
            run_staged(S, p2, seg)
        for seg in _hg_tiles(self, d):
            do_seg(seg)
        self.barrier()
    for d in range(2):
        do_dir(d)
    self.reset(m)


def _ssd_io(self):
    nc = self.nc
    self.XSs = nc.dram_tensor("XSs", [self.NTOK, 512], F32).ap()
    self.BCf = nc.dram_tensor("BCfs", [512, self.NTOK], BF16).ap()
    self.Bts = nc.dram_tensor("Bts", [self.NTOK, 256], BF16).ap()
    self.YS = nc.dram_tensor("YSs", [self.NTOK, 512], F32).ap()


def _phase_ssd(self, l):
    S = self.S
    m = self.mark()
    cm, fv, rv, ps = self.cm, self.fv, self.rv, self.ps
    NT = self.NT
    PFr = self.PF.rearrange("(f p) n -> p f n", p=128)
    BCr = self.BCf.rearrange("(f p) n -> p f n", p=128)
    MIXr = self.MIXT.rearrange("(f p) n -> p f n", p=128)
    IDENT = cm[:, CM_ID:CM_ID + 128]
    ONESF = cm[:, CM_ONES:CM_ONES + 128]
    fo = l * FV_L
    ro = l * RV_L
    xin_R = [self.sb("xin", [128, 8, 132], F32) for _ in range(2)]
    acc_R = [self.sb("acc", [128, 8, 128], F32) for _ in range(2)]
    ctmp_R = [self.sb("ctmp", [128, 8, 128], F32) for _ in range(2)]
    acc2_R = [self.sb("acc2", [128, 8, 128], F32) for _ in range(2)]
    dtmp_R = [self.sb("dtmp", [128, 8, 128], F32) for _ in range(2)]
    bcb_R = [self.sb("bcb", [128, 4, 128], BF16) for _ in range(2)]
    xs_st_R = [self.sb("xs_st", [128, 512], F32) for _ in range(2)]
    bt_st_R = [self.sb("bt_st", [128, 256], BF16) for _ in range(2)]
    pX, pBt = ps[1], ps[2]
    CW = fv[:, fo + FV_CW:fo + FV_CW + 40].rearrange("p (j k) -> p j k", j=5)
    CB = fv[:, fo + FV_CB:fo + FV_CB + 8]

    def conv_tile(tile):
        n0 = tile * 128
        xin, acc, ctmp, bcb, xs_st, bt_st = xin_R[tile % 2], acc_R[tile % 2], ctmp_R[tile % 2], bcb_R[tile % 2], xs_st_R[tile % 2], bt_st_R[tile % 2]
        acc2, dtmp = acc2_R[tile % 2], dtmp_R[tile % 2]
        s_lo, s_hi = (0, CTX) if tile < 2 else (CTX, self.NTOK)
        lo, hi = max(n0 - 2, s_lo), min(n0 + 130, s_hi)
        S.add("dve", lambda e: e.memset(xin[:, :, 0:2], 0.0), writes=[xin.key])
        S.add("dve", lambda e: e.memset(xin[:, :, 130:132], 0.0), writes=[xin.key])
        S.add("sp", lambda e: e.dma_start(out=xin[:, :, lo - (n0 - 2):hi - (n0 - 2)], in_=PFr[:, PF_XBC:PF_XBC + 8, lo:hi]),
              writes=[xin.key], dma=True)
        def cwb(j):
            return CW[:, j, :].unsqueeze(2).to_broadcast([128, 8, 128])
        S.add("pool", lambda e: e.tensor_tensor(out=acc2[:], in0=xin[:, :, 1:129], in1=cwb(1), op=ALU.mult),
              reads=[xin.key, fv.key], writes=[acc2.key])
        S.add("pool", lambda e: e.tensor_tensor(out=ctmp[:], in0=xin[:, :, 2:130], in1=cwb(2), op=ALU.mult),
              reads=[xin.key, fv.key], writes=[ctmp.key])
        S.add("pool", lambda e: e.tensor_tensor(out=acc2[:], in0=acc2[:], in1=ctmp[:], op=ALU.add),
              reads=[acc2.key, ctmp.key], writes=[acc2.key])
        S.add("dve", lambda e: e.tensor_tensor(out=acc[:], in0=xin[:, :, 0:128], in1=cwb(0), op=ALU.mult),
              reads=[xin.key, fv.key], writes=[acc.key])
        for j in (3, 4):
            S.add("dve", lambda e, j=j: e.tensor_tensor(out=dtmp[:], in0=xin[:, :, j:j + 128], in1=cwb(j), op=ALU.mult),
                  reads=[xin.key, fv.key], writes=[dtmp.key])
            S.add("dve", lambda e: e.tensor_tensor(out=acc[:], in0=acc[:], in1=dtmp[:], op=ALU.add),
                  reads=[acc.key, dtmp.key], writes=[acc.key])
        S.add("dve", lambda e: e.tensor_tensor(out=acc[:], in0=acc[:], in1=acc2[:], op=ALU.add),
              reads=[acc.key, acc2.key], writes=[acc.key])
        S.add("dve", lambda e: e.tensor_tensor(out=acc[:], in0=acc[:], in1=CB.unsqueeze(2).to_broadcast([128, 8, 128]), op=ALU.add),
              reads=[acc.key, fv.key], writes=[acc.key])
        S.add("act", lambda e: e.activation(out=acc[:], in_=acc[:], func=AF.Silu), reads=[acc.key], writes=[acc.key])
        S.add("dve", lambda e: e.tensor_copy(out=bcb[:], in_=acc[:, 4:8, :]), reads=[acc.key], writes=[bcb.key])
        S.add("sp", lambda e: e.dma_start(out=BCr[:, :, n0:n0 + 128], in_=bcb[:]), reads=[bcb.key], dma=True, semkey="st_" + bcb.key)
        for k in range(4):
            S.add("pe", lambda e, k=k: e.transpose(out=pX[:, k * 128:(k + 1) * 128], in_=acc[:, k, :], identity=IDENT),
                  reads=[acc.key, cm.key], writes=[pX.key])
        S.add("act", lambda e: e.copy(out=xs_st[:], in_=pX[:]), reads=[pX.key], writes=[xs_st.key])
        S.add("sp", lambda e: e.dma_start(out=self.XSs[n0:n0 + 128, :], in_=xs_st[:]), reads=[xs_st.key], dma=True, semkey="st_" + xs_st.key)
        for k in range(2):
            S.add("pe", lambda e, k=k: e.transpose(out=pBt[:, k * 128:(k + 1) * 128], in_=acc[:, 4 + k, :], identity=IDENT),
                  reads=[acc.key, cm.key], writes=[pBt.key])
        S.add("dve", lambda e: e.tensor_copy(out=bt_st[:], in_=pBt[:, 0:256]), reads=[pBt.key], writes=[bt_st.key])
        S.add("sp", lambda e: e.dma_start(out=self.Bts[n0:n0 + 128, :], in_=bt_st[:]), reads=[bt_st.key], dma=True, semkey="st_" + bt_st.key)

    for tile in range(NT):
        conv_tile(tile)
    self.barrier()
    self.reset(m)
    m = self.mark()
    Arow = self.sb("Arow", [128, 16], F32)
    S.add("act", lambda e: e.activation(out=Arow[:], in_=rv[:, ro + RV_ALOG:ro + RV_ALOG + 16], func=AF.Exp),
          reads=[rv.key], writes=[Arow.key])
    S.add("dve", lambda e: e.tensor_scalar_mul(out=Arow[:], in0=Arow[:], scalar1=-1.0), reads=[Arow.key], writes=[Arow.key])
    DTB = rv[:, ro + RV_DTB:ro + RV_DTB + 16]
    D0 = [self.sb("sD0", [128, 256, 9], F32) for _ in range(2)]
    D1 = [self.sb("sD1", [128, 256, 9], F32) for _ in range(2)]
    SO = [self.sb("sSO", [128, 256, 9], F32) for _ in range(2)]
    Hbf = [self.sb("Hbf", [128, 8, 256], BF16) for _ in range(2)]
    YD = [self.sb("YD", [128, 512], F32) for _ in range(8)]
    CFs = [self.sb("CFs", [128, 2, 128], BF16) for _ in range(8)]
    ECs = [self.sb("ECs", [128, 8], F32) for _ in range(8)]
    dtr_R4 = [self.sb("dtr", [128, 8], F32) for _ in range(4)]
    dt_R4 = [self.sb("dt", [128, 8], F32) for _ in range(4)]
    av_R4 = [self.sb("av", [128, 8], F32) for _ in range(4)]
    ABC_R4 = [self.sb("ABC", [128, 8, 128], F32) for _ in range(4)]
    ATRI_R4 = [self.sb("ATRI", [128, 8, 128], F32) for _ in range(4)]
    ncum_R4 = [self.sb("ncum", [128, 8], F32) for _ in range(4)]
    dend_R4 = [self.sb("dend", [128, 8], F32) for _ in range(4)]
    dect_R4 = [self.sb("dect", [128, 8], F32) for _ in range(4)]
    xst_R4 = [self.sb("xst", [128, 512], F32) for _ in range(4)]
    btl_R4 = [self.sb("btl", [128, 256], BF16) for _ in range(4)]
    bcl_R4 = [self.sb("bcl", [128, 4, 128], BF16) for _ in range(4)]
    Lsb_R4 = [self.sb("Lsb", [128, 8, 128], F32) for _ in range(4)]
    Wb_R4 = [self.sb("Wb", [128, 8, 128], BF16) for _ in range(4)]
    xdt_R4 = [self.sb("xdt", [128, 8, 64], BF16) for _ in range(4)]
    xw_R4 = [self.sb("xw", [128, 8, 64], BF16) for _ in range(4)]
    xst2_R = [self.sb("xst2", [128, 512], F32) for _ in range(2)]
    yo_R = [self.sb("yo", [128, 512], F32) for _ in range(2)]
    yf_R = [self.sb("yf", [128, 512], F32) for _ in range(2)]
    zt_R = [self.sb("zt", [128, 512], F32) for _ in range(2)]
    ssq_R = [self.sb("ssq", [128, 2], F32) for _ in range(2)]
    yT_R = [self.sb("yT", [128, 4, 128], BF16) for _ in range(2)]
    IDb = self.sb("sIDb", [128, 128], BF16)
    S.add("dve", lambda e: e.tensor_copy(out=IDb[:], in_=IDENT), reads=[cm.key], writes=[IDb.key])
    NEG4 = [self.sb("NEG4", [128, 4, 128], BF16) for _ in range(2)]
    for d_ in range(2):
        ng = cm[:, (CM_NEGF, CM_NEGB)[d_]:(CM_NEGF, CM_NEGB)[d_] + 128]
        S.add("dve", lambda e, d_=d_, ng=ng: e.tensor_copy(out=NEG4[d_][:], in_=ng.unsqueeze(1).to_broadcast([128, 4, 128])),
              reads=[cm.key], writes=[NEG4[d_].key])
    for g in range(2):
        S.add("dve", lambda e, g=g: e.memset(D0[g][:], 0.0), writes=[D0[g].key])
        S.add("dve", lambda e, g=g: e.memset(D1[g][:], 0.0), writes=[D1[g].key])
    pS, pLa, pLb, pG, pY, pH, pY2, pT = ps[0], ps[1], ps[2], ps[3], ps[4], ps[5], ps[6], ps[7]
    SSNrow = rv[:, ro + RV_SSN:ro + RV_SSN + 512]
    DSKrow = rv[:, ro + RV_DSK:ro + RV_DSK + 512]

    def do_dir(d):
        TRI = cm[:, (CM_TRIF, CM_TRIB)[d]:(CM_TRIF, CM_TRIB)[d] + 128]
        NEGM = cm[:, (CM_NEGF, CM_NEGB)[d]:(CM_NEGF, CM_NEGB)[d] + 128]
        for g in range(2):
            S.add("dve", lambda e, g=g: e.memset(D1[g][:, :, 0:1], 0.0), writes=[D1[g].key])

        def do_seg(seg):
            nt = len(seg)

            def p1(ti, tile):
                n0 = tile * 128
                ATRI = ATRI_R4[ti % 4]
                dtr, dt, av, ABC, ncum, dend, dect, xst, btl, bcl, Lsb, Wb, xdt, xw = dtr_R4[ti % 4], dt_R4[ti % 4], av_R4[ti % 4], ABC_R4[ti % 4], ncum_R4[ti % 4], dend_R4[ti % 4], dect_R4[ti % 4], xst_R4[ti % 4], btl_R4[ti % 4], bcl_R4[ti % 4], Lsb_R4[ti % 4], Wb_R4[ti % 4], xdt_R4[ti % 4], xw_R4[ti % 4]
                S.add("sp", lambda e: e.dma_start(out=dtr[:], in_=self.PT[n0:n0 + 128, PT_DT + d * 8:PT_DT + d * 8 + 8]),
                      writes=[dtr.key], dma=True)
                S.add("sp", lambda e: e.dma_start(out=xst[:], in_=self.XSs[n0:n0 + 128, :]), writes=[xst.key], dma=True)
                S.add("sp", lambda e: e.dma_start(out=btl[:], in_=self.Bts[n0:n0 + 128, :]), writes=[btl.key], dma=True)
                S.add("sp", lambda e: e.dma_start(out=bcl[:], in_=BCr[:, :, n0:n0 + 128]), writes=[bcl.key], dma=True)
                S.add("dve", lambda e: e.tensor_tensor(out=dt[:], in0=dtr[:], in1=DTB[:, d * 8:d * 8 + 8], op=ALU.add),
                      reads=[dtr.key, rv.key], writes=[dt.key])
                S.add("act", lambda e: e.activation(out=dt[:], in_=dt[:], func=AF.Exp), reads=[dt.key], writes=[dt.key])
                S.add("dve", lambda e: e.tensor_scalar_add(out=dt[:], in0=dt[:], scalar1=1.0), reads=[dt.key], writes=[dt.key])
                S.add("act", lambda e: e.activation(out=dt[:], in_=dt[:], func=AF.Ln), reads=[dt.key], writes=[dt.key])
                S.add("dve", lambda e: e.tensor_tensor(out=av[:], in0=dt[:], in1=Arow[:, d * 8:d * 8 + 8], op=ALU.mult),
                      reads=[dt.key, Arow.key], writes=[av.key])
                S.add("dve", lambda e: e.tensor_scalar_mul(out=ABC[:], in0=av[:].unsqueeze(2).to_broadcast([128, 8, 128]), scalar1=-1.0),
                      reads=[av.key], writes=[ABC.key])
                S.add("dve", lambda e: e.tensor_tensor(out=ATRI[:], in0=TRI.unsqueeze(1).to_broadcast([128, 8, 128]),
                                                       in1=av[:].unsqueeze(2).to_broadcast([128, 8, 128]), op=ALU.mult),
                      reads=[av.key, cm.key], writes=[ATRI.key])
                S.next_stage()
                S.add("pe", lambda e: e.matmul(pS[:, 0:8], lhsT=TRI, rhs=av[:], start=True, stop=True), reads=[av.key, cm.key], writes=[pS.key])
                S.add("pe", lambda e: e.matmul(pS[:, 8:16], lhsT=ONESF, rhs=av[:], start=True, stop=True), reads=[av.key, cm.key], writes=[pS.key])
                S.add("dve", lambda e: e.tensor_scalar_mul(out=ncum[:], in0=pS[:, 0:8], scalar1=-1.0), reads=[pS.key], writes=[ncum.key])
                EC = ECs[ti]
                S.add("act", lambda e: e.activation(out=EC[:], in_=pS[:, 0:8], func=AF.Exp), reads=[pS.key], writes=[EC.key])
                S.add("dve", lambda e: e.tensor_tensor(out=dend[:], in0=pS[:, 8:16], in1=ncum[:], op=ALU.add),
                      reads=[pS.key, ncum.key], writes=[dend.key])
                S.add("act", lambda e: e.activation(out=dend[:], in_=dend[:], func=AF.Exp), reads=[dend.key], writes=[dend.key])
                S.add("act", lambda e: e.activation(out=dect[:], in_=pS[:, 8:16], func=AF.Exp), reads=[pS.key], writes=[dect.key])
                S.next_stage()
                for hb in range(2):
                    pl = (pLa, pLb)[hb]
                    S.add("pe", lambda e, hb=hb, pl=pl: e.matmul(pl[:, 0:512], lhsT=ONESF, rhs=ATRI[:, 4 * hb:4 * hb + 4, :].rearrange("p a b -> p (a b)"),
                                                                 start=True, stop=False, skip_group_check=True),
                          reads=[ATRI.key, cm.key], writes=[pl.key])
                    S.add("pe", lambda e, hb=hb, pl=pl: e.matmul(pl[:, 0:512], lhsT=TRI, rhs=ABC[:, 4 * hb:4 * hb + 4, :].rearrange("p a b -> p (a b)"),
                                                                 start=False, stop=False, skip_group_check=True),
                          reads=[ABC.key, cm.key], writes=[pl.key])
                    S.add("pe", lambda e, hb=hb, pl=pl: e.matmul(pl[:, 0:512], lhsT=IDb[:], rhs=NEG4[d][:].rearrange("p a b -> p (a b)"),
                                                                 start=False, stop=True, skip_group_check=True),
                          reads=[IDb.key, NEG4[d].key], writes=[pl.key])
                    S.add("act", lambda e, hb=hb, pl=pl: e.activation(out=Lsb[:, 4 * hb:4 * hb + 4, :].rearrange("p a b -> p (a b)"), in_=pl[:, 0:512], func=AF.Exp),
                          reads=[pl.key], writes=[Lsb.key])
                S.next_stage()
                for g in range(2):
                    S.add("pe", lambda e, g=g: e.matmul(pG[:, g * 128:(g + 1) * 128], lhsT=bcl[:, g, :], rhs=bcl[:, 2 + g, :],
                                                        start=True, stop=True), reads=[bcl.key], writes=[pG.key])
                for g in range(2):
                    S.add("dve", lambda e, g=g: e.tensor_tensor(
                        out=Wb[:, g * 4:(g + 1) * 4, :], in0=Lsb[:, g * 4:(g + 1) * 4, :],
                        in1=pG[:, g * 128:(g + 1) * 128].unsqueeze(1).to_broadcast([128, 4, 128]), op=ALU.mult),
                        reads=[Lsb.key, pG.key], writes=[Wb.key])
                S.add("dve", lambda e: e.tensor_tensor(out=xdt[:], in0=xst[:].rearrange("p (h q) -> p h q", h=8),
                                                       in1=dt[:].unsqueeze(2).to_broadcast([128, 8, 64]), op=ALU.mult),
                      reads=[xst.key, dt.key], writes=[xdt.key])
                S.next_stage()
                for h in range(8):
                    S.add("pe", lambda e, h=h: e.matmul(pY[:, h * 64:(h + 1) * 64], lhsT=Wb[:, h, :], rhs=xdt[:, h, :], start=True, stop=True),
                          reads=[Wb.key, xdt.key], writes=[pY.key])
                Y = YD[ti]
                S.add("act", lambda e: e.copy(out=Y[:], in_=pY[:]), reads=[pY.key], writes=[Y.key])
                CF = CFs[ti]
                S.add("dve", lambda e: e.tensor_copy(out=CF[:], in_=bcl[:, 2:4, :]), reads=[bcl.key], writes=[CF.key])
                S.add("dve", lambda e: e.tensor_tensor(out=xw[:], in0=xdt[:], in1=dend[:].unsqueeze(2).to_broadcast([128, 8, 64]), op=ALU.mult),
                      reads=[xdt.key, dend.key], writes=[xw.key])
                S.next_stage()
                for g in range(2):
                    S.add("pe", lambda e, g=g: e.matmul(pH[:, g * 256:(g + 1) * 256], lhsT=btl[:, g * 128:(g + 1) * 128],
                                                        rhs=xw[:, g * 4:(g + 1) * 4, :].rearrange("p h q -> p (h q)"), start=True, stop=True),
                          reads=[btl.key, xw.key], writes=[pH.key])
                for g in range(2):
                    S.add("act", lambda e, g=g: e.copy(out=D1[g][:, :, 1 + ti:2 + ti], in_=pH[:, g * 256:(g + 1) * 256].unsqueeze(2)),
                          reads=[pH.key], writes=[D1[g].key])
                    S.add("dve", lambda e, g=g: e.tensor_copy(
                        out=D0[g][:, :, 1 + ti:2 + ti].rearrange("p (h q) o -> p h (q o)", h=4),
                        in_=dect[:, g * 4:(g + 1) * 4].unsqueeze(2).to_broadcast([128, 4, 64])),
                        reads=[dect.key], writes=[D0[g].key])

            for g0 in range(0, nt, 4):
                recs = []
                for ti in range(g0, min(g0 + 4, nt)):
                    S.begin_record()
                    p1(ti, seg[ti])
                    recs.append(S.end_record())
                S.replay_staged(recs)
            for g in range(2):
                S.add("dve", lambda e, g=g: e.tensor_tensor_scan(
                    out=SO[g][:].rearrange("p v c -> p (v c)"), data0=D0[g][:].rearrange("p v c -> p (v c)"),
                    data1=D1[g][:].rearrange("p v c -> p (v c)"), initial=0.0, op0=ALU.mult, op1=ALU.add),
                    reads=[D0[g].key, D1[g].key], writes=[SO[g].key])
                S.add("act", lambda e, g=g: e.copy(out=Hbf[g][:, 0:nt, :], in_=SO[g][:, :, 0:nt].rearrange("p v c -> p c v")),
                      reads=[SO[g].key], writes=[Hbf[g].key])
                S.add("dve", lambda e, g=g: e.tensor_copy(out=D1[g][:, :, 0:1], in_=SO[g][:, :, nt:nt + 1]),
                      reads=[SO[g].key], writes=[D1[g].key])

            def p2(ti, tile):
                n0 = tile * 128
                yo, yf, zt, ssq, yT, xst2 = yo_R[ti % 2], yf_R[ti % 2], zt_R[ti % 2], ssq_R[ti % 2], yT_R[ti % 2], xst2_R[ti % 2]
                xst = xst2
                CF, EC, Y = CFs[ti], ECs[ti], YD[ti]
                for g in range(2):
                    S.add("pe", lambda e, g=g: e.matmul(pY2[:, g * 256:(g + 1) * 256], lhsT=CF[:, g, :], rhs=Hbf[g][:, ti, :], start=True, stop=True),
                          reads=[CF.key, Hbf[g].key], writes=[pY2.key])
                S.add("dve", lambda e: e.tensor_tensor(out=yo[:].rearrange("p (h q) -> p h q", h=8), in0=pY2[:].rearrange("p (h q) -> p h q", h=8),
                                                       in1=EC[:].unsqueeze(2).to_broadcast([128, 8, 64]), op=ALU.mult),
                      reads=[pY2.key, EC.key], writes=[yo.key])
                S.add("dve", lambda e: e.tensor_tensor(out=yo[:], in0=yo[:], in1=Y[:], op=ALU.add), reads=[yo.key, Y.key], writes=[yo.key])
                if d == 0:
                    S.add("sp", lambda e: e.dma_start(out=self.YS[n0:n0 + 128, :], in_=yo[:]), reads=[yo.key], dma=True, semkey="st_" + yo.key)
                    return
                S.add("sp", lambda e: e.dma_start(out=yf[:], in_=self.YS[n0:n0 + 128, :]), writes=[yf.key], dma=True)
                S.add("sp", lambda e: e.dma_start(out=xst[:], in_=self.XSs[n0:n0 + 128, :]), writes=[xst.key], dma=True)
                S.add("sp", lambda e: e.dma_start(out=zt[:], in_=self.PT[n0:n0 + 128, PT_Z:PT_Z + 512]), writes=[zt.key], dma=True)
                S.next_stage()
                S.add("dve", lambda e: e.tensor_tensor(out=yo[:], in0=yo[:], in1=yf[:], op=ALU.add), reads=[yo.key, yf.key], writes=[yo.key])
                S.add("dve", lambda e: e.tensor_tensor(out=xst[:], in0=xst[:], in1=DSKrow, op=ALU.mult), reads=[xst.key, rv.key], writes=[xst.key])
                S.add("dve", lambda e: e.tensor_tensor(out=yo[:], in0=yo[:], in1=xst[:], op=ALU.add), reads=[yo.key, xst.key], writes=[yo.key])
                if "yssm" in self.dbg:
                    S.add("sp", lambda e: e.dma_start(out=self.dbg_out["yssm%d" % l][n0:n0 + 128, :], in_=yo[:]), reads=[yo.key], dma=True,
                          semkey="dbg_yssm")
                S.add("act", lambda e: e.activation(out=zt[:], in_=zt[:], func=AF.Silu), reads=[zt.key], writes=[zt.key])
                S.add("dve", lambda e: e.tensor_tensor(out=yo[:], in0=yo[:], in1=zt[:], op=ALU.mult), reads=[yo.key, zt.key], writes=[yo.key])
                S.add("act", lambda e: e.activation(out=zt[:], in_=yo[:], func=AF.Square, accum_out=ssq[:, 0:1]),
                      reads=[yo.key], writes=[zt.key, ssq.key])
                S.add("dve", lambda e: e.tensor_scalar(out=ssq[:, 1:2], in0=ssq[:, 0:1], scalar1=1.0 / 512, scalar2=EPS, op0=ALU.mult, op1=ALU.add),
                      reads=[ssq.key], writes=[ssq.key])
                S.add("act", lambda e: e.activation(out=ssq[:, 1:2], in_=ssq[:, 1:2], func=AF.Ln), reads=[ssq.key], writes=[ssq.key])
                S.add("act", lambda e: e.activation(out=ssq[:, 1:2], in_=ssq[:, 1:2], func=AF.Exp, scale=-0.5), reads=[ssq.key], writes=[ssq.key])
                S.add("dve", lambda e: e.scalar_tensor_tensor(out=yo[:], in0=yo[:], scalar=ssq[:, 1:2], in1=SSNrow, op0=ALU.mult, op1=ALU.mult),
                      reads=[yo.key, ssq.key, rv.key], writes=[yo.key])
                S.next_stage()
                for k in range(4):
                    S.add("pe", lambda e, k=k: e.transpose(out=pT[:, k * 128:(k + 1) * 128], in_=yo[:, k * 128:(k + 1) * 128], identity=IDENT),
                          reads=[yo.key, cm.key], writes=[pT.key])
                S.add("act", lambda e: e.copy(out=yT[:].rearrange("p a b -> p (a b)"), in_=pT[:]), reads=[pT.key], writes=[yT.key])
                S.add("sp", lambda e: e.dma_start(out=MIXr[:, 4:8, n0:n0 + 128], in_=yT[:]), reads=[yT.key], dma=True, semkey="st_" + yT.key)

            run_staged(S, p2, seg, group=2)

        for seg in _hg_tiles(self, d):
            do_seg(seg)
        self.barrier()

    for d in range(2):
        do_dir(d)
    self.reset(m)


NAB_N = 5 * 4 * 5 * 128


def make_nabias(rpb, nlat):
    rows = 2 * nlat
    out = np.full((128, 5, 4, 5, 128), NEG, np.float32)
    its = [0, 1, 2, nlat - 2, nlat - 1]
    p = np.arange(128)
    q = np.arange(128)
    for v, it in enumerate(its):
        kt0 = min(max(it - 2, 0), nlat - 5)
        r = 2 * it + q // 64
        cq = q % 64
        r0 = np.clip(r - 4, 0, rows - 8)
        c0 = np.clip(cq - 8, 0, 48)
        for kt in range(5):
            rk = 2 * (kt0 + kt) + p // 64
            ck = p % 64
            inw = ((rk[:, None] >= r0[None, :]) & (rk[:, None] < r0[None, :] + 8)
                   & (ck[:, None] >= c0[None, :]) & (ck[:, None] < c0[None, :] + 16))
            dr = np.clip(rk[:, None] - r[None, :] + 7, 0, 14)
            dc = np.clip(ck[:, None] - cq[None, :], -15, 15) + 15
            for h in range(4):
                b = rpb[h][dr, dc]
                out[:, v, h, kt, :] = np.where(inw, b, NEG)
    return out.reshape(128, NAB_N)


def _na_io(self):
    nc = self.nc
    self.nabias = nc.dram_tensor("nabias", [DEPTH, 128, NAB_N], F32, kind="ExternalInput").ap()


def _phase_na(self, l):
    S = self.S
    m = self.mark()
    cm, fv, rv, ps = self.cm, self.fv, self.rv, self.ps
    NT, nlat, NTOK = self.NT, self.nlat, self.NTOK
    last = (l == DEPTH - 1)
    PFr = self.PF.rearrange("(f p) n -> p f n", p=128)
    MIXr = self.MIXT.rearrange("(f p) n -> p f n", p=128)
    IDENT = cm[:, CM_ID:CM_ID + 128]
    ro = l * RV_L
    NANrow = rv[:, ro + RV_NAN:ro + RV_NAN + 256]
    KT = self.sb("KT", [128, 2, NTOK], BF16)
    Vaug = self.sb("Vaug", [128, NT, 4, 65], BF16)
    BT = self.sb("BT", [128, 5, 4, 5, 128], BF16)
    IDb = self.sb("IDb", [128, 128], BF16)
    ID4 = self.sb("ID4", [128, 4, 128], F32)
    qf_R = [self.sb("qf", [128, 2, 128], F32) for _ in range(2)]
    QZ_R = [self.sb("QZ", [128, 2, 2, 128], BF16) for _ in range(2)]
    mx_R = [self.sb("mx", [128, 2], F32) for _ in range(4)]
    DG_R = [self.sb("DG", [128, 512], BF16) for _ in range(4)]
    PTs_R = [self.sb("PTs", [128, 896], BF16) for _ in range(4)]
    rc_R = [self.sb("rc", [128, 4], F32) for _ in range(2)]
    onat_R = [self.sb("onat", [128, 4, 64], F32) for _ in range(2)]
    junk_R = [self.sb("junk", [128, 256], F32) for _ in range(2)]
    ssq_R = [self.sb("nssq", [128, 2], F32) for _ in range(2)]
    oT_R = [self.sb("oT", [128, 2, 128], BF16) for _ in range(2)]
    hrm = self.sb("hrm", [128, 2], F32)
    SB, ST = self.psbig[0], self.psbig[1]
    pOV, pT = ps[4], ps[5]
    for pr in range(2):
        S.add("pool", lambda e, pr=pr: e.dma_start(out=KT[:, pr, :], in_=PFr[:, PF_KA + pr, :]), writes=[KT.key], dma=True)
    S.add("pool", lambda e: e.dma_start(out=BT[:].rearrange("p a b c d -> p (a b c d)"), in_=self.nabias[l]), writes=[BT.key], dma=True)
    S.add("dve", lambda e: e.tensor_copy(out=IDb[:], in_=IDENT), reads=[cm.key], writes=[IDb.key])
    S.add("dve", lambda e: e.tensor_copy(out=ID4[:], in_=IDENT.unsqueeze(1).to_broadcast([128, 4, 128])), reads=[cm.key], writes=[ID4.key])
    S.add("dve", lambda e: e.memset(Vaug[:, :, :, 64:65], 1.0), writes=[Vaug.key])
    for t in range(NT):
        S.add("pool", lambda e, t=t: e.dma_start(out=Vaug[:, t, :, 0:64],
                                                  in_=self.PT[t * 128:(t + 1) * 128, PT_VA:PT_VA + 256].rearrange("p (h q) -> p h q", h=4)),
              writes=[Vaug.key], dma=True)
    for h2 in range(2):
        S.add("dve", lambda e, h2=h2: e.tensor_copy(out=hrm[:, h2:h2 + 1], in_=cm[:, CM_BLK + h2 * 64:CM_BLK + h2 * 64 + 1]),
              reads=[cm.key], writes=[hrm.key])

    def q_tile(tile, keytiles, var):
        n0 = tile * 128
        nk = len(keytiles)
        qf, QZ, rc, onat, junk, ssq, oT = qf_R[tile % 2], QZ_R[tile % 2], rc_R[tile % 2], onat_R[tile % 2], junk_R[tile % 2], ssq_R[tile % 2], oT_R[tile % 2]
        nloc = 5 if var is not None else 0
        S.add("sp", lambda e: e.dma_start(out=qf[:], in_=PFr[:, PF_QA:PF_QA + 2, n0:n0 + 128]), writes=[qf.key], dma=True)
        S.add("dve", lambda e: e.tensor_tensor(out=QZ[:], in0=qf[:].unsqueeze(2).to_broadcast([128, 2, 2, 128]),
                                               in1=hrm[:].unsqueeze(1).unsqueeze(3).to_broadcast([128, 2, 2, 128]), op=ALU.mult),
              reads=[qf.key, hrm.key], writes=[QZ.key])
        def head_fn(hi, h):
            pr, h2 = h // 2, h % 2
            mx, DG, PTs = mx_R[h % 4], DG_R[h % 4], PTs_R[h % 4]
            SB, ST = (self.psbig[0], self.psbig[1]) if h % 2 == 0 else (self.psbig[3], self.psbig[1])
            col = 0
            runs = []
            i = 0
            while i < nk:
                j = i
                while j + 1 < nk and keytiles[j + 1] == keytiles[j] + 1 and (j + 1 - i) < 4 and ((col + (j + 1 - i) * 128) % 512 != 0):
                    j += 1
                runs.append((keytiles[i], j - i + 1, col))
                col += (j - i + 1) * 128
                i = j + 1
            for (kt_, cnt, c_) in runs:
                S.add("pe", lambda e, pr=pr, h2=h2, kt_=kt_, cnt=cnt, c_=c_: e.matmul(
                    SB[:, c_:c_ + cnt * 128], lhsT=QZ[:, pr, h2, :], rhs=KT[:, pr, kt_ * 128:(kt_ + cnt) * 128], start=True, stop=True),
                    reads=[QZ.key, KT.key], writes=[SB.key])
            S.add("dve", lambda e: e.reduce_max(out=mx[:, 0:1], in_=SB[:, 0:nk * 128], axis=AX.X), reads=[SB.key], writes=[mx.key])
            S.add("dve", lambda e: e.tensor_scalar_mul(out=mx[:, 1:2], in0=mx[:, 0:1], scalar1=-1.0), reads=[mx.key], writes=[mx.key])
            S.add("dve", lambda e: e.tensor_scalar_mul(out=DG[:], in0=ID4[:].rearrange("p a b -> p (a b)"), scalar1=mx[:, 1:2]),
                  reads=[mx.key, ID4.key], writes=[DG.key])
            S.next_stage()
            for b0 in (0, 4):
                kks = [kk for kk in range(nk) if b0 <= kk < b0 + 4]
                if not kks:
                    continue
                ncols = len(kks) * 128
                nbias = len([kk for kk in kks if kk < nloc])
                S.add("pe", lambda e, b0=b0, ncols=ncols: e.matmul(ST[:, b0 * 128:b0 * 128 + ncols], lhsT=self.ones_bf[:], rhs=DG[:, 0:ncols],
                                                                   start=True, stop=False, skip_group_check=True),
                      reads=[DG.key, self.ones_bf.key], writes=[ST.key])
                for kk in kks:
                    kt_ = keytiles[kk]
                    lastmm = (kk == kks[-1]) and nbias == 0
                    S.add("pe", lambda e, pr=pr, h2=h2, kt_=kt_, kk=kk, lastmm=lastmm: e.matmul(
                        ST[:, kk * 128:(kk + 1) * 128], lhsT=KT[:, pr, kt_ * 128:(kt_ + 1) * 128], rhs=QZ[:, pr, h2, :],
                        start=False, stop=lastmm, skip_group_check=True),
                        reads=[QZ.key, KT.key], writes=[ST.key])
                if nbias:
                    k0 = kks[0]
                    S.add("pe", lambda e, h=h, k0=k0, nbias=nbias: e.matmul(
                        ST[:, k0 * 128:(k0 + nbias) * 128], lhsT=IDb[:],
                        rhs=BT[:, var, h, k0:k0 + nbias, :].rearrange("p a b -> p (a b)"),
                        start=False, stop=True, skip_group_check=True),
                        reads=[BT.key, IDb.key], writes=[ST.key])
            S.add("act", lambda e: e.activation(out=PTs[:, 0:nk * 128], in_=ST[:, 0:nk * 128], func=AF.Exp), reads=[ST.key], writes=[PTs.key])
            S.next_stage()
            for kk, kt_ in enumerate(keytiles):
                S.add("pe", lambda e, kk=kk, kt_=kt_, h=h: e.matmul(pOV[:, h * 65:(h + 1) * 65], lhsT=PTs[:, kk * 128:(kk + 1) * 128],
                                                                    rhs=Vaug[:, kt_, h, :], start=(kk == 0), stop=(kk == nk - 1)),
                      reads=[PTs.key, Vaug.key], writes=[pOV.key])
        run_staged(S, head_fn, [0, 1, 2, 3], group=4)
        OVv = pOV[:, 0:260].rearrange("p (h c) -> p h c", h=4)
        S.add("dve", lambda e: e.reciprocal(out=rc[:], in_=OVv[:, :, 64]), reads=[pOV.key], writes=[rc.key])
        S.add("dve", lambda e: e.tensor_tensor(out=onat[:], in0=OVv[:, :, 0:64], in1=rc[:].unsqueeze(2).to_broadcast([128, 4, 64]), op=ALU.mult),
              reads=[pOV.key, rc.key], writes=[onat.key])
        o2 = onat[:].rearrange("p h c -> p (h c)")
        if "naraw" in self.dbg:
            S.add("sp", lambda e: e.dma_start(out=self.dbg_out["naraw%d" % l][n0:n0 + 128, :], in_=o2), reads=[onat.key], dma=True,
                  semkey="dbg_naraw")
        S.add("act", lambda e: e.activation(out=junk[:], in_=o2, func=AF.Square, accum_out=ssq[:, 0:1]), reads=[onat.key],
              writes=[junk.key, ssq.key])
        S.add("dve", lambda e: e.tensor_scalar(out=ssq[:, 1:2], in0=ssq[:, 0:1], scalar1=1.0 / 256, scalar2=EPS, op0=ALU.mult, op1=ALU.add),
              reads=[ssq.key], writes=[ssq.key])
        S.add("act", lambda e: e.activation(out=ssq[:, 1:2], in_=ssq[:, 1:2], func=AF.Ln), reads=[ssq.key], writes=[ssq.key])
        S.add("act", lambda e: e.activation(out=ssq[:, 1:2], in_=ssq[:, 1:2], func=AF.Exp, scale=-0.5), reads=[ssq.key], writes=[ssq.key])
        S.add("dve", lambda e: e.scalar_tensor_tensor(out=junk[:], in0=o2, scalar=ssq[:, 1:2], in1=NANrow, op0=ALU.mult, op1=ALU.mult),
              reads=[onat.key, ssq.key, rv.key, junk.key], writes=[junk.key])
        for k in range(2):
            S.add("pe", lambda e, k=k: e.transpose(out=pT[:, k * 128:(k + 1) * 128], in_=junk[:, k * 128:(k + 1) * 128], identity=IDENT),
                  reads=[junk.key, cm.key], writes=[pT.key])
        S.add("act", lambda e: e.copy(out=oT[:].rearrange("p a b -> p (a b)"), in_=pT[:, 0:256]), reads=[pT.key], writes=[oT.key])
        S.add("sp", lambda e: e.dma_start(out=MIXr[:, 2:4, n0:n0 + 128], in_=oT[:]), reads=[oT.key], dma=True, semkey="st_" + oT.key)

    if not last:
        for tile in range(2):
            q_tile(tile, [0, 1], None)
    for it in range(nlat):
        var = 0 if it == 0 else 1 if it == 1 else 3 if it == nlat - 2 else 4 if it == nlat - 1 else 2
        kt0 = min(max(it - 2, 0), nlat - 5)
        q_tile(it + 2, [kt0 + 2 + k for k in range(5)] + [0, 1], var)
    self.barrier()
    self.reset(m)


NLAT_FULL = 64
_CACHE = {}


def kernel(**inputs):
    inp = {k: np.asarray(v) for k, v in inputs.items()}
    nlat = NLAT_FULL
    B = inp["x"].shape[0]
    kb = KB(nlat=nlat)
    nc = kb.build()
    maps = make_in_maps(inp, nlat, list(range(B)))
    res = run_bass_kernel_spmd(nc, maps, core_ids=list(range(B)))
    out = np.stack([np.asarray(res.results[b]["outT"]).T for b in range(B)])
    return np.ascontiguousarray(out.astype(np.float32))
```

```python
import contextlib
import numpy as np
import concourse.bass as bass
import concourse.mybir as mybir
from concourse.bass_utils import run_bass_kernel_spmd

F32 = mybir.dt.float32
BF16 = mybir.dt.bfloat16
AF = mybir.ActivationFunctionType
ALU = mybir.AluOpType
AX = mybir.AxisListType

D = 1024
DEPTH = 2
DFF = 2816
CTX = 256
GW = 64
EPS = 1e-6
ENGS = ("pe", "act", "dve", "pool", "sp")
PH = "__ph"


class Op:
    __slots__ = ("eng", "fn", "deps", "dma", "semkey", "sig", "val", "idx", "slot")

    def __init__(self, eng, fn, dma, semkey):
        self.eng = eng
        self.fn = fn
        self.deps = {}
        self.dma = dma
        self.semkey = semkey
        self.sig = False
        self.val = 0
        self.idx = 0
        self.slot = None


class Sched:
    def __init__(self, nc):
        self.nc = nc
        self.ops = []
        self.last_w = {}
        self.readers = {}
        self.slotmap = {}

    def begin_record(self):
        self.rec = []
        self.rec_stage = 0

    def next_stage(self):
        if getattr(self, "rec", None) is not None:
            self.rec_stage += 1

    def end_record(self):
        r = self.rec
        self.rec = None
        return r

    def replay_staged(self, recs):
        nst = 1 + max((st for r in recs for (st, a, k) in r), default=0)
        for st in range(nst):
            for r in recs:
                for (s_, a, k) in r:
                    if s_ == st:
                        self.add(*a, **k)

    def add(self, eng, fn, reads=(), writes=(), dma=False, semkey=None, barrier=False):
        if getattr(self, "rec", None) is not None:
            self.rec.append((self.rec_stage, (eng, fn), dict(reads=reads, writes=writes, dma=dma, semkey=semkey, barrier=barrier)))
            return None
        reads = list(reads)
        writes = list(writes)
        if barrier:
            writes.append(PH)
        else:
            reads.append(PH)
        op = Op(eng, fn, dma, semkey if semkey is not None else (writes[0] if (dma and writes) else None))
        op.idx = len(self.ops)
        if barrier:
            self.slotmap = {}
        if dma:
            if eng == "pool":
                op.slot = ("p", op.semkey)
            else:
                if op.semkey not in self.slotmap:
                    self.slotmap[op.semkey] = len(self.slotmap)
                op.slot = self.slotmap[op.semkey]
        cand = {}
        for k in reads + writes:
            w = self.last_w.get(k)
            if w is not None:
                cand[w.idx] = w
        for k in writes:
            for r in self.readers.get(k, ()):
                cand[r.idx] = r
        for d in cand.values():
            if (not d.dma) and (not op.dma) and d.eng == "pe" and op.eng == "pe":
                continue
            if d.dma and op.dma and d.semkey == op.semkey:
                pure_waw = all(self.last_w.get(k) is not d for k in reads) and \
                    all(d not in self.readers.get(k, ()) for k in writes)
                if pure_waw:
                    continue
            key = ("dma", d.slot) if d.dma else ("eng", d.eng)
            old = op.deps.get(key)
            if old is None or old.idx < d.idx:
                op.deps[key] = d
            d.sig = True
        for k in writes:
            self.last_w[k] = op
            self.readers[k] = []
        for k in reads:
            self.readers.setdefault(k, []).append(op)
        self.ops.append(op)
        return op

    def emit(self):
        nc = self.nc
        import os as _os
        _mx = int(_os.environ.get("MAXOPS", "0"))
        if _mx:
            self.ops = self.ops[:_mx]
        cnt = {}
        dma_keys = []
        for op in self.ops:
            if op.dma:
                k = ("dma", op.slot)
                if k not in cnt:
                    cnt[k] = 0
                    dma_keys.append(k)
                cnt[k] += 16
                op.val = cnt[k]
            elif op.sig:
                k = ("eng", op.eng)
                cnt[k] = cnt.get(k, 0) + 1
                op.val = cnt[k]
        self.maxvals = dict(cnt)
        sems = {}
        with contextlib.ExitStack() as es:
            for e in ENGS:
                sems[("eng", e)] = es.enter_context(nc.semaphore("s_" + e))
            for i, k in enumerate(dma_keys):
                sems[k] = es.enter_context(nc.semaphore("d%d" % i))
            self.nsems = len(sems)
            block = es.enter_context(nc.Block())
            ops = self.ops

            def run(engname, eng):
                waited = {}
                for op in ops:
                    if op.eng != engname:
                        continue
                    for k, d in op.deps.items():
                        if waited.get(k, 0) >= d.val:
                            continue
                        eng.wait_ge(sems[k], d.val)
                        waited[k] = d.val
                    ins = op.fn(eng)
                    if op.dma:
                        ins.then_inc(sems[("dma", op.slot)], 16)
                    elif op.sig:
                        ins.then_inc(sems[("eng", op.eng)], 1)
                last = {}
                for op in ops:
                    if op.eng == engname and op.dma:
                        last[("dma", op.slot)] = max(op.val, last.get(("dma", op.slot), 0))
                for k, v in last.items():
                    if waited.get(k, 0) < v:
                        eng.wait_ge(sems[k], v)

            @block.tensor
            def _(e):
                run("pe", e)

            @block.scalar
            def _(e):
                run("act", e)

            @block.vector
            def _(e):
                run("dve", e)

            @block.gpsimd
            def _(e):
                run("pool", e)

            @block.sync
            def _(e):
                run("sp", e)


def run_staged(S, fn, seg, group=4):
    for g0 in range(0, len(seg), group):
        recs = []
        for ti in range(g0, min(g0 + group, len(seg))):
            S.begin_record()
            fn(ti, seg[ti])
            recs.append(S.end_record())
        S.replay_staged(recs)


class T:
    def __init__(self, h, key):
        self.h = h
        self.key = key

    def __getitem__(self, idx):
        return self.h[idx]


def _dsize(dt):
    return 2 if dt == BF16 else 4


PF_COLS = ([0, 128] + [256, 384] + [512, 640] + [1024, 1152] + [1280, 1408] + [1536, 1664]
           + [2048, 2176, 2304, 2432] + [2560 + 128 * i for i in range(8)])
PF_Q, PF_FF, PF_FB, PF_G, PF_QA, PF_KA, PF_Z, PF_XBC = 0, 2, 4, 6, 8, 10, 12, 16
PF_NB = 24
PF_TR = ["silu"] * 2 + ["copy"] * 4 + ["silu"] * 2 + ["s8"] * 2 + ["copy"] * 2 + ["silu"] * 4 + ["copy"] * 8
PT_GROUPS = [(256, 768, 0), (768, 1024, 512), (1792, 2048, 768), (2048, 2560, 1024), (3584, 3600, 1536)]
PT_FF, PT_FB, PT_I, PT_VA, PT_Z, PT_DT = 0, 256, 512, 768, 1024, 1536
PT_W = 1552

FV_L = 72 + 8 + 8 + 8 + 2 + 4 + 40 + 8
FV_BMOD, FV_NF1, FV_NMX, FV_NF2, FV_HGN, FV_SSN, FV_CW, FV_CB = 0, 72, 80, 88, 96, 98, 102, 142
FV_G = DEPTH * FV_L
FV_C, FV_FN, FV_LB = FV_G, FV_G + 16, FV_G + 24
FV_N = FV_G + 24 + 8
RV_L = 256 + 512 + 16 + 16 + 512
RV_NAN, RV_DSK, RV_ALOG, RV_DTB, RV_SSN = 0, 256, 768, 784, 800
RV_G = DEPTH * RV_L
RV_LBR = RV_G
RV_N = RV_G + 1024


class KB:
    def __init__(self, nlat=64, dbg=(), layers=DEPTH, stop_after=None, parts=("hg", "ssd", "na")):
        self.parts = set(parts)
        self.nlat = nlat
        self.NT = nlat + 2
        self.NTOK = 128 * self.NT
        self.NLTOK = 128 * nlat
        self.dbg = set(dbg)
        self.layers = layers
        self.stop_after = stop_after
        self.nc = nc = bass.Bass("TRN2", target_bir_lowering=False)
        self.S = Sched(nc)
        self.uid = 0
        self.sb_lo = 16512
        self.sb_hi = 229344
        self.off = self.sb_lo
        self.NB = 512
        assert nlat % 4 == 0
        self.nblk = 1 + self.NLTOK // self.NB
        self.declare_io()

    def sb(self, name, shape, dt):
        size = int(np.prod(shape[1:])) * _dsize(dt)
        size = (size + 31) // 32 * 32
        assert self.off + size <= self.sb_hi, (name, self.off, size)
        self.uid += 1
        h = self.nc.alloc_sbuf_tensor_at("%s_%d" % (name, self.uid), list(shape), dt, offset=self.off)
        self.off += size
        return T(h, "%s_%d" % (name, self.uid))

    def mark(self):
        return self.off

    def reset(self, m):
        self.off = m

    def barrier(self):
        scr = self.scr
        self.S.add("dve", lambda e: e.memset(scr[:, 0:1], 0.0), writes=[scr.key], barrier=True)

    def declare_io(self):
        nc = self.nc
        ein = lambda n, s, dt=F32: nc.dram_tensor(n, list(s), dt, kind="ExternalInput").ap()
        self.xT = ein("xT", [D, self.NTOK])
        self.fvec = ein("fvec", [128, FV_N])
        self.rvec = ein("rvec", [128, RV_N])
        self.w_mod = ein("w_mod", [DEPTH, D, 9 * D])
        self.w13 = [ein("ffn1_w13", [DEPTH, D, 2 * DFF]), ein("ffn2_w13", [DEPTH, D, 2 * DFF])]
        self.w2 = [ein("ffn1_w2", [DEPTH, DFF, D]), ein("ffn2_w2", [DEPTH, DFF, D])]
        self.w_in = ein("w_in", [DEPTH, D, 3600])
        self.w_out = ein("w_out", [DEPTH, D, D])
        self.outT = nc.dram_tensor("outT", [D, self.NLTOK], F32, kind="ExternalOutput").ap()
        self.XT = nc.dram_tensor("XTs", [D, self.NTOK], F32).ap()
        self.PF = nc.dram_tensor("PFs", [PF_NB * 128, self.NTOK], F32).ap()
        self.PT = nc.dram_tensor("PTs", [self.NTOK, PT_W], F32).ap()
        self.MIXT = nc.dram_tensor("MIXTs", [D, self.NTOK], BF16).ap()
        self.dbg_out = {}
        _mixer_io(self)

    def dbg_tensor(self, name, shape, dt=F32):
        ap = self.nc.dram_tensor("dbg_" + name, list(shape), dt, kind="ExternalOutput").ap()
        self.dbg_out[name] = ap
        return ap

    def setup_persist(self):
        S = self.S
        self.scr = self.sb("scr", [128, 8], F32)
        self.fv = self.sb("fv", [128, FV_N], F32)
        self.ones_bf = self.sb("ones", [128, 128], BF16)
        self.MOD = self.sb("MOD", [128, 2, 72], F32)
        self.scb = self.sb("scb", [128, 16], F32)
        fv, ones = self.fv, self.ones_bf
        S.add("sp", lambda e: e.dma_start(out=fv[:], in_=self.fvec), writes=[fv.key], dma=True)
        S.add("dve", lambda e: e.memset(ones[:], 1.0), writes=[ones.key])
        scb = self.scb
        S.add("act", lambda e: e.activation(out=scb[:], in_=fv[:, FV_C:FV_C + 16], func=AF.Silu),
              reads=[fv.key], writes=[scb.key])
        self.persist_end = self.mark()

    def phase_mod(self, l):
        S, nc = self.S, self.nc
        m = self.mark()
        HALF = 4608
        wm = [self.sb("wm", [128, HALF], F32) for _ in range(2)]
        psm = self.ps[0]
        scb, fv, MOD = self.scb, self.fv, self.MOD
        i = 0
        for k in range(8):
            for hf in range(2):
                w = wm[i % 2]
                i += 1
                src = self.w_mod[l, k * 128:(k + 1) * 128, hf * HALF:(hf + 1) * HALF]
                S.add("sp", lambda e, w=w, src=src: e.dma_start(out=w[:], in_=src), writes=[w.key], dma=True)
                for j in range(36):
                    fb = hf * 36 + j
                    S.add("pe", lambda e, w=w, j=j, fb=fb, k=k: e.matmul(
                        psm[:, fb * 2:fb * 2 + 2], lhsT=w[:, j * 128:(j + 1) * 128], rhs=scb[:, 2 * k:2 * k + 2],
                        start=(k == 0 and fb == 0), stop=(k == 7), skip_group_check=True),
                        reads=[w.key, scb.key], writes=[psm.key])
        bo = l * FV_L
        for s in range(2):
            S.add("dve", lambda e, s=s: e.tensor_tensor(out=MOD[:, s, :], in0=psm[:, s:144:2],
                                                       in1=fv[:, bo + FV_BMOD:bo + FV_BMOD + 72], op=ALU.add),
                  reads=[psm.key, fv.key], writes=[MOD.key])
        for s in range(2):
            for (js, nw) in ((1, FV_NF1), (4, FV_NMX), (7, FV_NF2)):
                S.add("dve", lambda e, s=s, js=js, nw=nw: e.scalar_tensor_tensor(
                    out=MOD[:, s, js * 8:js * 8 + 8], in0=MOD[:, s, js * 8:js * 8 + 8], scalar=1.0,
                    in1=fv[:, bo + nw:bo + nw + 8], op0=ALU.add, op1=ALU.mult),
                    reads=[MOD.key, fv.key], writes=[MOD.key])
                S.add("dve", lambda e, s=s, js=js: e.tensor_scalar_mul(
                    out=MOD[:, s, js * 8:js * 8 + 8], in0=MOD[:, s, js * 8:js * 8 + 8], scalar1=32.0),
                    reads=[MOD.key], writes=[MOD.key])
            for jg in (2, 8):
                S.add("dve", lambda e, s=s, jg=jg: e.tensor_scalar_mul(
                    out=MOD[:, s, jg * 8:jg * 8 + 8], in0=MOD[:, s, jg * 8:jg * 8 + 8], scalar1=0.5),
                    reads=[MOD.key], writes=[MOD.key])
        if "mod" in self.dbg:
            d = self.dbg_tensor("mod%d" % l, [128, 144])
            S.add("sp", lambda e: e.dma_start(out=d, in_=MOD[:].rearrange("p s c -> p (s c)")), reads=[MOD.key],
                  dma=True, semkey="dbg_mod%d" % l)
        self.barrier()
        self.reset(m)

    def load_w(self, name, src, K, N):
        kc = K // 128
        w = self.sb(name, [128, kc, N], BF16)
        for k in range(kc):
            s = src[k * 128:(k + 1) * 128, :]
            self.S.add("pool", lambda e, k=k, s=s: e.dma_start(out=w[:, k, :], in_=s), writes=[w.key], dma=True)
        return w

    def alloc_work(self, nxb=2):
        NB = self.NB
        self.xb = [self.sb("xb", [128, 8, NB], F32) for _ in range(nxb)] * (2 // nxb)
        self.tk = [self.sb("tk", [128, NB], F32) for _ in range(2)]
        self.hb = self.sb("hb", [128, 8, NB], BF16)
        self.sq = self.hb
        self.rstd = self.sb("rstd", [128, NB], F32)

    def blk(self, i):
        if i == 0:
            return 0, CTX, 1
        return CTX + (i - 1) * self.NB, self.NB, 0

    def load_x(self, xb, src, n0, N):
        v = src.rearrange("(k p) n -> p k n", p=128)[:, :, n0:n0 + N]
        self.S.add("sp", lambda e: e.dma_start(out=xb[:, :, :N], in_=v), writes=[xb.key], dma=True)

    def store_x(self, xb, dst, n0, N, key="dram_x"):
        v = dst.rearrange("(k p) n -> p k n", p=128)[:, :, n0:n0 + N]
        self.S.add("sp", lambda e: e.dma_start(out=v, in_=xb[:, :, :N]), reads=[xb.key], dma=True,
                   semkey="st_" + xb.key)

    def modulate(self, xb, N, A, SH, out, out_keyed):
        S = self.S
        sq, rstd, ones = self.sq, self.rstd, self.ones_bf
        pss = self.ps[0]
        S.add("act", lambda e: e.activation(out=sq[:, :, :N], in_=xb[:, :, :N], func=AF.Square),
              reads=[xb.key], writes=[sq.key])
        for k in range(8):
            S.add("pe", lambda e, k=k: e.matmul(pss[:, :N], lhsT=ones[:], rhs=sq[:, k, :N], start=(k == 0), stop=(k == 7)),
                  reads=[sq.key, ones.key], writes=[pss.key])
        S.add("dve", lambda e: e.tensor_scalar_add(out=rstd[:, :N], in0=pss[:, :N], scalar1=float(D * EPS)),
              reads=[pss.key], writes=[rstd.key])
        S.add("act", lambda e: e.activation(out=rstd[:, :N], in_=rstd[:, :N], func=AF.Ln),
              reads=[rstd.key], writes=[rstd.key])
        S.add("act", lambda e: e.activation(out=rstd[:, :N], in_=rstd[:, :N], func=AF.Exp, scale=-0.5),
              reads=[rstd.key], writes=[rstd.key])
        for k in range(8):
            if SH is None:
                S.add("dve", lambda e, k=k: e.scalar_tensor_tensor(out=out[:, k, :N], in0=xb[:, k, :N], scalar=A[:, k:k + 1],
                                                                   in1=rstd[:, :N], op0=ALU.mult, op1=ALU.mult),
                      reads=[xb.key, rstd.key, self.fv.key], writes=[out_keyed])
            else:
                tk = self.tk[k % 2]
                S.add("dve", lambda e, k=k, tk=tk: e.scalar_tensor_tensor(out=tk[:, :N], in0=xb[:, k, :N], scalar=A[:, k:k + 1],
                                                                          in1=rstd[:, :N], op0=ALU.mult, op1=ALU.mult),
                      reads=[xb.key, rstd.key, self.MOD.key], writes=[tk.key])
                S.add("act", lambda e, k=k, tk=tk: e.activation(out=out[:, k, :N], in_=tk[:, :N], func=AF.Identity,
                                                                bias=SH[:, k:k + 1]),
                      reads=[tk.key, self.MOD.key], writes=[out_keyed])

    def ffn(self, xb, N, s, jbase, w13b, w2b):
        S = self.S
        MOD = self.MOD
        A = MOD[:, s, (jbase + 1) * 8:(jbase + 2) * 8]
        SH = MOD[:, s, jbase * 8:(jbase + 1) * 8]
        G = MOD[:, s, (jbase + 2) * 8:(jbase + 3) * 8]
        hb, ab, sg = self.hb, self.ab, self.sg
        self.modulate(xb, N, A, SH, hb, hb.key)
        for j in range(22):
            pu, pg = self.ps[1 + j % 2], self.ps[3 + j % 2]
            for k in range(8):
                S.add("pe", lambda e, j=j, k=k, pu=pu: e.matmul(pu[:, :N], lhsT=w13b[:, k, j * 128:(j + 1) * 128],
                                                                rhs=hb[:, k, :N], start=(k == 0), stop=(k == 7)),
                      reads=[w13b.key, hb.key], writes=[pu.key])
            for k in range(8):
                S.add("pe", lambda e, j=j, k=k, pg=pg: e.matmul(pg[:, :N], lhsT=w13b[:, k, DFF + j * 128:DFF + (j + 1) * 128],
                                                                rhs=hb[:, k, :N], start=(k == 0), stop=(k == 7)),
                      reads=[w13b.key, hb.key], writes=[pg.key])
            sgj = sg[j % 2]
            S.add("act", lambda e, pg=pg, sgj=sgj: e.activation(out=sgj[:, :N], in_=pg[:, :N], func=AF.Silu),
                  reads=[pg.key], writes=[sgj.key])
            S.add("dve", lambda e, j=j, pu=pu, sgj=sgj: e.tensor_tensor(out=ab[:, j, :N], in0=sgj[:, :N], in1=pu[:, :N],
                                                                          op=ALU.mult),
                  reads=[pu.key, sgj.key], writes=[ab.key])
        for fb in range(8):
            po = self.ps[5 + fb % 2]
            for j in range(22):
                S.add("pe", lambda e, j=j, fb=fb, po=po: e.matmul(po[:, :N], lhsT=w2b[:, j, fb * 128:(fb + 1) * 128],
                                                                  rhs=ab[:, j, :N], start=(j == 0), stop=(j == 21)),
                      reads=[w2b.key, ab.key], writes=[po.key])
            S.add("dve", lambda e, fb=fb, po=po: e.scalar_tensor_tensor(out=xb[:, fb, :N], in0=po[:, :N],
                                                                        scalar=G[:, fb:fb + 1], in1=xb[:, fb, :N],
                                                                        op0=ALU.mult, op1=ALU.add),
                  reads=[po.key, xb.key, MOD.key], writes=[xb.key])

    def phase_ffn1(self, l):
        m = self.mark()
        w13b = self.load_w("w13b", self.w13[0][l], D, 2 * DFF)
        w2b = self.load_w("w2b", self.w2[0][l], DFF, D)
        self.alloc_work()
        self.ab = self.sb("ab", [128, 22, self.NB], BF16)
        self.sg = [self.sb("sg", [128, self.NB], F32) for _ in range(2)]
        src = self.xT if l == 0 else self.XT
        def ld(i):
            n0, N, s = self.blk(i)
            self.load_x(self.xb[i % 2], src, n0, N)
        ld(0)
        for i in range(self.nblk):
            if i + 1 < self.nblk:
                ld(i + 1)
            n0, N, s = self.blk(i)
            xb = self.xb[i % 2]
            self.ffn(xb, N, s, 0, w13b, w2b)
            self.store_x(xb, self.XT, n0, N, key="dram_x1")
        self.barrier()
        self.reset(m)
        if "x1" in self.dbg:
            self.dump_dram("x1_%d" % l, self.XT, [D, self.NTOK], "dram_x1")

    def dump_dram(self, name, src, shape, key, dt=F32):
        d = self.dbg_tensor(name, shape, dt)
        self.S.add("sp", lambda e: e.dma_start(out=d, in_=src), dma=True, semkey="dbg_" + name)
        self.barrier()

    def phase_inproj(self, l):
        S = self.S
        m = self.mark()
        winb = self.load_w("winb", self.w_in[l], D, 3600)
        self.alloc_work()
        NB = self.NB
        pfst = self.sb("pfst", [128, PF_NB, NB], F32)
        ptsts = [self.sb("ptst", [128, PT_W], F32) for _ in range(NB // 128)]
        MOD, hb = self.MOD, self.hb
        def do_blk(i):
            n0, N, s = self.blk(i)
            xb = self.xb[i % 2]
            self.modulate(xb, N, MOD[:, s, 32:40], MOD[:, s, 24:32], hb, hb.key)
            for bi, c0 in enumerate(PF_COLS):
                p = self.ps[1 + bi % 4]
                for k in range(8):
                    S.add("pe", lambda e, k=k, c0=c0, p=p: e.matmul(p[:, :N], lhsT=winb[:, k, c0:c0 + 128], rhs=hb[:, k, :N],
                                                                    start=(k == 0), stop=(k == 7)),
                          reads=[winb.key, hb.key], writes=[p.key])
                tr = PF_TR[bi]
                if tr == "silu":
                    S.add("act", lambda e, bi=bi, p=p: e.activation(out=pfst[:, bi, :N], in_=p[:, :N], func=AF.Silu),
                          reads=[p.key], writes=[pfst.key])
                elif tr == "s8":
                    S.add("dve", lambda e, bi=bi, p=p: e.tensor_scalar_mul(out=pfst[:, bi, :N], in0=p[:, :N], scalar1=0.125),
                          reads=[p.key], writes=[pfst.key])
                else:
                    S.add("dve", lambda e, bi=bi, p=p: e.tensor_copy(out=pfst[:, bi, :N], in_=p[:, :N]),
                          reads=[p.key], writes=[pfst.key])
            dst = self.PF.rearrange("(f p) n -> p f n", p=128)[:, :, n0:n0 + N]
            S.add("sp", lambda e, dst=dst: e.dma_start(out=dst, in_=pfst[:, :, :N]), reads=[pfst.key],
                  dma=True, semkey="dram_pf")
            for t in range(N // 128):
                ptst = ptsts[t]
                for gi, (c0, c1, d0) in enumerate(PT_GROUPS):
                    p = self.ps[5 + gi % 2]
                    wdt = c1 - c0
                    for k in range(8):
                        S.add("pe", lambda e, k=k, t=t, c0=c0, c1=c1, p=p, wdt=wdt: e.matmul(
                            p[:, :wdt], lhsT=hb[:, k, t * 128:(t + 1) * 128], rhs=winb[:, k, c0:c1],
                            start=(k == 0), stop=(k == 7)), reads=[winb.key, hb.key], writes=[p.key])
                    S.add("act", lambda e, ptst=ptst, d0=d0, wdt=wdt, p=p: e.copy(out=ptst[:, d0:d0 + wdt], in_=p[:, :wdt]),
                          reads=[p.key], writes=[ptst.key])
                dstt = self.PT[n0 + t * 128:n0 + (t + 1) * 128, :]
                S.add("sp", lambda e, ptst=ptst, dstt=dstt: e.dma_start(out=dstt, in_=ptst[:]), reads=[ptst.key],
                      dma=True, semkey="st_" + ptst.key)
        def ld(i):
            n0, N, s = self.blk(i)
            self.load_x(self.xb[i % 2], self.XT, n0, N)
        ld(0)
        for i in range(self.nblk):
            if i + 1 < self.nblk:
                ld(i + 1)
            do_blk(i)
        self.barrier()
        self.reset(m)
        if "proj" in self.dbg:
            self.dump_dram("pf_%d" % l, self.PF, [PF_NB * 128, self.NTOK], "dram_pf")
            self.dump_dram("pt_%d" % l, self.PT, [self.NTOK, PT_W], "dram_pt")

    def phase_out(self, l):
        S = self.S
        last = (l == DEPTH - 1)
        m = self.mark()
        w13b = self.load_w("w13b", self.w13[1][l], D, 2 * DFF)
        w2b = self.load_w("w2b", self.w2[1][l], DFF, D)
        woutb = self.load_w("woutb", self.w_out[l], D, D)
        self.alloc_work(1)
        NB = self.NB
        self.ab = self.sb("ab", [128, 22, NB], BF16)
        self.sg = [self.sb("sg", [128, NB], F32) for _ in range(2)]
        mixb = [T(self.ab.h, self.ab.key)] * 2
        MOD = self.MOD
        def do_blk(i):
            n0, N, s = self.blk(i)
            if last and s == 1:
                return
            xb = self.xb[i % 2]
            mb = mixb[i % 2]
            self.load_x(xb, self.XT, n0, N)
            v = self.MIXT.rearrange("(k p) n -> p k n", p=128)[:, :, n0:n0 + N]
            S.add("sp", lambda e, mb=mb, v=v: e.dma_start(out=mb[:, 0:8, :N], in_=v), writes=[mb.key], dma=True)
            G = MOD[:, s, 40:48]
            for fb in range(8):
                p = self.ps[5 + fb % 2]
                for k in range(8):
                    S.add("pe", lambda e, k=k, fb=fb, p=p, mb=mb: e.matmul(p[:, :N], lhsT=woutb[:, k, fb * 128:(fb + 1) * 128],
                                                                          rhs=mb[:, k, :N], start=(k == 0), stop=(k == 7)),
                          reads=[woutb.key, mb.key], writes=[p.key])
                S.add("dve", lambda e, fb=fb, p=p, xb=xb, G=G: e.scalar_tensor_tensor(out=xb[:, fb, :N], in0=p[:, :N],
                                                                                  scalar=G[:, fb:fb + 1], in1=xb[:, fb, :N],
                                                                                  op0=ALU.mult, op1=ALU.add),
                      reads=[p.key, xb.key, MOD.key], writes=[xb.key])
            self.ffn(xb, N, s, 6, w13b, w2b)
            if not last:
                self.store_x(xb, self.XT, n0, N, key="dram_x3")
            else:
                ob = xb
                fn = self.fn32
                self.modulate(xb, N, fn, None, ob, ob.key)
                v = self.outT.rearrange("(k p) n -> p k n", p=128)[:, :, n0 - CTX:n0 - CTX + N]
                S.add("sp", lambda e, v=v, ob=ob: e.dma_start(out=v, in_=ob[:, :, :N]), reads=[ob.key],
                      dma=True, semkey="st_" + ob.key)
        for i in range(self.nblk):
            do_blk(i)
        self.barrier()
        self.reset(m)
        if "x3" in self.dbg and not last:
            self.dump_dram("x3_%d" % l, self.XT, [D, self.NTOK], "dram_x3")

    def build(self):
        S = self.S
        big = [self.nc.alloc_psum_tensor("psb%d" % i, [128, 1024], F32) for i in range(4)]
        self.ps = [T(big[i // 2][:, (i % 2) * 512:(i % 2) * 512 + 512], "ps%d" % i) for i in range(8)]
        self.psbig = [T(big[i], "psB%d" % i) for i in range(4)]
        self.setup_persist()
        self.fn32 = None
        for l in range(self.layers):
            if not getattr(self, "only_mixer", False):
                self.phase_mod(l)
                if self.stop_after == ("mod", l):
                    break
                self.phase_ffn1(l)
                if self.stop_after == ("ffn1", l):
                    break
                self.phase_inproj(l)
                if self.stop_after == ("inproj", l):
                    break
            self.phase_mixer(l)
            if self.stop_after == ("mixer", l):
                break
            self.phase_out_wrap(l)
        S.emit()
        return self.nc

    def phase_out_wrap(self, l):
        last = (l == DEPTH - 1)
        if last:
            m = self.mark()
            fn32 = self.sb("fn32", [128, 8], F32)
            fv = self.fv
            self.S.add("dve", lambda e: e.tensor_scalar_mul(out=fn32[:], in0=fv[:, FV_FN:FV_FN + 8], scalar1=32.0),
                       reads=[fv.key], writes=[fn32.key])
            self.fn32 = fn32
            self.persist_tmp = self.mark()
            self.phase_out(l)
            self.reset(m)
        else:
            self.phase_out(l)

    def phase_mixer(self, l):
        m = self.mark()
        _setup_mixer_consts(self)
        if "ohg" in self.dbg:
            self.dbg_tensor("ohg%d" % l, [256, self.NTOK])
        if "yssm" in self.dbg:
            self.dbg_tensor("yssm%d" % l, [self.NTOK, 512])
        if "naraw" in self.dbg:
            self.dbg_tensor("naraw%d" % l, [self.NTOK, 256])
        if "hg" in self.parts:
            _phase_hgrn2(self, l)
        if "ssd" in self.parts:
            _phase_ssd(self, l)
        if "na" in self.parts:
            _phase_na(self, l)
        if "mix" in self.dbg:
            self.dump_dram("mix_%d" % l, self.MIXT, [D, self.NTOK], "x", BF16)
        self.reset(m)


def fm(v):
    v = np.asarray(v, np.float32)
    return v.reshape(-1, 128).T


def prep_shared(inp):
    fv = np.zeros((128, FV_N), np.float32)
    rv = np.zeros((128, RV_N), np.float32)
    for l in range(DEPTH):
        o = l * FV_L
        fv[:, o + FV_BMOD:o + FV_BMOD + 72] = fm(inp["b_mod"][l])
        fv[:, o + FV_NF1:o + FV_NF1 + 8] = fm(inp["norm_ffn1"][l])
        fv[:, o + FV_NMX:o + FV_NMX + 8] = fm(inp["norm_mix"][l])
        fv[:, o + FV_NF2:o + FV_NF2 + 8] = fm(inp["norm_ffn2"][l])
        fv[:, o + FV_HGN:o + FV_HGN + 2] = fm(inp["hg_norm"][l])
        fv[:, o + FV_SSN:o + FV_SSN + 4] = fm(inp["ssm_norm"][l])
        for j in range(5):
            fv[:, o + FV_CW + j * 8:o + FV_CW + j * 8 + 8] = fm(inp["ssm_conv_w"][l, j])
        fv[:, o + FV_CB:o + FV_CB + 8] = fm(inp["ssm_conv_b"][l])
        r = l * RV_L
        rv[:, r + RV_NAN:r + RV_NAN + 256] = inp["na_norm"][l][None, :]
        rv[:, r + RV_DSK:r + RV_DSK + 512] = np.repeat(inp["ssm_d"][l], 64)[None, :]
        rv[:, r + RV_ALOG:r + RV_ALOG + 16] = inp["ssm_a_log"][l].reshape(-1)[None, :]
        rv[:, r + RV_DTB:r + RV_DTB + 16] = inp["ssm_dt_bias"][l].reshape(-1)[None, :]
        rv[:, r + RV_SSN:r + RV_SSN + 512] = inp["ssm_norm"][l][None, :]
    fv[:, FV_FN:FV_FN + 8] = fm(inp["final_norm"])
    for dr in range(2):
        for l in range(DEPTH):
            rv[:, RV_LBR + (dr * 2 + l) * 256:RV_LBR + (dr * 2 + l) * 256 + 256] = inp["hg_lower_bounds"][dr, l][None, :]
    for dr in range(2):
        for l in range(DEPTH):
            fv[:, FV_LB + dr * 4 + l * 2:FV_LB + dr * 4 + l * 2 + 2] = fm(inp["hg_lower_bounds"][dr, l])
    return fv, rv


def prep_core(inp, b, nlat, fv_shared):
    fv = fv_shared.copy()
    cc = np.stack([fm(inp["c"][b]), fm(inp["c_ctx"])], axis=2)
    fv[:, FV_C:FV_C + 16] = cc.reshape(128, 16)
    xT = np.ascontiguousarray(np.concatenate([inp["ctx"][b], inp["x"][b][:128 * nlat]], axis=0).T)
    return fv, xT


def make_in_maps(inp, nlat, batches):
    fvs, rv = prep_shared(inp)
    shared = {k: np.ascontiguousarray(inp[k], np.float32) for k in
              ("w_mod", "ffn1_w13", "ffn2_w13", "ffn1_w2", "ffn2_w2", "w_in", "w_out")}
    cmat = make_cmat()
    nab = np.stack([make_nabias(np.asarray(inp["na_rpb"][l], np.float32), nlat) for l in range(DEPTH)])
    maps = []
    for b in batches:
        fv, xT = prep_core(inp, b, nlat, fvs)
        m = dict(shared)
        m.update({"xT": xT, "fvec": fv, "rvec": rv, "cmat": cmat, "nabias": nab})
        maps.append(m)
    return maps


CM_ID, CM_M1F, CM_M2F, CM_M3F, CM_M1B, CM_M2B, CM_M3B = 0, 128, 256, 384, 512, 640, 768
CM_M4F, CM_M4B, CM_HM, CM_BLK, CM_TRIF, CM_TRIB, CM_NEGF, CM_NEGB, CM_ONES = 896, 900, 904, 1160, 1288, 1416, 1544, 1672, 1800
CM_N = 1928
NEG = -30000.0


def make_cmat():
    c = np.zeros((128, CM_N), np.float32)
    u = np.arange(128)[:, None]
    t = np.arange(128)[None, :]
    same = (u // 32) == (t // 32)
    c[:, CM_ID:CM_ID + 128] = (u == t)
    mf = (t // 32) * 32 + 15
    c[:, CM_M1F:CM_M1F + 128] = same * (((u > mf) & (u <= t)) * 1.0 - ((u > t) & (u <= mf)) * 1.0)
    c[:, CM_M2F:CM_M2F + 128] = same & (u <= t)
    c[:, CM_M3F:CM_M3F + 128] = same & (u > t)
    mb = (t // 32) * 32 + 16
    c[:, CM_M1B:CM_M1B + 128] = same * (((u >= t) & (u < mb)) * 1.0 - ((u >= mb) & (u < t)) * 1.0)
    c[:, CM_M2B:CM_M2B + 128] = same & (u >= t)
    c[:, CM_M3B:CM_M3B + 128] = same & (u < t)
    j = np.arange(4)[None, :]
    c[:, CM_M4F:CM_M4F + 4] = (u // 32) == j
    c[:, CM_M4B:CM_M4B + 4] = (u // 32) == (3 - j)
    col = np.arange(128)[None, :]
    c[:, CM_HM:CM_HM + 128] = (col // 64 == 0)
    c[:, CM_HM + 128:CM_HM + 256] = (col // 64 == 1)
    c[:, CM_BLK:CM_BLK + 128] = (u // 64) == (t // 64)
    c[:, CM_TRIF:CM_TRIF + 128] = (u <= t)
    c[:, CM_TRIB:CM_TRIB + 128] = (u >= t)
    c[:, CM_NEGF:CM_NEGF + 128] = NEG * (u > t)
    c[:, CM_NEGB:CM_NEGB + 128] = NEG * (u < t)
    c[:, CM_ONES:CM_ONES + 128] = 1.0
    return c


def _mixer_io(self):
    nc = self.nc
    self.cmat = nc.dram_tensor("cmat", [128, CM_N], F32, kind="ExternalInput").ap()
    self.OHG = nc.dram_tensor("OHGs", [256, self.NTOK], F32).ap()
    _ssd_io(self)
    _na_io(self)


def _setup_mixer_consts(self):
    S = self.S
    self.cm = cm = self.sb("cm", [128, CM_N], F32)
    S.add("sp", lambda e: e.dma_start(out=cm[:], in_=self.cmat), writes=[cm.key], dma=True)
    self.rv = rv = self.sb("rv", [128, RV_N], F32)
    S.add("sp", lambda e: e.dma_start(out=rv[:], in_=self.rvec), writes=[rv.key], dma=True)
    self.blk_bf = blk = self.sb("blkbf", [128, 128], BF16)
    S.add("dve", lambda e: e.tensor_copy(out=blk[:], in_=cm[:, CM_BLK:CM_BLK + 128]), reads=[cm.key], writes=[blk.key])


def _hg_tiles(self, d):
    NT = self.NT
    chain = list(range(NT)) if d == 0 else [1, 0] + list(range(NT - 1, 1, -1))
    return [chain[0:2]] + [chain[i:i + 8] for i in range(2, NT, 8)]


def _phase_hgrn2(self, l):
    S = self.S
    blkbf_l = self.blk_bf
    m = self.mark()
    cm, fv, rv = self.cm, self.fv, self.rv
    ps = self.ps
    LBt = self.sb("LBt", [128, 2, 256], F32)
    OMLt = self.sb("OMLt", [128, 2, 256], F32)
    omlf = self.sb("omlf", [128, 2, 2], F32)
    if l == 0:
        S.add("dve", lambda e: e.memset(LBt[:], 0.0), writes=[LBt.key])
        S.add("dve", lambda e: e.memset(OMLt[:], 1.0), writes=[OMLt.key])
        S.add("dve", lambda e: e.memset(omlf[:], 1.0), writes=[omlf.key])
    else:
        for d in range(2):
            a0 = rv[:, RV_LBR + (d * 2 + 0) * 256:RV_LBR + (d * 2 + 0) * 256 + 256]
            a1 = rv[:, RV_LBR + (d * 2 + 1) * 256:RV_LBR + (d * 2 + 1) * 256 + 256]
            S.add("dve", lambda e, d=d, a0=a0, a1=a1: e.tensor_tensor(out=LBt[:, d, :], in0=a1, in1=a0, op=ALU.subtract),
                  reads=[rv.key], writes=[LBt.key])
            f0 = fv[:, FV_LB + d * 4:FV_LB + d * 4 + 2]
            f1 = fv[:, FV_LB + d * 4 + 2:FV_LB + d * 4 + 4]
            S.add("dve", lambda e, d=d, f0=f0, f1=f1: e.tensor_tensor(out=omlf[:, d, :], in0=f1, in1=f0, op=ALU.subtract),
                  reads=[fv.key], writes=[omlf.key])
        S.add("act", lambda e: e.activation(out=LBt[:], in_=LBt[:], func=AF.Sigmoid), reads=[LBt.key], writes=[LBt.key])
        S.add("act", lambda e: e.activation(out=omlf[:], in_=omlf[:], func=AF.Sigmoid), reads=[omlf.key], writes=[omlf.key])
        S.add("dve", lambda e: e.tensor_scalar(out=OMLt[:], in0=LBt[:], scalar1=-1.0, scalar2=1.0, op0=ALU.mult, op1=ALU.add),
              reads=[LBt.key], writes=[OMLt.key])
        S.add("dve", lambda e: e.tensor_scalar(out=omlf[:], in0=omlf[:], scalar1=-1.0, scalar2=1.0, op0=ALU.mult, op1=ALU.add),
              reads=[omlf.key], writes=[omlf.key])
    omlfh = self.sb("omlfh", [128, 2, 2, 2], F32)
    for d_ in range(2):
        for pr_ in range(2):
            for h2_ in range(2):
                S.add("dve", lambda e, d_=d_, pr_=pr_, h2_=h2_: e.tensor_tensor(
                    out=omlfh[:, d_, pr_, h2_:h2_ + 1], in0=omlf[:, d_, pr_:pr_ + 1],
                    in1=cm[:, CM_BLK + h2_ * 64:CM_BLK + h2_ * 64 + 1], op=ALU.mult),
                    reads=[omlf.key, cm.key], writes=[omlfh.key])
    D1 = [self.sb("D1", [128, 64, 33], F32) for _ in range(2)]
    SO = [self.sb("SO", [128, 64, 33], F32) for _ in range(2)]
    D0 = [self.sb("D0", [128, 64, 33], F32) for _ in range(2)]
    Sblk = [self.sb("Sblk", [128, 32, 128], BF16) for _ in range(2)]
    DEC = self.sb("DEC", [128, 2, 32], F32)
    ATs = [self.sb("ATs", [128, 4, 128], BF16) for _ in range(8)]
    QHs = [self.sb("QHs", [128, 2, 128], BF16) for _ in range(8)]
    VZs = [self.sb("VZs", [128, 2, 2, 128], BF16) for _ in range(8)]
    rtok_R = [self.sb("rtok", [128, 256], F32) for _ in range(4)]
    vtok_R = [self.sb("vtok", [128, 256], F32) for _ in range(4)]
    vtok_bf_R = [self.sb("vtok_bf", [128, 256], BF16) for _ in range(4)]
    qfm_R = [self.sb("qfm", [128, 2, 128], F32) for _ in range(4)]
    rfm_R = [self.sb("rfm", [128, 2, 128], F32) for _ in range(4)]
    sig_R = [self.sb("sig", [128, 256], F32) for _ in range(4)]
    tmpk_R = [self.sb("tmpk", [128, 256], F32) for _ in range(4)]
    lf_R = [self.sb("lf", [128, 256], F32) for _ in range(4)]
    ktok_R = [self.sb("ktok", [128, 256], F32) for _ in range(4)]
    sneg_R = [self.sb("sneg", [128, 2, 128], F32) for _ in range(4)]
    P1c_R = [self.sb("P1c", [128, 256], F32) for _ in range(4)]
    Ep_R = [self.sb("Ep", [128, 256], F32) for _ in range(4)]
    En_R = [self.sb("En", [128, 256], F32) for _ in range(4)]
    E2_R = [self.sb("E2", [128, 256], F32) for _ in range(4)]
    E3_R = [self.sb("E3", [128, 256], F32) for _ in range(4)]
    qt_R = [self.sb("qt", [128, 2, 128], BF16) for _ in range(4)]
    kt_R = [self.sb("kt", [128, 2, 2, 128], BF16) for _ in range(4)]
    khat_R = [self.sb("khat", [128, 4, 256], BF16) for _ in range(4)]
    of_ld_R = [self.sb("of_ld", [128, 2, 128], F32) for _ in range(4)]
    osum_R = [self.sb("osum", [128, 2, 128], F32) for _ in range(4)]
    osq_R = [self.sb("osq", [128, 2, 128], BF16) for _ in range(4)]
    orstd_R = [self.sb("orstd", [128, 256], F32) for _ in range(4)]
    sgl_R = [self.sb("sgl", [128, 2, 128], F32) for _ in range(4)]
    hgo_R = [self.sb("hgo", [128, 2, 128], BF16) for _ in range(4)]
    for pr in range(2):
        S.add("dve", lambda e, pr=pr: e.memset(Sblk[pr][:], 0.0), writes=[Sblk[pr].key])
        S.add("dve", lambda e, pr=pr: e.memset(D0[pr][:], 0.0), writes=[D0[pr].key])
        S.add("dve", lambda e, pr=pr: e.memset(D1[pr][:], 0.0), writes=[D1[pr].key])
    PFr = self.PF.rearrange("(f p) n -> p f n", p=128)
    OHGr = self.OHG.rearrange("(f p) n -> p f n", p=128)
    MIXr = self.MIXT.rearrange("(f p) n -> p f n", p=128)
    pA, pB, pC, pU, pO, pN = ps[1], ps[2], ps[3], ps[4], ps[5], ps[6]
    def do_dir(d):
        M1 = cm[:, (CM_M1F, CM_M1B)[d]:(CM_M1F, CM_M1B)[d] + 128]
        M2 = cm[:, (CM_M2F, CM_M2B)[d]:(CM_M2F, CM_M2B)[d] + 128]
        M3 = cm[:, (CM_M3F, CM_M3B)[d]:(CM_M3F, CM_M3B)[d] + 128]
        M4 = cm[:, (CM_M4F, CM_M4B)[d]:(CM_M4F, CM_M4B)[d] + 4]
        M4n = cm[:, CM_M4F:CM_M4F + 4]
        for pr in range(2):
            S.add("dve", lambda e, pr=pr: e.memset(D1[pr][:, :, 0:1], 0.0), writes=[D1[pr].key])
        def do_seg(seg):
            nt = len(seg)
            nch = 4 * nt
            def p1(ti, tile):
                n0 = tile * 128
                rtok, vtok, vtok_bf, qfm, rfm, sig, tmpk, lf, ktok, sneg, P1c, Ep, En, E2, E3, qt, kt, khat = rtok_R[ti % 4], vtok_R[ti % 4], vtok_bf_R[ti % 4], qfm_R[ti % 4], rfm_R[ti % 4], sig_R[ti % 4], tmpk_R[ti % 4], lf_R[ti % 4], ktok_R[ti % 4], sneg_R[ti % 4], P1c_R[ti % 4], Ep_R[ti % 4], En_R[ti % 4], E2_R[ti % 4], E3_R[ti % 4], qt_R[ti % 4], kt_R[ti % 4], khat_R[ti % 4]
                pA, pB = (ps[1], ps[2]) if ti % 2 == 0 else (ps[0], ps[7])
                AT, QH, VZ = ATs[ti], QHs[ti], VZs[ti]
                S.add("sp", lambda e, n0=n0: e.dma_start(out=rtok[:], in_=self.PT[n0:n0 + 128, (PT_FF, PT_FB)[d]:(PT_FF, PT_FB)[d] + 256]),
                      writes=[rtok.key], dma=True)
                S.add("sp", lambda e, n0=n0: e.dma_start(out=vtok[:], in_=self.PT[n0:n0 + 128, PT_I:PT_I + 256]),
                      writes=[vtok.key], dma=True)
                S.add("sp", lambda e, n0=n0: e.dma_start(out=qfm[:], in_=PFr[:, PF_Q:PF_Q + 2, n0:n0 + 128]),
                      writes=[qfm.key], dma=True)
                fbk = (PF_FF, PF_FB)[d]
                S.add("sp", lambda e, n0=n0, fbk=fbk: e.dma_start(out=rfm[:], in_=PFr[:, fbk:fbk + 2, n0:n0 + 128]),
                      writes=[rfm.key], dma=True)
                S.add("act", lambda e: e.activation(out=sig[:], in_=rtok[:], func=AF.Sigmoid), reads=[rtok.key], writes=[sig.key])
                S.add("act", lambda e: e.copy(out=vtok_bf[:], in_=vtok[:]), reads=[vtok.key], writes=[vtok_bf.key])
                S.add("act", lambda e: e.activation(out=sneg[:], in_=rfm[:], func=AF.Sigmoid, scale=-1.0),
                      reads=[rfm.key], writes=[sneg.key])
                S.add("dve", lambda e: e.tensor_tensor(out=tmpk[:], in0=sig[:], in1=OMLt[:, d, :], op=ALU.mult),
                      reads=[sig.key, OMLt.key], writes=[tmpk.key])
                S.add("dve", lambda e: e.scalar_tensor_tensor(out=lf[:], in0=tmpk[:], scalar=1e-20, in1=LBt[:, d, :],
                                                              op0=ALU.max, op1=ALU.add),
                      reads=[tmpk.key, LBt.key], writes=[lf.key])
                S.add("dve", lambda e: e.tensor_tensor(out=ktok[:], in0=OMLt[:, d, :], in1=tmpk[:], op=ALU.subtract),
                      reads=[tmpk.key, OMLt.key], writes=[ktok.key])
                S.add("act", lambda e: e.activation(out=lf[:], in_=lf[:], func=AF.Ln), reads=[lf.key], writes=[lf.key])
                S.next_stage()
                for pr in range(2):
                    S.add("pe", lambda e, pr=pr: e.matmul(pA[:, pr * 128:(pr + 1) * 128], lhsT=lf[:, pr * 128:(pr + 1) * 128], rhs=M1,
                                                          start=True, stop=True), reads=[lf.key, cm.key], writes=[pA.key])
                for pr in range(2):
                    S.add("pe", lambda e, pr=pr: e.matmul(pA[:, 256 + pr * 128:256 + (pr + 1) * 128], lhsT=lf[:, pr * 128:(pr + 1) * 128],
                                                          rhs=M2, start=True, stop=True), reads=[lf.key, cm.key], writes=[pA.key])
                S.add("pe", lambda e: e.matmul(pB[:, 0:256], lhsT=M3, rhs=lf[:], start=True, stop=True),
                      reads=[lf.key, cm.key], writes=[pB.key])
                for pr in range(2):
                    S.add("pe", lambda e, pr=pr: e.matmul(pB[:, 256 + pr * 4:260 + pr * 4], lhsT=lf[:, pr * 128:(pr + 1) * 128], rhs=M4,
                                                          start=True, stop=True), reads=[lf.key, cm.key], writes=[pB.key])
                S.add("dve", lambda e: e.tensor_scalar(out=P1c[:], in0=pA[:, 0:256], scalar1=40.0, scalar2=-40.0, op0=ALU.min, op1=ALU.max),
                      reads=[pA.key], writes=[P1c.key])
                S.add("act", lambda e: e.activation(out=Ep[:], in_=P1c[:], func=AF.Exp), reads=[P1c.key], writes=[Ep.key])
                S.add("act", lambda e: e.activation(out=En[:], in_=P1c[:], func=AF.Exp, scale=-1.0), reads=[P1c.key], writes=[En.key])
                S.add("act", lambda e: e.activation(out=E2[:], in_=pA[:, 256:512], func=AF.Exp), reads=[pA.key], writes=[E2.key])
                S.add("act", lambda e: e.activation(out=E3[:], in_=pB[:, 0:256], func=AF.Exp), reads=[pB.key], writes=[E3.key])
                c0 = ti * 4
                for pr in range(2):
                    S.add("act", lambda e, c0=c0, pr=pr: e.activation(out=DEC[:, pr, c0:c0 + 4], in_=pB[:, 256 + pr * 4:260 + pr * 4],
                                                                      func=AF.Exp), reads=[pB.key], writes=[DEC.key])
                qf2 = qfm[:].rearrange("p a b -> p (a b)")
                S.add("dve", lambda e: e.tensor_tensor(out=qt[:].rearrange("p a b -> p (a b)"), in0=qf2, in1=Ep[:], op=ALU.mult),
                      reads=[qfm.key, Ep.key], writes=[qt.key])
                S.add("dve", lambda e, QH=QH: e.tensor_tensor(out=QH[:].rearrange("p a b -> p (a b)"), in0=qf2, in1=E2[:], op=ALU.mult),
                      reads=[qfm.key, E2.key], writes=[QH.key])
                for pr in range(2):
                    for h2 in range(2):
                        S.add("dve", lambda e, pr=pr, h2=h2: e.scalar_tensor_tensor(
                            out=kt[:, pr, h2, :], in0=sneg[:, pr, :], scalar=omlfh[:, d, pr, h2:h2 + 1],
                            in1=En[:, pr * 128:(pr + 1) * 128], op0=ALU.mult, op1=ALU.mult),
                            reads=[sneg.key, omlfh.key, En.key], writes=[kt.key])
                for j in range(4):
                    S.add("dve", lambda e, j=j: e.scalar_tensor_tensor(out=khat[:, j, :], in0=ktok[:], scalar=M4n[:, j:j + 1], in1=E3[:],
                                                                       op0=ALU.mult, op1=ALU.mult),
                          reads=[ktok.key, cm.key, E3.key], writes=[khat.key])
                if d == 0 or True:
                    hm = cm[:, CM_HM:CM_HM + 256].rearrange("p (a b) -> p a b", a=2)
                    S.add("dve", lambda e, VZ=VZ, hm=hm: e.tensor_tensor(
                        out=VZ[:], in0=vtok[:].rearrange("p (a b) -> p a b", a=2).unsqueeze(2).to_broadcast([128, 2, 2, 128]),
                        in1=hm.unsqueeze(1).to_broadcast([128, 2, 2, 128]), op=ALU.mult),
                        reads=[vtok.key, cm.key], writes=[VZ.key])
                S.next_stage()
                for h in range(4):
                    pr, h2 = h // 2, h % 2
                    S.add("pe", lambda e, h=h, pr=pr, h2=h2: e.matmul(pC[:, h * 128:(h + 1) * 128], lhsT=kt[:, pr, h2, :],
                                                                      rhs=qt[:, pr, :], start=True, stop=True),
                          reads=[kt.key, qt.key], writes=[pC.key])
                msk = cm[:, (CM_M2F, CM_M2B)[d]:(CM_M2F, CM_M2B)[d] + 128]
                S.add("dve", lambda e, AT=AT, msk=msk: e.tensor_tensor(out=AT[:], in0=pC[:].rearrange("p (a b) -> p a b", a=4),
                                                                       in1=msk.unsqueeze(1).to_broadcast([128, 4, 128]), op=ALU.mult),
                      reads=[pC.key, cm.key], writes=[AT.key])
                S.next_stage()
                for pr in range(2):
                    for jj in range(4):
                        j = jj if d == 0 else 3 - jj
                        S.add("pe", lambda e, pr=pr, jj=jj, j=j: e.matmul(pU[:, jj * 128:(jj + 1) * 128], lhsT=khat[:, j, pr * 128:(pr + 1) * 128],
                                                                          rhs=vtok_bf[:, pr * 128:(pr + 1) * 128], start=True, stop=True),
                              reads=[khat.key, vtok_bf.key], writes=[pU.key])
                    for h2 in range(2):
                        src = pU[h2 * 64:(h2 + 1) * 64, :].rearrange("p (a b) -> p a b", a=4)[:, :, h2 * 64:(h2 + 1) * 64]
                        dstv = D1[pr][h2 * 64:(h2 + 1) * 64, :, 1 + c0:1 + c0 + 4].rearrange("p v c -> p c v")
                        if h2 == 0:
                            S.add("act", lambda e, pr=pr, src=src, dstv=dstv: e.copy(out=dstv, in_=src),
                                  reads=[pU.key], writes=[D1[pr].key])
                        else:
                            S.add("dve", lambda e, pr=pr, src=src, dstv=dstv: e.tensor_copy(out=dstv, in_=src),
                                  reads=[pU.key], writes=[D1[pr].key])
            run_staged(S, p1, seg)
            for pr in range(2):
                SB = Sblk[pr]
                S.add("dve", lambda e, pr=pr: e.tensor_copy(out=D0[pr][:, :, 1:1 + nch],
                                                            in_=DEC[:, pr, 0:nch].unsqueeze(1).to_broadcast([128, 64, nch])),
                      reads=[DEC.key], writes=[D0[pr].key])
                S.add("dve", lambda e, pr=pr: e.tensor_tensor_scan(
                    out=SO[pr][:].rearrange("p v c -> p (v c)"), data0=D0[pr][:].rearrange("p v c -> p (v c)"),
                    data1=D1[pr][:].rearrange("p v c -> p (v c)"), initial=0.0, op0=ALU.mult, op1=ALU.add),
                    reads=[D0[pr].key, D1[pr].key], writes=[SO[pr].key])
                S.add("act", lambda e, pr=pr, SB=SB: e.copy(out=SB[0:64, 0:nch, 0:64], in_=SO[pr][0:64, :, 0:nch].rearrange("p v c -> p c v")),
                      reads=[SO[pr].key], writes=[SB.key])
                S.add("act", lambda e, pr=pr, SB=SB: e.copy(out=SB[64:128, 0:nch, 64:128], in_=SO[pr][64:128, :, 0:nch].rearrange("p v c -> p c v")),
                      reads=[SO[pr].key], writes=[SB.key])
                S.add("dve", lambda e, pr=pr: e.tensor_copy(out=D1[pr][:, :, 0:1], in_=SO[pr][:, :, nch:nch + 1]),
                      reads=[SO[pr].key], writes=[D1[pr].key])
            def p2(ti, tile):
                n0 = tile * 128
                of_ld, osum, osq, orstd, sgl, hgo = of_ld_R[ti % 4], osum_R[ti % 4], osq_R[ti % 4], orstd_R[ti % 4], sgl_R[ti % 4], hgo_R[ti % 4]
                AT, QH, VZ = ATs[ti], QHs[ti], VZs[ti]
                for pr in range(2):
                    for h2 in range(2):
                        h = pr * 2 + h2
                        S.add("pe", lambda e, pr=pr, h2=h2, h=h, AT=AT, VZ=VZ: e.matmul(
                            pO[:, pr * 128:(pr + 1) * 128], lhsT=VZ[:, pr, h2, :], rhs=AT[:, h, :],
                            start=(pr == 0 and h2 == 0), stop=False, skip_group_check=True),
                            reads=[VZ.key, AT.key], writes=[pO.key])
                for pr in range(2):
                    for j in range(4):
                        jj = j if d == 0 else 3 - j
                        c = ti * 4 + jj
                        S.add("pe", lambda e, pr=pr, j=j, c=c, QH=QH: e.matmul(
                            pO[:, pr * 128 + j * 32:pr * 128 + (j + 1) * 32], lhsT=Sblk[pr][:, c, :], rhs=QH[:, pr, j * 32:(j + 1) * 32],
                            start=False, stop=(pr == 1 and j == 3), skip_group_check=True),
                            reads=[Sblk[pr].key, QH.key], writes=[pO.key])
                if d == 0:
                    S.add("act", lambda e: e.copy(out=osum[:].rearrange("p a b -> p (a b)"), in_=pO[:, 0:256]), reads=[pO.key], writes=[osum.key])
                    S.add("sp", lambda e, n0=n0: e.dma_start(out=OHGr[:, :, n0:n0 + 128], in_=osum[:]), reads=[osum.key], dma=True,
                          semkey="st_" + osum.key)
                else:
                    S.add("sp", lambda e, n0=n0: e.dma_start(out=of_ld[:], in_=OHGr[:, :, n0:n0 + 128]), writes=[of_ld.key], dma=True)
                    S.add("sp", lambda e, n0=n0: e.dma_start(out=sgl[:], in_=PFr[:, PF_G:PF_G + 2, n0:n0 + 128]), writes=[sgl.key], dma=True)
                    S.add("dve", lambda e: e.tensor_tensor(out=osum[:].rearrange("p a b -> p (a b)"), in0=of_ld[:].rearrange("p a b -> p (a b)"),
                                                           in1=pO[:, 0:256], op=ALU.add), reads=[of_ld.key, pO.key], writes=[osum.key])
                    if "ohg" in self.dbg:
                        S.add("sp", lambda e, n0=n0: e.dma_start(out=self.dbg_out["ohg%d" % l].rearrange("(f p) n -> p f n", p=128)[:, :, n0:n0 + 128],
                                                                 in_=osum[:]), reads=[osum.key], dma=True, semkey="dbg_ohg")
                    S.next_stage()
                    S.add("act", lambda e: e.activation(out=osq[:], in_=osum[:], func=AF.Square), reads=[osum.key], writes=[osq.key])
                    S.add("pe", lambda e: e.matmul(pN[:, 0:256], lhsT=blkbf_l[:], rhs=osq[:].rearrange("p a b -> p (a b)"),
                                                   start=True, stop=True), reads=[osq.key, blkbf_l.key], writes=[pN.key])
                    S.add("dve", lambda e: e.tensor_scalar(out=orstd[:], in0=pN[:, 0:256], scalar1=1.0 / 64, scalar2=EPS,
                                                           op0=ALU.mult, op1=ALU.add), reads=[pN.key], writes=[orstd.key])
                    S.add("act", lambda e: e.activation(out=orstd[:], in_=orstd[:], func=AF.Ln), reads=[orstd.key], writes=[orstd.key])
                    S.add("act", lambda e: e.activation(out=orstd[:], in_=orstd[:], func=AF.Exp, scale=-0.5), reads=[orstd.key], writes=[orstd.key])
                    S.add("dve", lambda e: e.tensor_tensor(out=osum[:].rearrange("p a b -> p (a b)"), in0=osum[:].rearrange("p a b -> p (a b)"),
                                                           in1=orstd[:], op=ALU.mult), reads=[osum.key, orstd.key], writes=[osum.key])
                    wo = l * FV_L + FV_HGN
                    for pr in range(2):
                        S.add("dve", lambda e, pr=pr: e.scalar_tensor_tensor(out=hgo[:, pr, :], in0=osum[:, pr, :], scalar=fv[:, wo + pr:wo + pr + 1],
                                                                             in1=sgl[:, pr, :], op0=ALU.mult, op1=ALU.mult),
                              reads=[osum.key, fv.key, sgl.key], writes=[hgo.key])
                    S.add("sp", lambda e, n0=n0: e.dma_start(out=MIXr[:, 0:2, n0:n0 + 128], in_=hgo[:]), reads=[hgo.key], dma=True,
                          semkey="st_" + hgo.key)
            run_staged(S, p2, seg)
        for seg in _hg_tiles(self, d):
            do_seg(seg)
        self.barrier()
    for d in range(2):
        do_dir(d)
    self.reset(m)


def _ssd_io(self):
    nc = self.nc
    self.XSs = nc.dram_tensor("XSs", [self.NTOK, 512], F32).ap()
    self.BCf = nc.dram_tensor("BCfs", [512, self.NTOK], BF16).ap()
    self.Bts = nc.dram_tensor("Bts", [self.NTOK, 256], BF16).ap()
    self.YS = nc.dram_tensor("YSs", [self.NTOK, 512], F32).ap()


def _phase_ssd(self, l):
    S = self.S
    m = self.mark()
    cm, fv, rv, ps = self.cm, self.fv, self.rv, self.ps
    NT = self.NT
    PFr = self.PF.rearrange("(f p) n -> p f n", p=128)
    BCr = self.BCf.rearrange("(f p) n -> p f n", p=128)
    MIXr = self.MIXT.rearrange("(f p) n -> p f n", p=128)
    IDENT = cm[:, CM_ID:CM_ID + 128]
    ONESF = cm[:, CM_ONES:CM_ONES + 128]
    fo = l * FV_L
    ro = l * RV_L
    xin_R = [self.sb("xin", [128, 8, 132], F32) for _ in range(2)]
    acc_R = [self.sb("acc", [128, 8, 128], F32) for _ in range(2)]
    ctmp_R = [self.sb("ctmp", [128, 8, 128], F32) for _ in range(2)]
    acc2_R = [self.sb("acc2", [128, 8, 128], F32) for _ in range(2)]
    dtmp_R = [self.sb("dtmp", [128, 8, 128], F32) for _ in range(2)]
    bcb_R = [self.sb("bcb", [128, 4, 128], BF16) for _ in range(2)]
    xs_st_R = [self.sb("xs_st", [128, 512], F32) for _ in range(2)]
    bt_st_R = [self.sb("bt_st", [128, 256], BF16) for _ in range(2)]
    pX, pBt = ps[1], ps[2]
    CW = fv[:, fo + FV_CW:fo + FV_CW + 40].rearrange("p (j k) -> p j k", j=5)
    CB = fv[:, fo + FV_CB:fo + FV_CB + 8]

    def conv_tile(tile):
        n0 = tile * 128
        xin, acc, ctmp, bcb, xs_st, bt_st = xin_R[tile % 2], acc_R[tile % 2], ctmp_R[tile % 2], bcb_R[tile % 2], xs_st_R[tile % 2], bt_st_R[tile % 2]
        acc2, dtmp = acc2_R[tile % 2], dtmp_R[tile % 2]
        s_lo, s_hi = (0, CTX) if tile < 2 else (CTX, self.NTOK)
        lo, hi = max(n0 - 2, s_lo), min(n0 + 130, s_hi)
        S.add("dve", lambda e: e.memset(xin[:, :, 0:2], 0.0), writes=[xin.key])
        S.add("dve", lambda e: e.memset(xin[:, :, 130:132], 0.0), writes=[xin.key])
        S.add("sp", lambda e: e.dma_start(out=xin[:, :, lo - (n0 - 2):hi - (n0 - 2)], in_=PFr[:, PF_XBC:PF_XBC + 8, lo:hi]),
              writes=[xin.key], dma=True)
        def cwb(j):
            return CW[:, j, :].unsqueeze(2).to_broadcast([128, 8, 128])
        S.add("pool", lambda e: e.tensor_tensor(out=acc2[:], in0=xin[:, :, 1:129], in1=cwb(1), op=ALU.mult),
              reads=[xin.key, fv.key], writes=[acc2.key])
        S.add("pool", lambda e: e.tensor_tensor(out=ctmp[:], in0=xin[:, :, 2:130], in1=cwb(2), op=ALU.mult),
              reads=[xin.key, fv.key], writes=[ctmp.key])
        S.add("pool", lambda e: e.tensor_tensor(out=acc2[:], in0=acc2[:], in1=ctmp[:], op=ALU.add),
              reads=[acc2.key, ctmp.key], writes=[acc2.key])
        S.add("dve", lambda e: e.tensor_tensor(out=acc[:], in0=xin[:, :, 0:128], in1=cwb(0), op=ALU.mult),
              reads=[xin.key, fv.key], writes=[acc.key])
        for j in (3, 4):
            S.add("dve", lambda e, j=j: e.tensor_tensor(out=dtmp[:], in0=xin[:, :, j:j + 128], in1=cwb(j), op=ALU.mult),
                  reads=[xin.key, fv.key], writes=[dtmp.key])
            S.add("dve", lambda e: e.tensor_tensor(out=acc[:], in0=acc[:], in1=dtmp[:], op=ALU.add),
                  reads=[acc.key, dtmp.key], writes=[acc.key])
        S.add("dve", lambda e: e.tensor_tensor(out=acc[:], in0=acc[:], in1=acc2[:], op=ALU.add),
              reads=[acc.key, acc2.key], writes=[acc.key])
        S.add("dve", lambda e: e.tensor_tensor(out=acc[:], in0=acc[:], in1=CB.unsqueeze(2).to_broadcast([128, 8, 128]), op=ALU.add),
              reads=[acc.key, fv.key], writes=[acc.key])
        S.add("act", lambda e: e.activation(out=acc[:], in_=acc[:], func=AF.Silu), reads=[acc.key], writes=[acc.key])
        S.add("dve", lambda e: e.tensor_copy(out=bcb[:], in_=acc[:, 4:8, :]), reads=[acc.key], writes=[bcb.key])
        S.add("sp", lambda e: e.dma_start(out=BCr[:, :, n0:n0 + 128], in_=bcb[:]), reads=[bcb.key], dma=True, semkey="st_" + bcb.key)
        for k in range(4):
            S.add("pe", lambda e, k=k: e.transpose(out=pX[:, k * 128:(k + 1) * 128], in_=acc[:, k, :], identity=IDENT),
                  reads=[acc.key, cm.key], writes=[pX.key])
        S.add("act", lambda e: e.copy(out=xs_st[:], in_=pX[:]), reads=[pX.key], writes=[xs_st.key])
        S.add("sp", lambda e: e.dma_start(out=self.XSs[n0:n0 + 128, :], in_=xs_st[:]), reads=[xs_st.key], dma=True, semkey="st_" + xs_st.key)
        for k in range(2):
            S.add("pe", lambda e, k=k: e.transpose(out=pBt[:, k * 128:(k + 1) * 128], in_=acc[:, 4 + k, :], identity=IDENT),
                  reads=[acc.key, cm.key], writes=[pBt.key])
        S.add("dve", lambda e: e.tensor_copy(out=bt_st[:], in_=pBt[:, 0:256]), reads=[pBt.key], writes=[bt_st.key])
        S.add("sp", lambda e: e.dma_start(out=self.Bts[n0:n0 + 128, :], in_=bt_st[:]), reads=[bt_st.key], dma=True, semkey="st_" + bt_st.key)

    for tile in range(NT):
        conv_tile(tile)
    self.barrier()
    self.reset(m)
    m = self.mark()
    Arow = self.sb("Arow", [128, 16], F32)
    S.add("act", lambda e: e.activation(out=Arow[:], in_=rv[:, ro + RV_ALOG:ro + RV_ALOG + 16], func=AF.Exp),
          reads=[rv.key], writes=[Arow.key])
    S.add("dve", lambda e: e.tensor_scalar_mul(out=Arow[:], in0=Arow[:], scalar1=-1.0), reads=[Arow.key], writes=[Arow.key])
    DTB = rv[:, ro + RV_DTB:ro + RV_DTB + 16]
    D0 = [self.sb("sD0", [128, 256, 9], F32) for _ in range(2)]
    D1 = [self.sb("sD1", [128, 256, 9], F32) for _ in range(2)]
    SO = [self.sb("sSO", [128, 256, 9], F32) for _ in range(2)]
    Hbf = [self.sb("Hbf", [128, 8, 256], BF16) for _ in range(2)]
    YD = [self.sb("YD", [128, 512], F32) for _ in range(8)]
    CFs = [self.sb("CFs", [128, 2, 128], BF16) for _ in range(8)]
    ECs = [self.sb("ECs", [128, 8], F32) for _ in range(8)]
    dtr_R4 = [self.sb("dtr", [128, 8], F32) for _ in range(4)]
    dt_R4 = [self.sb("dt", [128, 8], F32) for _ in range(4)]
    av_R4 = [self.sb("av", [128, 8], F32) for _ in range(4)]
    ABC_R4 = [self.sb("ABC", [128, 8, 128], F32) for _ in range(4)]
    ATRI_R4 = [self.sb("ATRI", [128, 8, 128], F32) for _ in range(4)]
    ncum_R4 = [self.sb("ncum", [128, 8], F32) for _ in range(4)]
    dend_R4 = [self.sb("dend", [128, 8], F32) for _ in range(4)]
    dect_R4 = [self.sb("dect", [128, 8], F32) for _ in range(4)]
    xst_R4 = [self.sb("xst", [128, 512], F32) for _ in range(4)]
    btl_R4 = [self.sb("btl", [128, 256], BF16) for _ in range(4)]
    bcl_R4 = [self.sb("bcl", [128, 4, 128], BF16) for _ in range(4)]
    Lsb_R4 = [self.sb("Lsb", [128, 8, 128], F32) for _ in range(4)]
    Wb_R4 = [self.sb("Wb", [128, 8, 128], BF16) for _ in range(4)]
    xdt_R4 = [self.sb("xdt", [128, 8, 64], BF16) for _ in range(4)]
    xw_R4 = [self.sb("xw", [128, 8, 64], BF16) for _ in range(4)]
    xst2_R = [self.sb("xst2", [128, 512], F32) for _ in range(2)]
    yo_R = [self.sb("yo", [128, 512], F32) for _ in range(2)]
    yf_R = [self.sb("yf", [128, 512], F32) for _ in range(2)]
    zt_R = [self.sb("zt", [128, 512], F32) for _ in range(2)]
    ssq_R = [self.sb("ssq", [128, 2], F32) for _ in range(2)]
    yT_R = [self.sb("yT", [128, 4, 128], BF16) for _ in range(2)]
    IDb = self.sb("sIDb", [128, 128], BF16)
    S.add("dve", lambda e: e.tensor_copy(out=IDb[:], in_=IDENT), reads=[cm.key], writes=[IDb.key])
    NEG4 = [self.sb("NEG4", [128, 4, 128], BF16) for _ in range(2)]
    for d_ in range(2):
        ng = cm[:, (CM_NEGF, CM_NEGB)[d_]:(CM_NEGF, CM_NEGB)[d_] + 128]
        S.add("dve", lambda e, d_=d_, ng=ng: e.tensor_copy(out=NEG4[d_][:], in_=ng.unsqueeze(1).to_broadcast([128, 4, 128])),
              reads=[cm.key], writes=[NEG4[d_].key])
    for g in range(2):
        S.add("dve", lambda e, g=g: e.memset(D0[g][:], 0.0), writes=[D0[g].key])
        S.add("dve", lambda e, g=g: e.memset(D1[g][:], 0.0), writes=[D1[g].key])
    pS, pLa, pLb, pG, pY, pH, pY2, pT = ps[0], ps[1], ps[2], ps[3], ps[4], ps[5], ps[6], ps[7]
    SSNrow = rv[:, ro + RV_SSN:ro + RV_SSN + 512]
    DSKrow = rv[:, ro + RV_DSK:ro + RV_DSK + 512]

    def do_dir(d):
        TRI = cm[:, (CM_TRIF, CM_TRIB)[d]:(CM_TRIF, CM_TRIB)[d] + 128]
        NEGM = cm[:, (CM_NEGF, CM_NEGB)[d]:(CM_NEGF, CM_NEGB)[d] + 128]
        for g in range(2):
            S.add("dve", lambda e, g=g: e.memset(D1[g][:, :, 0:1], 0.0), writes=[D1[g].key])

        def do_seg(seg):
            nt = len(seg)

            def p1(ti, tile):
                n0 = tile * 128
                ATRI = ATRI_R4[ti % 4]
                dtr, dt, av, ABC, ncum, dend, dect, xst, btl, bcl, Lsb, Wb, xdt, xw = dtr_R4[ti % 4], dt_R4[ti % 4], av_R4[ti % 4], ABC_R4[ti % 4], ncum_R4[ti % 4], dend_R4[ti % 4], dect_R4[ti % 4], xst_R4[ti % 4], btl_R4[ti % 4], bcl_R4[ti % 4], Lsb_R4[ti % 4], Wb_R4[ti % 4], xdt_R4[ti % 4], xw_R4[ti % 4]
                S.add("sp", lambda e: e.dma_start(out=dtr[:], in_=self.PT[n0:n0 + 128, PT_DT + d * 8:PT_DT + d * 8 + 8]),
                      writes=[dtr.key], dma=True)
                S.add("sp", lambda e: e.dma_start(out=xst[:], in_=self.XSs[n0:n0 + 128, :]), writes=[xst.key], dma=True)
                S.add("sp", lambda e: e.dma_start(out=btl[:], in_=self.Bts[n0:n0 + 128, :]), writes=[btl.key], dma=True)
                S.add("sp", lambda e: e.dma_start(out=bcl[:], in_=BCr[:, :, n0:n0 + 128]), writes=[bcl.key], dma=True)
                S.add("dve", lambda e: e.tensor_tensor(out=dt[:], in0=dtr[:], in1=DTB[:, d * 8:d * 8 + 8], op=ALU.add),
                      reads=[dtr.key, rv.key], writes=[dt.key])
                S.add("act", lambda e: e.activation(out=dt[:], in_=dt[:], func=AF.Exp), reads=[dt.key], writes=[dt.key])
                S.add("dve", lambda e: e.tensor_scalar_add(out=dt[:], in0=dt[:], scalar1=1.0), reads=[dt.key], writes=[dt.key])
                S.add("act", lambda e: e.activation(out=dt[:], in_=dt[:], func=AF.Ln), reads=[dt.key], writes=[dt.key])
                S.add("dve", lambda e: e.tensor_tensor(out=av[:], in0=dt[:], in1=Arow[:, d * 8:d * 8 + 8], op=ALU.mult),
                      reads=[dt.key, Arow.key], writes=[av.key])
                S.add("dve", lambda e: e.tensor_scalar_mul(out=ABC[:], in0=av[:].unsqueeze(2).to_broadcast([128, 8, 128]), scalar1=-1.0),
                      reads=[av.key], writes=[ABC.key])
                S.add("dve", lambda e: e.tensor_tensor(out=ATRI[:], in0=TRI.unsqueeze(1).to_broadcast([128, 8, 128]),
                                                       in1=av[:].unsqueeze(2).to_broadcast([128, 8, 128]), op=ALU.mult),
                      reads=[av.key, cm.key], writes=[ATRI.key])
                S.next_stage()
                S.add("pe", lambda e: e.matmul(pS[:, 0:8], lhsT=TRI, rhs=av[:], start=True, stop=True), reads=[av.key, cm.key], writes=[pS.key])
                S.add("pe", lambda e: e.matmul(pS[:, 8:16], lhsT=ONESF, rhs=av[:], start=True, stop=True), reads=[av.key, cm.key], writes=[pS.key])
                S.add("dve", lambda e: e.tensor_scalar_mul(out=ncum[:], in0=pS[:, 0:8], scalar1=-1.0), reads=[pS.key], writes=[ncum.key])
                EC = ECs[ti]
                S.add("act", lambda e: e.activation(out=EC[:], in_=pS[:, 0:8], func=AF.Exp), reads=[pS.key], writes=[EC.key])
                S.add("dve", lambda e: e.tensor_tensor(out=dend[:], in0=pS[:, 8:16], in1=ncum[:], op=ALU.add),
                      reads=[pS.key, ncum.key], writes=[dend.key])
                S.add("act", lambda e: e.activation(out=dend[:], in_=dend[:], func=AF.Exp), reads=[dend.key], writes=[dend.key])
                S.add("act", lambda e: e.activation(out=dect[:], in_=pS[:, 8:16], func=AF.Exp), reads=[pS.key], writes=[dect.key])
                S.next_stage()
                for hb in range(2):
                    pl = (pLa, pLb)[hb]
                    S.add("pe", lambda e, hb=hb, pl=pl: e.matmul(pl[:, 0:512], lhsT=ONESF, rhs=ATRI[:, 4 * hb:4 * hb + 4, :].rearrange("p a b -> p (a b)"),
                                                                 start=True, stop=False, skip_group_check=True),
                          reads=[ATRI.key, cm.key], writes=[pl.key])
                    S.add("pe", lambda e, hb=hb, pl=pl: e.matmul(pl[:, 0:512], lhsT=TRI, rhs=ABC[:, 4 * hb:4 * hb + 4, :].rearrange("p a b -> p (a b)"),
                                                                 start=False, stop=False, skip_group_check=True),
                          reads=[ABC.key, cm.key], writes=[pl.key])
                    S.add("pe", lambda e, hb=hb, pl=pl: e.matmul(pl[:, 0:512], lhsT=IDb[:], rhs=NEG4[d][:].rearrange("p a b -> p (a b)"),
                                                                 start=False, stop=True, skip_group_check=True),
                          reads=[IDb.key, NEG4[d].key], writes=[pl.key])
                    S.add("act", lambda e, hb=hb, pl=pl: e.activation(out=Lsb[:, 4 * hb:4 * hb + 4, :].rearrange("p a b -> p (a b)"), in_=pl[:, 0:512], func=AF.Exp),
                          reads=[pl.key], writes=[Lsb.key])
                S.next_stage()
                for g in range(2):
                    S.add("pe", lambda e, g=g: e.matmul(pG[:, g * 128:(g + 1) * 128], lhsT=bcl[:, g, :], rhs=bcl[:, 2 + g, :],
                                                        start=True, stop=True), reads=[bcl.key], writes=[pG.key])
                for g in range(2):
                    S.add("dve", lambda e, g=g: e.tensor_tensor(
                        out=Wb[:, g * 4:(g + 1) * 4, :], in0=Lsb[:, g * 4:(g + 1) * 4, :],
                        in1=pG[:, g * 128:(g + 1) * 128].unsqueeze(1).to_broadcast([128, 4, 128]), op=ALU.mult),
                        reads=[Lsb.key, pG.key], writes=[Wb.key])
                S.add("dve", lambda e: e.tensor_tensor(out=xdt[:], in0=xst[:].rearrange("p (h q) -> p h q", h=8),
                                                       in1=dt[:].unsqueeze(2).to_broadcast([128, 8, 64]), op=ALU.mult),
                      reads=[xst.key, dt.key], writes=[xdt.key])
                S.next_stage()
                for h in range(8):
                    S.add("pe", lambda e, h=h: e.matmul(pY[:, h * 64:(h + 1) * 64], lhsT=Wb[:, h, :], rhs=xdt[:, h, :], start=True, stop=True),
                          reads=[Wb.key, xdt.key], writes=[pY.key])
                Y = YD[ti]
                S.add("act", lambda e: e.copy(out=Y[:], in_=pY[:]), reads=[pY.key], writes=[Y.key])
                CF = CFs[ti]
                S.add("dve", lambda e: e.tensor_copy(out=CF[:], in_=bcl[:, 2:4, :]), reads=[bcl.key], writes=[CF.key])
                S.add("dve", lambda e: e.tensor_tensor(out=xw[:], in0=xdt[:], in1=dend[:].unsqueeze(2).to_broadcast([128, 8, 64]), op=ALU.mult),
                      reads=[xdt.key, dend.key], writes=[xw.key])
                S.next_stage()
                for g in range(2):
                    S.add("pe", lambda e, g=g: e.matmul(pH[:, g * 256:(g + 1) * 256], lhsT=btl[:, g * 128:(g + 1) * 128],
                                                        rhs=xw[:, g * 4:(g + 1) * 4, :].rearrange("p h q -> p (h q)"), start=True, stop=True),
                          reads=[btl.key, xw.key], writes=[pH.key])
                for g in range(2):
                    S.add("act", lambda e, g=g: e.copy(out=D1[g][:, :, 1 + ti:2 + ti], in_=pH[:, g * 256:(g + 1) * 256].unsqueeze(2)),
                          reads=[pH.key], writes=[D1[g].key])
                    S.add("dve", lambda e, g=g: e.tensor_copy(
                        out=D0[g][:, :, 1 + ti:2 + ti].rearrange("p (h q) o -> p h (q o)", h=4),
                        in_=dect[:, g * 4:(g + 1) * 4].unsqueeze(2).to_broadcast([128, 4, 64])),
                        reads=[dect.key], writes=[D0[g].key])

            for g0 in range(0, nt, 4):
                recs = []
                for ti in range(g0, min(g0 + 4, nt)):
                    S.begin_record()
                    p1(ti, seg[ti])
                    recs.append(S.end_record())
                S.replay_staged(recs)
            for g in range(2):
                S.add("dve", lambda e, g=g: e.tensor_tensor_scan(
                    out=SO[g][:].rearrange("p v c -> p (v c)"), data0=D0[g][:].rearrange("p v c -> p (v c)"),
                    data1=D1[g][:].rearrange("p v c -> p (v c)"), initial=0.0, op0=ALU.mult, op1=ALU.add),
                    reads=[D0[g].key, D1[g].key], writes=[SO[g].key])
                S.add("act", lambda e, g=g: e.copy(out=Hbf[g][:, 0:nt, :], in_=SO[g][:, :, 0:nt].rearrange("p v c -> p c v")),
                      reads=[SO[g].key], writes=[Hbf[g].key])
                S.add("dve", lambda e, g=g: e.tensor_copy(out=D1[g][:, :, 0:1], in_=SO[g][:, :, nt:nt + 1]),
                      reads=[SO[g].key], writes=[D1[g].key])

            def p2(ti, tile):
                n0 = tile * 128
                yo, yf, zt, ssq, yT, xst2 = yo_R[ti % 2], yf_R[ti % 2], zt_R[ti % 2], ssq_R[ti % 2], yT_R[ti % 2], xst2_R[ti % 2]
                xst = xst2
                CF, EC, Y = CFs[ti], ECs[ti], YD[ti]
                for g in range(2):
                    S.add("pe", lambda e, g=g: e.matmul(pY2[:, g * 256:(g + 1) * 256], lhsT=CF[:, g, :], rhs=Hbf[g][:, ti, :], start=True, stop=True),
                          reads=[CF.key, Hbf[g].key], writes=[pY2.key])
                S.add("dve", lambda e: e.tensor_tensor(out=yo[:].rearrange("p (h q) -> p h q", h=8), in0=pY2[:].rearrange("p (h q) -> p h q", h=8),
                                                       in1=EC[:].unsqueeze(2).to_broadcast([128, 8, 64]), op=ALU.mult),
                      reads=[pY2.key, EC.key], writes=[yo.key])
                S.add("dve", lambda e: e.tensor_tensor(out=yo[:], in0=yo[:], in1=Y[:], op=ALU.add), reads=[yo.key, Y.key], writes=[yo.key])
                if d == 0:
                    S.add("sp", lambda e: e.dma_start(out=self.YS[n0:n0 + 128, :], in_=yo[:]), reads=[yo.key], dma=True, semkey="st_" + yo.key)
                    return
                S.add("sp", lambda e: e.dma_start(out=yf[:], in_=self.YS[n0:n0 + 128, :]), writes=[yf.key], dma=True)
                S.add("sp", lambda e: e.dma_start(out=xst[:], in_=self.XSs[n0:n0 + 128, :]), writes=[xst.key], dma=True)
                S.add("sp", lambda e: e.dma_start(out=zt[:], in_=self.PT[n0:n0 + 128, PT_Z:PT_Z + 512]), writes=[zt.key], dma=True)
                S.next_stage()
                S.add("dve", lambda e: e.tensor_tensor(out=yo[:], in0=yo[:], in1=yf[:], op=ALU.add), reads=[yo.key, yf.key], writes=[yo.key])
                S.add("dve", lambda e: e.tensor_tensor(out=xst[:], in0=xst[:], in1=DSKrow, op=ALU.mult), reads=[xst.key, rv.key], writes=[xst.key])
                S.add("dve", lambda e: e.tensor_tensor(out=yo[:], in0=yo[:], in1=xst[:], op=ALU.add), reads=[yo.key, xst.key], writes=[yo.key])
                if "yssm" in self.dbg:
                    S.add("sp", lambda e: e.dma_start(out=self.dbg_out["yssm%d" % l][n0:n0 + 128, :], in_=yo[:]), reads=[yo.key], dma=True,
                          semkey="dbg_yssm")
                S.add("act", lambda e: e.activation(out=zt[:], in_=zt[:], func=AF.Silu), reads=[zt.key], writes=[zt.key])
                S.add("dve", lambda e: e.tensor_tensor(out=yo[:], in0=yo[:], in1=zt[:], op=ALU.mult), reads=[yo.key, zt.key], writes=[yo.key])
                S.add("act", lambda e: e.activation(out=zt[:], in_=yo[:], func=AF.Square, accum_out=ssq[:, 0:1]),
                      reads=[yo.key], writes=[zt.key, ssq.key])
                S.add("dve", lambda e: e.tensor_scalar(out=ssq[:, 1:2], in0=ssq[:, 0:1], scalar1=1.0 / 512, scalar2=EPS, op0=ALU.mult, op1=ALU.add),
                      reads=[ssq.key], writes=[ssq.key])
                S.add("act", lambda e: e.activation(out=ssq[:, 1:2], in_=ssq[:, 1:2], func=AF.Ln), reads=[ssq.key], writes=[ssq.key])
                S.add("act", lambda e: e.activation(out=ssq[:, 1:2], in_=ssq[:, 1:2], func=AF.Exp, scale=-0.5), reads=[ssq.key], writes=[ssq.key])
                S.add("dve", lambda e: e.scalar_tensor_tensor(out=yo[:], in0=yo[:], scalar=ssq[:, 1:2], in1=SSNrow, op0=ALU.mult, op1=ALU.mult),
                      reads=[yo.key, ssq.key, rv.key], writes=[yo.key])
                S.next_stage()
                for k in range(4):
                    S.add("pe", lambda e, k=k: e.transpose(out=pT[:, k * 128:(k + 1) * 128], in_=yo[:, k * 128:(k + 1) * 128], identity=IDENT),
                          reads=[yo.key, cm.key], writes=[pT.key])
                S.add("act", lambda e: e.copy(out=yT[:].rearrange("p a b -> p (a b)"), in_=pT[:]), reads=[pT.key], writes=[yT.key])
                S.add("sp", lambda e: e.dma_start(out=MIXr[:, 4:8, n0:n0 + 128], in_=yT[:]), reads=[yT.key], dma=True, semkey="st_" + yT.key)

            run_staged(S, p2, seg, group=2)

        for seg in _hg_tiles(self, d):
            do_seg(seg)
        self.barrier()

    for d in range(2):
        do_dir(d)
    self.reset(m)


NAB_N = 5 * 4 * 5 * 128


def make_nabias(rpb, nlat):
    rows = 2 * nlat
    out = np.full((128, 5, 4, 5, 128), NEG, np.float32)
    its = [0, 1, 2, nlat - 2, nlat - 1]
    p = np.arange(128)
    q = np.arange(128)
    for v, it in enumerate(its):
        kt0 = min(max(it - 2, 0), nlat - 5)
        r = 2 * it + q // 64
        cq = q % 64
        r0 = np.clip(r - 4, 0, rows - 8)
        c0 = np.clip(cq - 8, 0, 48)
        for kt in range(5):
            rk = 2 * (kt0 + kt) + p // 64
            ck = p % 64
            inw = ((rk[:, None] >= r0[None, :]) & (rk[:, None] < r0[None, :] + 8)
                   & (ck[:, None] >= c0[None, :]) & (ck[:, None] < c0[None, :] + 16))
            dr = np.clip(rk[:, None] - r[None, :] + 7, 0, 14)
            dc = np.clip(ck[:, None] - cq[None, :], -15, 15) + 15
            for h in range(4):
                b = rpb[h][dr, dc]
                out[:, v, h, kt, :] = np.where(inw, b, NEG)
    return out.reshape(128, NAB_N)


def _na_io(self):
    nc = self.nc
    self.nabias = nc.dram_tensor("nabias", [DEPTH, 128, NAB_N], F32, kind="ExternalInput").ap()


def _phase_na(self, l):
    S = self.S
    m = self.mark()
    cm, fv, rv, ps = self.cm, self.fv, self.rv, self.ps
    NT, nlat, NTOK = self.NT, self.nlat, self.NTOK
    last = (l == DEPTH - 1)
    PFr = self.PF.rearrange("(f p) n -> p f n", p=128)
    MIXr = self.MIXT.rearrange("(f p) n -> p f n", p=128)
    IDENT = cm[:, CM_ID:CM_ID + 128]
    ro = l * RV_L
    NANrow = rv[:, ro + RV_NAN:ro + RV_NAN + 256]
    KT = self.sb("KT", [128, 2, NTOK], BF16)
    Vaug = self.sb("Vaug", [128, NT, 4, 65], BF16)
    BT = self.sb("BT", [128, 5, 4, 5, 128], BF16)
    IDb = self.sb("IDb", [128, 128], BF16)
    ID4 = self.sb("ID4", [128, 4, 128], F32)
    qf_R = [self.sb("qf", [128, 2, 128], F32) for _ in range(2)]
    QZ_R = [self.sb("QZ", [128, 2, 2, 128], BF16) for _ in range(2)]
    mx_R = [self.sb("mx", [128, 2], F32) for _ in range(4)]
    DG_R = [self.sb("DG", [128, 512], BF16) for _ in range(4)]
    PTs_R = [self.sb("PTs", [128, 896], BF16) for _ in range(4)]
    rc_R = [self.sb("rc", [128, 4], F32) for _ in range(2)]
    onat_R = [self.sb("onat", [128, 4, 64], F32) for _ in range(2)]
    junk_R = [self.sb("junk", [128, 256], F32) for _ in range(2)]
    ssq_R = [self.sb("nssq", [128, 2], F32) for _ in range(2)]
    oT_R = [self.sb("oT", [128, 2, 128], BF16) for _ in range(2)]
    hrm = self.sb("hrm", [128, 2], F32)
    SB, ST = self.psbig[0], self.psbig[1]
    pOV, pT = ps[4], ps[5]
    for pr in range(2):
        S.add("pool", lambda e, pr=pr: e.dma_start(out=KT[:, pr, :], in_=PFr[:, PF_KA + pr, :]), writes=[KT.key], dma=True)
    S.add("pool", lambda e: e.dma_start(out=BT[:].rearrange("p a b c d -> p (a b c d)"), in_=self.nabias[l]), writes=[BT.key], dma=True)
    S.add("dve", lambda e: e.tensor_copy(out=IDb[:], in_=IDENT), reads=[cm.key], writes=[IDb.key])
    S.add("dve", lambda e: e.tensor_copy(out=ID4[:], in_=IDENT.unsqueeze(1).to_broadcast([128, 4, 128])), reads=[cm.key], writes=[ID4.key])
    S.add("dve", lambda e: e.memset(Vaug[:, :, :, 64:65], 1.0), writes=[Vaug.key])
    for t in range(NT):
        S.add("pool", lambda e, t=t: e.dma_start(out=Vaug[:, t, :, 0:64],
                                                  in_=self.PT[t * 128:(t + 1) * 128, PT_VA:PT_VA + 256].rearrange("p (h q) -> p h q", h=4)),
              writes=[Vaug.key], dma=True)
    for h2 in range(2):
        S.add("dve", lambda e, h2=h2: e.tensor_copy(out=hrm[:, h2:h2 + 1], in_=cm[:, CM_BLK + h2 * 64:CM_BLK + h2 * 64 + 1]),
              reads=[cm.key], writes=[hrm.key])

    def q_tile(tile, keytiles, var):
        n0 = tile * 128
        nk = len(keytiles)
        qf, QZ, rc, onat, junk, ssq, oT = qf_R[tile % 2], QZ_R[tile % 2], rc_R[tile % 2], onat_R[tile % 2], junk_R[tile % 2], ssq_R[tile % 2], oT_R[tile % 2]
        nloc = 5 if var is not None else 0
        S.add("sp", lambda e: e.dma_start(out=qf[:], in_=PFr[:, PF_QA:PF_QA + 2, n0:n0 + 128]), writes=[qf.key], dma=True)
        S.add("dve", lambda e: e.tensor_tensor(out=QZ[:], in0=qf[:].unsqueeze(2).to_broadcast([128, 2, 2, 128]),
                                               in1=hrm[:].unsqueeze(1).unsqueeze(3).to_broadcast([128, 2, 2, 128]), op=ALU.mult),
              reads=[qf.key, hrm.key], writes=[QZ.key])
        def head_fn(hi, h):
            pr, h2 = h // 2, h % 2
            mx, DG, PTs = mx_R[h % 4], DG_R[h % 4], PTs_R[h % 4]
            SB, ST = (self.psbig[0], self.psbig[1]) if h % 2 == 0 else (self.psbig[3], self.psbig[1])
            col = 0
            runs = []
            i = 0
            while i < nk:
                j = i
                while j + 1 < nk and keytiles[j + 1] == keytiles[j] + 1 and (j + 1 - i) < 4 and ((col + (j + 1 - i) * 128) % 512 != 0):
                    j += 1
                runs.append((keytiles[i], j - i + 1, col))
                col += (j - i + 1) * 128
                i = j + 1
            for (kt_, cnt, c_) in runs:
                S.add("pe", lambda e, pr=pr, h2=h2, kt_=kt_, cnt=cnt, c_=c_: e.matmul(
                    SB[:, c_:c_ + cnt * 128], lhsT=QZ[:, pr, h2, :], rhs=KT[:, pr, kt_ * 128:(kt_ + cnt) * 128], start=True, stop=True),
                    reads=[QZ.key, KT.key], writes=[SB.key])
            S.add("dve", lambda e: e.reduce_max(out=mx[:, 0:1], in_=SB[:, 0:nk * 128], axis=AX.X), reads=[SB.key], writes=[mx.key])
            S.add("dve", lambda e: e.tensor_scalar_mul(out=mx[:, 1:2], in0=mx[:, 0:1], scalar1=-1.0), reads=[mx.key], writes=[mx.key])
            S.add("dve", lambda e: e.tensor_scalar_mul(out=DG[:], in0=ID4[:].rearrange("p a b -> p (a b)"), scalar1=mx[:, 1:2]),
                  reads=[mx.key, ID4.key], writes=[DG.key])
            S.next_stage()
            for b0 in (0, 4):
                kks = [kk for kk in range(nk) if b0 <= kk < b0 + 4]
                if not kks:
                    continue
                ncols = len(kks) * 128
                nbias = len([kk for kk in kks if kk < nloc])
                S.add("pe", lambda e, b0=b0, ncols=ncols: e.matmul(ST[:, b0 * 128:b0 * 128 + ncols], lhsT=self.ones_bf[:], rhs=DG[:, 0:ncols],
                                                                   start=True, stop=False, skip_group_check=True),
                      reads=[DG.key, self.ones_bf.key], writes=[ST.key])
                for kk in kks:
                    kt_ = keytiles[kk]
                    lastmm = (kk == kks[-1]) and nbias == 0
                    S.add("pe", lambda e, pr=pr, h2=h2, kt_=kt_, kk=kk, lastmm=lastmm: e.matmul(
                        ST[:, kk * 128:(kk + 1) * 128], lhsT=KT[:, pr, kt_ * 128:(kt_ + 1) * 128], rhs=QZ[:, pr, h2, :],
                        start=False, stop=lastmm, skip_group_check=True),
                        reads=[QZ.key, KT.key], writes=[ST.key])
                if nbias:
                    k0 = kks[0]
                    S.add("pe", lambda e, h=h, k0=k0, nbias=nbias: e.matmul(
                        ST[:, k0 * 128:(k0 + nbias) * 128], lhsT=IDb[:],
                        rhs=BT[:, var, h, k0:k0 + nbias, :].rearrange("p a b -> p (a b)"),
                        start=False, stop=True, skip_group_check=True),
                        reads=[BT.key, IDb.key], writes=[ST.key])
            S.add("act", lambda e: e.activation(out=PTs[:, 0:nk * 128], in_=ST[:, 0:nk * 128], func=AF.Exp), reads=[ST.key], writes=[PTs.key])
            S.next_stage()
            for kk, kt_ in enumerate(keytiles):
                S.add("pe", lambda e, kk=kk, kt_=kt_, h=h: e.matmul(pOV[:, h * 65:(h + 1) * 65], lhsT=PTs[:, kk * 128:(kk + 1) * 128],
                                                                    rhs=Vaug[:, kt_, h, :], start=(kk == 0), stop=(kk == nk - 1)),
                      reads=[PTs.key, Vaug.key], writes=[pOV.key])
        run_staged(S, head_fn, [0, 1, 2, 3], group=4)
        OVv = pOV[:, 0:260].rearrange("p (h c) -> p h c", h=4)
        S.add("dve", lambda e: e.reciprocal(out=rc[:], in_=OVv[:, :, 64]), reads=[pOV.key], writes=[rc.key])
        S.add("dve", lambda e: e.tensor_tensor(out=onat[:], in0=OVv[:, :, 0:64], in1=rc[:].unsqueeze(2).to_broadcast([128, 4, 64]), op=ALU.mult),
              reads=[pOV.key, rc.key], writes=[onat.key])
        o2 = onat[:].rearrange("p h c -> p (h c)")
        if "naraw" in self.dbg:
            S.add("sp", lambda e: e.dma_start(out=self.dbg_out["naraw%d" % l][n0:n0 + 128, :], in_=o2), reads=[onat.key], dma=True,
                  semkey="dbg_naraw")
        S.add("act", lambda e: e.activation(out=junk[:], in_=o2, func=AF.Square, accum_out=ssq[:, 0:1]), reads=[onat.key],
              writes=[junk.key, ssq.key])
        S.add("dve", lambda e: e.tensor_scalar(out=ssq[:, 1:2], in0=ssq[:, 0:1], scalar1=1.0 / 256, scalar2=EPS, op0=ALU.mult, op1=ALU.add),
              reads=[ssq.key], writes=[ssq.key])
        S.add("act", lambda e: e.activation(out=ssq[:, 1:2], in_=ssq[:, 1:2], func=AF.Ln), reads=[ssq.key], writes=[ssq.key])
        S.add("act", lambda e: e.activation(out=ssq[:, 1:2], in_=ssq[:, 1:2], func=AF.Exp, scale=-0.5), reads=[ssq.key], writes=[ssq.key])
        S.add("dve", lambda e: e.scalar_tensor_tensor(out=junk[:], in0=o2, scalar=ssq[:, 1:2], in1=NANrow, op0=ALU.mult, op1=ALU.mult),
              reads=[onat.key, ssq.key, rv.key, junk.key], writes=[junk.key])
        for k in range(2):
            S.add("pe", lambda e, k=k: e.transpose(out=pT[:, k * 128:(k + 1) * 128], in_=junk[:, k * 128:(k + 1) * 128], identity=IDENT),
                  reads=[junk.key, cm.key], writes=[pT.key])
        S.add("act", lambda e: e.copy(out=oT[:].rearrange("p a b -> p (a b)"), in_=pT[:, 0:256]), reads=[pT.key], writes=[oT.key])
        S.add("sp", lambda e: e.dma_start(out=MIXr[:, 2:4, n0:n0 + 128], in_=oT[:]), reads=[oT.key], dma=True, semkey="st_" + oT.key)

    if not last:
        for tile in range(2):
            q_tile(tile, [0, 1], None)
    for it in range(nlat):
        var = 0 if it == 0 else 1 if it == 1 else 3 if it == nlat - 2 else 4 if it == nlat - 1 else 2
        kt0 = min(max(it - 2, 0), nlat - 5)
        q_tile(it + 2, [kt0 + 2 + k for k in range(5)] + [0, 1], var)
    self.barrier()
    self.reset(m)


NLAT_FULL = 64
_CACHE = {}


def kernel(**inputs):
    inp = {k: np.asarray(v) for k, v in inputs.items()}
    nlat = NLAT_FULL
    B = inp["x"].shape[0]
    kb = KB(nlat=nlat)
    nc = kb.build()
    maps = make_in_maps(inp, nlat, list(range(B)))
    res = run_bass_kernel_spmd(nc, maps, core_ids=list(range(B)))
    out = np.stack([np.asarray(res.results[b]["outT"]).T for b in range(B)])
    return np.ascontiguousarray(out.astype(np.float32))
```

```python
import contextlib
import numpy as np
import concourse.bass as bass
import concourse.mybir as mybir
from concourse.bass_utils import run_bass_kernel_spmd

F32 = mybir.dt.float32
BF16 = mybir.dt.bfloat16
AF = mybir.ActivationFunctionType
ALU = mybir.AluOpType
AX = mybir.AxisListType

D = 1024
DEPTH = 2
DFF = 2816
CTX = 256
GW = 64
EPS = 1e-6
ENGS = ("pe", "act", "dve", "pool", "sp")
PH = "__ph"


class Op:
    __slots__ = ("eng", "fn", "deps", "dma", "semkey", "sig", "val", "idx", "slot")

    def __init__(self, eng, fn, dma, semkey):
        self.eng = eng
        self.fn = fn
        self.deps = {}
        self.dma = dma
        self.semkey = semkey
        self.sig = False
        self.val = 0
        self.idx = 0
        self.slot = None


class Sched:
    def __init__(self, nc):
        self.nc = nc
        self.ops = []
        self.last_w = {}
        self.readers = {}
        self.slotmap = {}

    def begin_record(self):
        self.rec = []
        self.rec_stage = 0

    def next_stage(self):
        if getattr(self, "rec", None) is not None:
            self.rec_stage += 1

    def end_record(self):
        r = self.rec
        self.rec = None
        return r

    def replay_staged(self, recs):
        nst = 1 + max((st for r in recs for (st, a, k) in r), default=0)
        for st in range(nst):
            for r in recs:
                for (s_, a, k) in r:
                    if s_ == st:
                        self.add(*a, **k)

    def add(self, eng, fn, reads=(), writes=(), dma=False, semkey=None, barrier=False):
        if getattr(self, "rec", None) is not None:
            self.rec.append((self.rec_stage, (eng, fn), dict(reads=reads, writes=writes, dma=dma, semkey=semkey, barrier=barrier)))
            return None
        reads = list(reads)
        writes = list(writes)
        if barrier:
            writes.append(PH)
        else:
            reads.append(PH)
        op = Op(eng, fn, dma, semkey if semkey is not None else (writes[0] if (dma and writes) else None))
        op.idx = len(self.ops)
        if barrier:
            self.slotmap = {}
        if dma:
            if eng == "pool":
                op.slot = ("p", op.semkey)
            else:
                if op.semkey not in self.slotmap:
                    self.slotmap[op.semkey] = len(self.slotmap)
                op.slot = self.slotmap[op.semkey]
        cand = {}
        for k in reads + writes:
            w = self.last_w.get(k)
            if w is not None:
                cand[w.idx] = w
        for k in writes:
            for r in self.readers.get(k, ()):
                cand[r.idx] = r
        for d in cand.values():
            if (not d.dma) and (not op.dma) and d.eng == "pe" and op.eng == "pe":
                continue
            if d.dma and op.dma and d.semkey == op.semkey:
                pure_waw = all(self.last_w.get(k) is not d for k in reads) and \
                    all(d not in self.readers.get(k, ()) for k in writes)
                if pure_waw:
                    continue
            key = ("dma", d.slot) if d.dma else ("eng", d.eng)
            old = op.deps.get(key)
            if old is None or old.idx < d.idx:
                op.deps[key] = d
            d.sig = True
        for k in writes:
            self.last_w[k] = op
            self.readers[k] = []
        for k in reads:
            self.readers.setdefault(k, []).append(op)
        self.ops.append(op)
        return op

    def emit(self):
        nc = self.nc
        import os as _os
        _mx = int(_os.environ.get("MAXOPS", "0"))
        if _mx:
            self.ops = self.ops[:_mx]
        cnt = {}
        dma_keys = []
        for op in self.ops:
            if op.dma:
                k = ("dma", op.slot)
                if k not in cnt:
                    cnt[k] = 0
                    dma_keys.append(k)
                cnt[k] += 16
                op.val = cnt[k]
            elif op.sig:
                k = ("eng", op.eng)
                cnt[k] = cnt.get(k, 0) + 1
                op.val = cnt[k]
        self.maxvals = dict(cnt)
        sems = {}
        with contextlib.ExitStack() as es:
            for e in ENGS:
                sems[("eng", e)] = es.enter_context(nc.semaphore("s_" + e))
            for i, k in enumerate(dma_keys):
                sems[k] = es.enter_context(nc.semaphore("d%d" % i))
            self.nsems = len(sems)
            block = es.enter_context(nc.Block())
            ops = self.ops

            def run(engname, eng):
                waited = {}
                for op in ops:
                    if op.eng != engname:
                        continue
                    for k, d in op.deps.items():
                        if waited.get(k, 0) >= d.val:
                            continue
                        eng.wait_ge(sems[k], d.val)
                        waited[k] = d.val
                    ins = op.fn(eng)
                    if op.dma:
                        ins.then_inc(sems[("dma", op.slot)], 16)
                    elif op.sig:
                        ins.then_inc(sems[("eng", op.eng)], 1)
                last = {}
                for op in ops:
                    if op.eng == engname and op.dma:
                        last[("dma", op.slot)] = max(op.val, last.get(("dma", op.slot), 0))
                for k, v in last.items():
                    if waited.get(k, 0) < v:
                        eng.wait_ge(sems[k], v)

            @block.tensor
            def _(e):
                run("pe", e)

            @block.scalar
            def _(e):
                run("act", e)

            @block.vector
            def _(e):
                run("dve", e)

            @block.gpsimd
            def _(e):
                run("pool", e)

            @block.sync
            def _(e):
                run("sp", e)


def run_staged(S, fn, seg, group=4):
    for g0 in range(0, len(seg), group):
        recs = []
        for ti in range(g0, min(g0 + group, len(seg))):
            S.begin_record()
            fn(ti, seg[ti])
            recs.append(S.end_record())
        S.replay_staged(recs)


class T:
    def __init__(self, h, key):
        self.h = h
        self.key = key

    def __getitem__(self, idx):
        return self.h[idx]


def _dsize(dt):
    return 2 if dt == BF16 else 4


PF_COLS = ([0, 128] + [256, 384] + [512, 640] + [1024, 1152] + [1280, 1408] + [1536, 1664]
           + [2048, 2176, 2304, 2432] + [2560 + 128 * i for i in range(8)])
PF_Q, PF_FF, PF_FB, PF_G, PF_QA, PF_KA, PF_Z, PF_XBC = 0, 2, 4, 6, 8, 10, 12, 16
PF_NB = 24
PF_TR = ["silu"] * 2 + ["copy"] * 4 + ["silu"] * 2 + ["s8"] * 2 + ["copy"] * 2 + ["silu"] * 4 + ["copy"] * 8
PT_GROUPS = [(256, 768, 0), (768, 1024, 512), (1792, 2048, 768), (2048, 2560, 1024), (3584, 3600, 1536)]
PT_FF, PT_FB, PT_I, PT_VA, PT_Z, PT_DT = 0, 256, 512, 768, 1024, 1536
PT_W = 1552

FV_L = 72 + 8 + 8 + 8 + 2 + 4 + 40 + 8
FV_BMOD, FV_NF1, FV_NMX, FV_NF2, FV_HGN, FV_SSN, FV_CW, FV_CB = 0, 72, 80, 88, 96, 98, 102, 142
FV_G = DEPTH * FV_L
FV_C, FV_FN, FV_LB = FV_G, FV_G + 16, FV_G + 24
FV_N = FV_G + 24 + 8
RV_L = 256 + 512 + 16 + 16 + 512
RV_NAN, RV_DSK, RV_ALOG, RV_DTB, RV_SSN = 0, 256, 768, 784, 800
RV_G = DEPTH * RV_L
RV_LBR = RV_G
RV_N = RV_G + 1024


class KB:
    def __init__(self, nlat=64, dbg=(), layers=DEPTH, stop_after=None, parts=("hg", "ssd", "na")):
        self.parts = set(parts)
        self.nlat = nlat
        self.NT = nlat + 2
        self.NTOK = 128 * self.NT
        self.NLTOK = 128 * nlat
        self.dbg = set(dbg)
        self.layers = layers
        self.stop_after = stop_after
        self.nc = nc = bass.Bass("TRN2", target_bir_lowering=False)
        self.S = Sched(nc)
        self.uid = 0
        self.sb_lo = 16512
        self.sb_hi = 229344
        self.off = self.sb_lo
        self.NB = 512
        assert nlat % 4 == 0
        self.nblk = 1 + self.NLTOK // self.NB
        self.declare_io()

    def sb(self, name, shape, dt):
        size = int(np.prod(shape[1:])) * _dsize(dt)
        size = (size + 31) // 32 * 32
        assert self.off + size <= self.sb_hi, (name, self.off, size)
        self.uid += 1
        h = self.nc.alloc_sbuf_tensor_at("%s_%d" % (name, self.uid), list(shape), dt, offset=self.off)
        self.off += size
        return T(h, "%s_%d" % (name, self.uid))

    def mark(self):
        return self.off

    def reset(self, m):
        self.off = m

    def barrier(self):
        scr = self.scr
        self.S.add("dve", lambda e: e.memset(scr[:, 0:1], 0.0), writes=[scr.key], barrier=True)

    def declare_io(self):
        nc = self.nc
        ein = lambda n, s, dt=F32: nc.dram_tensor(n, list(s), dt, kind="ExternalInput").ap()
        self.xT = ein("xT", [D, self.NTOK])
        self.fvec = ein("fvec", [128, FV_N])
        self.rvec = ein("rvec", [128, RV_N])
        self.w_mod = ein("w_mod", [DEPTH, D, 9 * D])
        self.w13 = [ein("ffn1_w13", [DEPTH, D, 2 * DFF]), ein("ffn2_w13", [DEPTH, D, 2 * DFF])]
        self.w2 = [ein("ffn1_w2", [DEPTH, DFF, D]), ein("ffn2_w2", [DEPTH, DFF, D])]
        self.w_in = ein("w_in", [DEPTH, D, 3600])
        self.w_out = ein("w_out", [DEPTH, D, D])
        self.outT = nc.dram_tensor("outT", [D, self.NLTOK], F32, kind="ExternalOutput").ap()
        self.XT = nc.dram_tensor("XTs", [D, self.NTOK], F32).ap()
        self.PF = nc.dram_tensor("PFs", [PF_NB * 128, self.NTOK], F32).ap()
        self.PT = nc.dram_tensor("PTs", [self.NTOK, PT_W], F32).ap()
        self.MIXT = nc.dram_tensor("MIXTs", [D, self.NTOK], BF16).ap()
        self.dbg_out = {}
        _mixer_io(self)

    def dbg_tensor(self, name, shape, dt=F32):
        ap = self.nc.dram_tensor("dbg_" + name, list(shape), dt, kind="ExternalOutput").ap()
        self.dbg_out[name] = ap
        return ap

    def setup_persist(self):
        S = self.S
        self.scr = self.sb("scr", [128, 8], F32)
        self.fv = self.sb("fv", [128, FV_N], F32)
        self.ones_bf = self.sb("ones", [128, 128], BF16)
        self.MOD = self.sb("MOD", [128, 2, 72], F32)
        self.scb = self.sb("scb", [128, 16], F32)
        fv, ones = self.fv, self.ones_bf
        S.add("sp", lambda e: e.dma_start(out=fv[:], in_=self.fvec), writes=[fv.key], dma=True)
        S.add("dve", lambda e: e.memset(ones[:], 1.0), writes=[ones.key])
        scb = self.scb
        S.add("act", lambda e: e.activation(out=scb[:], in_=fv[:, FV_C:FV_C + 16], func=AF.Silu),
              reads=[fv.key], writes=[scb.key])
        self.persist_end = self.mark()

    def phase_mod(self, l):
        S, nc = self.S, self.nc
        m = self.mark()
        HALF = 4608
        wm = [self.sb("wm", [128, HALF], F32) for _ in range(2)]
        psm = self.ps[0]
        scb, fv, MOD = self.scb, self.fv, self.MOD
        i = 0
        for k in range(8):
            for hf in range(2):
                w = wm[i % 2]
                i += 1
                src = self.w_mod[l, k * 128:(k + 1) * 128, hf * HALF:(hf + 1) * HALF]
                S.add("sp", lambda e, w=w, src=src: e.dma_start(out=w[:], in_=src), writes=[w.key], dma=True)
                for j in range(36):
                    fb = hf * 36 + j
                    S.add("pe", lambda e, w=w, j=j, fb=fb, k=k: e.matmul(
                        psm[:, fb * 2:fb * 2 + 2], lhsT=w[:, j * 128:(j + 1) * 128], rhs=scb[:, 2 * k:2 * k + 2],
                        start=(k == 0 and fb == 0), stop=(k == 7), skip_group_check=True),
                        reads=[w.key, scb.key], writes=[psm.key])
        bo = l * FV_L
        for s in range(2):
            S.add("dve", lambda e, s=s: e.tensor_tensor(out=MOD[:, s, :], in0=psm[:, s:144:2],
                                                       in1=fv[:, bo + FV_BMOD:bo + FV_BMOD + 72], op=ALU.add),
                  reads=[psm.key, fv.key], writes=[MOD.key])
        for s in range(2):
            for (js, nw) in ((1, FV_NF1), (4, FV_NMX), (7, FV_NF2)):
                S.add("dve", lambda e, s=s, js=js, nw=nw: e.scalar_tensor_tensor(
                    out=MOD[:, s, js * 8:js * 8 + 8], in0=MOD[:, s, js * 8:js * 8 + 8], scalar=1.0,
                    in1=fv[:, bo + nw:bo + nw + 8], op0=ALU.add, op1=ALU.mult),
                    reads=[MOD.key, fv.key], writes=[MOD.key])
                S.add("dve", lambda e, s=s, js=js: e.tensor_scalar_mul(
                    out=MOD[:, s, js * 8:js * 8 + 8], in0=MOD[:, s, js * 8:js * 8 + 8], scalar1=32.0),
                    reads=[MOD.key], writes=[MOD.key])
            for jg in (2, 8):
                S.add("dve", lambda e, s=s, jg=jg: e.tensor_scalar_mul(
                    out=MOD[:, s, jg * 8:jg * 8 + 8], in0=MOD[:, s, jg * 8:jg * 8 + 8], scalar1=0.5),
                    reads=[MOD.key], writes=[MOD.key])
        if "mod" in self.dbg:
            d = self.dbg_tensor("mod%d" % l, [128, 144])
            S.add("sp", lambda e: e.dma_start(out=d, in_=MOD[:].rearrange("p s c -> p (s c)")), reads=[MOD.key],
                  dma=True, semkey="dbg_mod%d" % l)
        self.barrier()
        self.reset(m)

    def load_w(self, name, src, K, N):
        kc = K // 128
        w = self.sb(name, [128, kc, N], BF16)
        for k in range(kc):
            s = src[k * 128:(k + 1) * 128, :]
            self.S.add("pool", lambda e, k=k, s=s: e.dma_start(out=w[:, k, :], in_=s), writes=[w.key], dma=True)
        return w

    def alloc_work(self, nxb=2):
        NB = self.NB
        self.xb = [self.sb("xb", [128, 8, NB], F32) for _ in range(nxb)] * (2 // nxb)
        self.tk = [self.sb("tk", [128, NB], F32) for _ in range(2)]
        self.hb = self.sb("hb", [128, 8, NB], BF16)
        self.sq = self.hb
        self.rstd = self.sb("rstd", [128, NB], F32)

    def blk(self, i):
        if i == 0:
            return 0, CTX, 1
        return CTX + (i - 1) * self.NB, self.NB, 0

    def load_x(self, xb, src, n0, N):
        v = src.rearrange("(k p) n -> p k n", p=128)[:, :, n0:n0 + N]
        self.S.add("sp", lambda e: e.dma_start(out=xb[:, :, :N], in_=v), writes=[xb.key], dma=True)

    def store_x(self, xb, dst, n0, N, key="dram_x"):
        v = dst.rearrange("(k p) n -> p k n", p=128)[:, :, n0:n0 + N]
        self.S.add("sp", lambda e: e.dma_start(out=v, in_=xb[:, :, :N]), reads=[xb.key], dma=True,
                   semkey="st_" + xb.key)

    def modulate(self, xb, N, A, SH, out, out_keyed):
        S = self.S
        sq, rstd, ones = self.sq, self.rstd, self.ones_bf
        pss = self.ps[0]
        S.add("act", lambda e: e.activation(out=sq[:, :, :N], in_=xb[:, :, :N], func=AF.Square),
              reads=[xb.key], writes=[sq.key])
        for k in range(8):
            S.add("pe", lambda e, k=k: e.matmul(pss[:, :N], lhsT=ones[:], rhs=sq[:, k, :N], start=(k == 0), stop=(k == 7)),
                  reads=[sq.key, ones.key], writes=[pss.key])
        S.add("dve", lambda e: e.tensor_scalar_add(out=rstd[:, :N], in0=pss[:, :N], scalar1=float(D * EPS)),
              reads=[pss.key], writes=[rstd.key])
        S.add("act", lambda e: e.activation(out=rstd[:, :N], in_=rstd[:, :N], func=AF.Ln),
              reads=[rstd.key], writes=[rstd.key])
        S.add("act", lambda e: e.activation(out=rstd[:, :N], in_=rstd[:, :N], func=AF.Exp, scale=-0.5),
              reads=[rstd.key], writes=[rstd.key])
        for k in range(8):
            if SH is None:
                S.add("dve", lambda e, k=k: e.scalar_tensor_tensor(out=out[:, k, :N], in0=xb[:, k, :N], scalar=A[:, k:k + 1],
                                                                   in1=rstd[:, :N], op0=ALU.mult, op1=ALU.mult),
                      reads=[xb.key, rstd.key, self.fv.key], writes=[out_keyed])
            else:
                tk = self.tk[k % 2]
                S.add("dve", lambda e, k=k, tk=tk: e.scalar_tensor_tensor(out=tk[:, :N], in0=xb[:, k, :N], scalar=A[:, k:k + 1],
                                                                          in1=rstd[:, :N], op0=ALU.mult, op1=ALU.mult),
                      reads=[xb.key, rstd.key, self.MOD.key], writes=[tk.key])
                S.add("act", lambda e, k=k, tk=tk: e.activation(out=out[:, k, :N], in_=tk[:, :N], func=AF.Identity,
                                                                bias=SH[:, k:k + 1]),
                      reads=[tk.key, self.MOD.key], writes=[out_keyed])

    def ffn(self, xb, N, s, jbase, w13b, w2b):
        S = self.S
        MOD = self.MOD
        A = MOD[:, s, (jbase + 1) * 8:(jbase + 2) * 8]
        SH = MOD[:, s, jbase * 8:(jbase + 1) * 8]
        G = MOD[:, s, (jbase + 2) * 8:(jbase + 3) * 8]
        hb, ab, sg = self.hb, self.ab, self.sg
        self.modulate(xb, N, A, SH, hb, hb.key)
        for j in range(22):
            pu, pg = self.ps[1 + j % 2], self.ps[3 + j % 2]
            for k in range(8):
                S.add("pe", lambda e, j=j, k=k, pu=pu: e.matmul(pu[:, :N], lhsT=w13b[:, k, j * 128:(j + 1) * 128],
                                                                rhs=hb[:, k, :N], start=(k == 0), stop=(k == 7)),
                      reads=[w13b.key, hb.key], writes=[pu.key])
            for k in range(8):
                S.add("pe", lambda e, j=j, k=k, pg=pg: e.matmul(pg[:, :N], lhsT=w13b[:, k, DFF + j * 128:DFF + (j + 1) * 128],
                                                                rhs=hb[:, k, :N], start=(k == 0), stop=(k == 7)),
                      reads=[w13b.key, hb.key], writes=[pg.key])
            sgj = sg[j % 2]
            S.add("act", lambda e, pg=pg, sgj=sgj: e.activation(out=sgj[:, :N], in_=pg[:, :N], func=AF.Silu),
                  reads=[pg.key], writes=[sgj.key])
            S.add("dve", lambda e, j=j, pu=pu, sgj=sgj: e.tensor_tensor(out=ab[:, j, :N], in0=sgj[:, :N], in1=pu[:, :N],
                                                                          op=ALU.mult),
                  reads=[pu.key, sgj.key], writes=[ab.key])
        for fb in range(8):
            po = self.ps[5 + fb % 2]
            for j in range(22):
                S.add("pe", lambda e, j=j, fb=fb, po=po: e.matmul(po[:, :N], lhsT=w2b[:, j, fb * 128:(fb + 1) * 128],
                                                                  rhs=ab[:, j, :N], start=(j == 0), stop=(j == 21)),
                      reads=[w2b.key, ab.key], writes=[po.key])
            S.add("dve", lambda e, fb=fb, po=po: e.scalar_tensor_tensor(out=xb[:, fb, :N], in0=po[:, :N],
                                                                        scalar=G[:, fb:fb + 1], in1=xb[:, fb, :N],
                                                                        op0=ALU.mult, op1=ALU.add),
                  reads=[po.key, xb.key, MOD.key], writes=[xb.key])

    def phase_ffn1(self, l):
        m = self.mark()
        w13b = self.load_w("w13b", self.w13[0][l], D, 2 * DFF)
        w2b = self.load_w("w2b", self.w2[0][l], DFF, D)
        self.alloc_work()
        self.ab = self.sb("ab", [128, 22, self.NB], BF16)
        self.sg = [self.sb("sg", [128, self.NB], F32) for _ in range(2)]
        src = self.xT if l == 0 else self.XT
        def ld(i):
            n0, N, s = self.blk(i)
            self.load_x(self.xb[i % 2], src, n0, N)
        ld(0)
        for i in range(self.nblk):
            if i + 1 < self.nblk:
                ld(i + 1)
            n0, N, s = self.blk(i)
            xb = self.xb[i % 2]
            self.ffn(xb, N, s, 0, w13b, w2b)
            self.store_x(xb, self.XT, n0, N, key="dram_x1")
        self.barrier()
        self.reset(m)
        if "x1" in self.dbg:
            self.dump_dram("x1_%d" % l, self.XT, [D, self.NTOK], "dram_x1")

    def dump_dram(self, name, src, shape, key, dt=F32):
        d = self.dbg_tensor(name, shape, dt)
        self.S.add("sp", lambda e: e.dma_start(out=d, in_=src), dma=True, semkey="dbg_" + name)
        self.barrier()

    def phase_inproj(self, l):
        S = self.S
        m = self.mark()
        winb = self.load_w("winb", self.w_in[l], D, 3600)
        self.alloc_work()
        NB = self.NB
        pfst = self.sb("pfst", [128, PF_NB, NB], F32)
        ptsts = [self.sb("ptst", [128, PT_W], F32) for _ in range(NB // 128)]
        MOD, hb = self.MOD, self.hb
        def do_blk(i):
            n0, N, s = self.blk(i)
            xb = self.xb[i % 2]
            self.modulate(xb, N, MOD[:, s, 32:40], MOD[:, s, 24:32], hb, hb.key)
            for bi, c0 in enumerate(PF_COLS):
                p = self.ps[1 + bi % 4]
                for k in range(8):
                    S.add("pe", lambda e, k=k, c0=c0, p=p: e.matmul(p[:, :N], lhsT=winb[:, k, c0:c0 + 128], rhs=hb[:, k, :N],
                                                                    start=(k == 0), stop=(k == 7)),
                          reads=[winb.key, hb.key], writes=[p.key])
                tr = PF_TR[bi]
                if tr == "silu":
                    S.add("act", lambda e, bi=bi, p=p: e.activation(out=pfst[:, bi, :N], in_=p[:, :N], func=AF.Silu),
                          reads=[p.key], writes=[pfst.key])
                elif tr == "s8":
                    S.add("dve", lambda e, bi=bi, p=p: e.tensor_scalar_mul(out=pfst[:, bi, :N], in0=p[:, :N], scalar1=0.125),
                          reads=[p.key], writes=[pfst.key])
                else:
                    S.add("dve", lambda e, bi=bi, p=p: e.tensor_copy(out=pfst[:, bi, :N], in_=p[:, :N]),
                          reads=[p.key], writes=[pfst.key])
            dst = self.PF.rearrange("(f p) n -> p f n", p=128)[:, :, n0:n0 + N]
            S.add("sp", lambda e, dst=dst: e.dma_start(out=dst, in_=pfst[:, :, :N]), reads=[pfst.key],
                  dma=True, semkey="dram_pf")
            for t in range(N // 128):
                ptst = ptsts[t]
                for gi, (c0, c1, d0) in enumerate(PT_GROUPS):
                    p = self.ps[5 + gi % 2]
                    wdt = c1 - c0
                    for k in range(8):
                        S.add("pe", lambda e, k=k, t=t, c0=c0, c1=c1, p=p, wdt=wdt: e.matmul(
                            p[:, :wdt], lhsT=hb[:, k, t * 128:(t + 1) * 128], rhs=winb[:, k, c0:c1],
                            start=(k == 0), stop=(k == 7)), reads=[winb.key, hb.key], writes=[p.key])
                    S.add("act", lambda e, ptst=ptst, d0=d0, wdt=wdt, p=p: e.copy(out=ptst[:, d0:d0 + wdt], in_=p[:, :wdt]),
                          reads=[p.key], writes=[ptst.key])
                dstt = self.PT[n0 + t * 128:n0 + (t + 1) * 128, :]
                S.add("sp", lambda e, ptst=ptst, dstt=dstt: e.dma_start(out=dstt, in_=ptst[:]), reads=[ptst.key],
                      dma=True, semkey="st_" + ptst.key)
        def ld(i):
            n0, N, s = self.blk(i)
            self.load_x(self.xb[i % 2], self.XT, n0, N)
        ld(0)
        for i in range(self.nblk):
            if i + 1 < self.nblk:
                ld(i + 1)
            do_blk(i)
        self.barrier()
        self.reset(m)
        if "proj" in self.dbg:
            self.dump_dram("pf_%d" % l, self.PF, [PF_NB * 128, self.NTOK], "dram_pf")
            self.dump_dram("pt_%d" % l, self.PT, [self.NTOK, PT_W], "dram_pt")

    def phase_out(self, l):
        S = self.S
        last = (l == DEPTH - 1)
        m = self.mark()
        w13b = self.load_w("w13b", self.w13[1][l], D, 2 * DFF)
        w2b = self.load_w("w2b", self.w2[1][l], DFF, D)
        woutb = self.load_w("woutb", self.w_out[l], D, D)
        self.alloc_work(1)
        NB = self.NB
        self.ab = self.sb("ab", [128, 22, NB], BF16)
        self.sg = [self.sb("sg", [128, NB], F32) for _ in range(2)]
        mixb = [T(self.ab.h, self.ab.key)] * 2
        MOD = self.MOD
        def do_blk(i):
            n0, N, s = self.blk(i)
            if last and s == 1:
                return
            xb = self.xb[i % 2]
            mb = mixb[i % 2]
            self.load_x(xb, self.XT, n0, N)
            v = self.MIXT.rearrange("(k p) n -> p k n", p=128)[:, :, n0:n0 + N]
            S.add("sp", lambda e, mb=mb, v=v: e.dma_start(out=mb[:, 0:8, :N], in_=v), writes=[mb.key], dma=True)
            G = MOD[:, s, 40:48]
            for fb in range(8):
                p = self.ps[5 + fb % 2]
                for k in range(8):
                    S.add("pe", lambda e, k=k, fb=fb, p=p, mb=mb: e.matmul(p[:, :N], lhsT=woutb[:, k, fb * 128:(fb + 1) * 128],
                                                                          rhs=mb[:, k, :N], start=(k == 0), stop=(k == 7)),
                          reads=[woutb.key, mb.key], writes=[p.key])
                S.add("dve", lambda e, fb=fb, p=p, xb=xb, G=G: e.scalar_tensor_tensor(out=xb[:, fb, :N], in0=p[:, :N],
                                                                                  scalar=G[:, fb:fb + 1], in1=xb[:, fb, :N],
                                                                                  op0=ALU.mult, op1=ALU.add),
                      reads=[p.key, xb.key, MOD.key], writes=[xb.key])
            self.ffn(xb, N, s, 6, w13b, w2b)
            if not last:
                self.store_x(xb, self.XT, n0, N, key="dram_x3")
            else:
                ob = xb
                fn = self.fn32
                self.modulate(xb, N, fn, None, ob, ob.key)
                v = self.outT.rearrange("(k p) n -> p k n", p=128)[:, :, n0 - CTX:n0 - CTX + N]
                S.add("sp", lambda e, v=v, ob=ob: e.dma_start(out=v, in_=ob[:, :, :N]), reads=[ob.key],
                      dma=True, semkey="st_" + ob.key)
        for i in range(self.nblk):
            do_blk(i)
        self.barrier()
        self.reset(m)
        if "x3" in self.dbg and not last:
            self.dump_dram("x3_%d" % l, self.XT, [D, self.NTOK], "dram_x3")

    def build(self):
        S = self.S
        big = [self.nc.alloc_psum_tensor("psb%d" % i, [128, 1024], F32) for i in range(4)]
        self.ps = [T(big[i // 2][:, (i % 2) * 512:(i % 2) * 512 + 512], "ps%d" % i) for i in range(8)]
        self.psbig = [T(big[i], "psB%d" % i) for i in range(4)]
        self.setup_persist()
        self.fn32 = None
        for l in range(self.layers):
            if not getattr(self, "only_mixer", False):
                self.phase_mod(l)
                if self.stop_after == ("mod", l):
                    break
                self.phase_ffn1(l)
                if self.stop_after == ("ffn1", l):
                    break
                self.phase_inproj(l)
                if self.stop_after == ("inproj", l):
                    break
            self.phase_mixer(l)
            if self.stop_after == ("mixer", l):
                break
            self.phase_out_wrap(l)
        S.emit()
        return self.nc

    def phase_out_wrap(self, l):
        last = (l == DEPTH - 1)
        if last:
            m = self.mark()
            fn32 = self.sb("fn32", [128, 8], F32)
            fv = self.fv
            self.S.add("dve", lambda e: e.tensor_scalar_mul(out=fn32[:], in0=fv[:, FV_FN:FV_FN + 8], scalar1=32.0),
                       reads=[fv.key], writes=[fn32.key])
            self.fn32 = fn32
            self.persist_tmp = self.mark()
            self.phase_out(l)
            self.reset(m)
        else:
            self.phase_out(l)

    def phase_mixer(self, l):
        m = self.mark()
        _setup_mixer_consts(self)
        if "ohg" in self.dbg:
            self.dbg_tensor("ohg%d" % l, [256, self.NTOK])
        if "yssm" in self.dbg:
            self.dbg_tensor("yssm%d" % l, [self.NTOK, 512])
        if "naraw" in self.dbg:
            self.dbg_tensor("naraw%d" % l, [self.NTOK, 256])
        if "hg" in self.parts:
            _phase_hgrn2(self, l)
        if "ssd" in self.parts:
            _phase_ssd(self, l)
        if "na" in self.parts:
            _phase_na(self, l)
        if "mix" in self.dbg:
            self.dump_dram("mix_%d" % l, self.MIXT, [D, self.NTOK], "x", BF16)
        self.reset(m)


def fm(v):
    v = np.asarray(v, np.float32)
    return v.reshape(-1, 128).T


def prep_shared(inp):
    fv = np.zeros((128, FV_N), np.float32)
    rv = np.zeros((128, RV_N), np.float32)
    for l in range(DEPTH):
        o = l * FV_L
        fv[:, o + FV_BMOD:o + FV_BMOD + 72] = fm(inp["b_mod"][l])
        fv[:, o + FV_NF1:o + FV_NF1 + 8] = fm(inp["norm_ffn1"][l])
        fv[:, o + FV_NMX:o + FV_NMX + 8] = fm(inp["norm_mix"][l])
        fv[:, o + FV_NF2:o + FV_NF2 + 8] = fm(inp["norm_ffn2"][l])
        fv[:, o + FV_HGN:o + FV_HGN + 2] = fm(inp["hg_norm"][l])
        fv[:, o + FV_SSN:o + FV_SSN + 4] = fm(inp["ssm_norm"][l])
        for j in range(5):
            fv[:, o + FV_CW + j * 8:o + FV_CW + j * 8 + 8] = fm(inp["ssm_conv_w"][l, j])
        fv[:, o + FV_CB:o + FV_CB + 8] = fm(inp["ssm_conv_b"][l])
        r = l * RV_L
        rv[:, r + RV_NAN:r + RV_NAN + 256] = inp["na_norm"][l][None, :]
        rv[:, r + RV_DSK:r + RV_DSK + 512] = np.repeat(inp["ssm_d"][l], 64)[None, :]
        rv[:, r + RV_ALOG:r + RV_ALOG + 16] = inp["ssm_a_log"][l].reshape(-1)[None, :]
        rv[:, r + RV_DTB:r + RV_DTB + 16] = inp["ssm_dt_bias"][l].reshape(-1)[None, :]
        rv[:, r + RV_SSN:r + RV_SSN + 512] = inp["ssm_norm"][l][None, :]
    fv[:, FV_FN:FV_FN + 8] = fm(inp["final_norm"])
    for dr in range(2):
        for l in range(DEPTH):
            rv[:, RV_LBR + (dr * 2 + l) * 256:RV_LBR + (dr * 2 + l) * 256 + 256] = inp["hg_lower_bounds"][dr, l][None, :]
    for dr in range(2):
        for l in range(DEPTH):
            fv[:, FV_LB + dr * 4 + l * 2:FV_LB + dr * 4 + l * 2 + 2] = fm(inp["hg_lower_bounds"][dr, l])
    return fv, rv


def prep_core(inp, b, nlat, fv_shared):
    fv = fv_shared.copy()
    cc = np.stack([fm(inp["c"][b]), fm(inp["c_ctx"])], axis=2)
    fv[:, FV_C:FV_C + 16] = cc.reshape(128, 16)
    xT = np.ascontiguousarray(np.concatenate([inp["ctx"][b], inp["x"][b][:128 * nlat]], axis=0).T)
    return fv, xT


def make_in_maps(inp, nlat, batches):
    fvs, rv = prep_shared(inp)
    shared = {k: np.ascontiguousarray(inp[k], np.float32) for k in
              ("w_mod", "ffn1_w13", "ffn2_w13", "ffn1_w2", "ffn2_w2", "w_in", "w_out")}
    cmat = make_cmat()
    nab = np.stack([make_nabias(np.asarray(inp["na_rpb"][l], np.float32), nlat) for l in range(DEPTH)])
    maps = []
    for b in batches:
        fv, xT = prep_core(inp, b, nlat, fvs)
        m = dict(shared)
        m.update({"xT": xT, "fvec": fv, "rvec": rv, "cmat": cmat, "nabias": nab})
        maps.append(m)
    return maps


CM_ID, CM_M1F, CM_M2F, CM_M3F, CM_M1B, CM_M2B, CM_M3B = 0, 128, 256, 384, 512, 640, 768
CM_M4F, CM_M4B, CM_HM, CM_BLK, CM_TRIF, CM_TRIB, CM_NEGF, CM_NEGB, CM_ONES = 896, 900, 904, 1160, 1288, 1416, 1544, 1672, 1800
CM_N = 1928
NEG = -30000.0


def make_cmat():
    c = np.zeros((128, CM_N), np.float32)
    u = np.arange(128)[:, None]
    t = np.arange(128)[None, :]
    same = (u // 32) == (t // 32)
    c[:, CM_ID:CM_ID + 128] = (u == t)
    mf = (t // 32) * 32 + 15
    c[:, CM_M1F:CM_M1F + 128] = same * (((u > mf) & (u <= t)) * 1.0 - ((u > t) & (u <= mf)) * 1.0)
    c[:, CM_M2F:CM_M2F + 128] = same & (u <= t)
    c[:, CM_M3F:CM_M3F + 128] = same & (u > t)
    mb = (t // 32) * 32 + 16
    c[:, CM_M1B:CM_M1B + 128] = same * (((u >= t) & (u < mb)) * 1.0 - ((u >= mb) & (u < t)) * 1.0)
    c[:, CM_M2B:CM_M2B + 128] = same & (u >= t)
    c[:, CM_M3B:CM_M3B + 128] = same & (u < t)
    j = np.arange(4)[None, :]
    c[:, CM_M4F:CM_M4F + 4] = (u // 32) == j
    c[:, CM_M4B:CM_M4B + 4] = (u // 32) == (3 - j)
    col = np.arange(128)[None, :]
    c[:, CM_HM:CM_HM + 128] = (col // 64 == 0)
    c[:, CM_HM + 128:CM_HM + 256] = (col // 64 == 1)
    c[:, CM_BLK:CM_BLK + 128] = (u // 64) == (t // 64)
    c[:, CM_TRIF:CM_TRIF + 128] = (u <= t)
    c[:, CM_TRIB:CM_TRIB + 128] = (u >= t)
    c[:, CM_NEGF:CM_NEGF + 128] = NEG * (u > t)
    c[:, CM_NEGB:CM_NEGB + 128] = NEG * (u < t)
    c[:, CM_ONES:CM_ONES + 128] = 1.0
    return c


def _mixer_io(self):
    nc = self.nc
    self.cmat = nc.dram_tensor("cmat", [128, CM_N], F32, kind="ExternalInput").ap()
    self.OHG = nc.dram_tensor("OHGs", [256, self.NTOK], F32).ap()
    _ssd_io(self)
    _na_io(self)


def _setup_mixer_consts(self):
    S = self.S
    self.cm = cm = self.sb("cm", [128, CM_N], F32)
    S.add("sp", lambda e: e.dma_start(out=cm[:], in_=self.cmat), writes=[cm.key], dma=True)
    self.rv = rv = self.sb("rv", [128, RV_N], F32)
    S.add("sp", lambda e: e.dma_start(out=rv[:], in_=self.rvec), writes=[rv.key], dma=True)
    self.blk_bf = blk = self.sb("blkbf", [128, 128], BF16)
    S.add("dve", lambda e: e.tensor_copy(out=blk[:], in_=cm[:, CM_BLK:CM_BLK + 128]), reads=[cm.key], writes=[blk.key])


def _hg_tiles(self, d):
    NT = self.NT
    chain = list(range(NT)) if d == 0 else [1, 0] + list(range(NT - 1, 1, -1))
    return [chain[0:2]] + [chain[i:i + 8] for i in range(2, NT, 8)]


def _phase_hgrn2(self, l):
    S = self.S
    blkbf_l = self.blk_bf
    m = self.mark()
    cm, fv, rv = self.cm, self.fv, self.rv
    ps = self.ps
    LBt = self.sb("LBt", [128, 2, 256], F32)
    OMLt = self.sb("OMLt", [128, 2, 256], F32)
    omlf = self.sb("omlf", [128, 2, 2], F32)
    if l == 0:
        S.add("dve", lambda e: e.memset(LBt[:], 0.0), writes=[LBt.key])
        S.add("dve", lambda e: e.memset(OMLt[:], 1.0), writes=[OMLt.key])
        S.add("dve", lambda e: e.memset(omlf[:], 1.0), writes=[omlf.key])
    else:
        for d in range(2):
            a0 = rv[:, RV_LBR + (d * 2 + 0) * 256:RV_LBR + (d * 2 + 0) * 256 + 256]
            a1 = rv[:, RV_LBR + (d * 2 + 1) * 256:RV_LBR + (d * 2 + 1) * 256 + 256]
            S.add("dve", lambda e, d=d, a0=a0, a1=a1: e.tensor_tensor(out=LBt[:, d, :], in0=a1, in1=a0, op=ALU.subtract),
                  reads=[rv.key], writes=[LBt.key])
            f0 = fv[:, FV_LB + d * 4:FV_LB + d * 4 + 2]
            f1 = fv[:, FV_LB + d * 4 + 2:FV_LB + d * 4 + 4]
            S.add("dve", lambda e, d=d, f0=f0, f1=f1: e.tensor_tensor(out=omlf[:, d, :], in0=f1, in1=f0, op=ALU.subtract),
                  reads=[fv.key], writes=[omlf.key])
        S.add("act", lambda e: e.activation(out=LBt[:], in_=LBt[:], func=AF.Sigmoid), reads=[LBt.key], writes=[LBt.key])
        S.add("act", lambda e: e.activation(out=omlf[:], in_=omlf[:], func=AF.Sigmoid), reads=[omlf.key], writes=[omlf.key])
        S.add("dve", lambda e: e.tensor_scalar(out=OMLt[:], in0=LBt[:], scalar1=-1.0, scalar2=1.0, op0=ALU.mult, op1=ALU.add),
              reads=[LBt.key], writes=[OMLt.key])
        S.add("dve", lambda e: e.tensor_scalar(out=omlf[:], in0=omlf[:], scalar1=-1.0, scalar2=1.0, op0=ALU.mult, op1=ALU.add),
              reads=[omlf.key], writes=[omlf.key])
    omlfh = self.sb("omlfh", [128, 2, 2, 2], F32)
    for d_ in range(2):
        for pr_ in range(2):
            for h2_ in range(2):
                S.add("dve", lambda e, d_=d_, pr_=pr_, h2_=h2_: e.tensor_tensor(
                    out=omlfh[:, d_, pr_, h2_:h2_ + 1], in0=omlf[:, d_, pr_:pr_ + 1],
                    in1=cm[:, CM_BLK + h2_ * 64:CM_BLK + h2_ * 64 + 1], op=ALU.mult),
                    reads=[omlf.key, cm.key], writes=[omlfh.key])
    D1 = [self.sb("D1", [128, 64, 33], F32) for _ in range(2)]
    SO = [self.sb("SO", [128, 64, 33], F32) for _ in range(2)]
    D0 = [self.sb("D0", [128, 64, 33], F32) for _ in range(2)]
    Sblk = [self.sb("Sblk", [128, 32, 128], BF16) for _ in range(2)]
    DEC = self.sb("DEC", [128, 2, 32], F32)
    ATs = [self.sb("ATs", [128, 4, 128], BF16) for _ in range(8)]
    QHs = [self.sb("QHs", [128, 2, 128], BF16) for _ in range(8)]
    VZs = [self.sb("VZs", [128, 2, 2, 128], BF16) for _ in range(8)]
    rtok_R = [self.sb("rtok", [128, 256], F32) for _ in range(4)]
    vtok_R = [self.sb("vtok", [128, 256], F32) for _ in range(4)]
    vtok_bf_R = [self.sb("vtok_bf", [128, 256], BF16) for _ in range(4)]
    qfm_R = [self.sb("qfm", [128, 2, 128], F32) for _ in range(4)]
    rfm_R = [self.sb("rfm", [128, 2, 128], F32) for _ in range(4)]
    sig_R = [self.sb("sig", [128, 256], F32) for _ in range(4)]
    tmpk_R = [self.sb("tmpk", [128, 256], F32) for _ in range(4)]
    lf_R = [self.sb("lf", [128, 256], F32) for _ in range(4)]
    ktok_R = [self.sb("ktok", [128, 256], F32) for _ in range(4)]
    sneg_R = [self.sb("sneg", [128, 2, 128], F32) for _ in range(4)]
    P1c_R = [self.sb("P1c", [128, 256], F32) for _ in range(4)]
    Ep_R = [self.sb("Ep", [128, 256], F32) for _ in range(4)]
    En_R = [self.sb("En", [128, 256], F32) for _ in range(4)]
    E2_R = [self.sb("E2", [128, 256], F32) for _ in range(4)]
    E3_R = [self.sb("E3", [128, 256], F32) for _ in range(4)]
    qt_R = [self.sb("qt", [128, 2, 128], BF16) for _ in range(4)]
    kt_R = [self.sb("kt", [128, 2, 2, 128], BF16) for _ in range(4)]
    khat_R = [self.sb("khat", [128, 4, 256], BF16) for _ in range(4)]
    of_ld_R = [self.sb("of_ld", [128, 2, 128], F32) for _ in range(4)]
    osum_R = [self.sb("osum", [128, 2, 128], F32) for _ in range(4)]
    osq_R = [self.sb("osq", [128, 2, 128], BF16) for _ in range(4)]
    orstd_R = [self.sb("orstd", [128, 256], F32) for _ in range(4)]
    sgl_R = [self.sb("sgl", [128, 2, 128], F32) for _ in range(4)]
    hgo_R = [self.sb("hgo", [128, 2, 128], BF16) for _ in range(4)]
    for pr in range(2):
        S.add("dve", lambda e, pr=pr: e.memset(Sblk[pr][:], 0.0), writes=[Sblk[pr].key])
        S.add("dve", lambda e, pr=pr: e.memset(D0[pr][:], 0.0), writes=[D0[pr].key])
        S.add("dve", lambda e, pr=pr: e.memset(D1[pr][:], 0.0), writes=[D1[pr].key])
    PFr = self.PF.rearrange("(f p) n -> p f n", p=128)
    OHGr = self.OHG.rearrange("(f p) n -> p f n", p=128)
    MIXr = self.MIXT.rearrange("(f p) n -> p f n", p=128)
    pA, pB, pC, pU, pO, pN = ps[1], ps[2], ps[3], ps[4], ps[5], ps[6]
    def do_dir(d):
        M1 = cm[:, (CM_M1F, CM_M1B)[d]:(CM_M1F, CM_M1B)[d] + 128]
        M2 = cm[:, (CM_M2F, CM_M2B)[d]:(CM_M2F, CM_M2B)[d] + 128]
        M3 = cm[:, (CM_M3F, CM_M3B)[d]:(CM_M3F, CM_M3B)[d] + 128]
        M4 = cm[:, (CM_M4F, CM_M4B)[d]:(CM_M4F, CM_M4B)[d] + 4]
        M4n = cm[:, CM_M4F:CM_M4F + 4]
        for pr in range(2):
            S.add("dve", lambda e, pr=pr: e.memset(D1[pr][:, :, 0:1], 0.0), writes=[D1[pr].key])
        def do_seg(seg):
            nt = len(seg)
            nch = 4 * nt
            def p1(ti, tile):
                n0 = tile * 128
                rtok, vtok, vtok_bf, qfm, rfm, sig, tmpk, lf, ktok, sneg, P1c, Ep, En, E2, E3, qt, kt, khat = rtok_R[ti % 4], vtok_R[ti % 4], vtok_bf_R[ti % 4], qfm_R[ti % 4], rfm_R[ti % 4], sig_R[ti % 4], tmpk_R[ti % 4], lf_R[ti % 4], ktok_R[ti % 4], sneg_R[ti % 4], P1c_R[ti % 4], Ep_R[ti % 4], En_R[ti % 4], E2_R[ti % 4], E3_R[ti % 4], qt_R[ti % 4], kt_R[ti % 4], khat_R[ti % 4]
                pA, pB = (ps[1], ps[2]) if ti % 2 == 0 else (ps[0], ps[7])
                AT, QH, VZ = ATs[ti], QHs[ti], VZs[ti]
                S.add("sp", lambda e, n0=n0: e.dma_start(out=rtok[:], in_=self.PT[n0:n0 + 128, (PT_FF, PT_FB)[d]:(PT_FF, PT_FB)[d] + 256]),
                      writes=[rtok.key], dma=True)
                S.add("sp", lambda e, n0=n0: e.dma_start(out=vtok[:], in_=self.PT[n0:n0 + 128, PT_I:PT_I + 256]),
                      writes=[vtok.key], dma=True)
                S.add("sp", lambda e, n0=n0: e.dma_start(out=qfm[:], in_=PFr[:, PF_Q:PF_Q + 2, n0:n0 + 128]),
                      writes=[qfm.key], dma=True)
                fbk = (PF_FF, PF_FB)[d]
                S.add("sp", lambda e, n0=n0, fbk=fbk: e.dma_start(out=rfm[:], in_=PFr[:, fbk:fbk + 2, n0:n0 + 128]),
                      writes=[rfm.key], dma=True)
                S.add("act", lambda e: e.activation(out=sig[:], in_=rtok[:], func=AF.Sigmoid), reads=[rtok.key], writes=[sig.key])
                S.add("act", lambda e: e.copy(out=vtok_bf[:], in_=vtok[:]), reads=[vtok.key], writes=[vtok_bf.key])
                S.add("act", lambda e: e.activation(out=sneg[:], in_=rfm[:], func=AF.Sigmoid, scale=-1.0),
                      reads=[rfm.key], writes=[sneg.key])
                S.add("dve", lambda e: e.tensor_tensor(out=tmpk[:], in0=sig[:], in1=OMLt[:, d, :], op=ALU.mult),
                      reads=[sig.key, OMLt.key], writes=[tmpk.key])
                S.add("dve", lambda e: e.scalar_tensor_tensor(out=lf[:], in0=tmpk[:], scalar=1e-20, in1=LBt[:, d, :],
                                                              op0=ALU.max, op1=ALU.add),
                      reads=[tmpk.key, LBt.key], writes=[lf.key])
                S.add("dve", lambda e: e.tensor_tensor(out=ktok[:], in0=OMLt[:, d, :], in1=tmpk[:], op=ALU.subtract),
                      reads=[tmpk.key, OMLt.key], writes=[ktok.key])
                S.add("act", lambda e: e.activation(out=lf[:], in_=lf[:], func=AF.Ln), reads=[lf.key], writes=[lf.key])
                S.next_stage()
                for pr in range(2):
                    S.add("pe", lambda e, pr=pr: e.matmul(pA[:, pr * 128:(pr + 1) * 128], lhsT=lf[:, pr * 128:(pr + 1) * 128], rhs=M1,
                                                          start=True, stop=True), reads=[lf.key, cm.key], writes=[pA.key])
                for pr in range(2):
                    S.add("pe", lambda e, pr=pr: e.matmul(pA[:, 256 + pr * 128:256 + (pr + 1) * 128], lhsT=lf[:, pr * 128:(pr + 1) * 128],
                                                          rhs=M2, start=True, stop=True), reads=[lf.key, cm.key], writes=[pA.key])
                S.add("pe", lambda e: e.matmul(pB[:, 0:256], lhsT=M3, rhs=lf[:], start=True, stop=True),
                      reads=[lf.key, cm.key], writes=[pB.key])
                for pr in range(2):
                    S.add("pe", lambda e, pr=pr: e.matmul(pB[:, 256 + pr * 4:260 + pr * 4], lhsT=lf[:, pr * 128:(pr + 1) * 128], rhs=M4,
                                                          start=True, stop=True), reads=[lf.key, cm.key], writes=[pB.key])
                S.add("dve", lambda e: e.tensor_scalar(out=P1c[:], in0=pA[:, 0:256], scalar1=40.0, scalar2=-40.0, op0=ALU.min, op1=ALU.max),
                      reads=[pA.key], writes=[P1c.key])
                S.add("act", lambda e: e.activation(out=Ep[:], in_=P1c[:], func=AF.Exp), reads=[P1c.key], writes=[Ep.key])
                S.add("act", lambda e: e.activation(out=En[:], in_=P1c[:], func=AF.Exp, scale=-1.0), reads=[P1c.key], writes=[En.key])
                S.add("act", lambda e: e.activation(out=E2[:], in_=pA[:, 256:512], func=AF.Exp), reads=[pA.key], writes=[E2.key])
                S.add("act", lambda e: e.activation(out=E3[:], in_=pB[:, 0:256], func=AF.Exp), reads=[pB.key], writes=[E3.key])
                c0 = ti * 4
                for pr in range(2):
                    S.add("act", lambda e, c0=c0, pr=pr: e.activation(out=DEC[:, pr, c0:c0 + 4], in_=pB[:, 256 + pr * 4:260 + pr * 4],
                                                                      func=AF.Exp), reads=[pB.key], writes=[DEC.key])
                qf2 = qfm[:].rearrange("p a b -> p (a b)")
                S.add("dve", lambda e: e.tensor_tensor(out=qt[:].rearrange("p a b -> p (a b)"), in0=qf2, in1=Ep[:], op=ALU.mult),
                      reads=[qfm.key, Ep.key], writes=[qt.key])
                S.add("dve", lambda e, QH=QH: e.tensor_tensor(out=QH[:].rearrange("p a b -> p (a b)"), in0=qf2, in1=E2[:], op=ALU.mult),
                      reads=[qfm.key, E2.key], writes=[QH.key])
                for pr in range(2):
                    for h2 in range(2):
                        S.add("dve", lambda e, pr=pr, h2=h2: e.scalar_tensor_tensor(
                            out=kt[:, pr, h2, :], in0=sneg[:, pr, :], scalar=omlfh[:, d, pr, h2:h2 + 1],
                            in1=En[:, pr * 128:(pr + 1) * 128], op0=ALU.mult, op1=ALU.mult),
                            reads=[sneg.key, omlfh.key, En.key], writes=[kt.key])
                for j in range(4):
                    S.add("dve", lambda e, j=j: e.scalar_tensor_tensor(out=khat[:, j, :], in0=ktok[:], scalar=M4n[:, j:j + 1], in1=E3[:],
                                                                       op0=ALU.mult, op1=ALU.mult),
                          reads=[ktok.key, cm.key, E3.key], writes=[khat.key])
                if d == 0 or True:
                    hm = cm[:, CM_HM:CM_HM + 256].rearrange("p (a b) -> p a b", a=2)
                    S.add("dve", lambda e, VZ=VZ, hm=hm: e.tensor_tensor(
                        out=VZ[:], in0=vtok[:].rearrange("p (a b) -> p a b", a=2).unsqueeze(2).to_broadcast([128, 2, 2, 128]),
                        in1=hm.unsqueeze(1).to_broadcast([128, 2, 2, 128]), op=ALU.mult),
                        reads=[vtok.key, cm.key], writes=[VZ.key])
                S.next_stage()
                for h in range(4):
                    pr, h2 = h // 2, h % 2
                    S.add("pe", lambda e, h=h, pr=pr, h2=h2: e.matmul(pC[:, h * 128:(h + 1) * 128], lhsT=kt[:, pr, h2, :],
                                                                      rhs=qt[:, pr, :], start=True, stop=True),
                          reads=[kt.key, qt.key], writes=[pC.key])
                msk = cm[:, (CM_M2F, CM_M2B)[d]:(CM_M2F, CM_M2B)[d] + 128]
                S.add("dve", lambda e, AT=AT, msk=msk: e.tensor_tensor(out=AT[:], in0=pC[:].rearrange("p (a b) -> p a b", a=4),
                                                                       in1=msk.unsqueeze(1).to_broadcast([128, 4, 128]), op=ALU.mult),
                      reads=[pC.key, cm.key], writes=[AT.key])
                S.next_stage()
                for pr in range(2):
                    for jj in range(4):
                        j = jj if d == 0 else 3 - jj
                        S.add("pe", lambda e, pr=pr, jj=jj, j=j: e.matmul(pU[:, jj * 128:(jj + 1) * 128], lhsT=khat[:, j, pr * 128:(pr + 1) * 128],
                                                                          rhs=vtok_bf[:, pr * 128:(pr + 1) * 128], start=True, stop=True),
                              reads=[khat.key, vtok_bf.key], writes=[pU.key])
                    for h2 in range(2):
                        src = pU[h2 * 64:(h2 + 1) * 64, :].rearrange("p (a b) -> p a b", a=4)[:, :, h2 * 64:(h2 + 1) * 64]
                        dstv = D1[pr][h2 * 64:(h2 + 1) * 64, :, 1 + c0:1 + c0 + 4].rearrange("p v c -> p c v")
                        if h2 == 0:
                            S.add("act", lambda e, pr=pr, src=src, dstv=dstv: e.copy(out=dstv, in_=src),
                                  reads=[pU.key], writes=[D1[pr].key])
                        else:
                            S.add("dve", lambda e, pr=pr, src=src, dstv=dstv: e.tensor_copy(out=dstv, in_=src),
                                  reads=[pU.key], writes=[D1[pr].key])
            run_staged(S, p1, seg)
            for pr in range(2):
                SB = Sblk[pr]
                S.add("dve", lambda e, pr=pr: e.tensor_copy(out=D0[pr][:, :, 1:1 + nch],
                                                            in_=DEC[:, pr, 0:nch].unsqueeze(1).to_broadcast([128, 64, nch])),
                      reads=[DEC.key], writes=[D0[pr].key])
                S.add("dve", lambda e, pr=pr: e.tensor_tensor_scan(
                    out=SO[pr][:].rearrange("p v c -> p (v c)"), data0=D0[pr][:].rearrange("p v c -> p (v c)"),
                    data1=D1[pr][:].rearrange("p v c -> p (v c)"), initial=0.0, op0=ALU.mult, op1=ALU.add),
                    reads=[D0[pr].key, D1[pr].key], writes=[SO[pr].key])
                S.add("act", lambda e, pr=pr, SB=SB: e.copy(out=SB[0:64, 0:nch, 0:64], in_=SO[pr][0:64, :, 0:nch].rearrange("p v c -> p c v")),
                      reads=[SO[pr].key], writes=[SB.key])
                S.add("act", lambda e, pr=pr, SB=SB: e.copy(out=SB[64:128, 0:nch, 64:128], in_=SO[pr][64:128, :, 0:nch].rearrange("p v c -> p c v")),
                      reads=[SO[pr].key], writes=[SB.key])
                S.add("dve", lambda e, pr=pr: e.tensor_copy(out=D1[pr][:, :, 0:1], in_=SO[pr][:, :, nch:nch + 1]),
                      reads=[SO[pr].key], writes=[D1[pr].key])
            def p2(ti, tile):
                n0 = tile * 128
                of_ld, osum, osq, orstd, sgl, hgo = of_ld_R[ti % 4], osum_R[ti % 4], osq_R[ti % 4], orstd_R[ti % 4], sgl_R[ti % 4], hgo_R[ti % 4]
                AT, QH, VZ = ATs[ti], QHs[ti], VZs[ti]
                for pr in range(2):
                    for h2 in range(2):
                        h = pr * 2 + h2
                        S.add("pe", lambda e, pr=pr, h2=h2, h=h, AT=AT, VZ=VZ: e.matmul(
                            pO[:, pr * 128:(pr + 1) * 128], lhsT=VZ[:, pr, h2, :], rhs=AT[:, h, :],
                            start=(pr == 0 and h2 == 0), stop=False, skip_group_check=True),
                            reads=[VZ.key, AT.key], writes=[pO.key])
                for pr in range(2):
                    for j in range(4):
                        jj = j if d == 0 else 3 - j
                        c = ti * 4 + jj
                        S.add("pe", lambda e, pr=pr, j=j, c=c, QH=QH: e.matmul(
                            pO[:, pr * 128 + j * 32:pr * 128 + (j + 1) * 32], lhsT=Sblk[pr][:, c, :], rhs=QH[:, pr, j * 32:(j + 1) * 32],
                            start=False, stop=(pr == 1 and j == 3), skip_group_check=True),
                            reads=[Sblk[pr].key, QH.key], writes=[pO.key])
                if d == 0:
                    S.add("act", lambda e: e.copy(out=osum[:].rearrange("p a b -> p (a b)"), in_=pO[:, 0:256]), reads=[pO.key], writes=[osum.key])
                    S.add("sp", lambda e, n0=n0: e.dma_start(out=OHGr[:, :, n0:n0 + 128], in_=osum[:]), reads=[osum.key], dma=True,
                          semkey="st_" + osum.key)
                else:
                    S.add("sp", lambda e, n0=n0: e.dma_start(out=of_ld[:], in_=OHGr[:, :, n0:n0 + 128]), writes=[of_ld.key], dma=True)
                    S.add("sp", lambda e, n0=n0: e.dma_start(out=sgl[:], in_=PFr[:, PF_G:PF_G + 2, n0:n0 + 128]), writes=[sgl.key], dma=True)
                    S.add("dve", lambda e: e.tensor_tensor(out=osum[:].rearrange("p a b -> p (a b)"), in0=of_ld[:].rearrange("p a b -> p (a b)"),
                                                           in1=pO[:, 0:256], op=ALU.add), reads=[of_ld.key, pO.key], writes=[osum.key])
                    if "ohg" in self.dbg:
                        S.add("sp", lambda e, n0=n0: e.dma_start(out=self.dbg_out["ohg%d" % l].rearrange("(f p) n -> p f n", p=128)[:, :, n0:n0 + 128],
                                                                 in_=osum[:]), reads=[osum.key], dma=True, semkey="dbg_ohg")
                    S.next_stage()
                    S.add("act", lambda e: e.activation(out=osq[:], in_=osum[:], func=AF.Square), reads=[osum.key], writes=[osq.key])
                    S.add("pe", lambda e: e.matmul(pN[:, 0:256], lhsT=blkbf_l[:], rhs=osq[:].rearrange("p a b -> p (a b)"),
                                                   start=True, stop=True), reads=[osq.key, blkbf_l.key], writes=[pN.key])
                    S.add("dve", lambda e: e.tensor_scalar(out=orstd[:], in0=pN[:, 0:256], scalar1=1.0 / 64, scalar2=EPS,
                                                           op0=ALU.mult, op1=ALU.add), reads=[pN.key], writes=[orstd.key])
                    S.add("act", lambda e: e.activation(out=orstd[:], in_=orstd[:], func=AF.Ln), reads=[orstd.key], writes=[orstd.key])
                    S.add("act", lambda e: e.activation(out=orstd[:], in_=orstd[:], func=AF.Exp, scale=-0.5), reads=[orstd.key], writes=[orstd.key])
                    S.add("dve", lambda e: e.tensor_tensor(out=osum[:].rearrange("p a b -> p (a b)"), in0=osum[:].rearrange("p a b -> p (a b)"),
                                                           in1=orstd[:], op=ALU.mult), reads=[osum.key, orstd.key], writes=[osum.key])
                    wo = l * FV_L + FV_HGN
                    for pr in range(2):
                        S.add("dve", lambda e, pr=pr: e.scalar_tensor_tensor(out=hgo[:, pr, :], in0=osum[:, pr, :], scalar=fv[:, wo + pr:wo + pr + 1],
                                                                             in1=sgl[:, pr, :], op0=ALU.mult, op1=ALU.mult),
                              reads=[osum.key, fv.key, sgl.key], writes=[hgo.key])
                    S.add("sp", lambda e, n0=n0: e.dma_start(out=MIXr[:, 0:2, n0:n0 + 128], in_=hgo[:]), reads=[hgo.key], dma=True,
                          semkey="st_" + hgo.key)
            run_staged(S, p2, seg)
        for seg in _hg_tiles(self, d):
            do_seg(seg)
        self.barrier()
    for d in range(2):
        do_dir(d)
    self.reset(m)


def _ssd_io(self):
    nc = self.nc
    self.XSs = nc.dram_tensor("XSs", [self.NTOK, 512], F32).ap()
    self.BCf = nc.dram_tensor("BCfs", [512, self.NTOK], BF16).ap()
    self.Bts = nc.dram_tensor("Bts", [self.NTOK, 256], BF16).ap()
    self.YS = nc.dram_tensor("YSs", [self.NTOK, 512], F32).ap()


def _phase_ssd(self, l):
    S = self.S
    m = self.mark()
    cm, fv, rv, ps = self.cm, self.fv, self.rv, self.ps
    NT = self.NT
    PFr = self.PF.rearrange("(f p) n -> p f n", p=128)
    BCr = self.BCf.rearrange("(f p) n -> p f n", p=128)
    MIXr = self.MIXT.rearrange("(f p) n -> p f n", p=128)
    IDENT = cm[:, CM_ID:CM_ID + 128]
    ONESF = cm[:, CM_ONES:CM_ONES + 128]
    fo = l * FV_L
    ro = l * RV_L
    xin_R = [self.sb("xin", [128, 8, 132], F32) for _ in range(2)]
    acc_R = [self.sb("acc", [128, 8, 128], F32) for _ in range(2)]
    ctmp_R = [self.sb("ctmp", [128, 8, 128], F32) for _ in range(2)]
    acc2_R = [self.sb("acc2", [128, 8, 128], F32) for _ in range(2)]
    dtmp_R = [self.sb("dtmp", [128, 8, 128], F32) for _ in range(2)]
    bcb_R = [self.sb("bcb", [128, 4, 128], BF16) for _ in range(2)]
    xs_st_R = [self.sb("xs_st", [128, 512], F32) for _ in range(2)]
    bt_st_R = [self.sb("bt_st", [128, 256], BF16) for _ in range(2)]
    pX, pBt = ps[1], ps[2]
    CW = fv[:, fo + FV_CW:fo + FV_CW + 40].rearrange("p (j k) -> p j k", j=5)
    CB = fv[:, fo + FV_CB:fo + FV_CB + 8]

    def conv_load(tile):
        n0 = tile * 128
        xin, acc, ctmp, bcb, xs_st, bt_st = xin_R[tile % 2], acc_R[tile % 2], ctmp_R[tile % 2], bcb_R[tile % 2], xs_st_R[tile % 2], bt_st_R[tile % 2]
        acc2, dtmp = acc2_R[tile % 2], dtmp_R[tile % 2]
        s_lo, s_hi = (0, CTX) if tile < 2 else (CTX, self.NTOK)
        lo, hi = max(n0 - 2, s_lo), min(n0 + 130, s_hi)
        S.add("dve", lambda e: e.memset(xin[:, :, 0:2], 0.0), writes=[xin.key])
        S.add("dve", lambda e: e.memset(xin[:, :, 130:132], 0.0), writes=[xin.key])
        S.add("sp", lambda e: e.dma_start(out=xin[:, :, lo - (n0 - 2):hi - (n0 - 2)], in_=PFr[:, PF_XBC:PF_XBC + 8, lo:hi]),
              writes=[xin.key], dma=True)

    def conv_rest(tile):
        n0 = tile * 128
        xin, acc, ctmp, bcb, xs_st, bt_st = xin_R[tile % 2], acc_R[tile % 2], ctmp_R[tile % 2], bcb_R[tile % 2], xs_st_R[tile % 2], bt_st_R[tile % 2]
        acc2, dtmp = acc2_R[tile % 2], dtmp_R[tile % 2]
        def cwb(j):
            return CW[:, j, :].unsqueeze(2).to_broadcast([128, 8, 128])
        S.add("pool", lambda e: e.tensor_tensor(out=acc2[:], in0=xin[:, :, 1:129], in1=cwb(1), op=ALU.mult),
              reads=[xin.key, fv.key], writes=[acc2.key])
        S.add("pool", lambda e: e.tensor_tensor(out=ctmp[:], in0=xin[:, :, 2:130], in1=cwb(2), op=ALU.mult),
              reads=[xin.key, fv.key], writes=[ctmp.key])
        S.add("pool", lambda e: e.tensor_tensor(out=acc2[:], in0=acc2[:], in1=ctmp[:], op=ALU.add),
              reads=[acc2.key, ctmp.key], writes=[acc2.key])
        S.add("dve", lambda e: e.tensor_tensor(out=acc[:], in0=xin[:, :, 0:128], in1=cwb(0), op=ALU.mult),
              reads=[xin.key, fv.key], writes=[acc.key])
        for j in (3, 4):
            S.add("dve", lambda e, j=j: e.tensor_tensor(out=dtmp[:], in0=xin[:, :, j:j + 128], in1=cwb(j), op=ALU.mult),
                  reads=[xin.key, fv.key], writes=[dtmp.key])
            S.add("dve", lambda e: e.tensor_tensor(out=acc[:], in0=acc[:], in1=dtmp[:], op=ALU.add),
                  reads=[acc.key, dtmp.key], writes=[acc.key])
        S.add("dve", lambda e: e.tensor_tensor(out=acc[:], in0=acc[:], in1=acc2[:], op=ALU.add),
              reads=[acc.key, acc2.key], writes=[acc.key])
        S.add("dve", lambda e: e.tensor_tensor(out=acc[:], in0=acc[:], in1=CB.unsqueeze(2).to_broadcast([128, 8, 128]), op=ALU.add),
              reads=[acc.key, fv.key], writes=[acc.key])
        S.add("act", lambda e: e.activation(out=acc[:], in_=acc[:], func=AF.Silu), reads=[acc.key], writes=[acc.key])
        S.add("dve", lambda e: e.tensor_copy(out=bcb[:], in_=acc[:, 4:8, :]), reads=[acc.key], writes=[bcb.key])
        S.add("sp", lambda e: e.dma_start(out=BCr[:, :, n0:n0 + 128], in_=bcb[:]), reads=[bcb.key], dma=True, semkey="st_" + bcb.key)
        for k in range(4):
            S.add("pe", lambda e, k=k: e.transpose(out=pX[:, k * 128:(k + 1) * 128], in_=acc[:, k, :], identity=IDENT),
                  reads=[acc.key, cm.key], writes=[pX.key])
        S.add("act", lambda e: e.copy(out=xs_st[:], in_=pX[:]), reads=[pX.key], writes=[xs_st.key])
        S.add("sp", lambda e: e.dma_start(out=self.XSs[n0:n0 + 128, :], in_=xs_st[:]), reads=[xs_st.key], dma=True, semkey="st_" + xs_st.key)
        for k in range(2):
            S.add("pe", lambda e, k=k: e.transpose(out=pBt[:, k * 128:(k + 1) * 128], in_=acc[:, 4 + k, :], identity=IDENT),
                  reads=[acc.key, cm.key], writes=[pBt.key])
        S.add("dve", lambda e: e.tensor_copy(out=bt_st[:], in_=pBt[:, 0:256]), reads=[pBt.key], writes=[bt_st.key])
        S.add("sp", lambda e: e.dma_start(out=self.Bts[n0:n0 + 128, :], in_=bt_st[:]), reads=[bt_st.key], dma=True, semkey="st_" + bt_st.key)

    conv_load(0)
    for tile in range(NT):
        if tile + 1 < NT:
            conv_load(tile + 1)
        conv_rest(tile)
    self.barrier()
    self.reset(m)
    m = self.mark()
    Arow = self.sb("Arow", [128, 16], F32)
    S.add("act", lambda e: e.activation(out=Arow[:], in_=rv[:, ro + RV_ALOG:ro + RV_ALOG + 16], func=AF.Exp),
          reads=[rv.key], writes=[Arow.key])
    S.add("dve", lambda e: e.tensor_scalar_mul(out=Arow[:], in0=Arow[:], scalar1=-1.0), reads=[Arow.key], writes=[Arow.key])
    DTB = rv[:, ro + RV_DTB:ro + RV_DTB + 16]
    D0 = [self.sb("sD0", [128, 256, 9], F32) for _ in range(2)]
    D1 = [self.sb("sD1", [128, 256, 9], F32) for _ in range(2)]
    SO = [self.sb("sSO", [128, 256, 9], F32) for _ in range(2)]
    Hbf = [self.sb("Hbf", [128, 8, 256], BF16) for _ in range(2)]
    YD = [self.sb("YD", [128, 512], F32) for _ in range(8)]
    CFs = [self.sb("CFs", [128, 2, 128], BF16) for _ in range(8)]
    ECs = [self.sb("ECs", [128, 8], F32) for _ in range(8)]
    dtr_R4 = [self.sb("dtr", [128, 8], F32) for _ in range(4)]
    dt_R4 = [self.sb("dt", [128, 8], F32) for _ in range(4)]
    av_R4 = [self.sb("av", [128, 8], F32) for _ in range(4)]
    ABC_R4 = [self.sb("ABC", [128, 8, 128], F32) for _ in range(4)]
    ATRI_R4 = [self.sb("ATRI", [128, 8, 128], F32) for _ in range(4)]
    ncum_R4 = [self.sb("ncum", [128, 8], F32) for _ in range(4)]
    dend_R4 = [self.sb("dend", [128, 8], F32) for _ in range(4)]
    dect_R4 = [self.sb("dect", [128, 8], F32) for _ in range(4)]
    xst_R4 = [self.sb("xst", [128, 512], F32) for _ in range(4)]
    btl_R4 = [self.sb("btl", [128, 256], BF16) for _ in range(4)]
    bcl_R4 = [self.sb("bcl", [128, 4, 128], BF16) for _ in range(4)]
    Lsb_R4 = [self.sb("Lsb", [128, 8, 128], F32) for _ in range(4)]
    Wb_R4 = [self.sb("Wb", [128, 8, 128], BF16) for _ in range(4)]
    xdt_R4 = [self.sb("xdt", [128, 8, 64], BF16) for _ in range(4)]
    xw_R4 = [self.sb("xw", [128, 8, 64], BF16) for _ in range(4)]
    xst2_R = [self.sb("xst2", [128, 512], F32) for _ in range(2)]
    yo_R = [self.sb("yo", [128, 512], F32) for _ in range(2)]
    yf_R = [self.sb("yf", [128, 512], F32) for _ in range(2)]
    zt_R = [self.sb("zt", [128, 512], F32) for _ in range(2)]
    ssq_R = [self.sb("ssq", [128, 2], F32) for _ in range(2)]
    yT_R = [self.sb("yT", [128, 4, 128], BF16) for _ in range(2)]
    IDb = self.sb("sIDb", [128, 128], BF16)
    S.add("dve", lambda e: e.tensor_copy(out=IDb[:], in_=IDENT), reads=[cm.key], writes=[IDb.key])
    NEG4 = [self.sb("NEG4", [128, 4, 128], BF16) for _ in range(2)]
    for d_ in range(2):
        ng = cm[:, (CM_NEGF, CM_NEGB)[d_]:(CM_NEGF, CM_NEGB)[d_] + 128]
        S.add("dve", lambda e, d_=d_, ng=ng: e.tensor_copy(out=NEG4[d_][:], in_=ng.unsqueeze(1).to_broadcast([128, 4, 128])),
              reads=[cm.key], writes=[NEG4[d_].key])
    for g in range(2):
        S.add("dve", lambda e, g=g: e.memset(D0[g][:], 0.0), writes=[D0[g].key])
        S.add("dve", lambda e, g=g: e.memset(D1[g][:], 0.0), writes=[D1[g].key])
    pS, pLa, pLb, pG, pY, pH, pY2, pT = ps[0], ps[1], ps[2], ps[3], ps[4], ps[5], ps[6], ps[7]
    SSNrow = rv[:, ro + RV_SSN:ro + RV_SSN + 512]
    DSKrow = rv[:, ro + RV_DSK:ro + RV_DSK + 512]

    def do_dir(d):
        TRI = cm[:, (CM_TRIF, CM_TRIB)[d]:(CM_TRIF, CM_TRIB)[d] + 128]
        NEGM = cm[:, (CM_NEGF, CM_NEGB)[d]:(CM_NEGF, CM_NEGB)[d] + 128]
        for g in range(2):
            S.add("dve", lambda e, g=g: e.memset(D1[g][:, :, 0:1], 0.0), writes=[D1[g].key])

        def do_seg(seg):
            nt = len(seg)

            def p1(ti, tile):
                n0 = tile * 128
                ATRI = ATRI_R4[ti % 4]
                dtr, dt, av, ABC, ncum, dend, dect, xst, btl, bcl, Lsb, Wb, xdt, xw = dtr_R4[ti % 4], dt_R4[ti % 4], av_R4[ti % 4], ABC_R4[ti % 4], ncum_R4[ti % 4], dend_R4[ti % 4], dect_R4[ti % 4], xst_R4[ti % 4], btl_R4[ti % 4], bcl_R4[ti % 4], Lsb_R4[ti % 4], Wb_R4[ti % 4], xdt_R4[ti % 4], xw_R4[ti % 4]
                S.add("sp", lambda e: e.dma_start(out=dtr[:], in_=self.PT[n0:n0 + 128, PT_DT + d * 8:PT_DT + d * 8 + 8]),
                      writes=[dtr.key], dma=True)
                S.add("sp", lambda e: e.dma_start(out=xst[:], in_=self.XSs[n0:n0 + 128, :]), writes=[xst.key], dma=True)
                S.add("sp", lambda e: e.dma_start(out=btl[:], in_=self.Bts[n0:n0 + 128, :]), writes=[btl.key], dma=True)
                S.add("sp", lambda e: e.dma_start(out=bcl[:], in_=BCr[:, :, n0:n0 + 128]), writes=[bcl.key], dma=True)
                S.add("dve", lambda e: e.tensor_tensor(out=dt[:], in0=dtr[:], in1=DTB[:, d * 8:d * 8 + 8], op=ALU.add),
                      reads=[dtr.key, rv.key], writes=[dt.key])
                S.add("act", lambda e: e.activation(out=dt[:], in_=dt[:], func=AF.Exp), reads=[dt.key], writes=[dt.key])
                S.add("dve", lambda e: e.tensor_scalar_add(out=dt[:], in0=dt[:], scalar1=1.0), reads=[dt.key], writes=[dt.key])
                S.add("act", lambda e: e.activation(out=dt[:], in_=dt[:], func=AF.Ln), reads=[dt.key], writes=[dt.key])
                S.add("dve", lambda e: e.tensor_tensor(out=av[:], in0=dt[:], in1=Arow[:, d * 8:d * 8 + 8], op=ALU.mult),
                      reads=[dt.key, Arow.key], writes=[av.key])
                S.add("dve", lambda e: e.tensor_scalar_mul(out=ABC[:], in0=av[:].unsqueeze(2).to_broadcast([128, 8, 128]), scalar1=-1.0),
                      reads=[av.key], writes=[ABC.key])
                S.add("dve", lambda e: e.tensor_tensor(out=ATRI[:], in0=TRI.unsqueeze(1).to_broadcast([128, 8, 128]),
                                                       in1=av[:].unsqueeze(2).to_broadcast([128, 8, 128]), op=ALU.mult),
                      reads=[av.key, cm.key], writes=[ATRI.key])
                S.next_stage()
                S.add("pe", lambda e: e.matmul(pS[:, 0:8], lhsT=TRI, rhs=av[:], start=True, stop=True), reads=[av.key, cm.key], writes=[pS.key])
                S.add("pe", lambda e: e.matmul(pS[:, 8:16], lhsT=ONESF, rhs=av[:], start=True, stop=True), reads=[av.key, cm.key], writes=[pS.key])
                S.add("dve", lambda e: e.tensor_scalar_mul(out=ncum[:], in0=pS[:, 0:8], scalar1=-1.0), reads=[pS.key], writes=[ncum.key])
                EC = ECs[ti]
                S.add("act", lambda e: e.activation(out=EC[:], in_=pS[:, 0:8], func=AF.Exp), reads=[pS.key], writes=[EC.key])
                S.add("dve", lambda e: e.tensor_tensor(out=dend[:], in0=pS[:, 8:16], in1=ncum[:], op=ALU.add),
                      reads=[pS.key, ncum.key], writes=[dend.key])
                S.add("act", lambda e: e.activation(out=dend[:], in_=dend[:], func=AF.Exp), reads=[dend.key], writes=[dend.key])
                S.add("act", lambda e: e.activation(out=dect[:], in_=pS[:, 8:16], func=AF.Exp), reads=[pS.key], writes=[dect.key])
                S.next_stage()
                for hb in range(2):
                    pl = (pLa, pLb)[hb]
                    S.add("pe", lambda e, hb=hb, pl=pl: e.matmul(pl[:, 0:512], lhsT=ONESF, rhs=ATRI[:, 4 * hb:4 * hb + 4, :].rearrange("p a b -> p (a b)"),
                                                                 start=True, stop=False, skip_group_check=True),
                          reads=[ATRI.key, cm.key], writes=[pl.key])
                    S.add("pe", lambda e, hb=hb, pl=pl: e.matmul(pl[:, 0:512], lhsT=TRI, rhs=ABC[:, 4 * hb:4 * hb + 4, :].rearrange("p a b -> p (a b)"),
                                                                 start=False, stop=False, skip_group_check=True),
                          reads=[ABC.key, cm.key], writes=[pl.key])
                    S.add("pe", lambda e, hb=hb, pl=pl: e.matmul(pl[:, 0:512], lhsT=IDb[:], rhs=NEG4[d][:].rearrange("p a b -> p (a b)"),
                                                                 start=False, stop=True, skip_group_check=True),
                          reads=[IDb.key, NEG4[d].key], writes=[pl.key])
                    S.add("act", lambda e, hb=hb, pl=pl: e.activation(out=Lsb[:, 4 * hb:4 * hb + 4, :].rearrange("p a b -> p (a b)"), in_=pl[:, 0:512], func=AF.Exp),
                          reads=[pl.key], writes=[Lsb.key])
                S.next_stage()
                for g in range(2):
                    S.add("pe", lambda e, g=g: e.matmul(pG[:, g * 128:(g + 1) * 128], lhsT=bcl[:, g, :], rhs=bcl[:, 2 + g, :],
                                                        start=True, stop=True), reads=[bcl.key], writes=[pG.key])
                for g in range(2):
                    S.add("dve", lambda e, g=g: e.tensor_tensor(
                        out=Wb[:, g * 4:(g + 1) * 4, :], in0=Lsb[:, g * 4:(g + 1) * 4, :],
                        in1=pG[:, g * 128:(g + 1) * 128].unsqueeze(1).to_broadcast([128, 4, 128]), op=ALU.mult),
                        reads=[Lsb.key, pG.key], writes=[Wb.key])
                S.add("dve", lambda e: e.tensor_tensor(out=xdt[:], in0=xst[:].rearrange("p (h q) -> p h q", h=8),
                                                       in1=dt[:].unsqueeze(2).to_broadcast([128, 8, 64]), op=ALU.mult),
                      reads=[xst.key, dt.key], writes=[xdt.key])
                S.next_stage()
                for h in range(8):
                    S.add("pe", lambda e, h=h: e.matmul(pY[:, h * 64:(h + 1) * 64], lhsT=Wb[:, h, :], rhs=xdt[:, h, :], start=True, stop=True),
                          reads=[Wb.key, xdt.key], writes=[pY.key])
                Y = YD[ti]
                S.add("act", lambda e: e.copy(out=Y[:], in_=pY[:]), reads=[pY.key], writes=[Y.key])
                CF = CFs[ti]
                S.add("dve", lambda e: e.tensor_copy(out=CF[:], in_=bcl[:, 2:4, :]), reads=[bcl.key], writes=[CF.key])
                S.add("dve", lambda e: e.tensor_tensor(out=xw[:], in0=xdt[:], in1=dend[:].unsqueeze(2).to_broadcast([128, 8, 64]), op=ALU.mult),
                      reads=[xdt.key, dend.key], writes=[xw.key])
                S.next_stage()
                for g in range(2):
                    S.add("pe", lambda e, g=g: e.matmul(pH[:, g * 256:(g + 1) * 256], lhsT=btl[:, g * 128:(g + 1) * 128],
                                                        rhs=xw[:, g * 4:(g + 1) * 4, :].rearrange("p h q -> p (h q)"), start=True, stop=True),
                          reads=[btl.key, xw.key], writes=[pH.key])
                for g in range(2):
                    S.add("act", lambda e, g=g: e.copy(out=D1[g][:, :, 1 + ti:2 + ti], in_=pH[:, g * 256:(g + 1) * 256].unsqueeze(2)),
                          reads=[pH.key], writes=[D1[g].key])
                    S.add("dve", lambda e, g=g: e.tensor_copy(
                        out=D0[g][:, :, 1 + ti:2 + ti].rearrange("p (h q) o -> p h (q o)", h=4),
                        in_=dect[:, g * 4:(g + 1) * 4].unsqueeze(2).to_broadcast([128, 4, 64])),
                        reads=[dect.key], writes=[D0[g].key])

            for g0 in range(0, nt, 4):
                recs = []
                for ti in range(g0, min(g0 + 4, nt)):
                    S.begin_record()
                    p1(ti, seg[ti])
                    recs.append(S.end_record())
                S.replay_staged(recs)
            for g in range(2):
                S.add("dve", lambda e, g=g: e.tensor_tensor_scan(
                    out=SO[g][:].rearrange("p v c -> p (v c)"), data0=D0[g][:].rearrange("p v c -> p (v c)"),
                    data1=D1[g][:].rearrange("p v c -> p (v c)"), initial=0.0, op0=ALU.mult, op1=ALU.add),
                    reads=[D0[g].key, D1[g].key], writes=[SO[g].key])
                S.add("act", lambda e, g=g: e.copy(out=Hbf[g][:, 0:nt, :], in_=SO[g][:, :, 0:nt].rearrange("p v c -> p c v")),
                      reads=[SO[g].key], writes=[Hbf[g].key])
                S.add("dve", lambda e, g=g: e.tensor_copy(out=D1[g][:, :, 0:1], in_=SO[g][:, :, nt:nt + 1]),
                      reads=[SO[g].key], writes=[D1[g].key])

            def p2(ti, tile):
                n0 = tile * 128
                yo, yf, zt, ssq, yT, xst2 = yo_R[ti % 2], yf_R[ti % 2], zt_R[ti % 2], ssq_R[ti % 2], yT_R[ti % 2], xst2_R[ti % 2]
                xst = xst2
                CF, EC, Y = CFs[ti], ECs[ti], YD[ti]
                for g in range(2):
                    S.add("pe", lambda e, g=g: e.matmul(pY2[:, g * 256:(g + 1) * 256], lhsT=CF[:, g, :], rhs=Hbf[g][:, ti, :], start=True, stop=True),
                          reads=[CF.key, Hbf[g].key], writes=[pY2.key])
                S.add("dve", lambda e: e.tensor_tensor(out=yo[:].rearrange("p (h q) -> p h q", h=8), in0=pY2[:].rearrange("p (h q) -> p h q", h=8),
                                                       in1=EC[:].unsqueeze(2).to_broadcast([128, 8, 64]), op=ALU.mult),
                      reads=[pY2.key, EC.key], writes=[yo.key])
                S.add("dve", lambda e: e.tensor_tensor(out=yo[:], in0=yo[:], in1=Y[:], op=ALU.add), reads=[yo.key, Y.key], writes=[yo.key])
                if d == 0:
                    S.add("sp", lambda e: e.dma_start(out=self.YS[n0:n0 + 128, :], in_=yo[:]), reads=[yo.key], dma=True, semkey="st_" + yo.key)
                    return
                S.add("sp", lambda e: e.dma_start(out=yf[:], in_=self.YS[n0:n0 + 128, :]), writes=[yf.key], dma=True)
                S.add("sp", lambda e: e.dma_start(out=xst[:], in_=self.XSs[n0:n0 + 128, :]), writes=[xst.key], dma=True)
                S.add("sp", lambda e: e.dma_start(out=zt[:], in_=self.PT[n0:n0 + 128, PT_Z:PT_Z + 512]), writes=[zt.key], dma=True)
                S.next_stage()
                S.add("dve", lambda e: e.tensor_tensor(out=yo[:], in0=yo[:], in1=yf[:], op=ALU.add), reads=[yo.key, yf.key], writes=[yo.key])
                S.add("dve", lambda e: e.tensor_tensor(out=xst[:], in0=xst[:], in1=DSKrow, op=ALU.mult), reads=[xst.key, rv.key], writes=[xst.key])
                S.add("dve", lambda e: e.tensor_tensor(out=yo[:], in0=yo[:], in1=xst[:], op=ALU.add), reads=[yo.key, xst.key], writes=[yo.key])
                if "yssm" in self.dbg:
                    S.add("sp", lambda e: e.dma_start(out=self.dbg_out["yssm%d" % l][n0:n0 + 128, :], in_=yo[:]), reads=[yo.key], dma=True,
                          semkey="dbg_yssm")
                S.add("act", lambda e: e.activation(out=zt[:], in_=zt[:], func=AF.Silu), reads=[zt.key], writes=[zt.key])
                S.add("dve", lambda e: e.tensor_tensor(out=yo[:], in0=yo[:], in1=zt[:], op=ALU.mult), reads=[yo.key, zt.key], writes=[yo.key])
                S.add("act", lambda e: e.activation(out=zt[:], in_=yo[:], func=AF.Square, accum_out=ssq[:, 0:1]),
                      reads=[yo.key], writes=[zt.key, ssq.key])
                S.add("dve", lambda e: e.tensor_scalar(out=ssq[:, 1:2], in0=ssq[:, 0:1], scalar1=1.0 / 512, scalar2=EPS, op0=ALU.mult, op1=ALU.add),
                      reads=[ssq.key], writes=[ssq.key])
                S.add("act", lambda e: e.activation(out=ssq[:, 1:2], in_=ssq[:, 1:2], func=AF.Ln), reads=[ssq.key], writes=[ssq.key])
                S.add("act", lambda e: e.activation(out=ssq[:, 1:2], in_=ssq[:, 1:2], func=AF.Exp, scale=-0.5), reads=[ssq.key], writes=[ssq.key])
                S.add("dve", lambda e: e.scalar_tensor_tensor(out=yo[:], in0=yo[:], scalar=ssq[:, 1:2], in1=SSNrow, op0=ALU.mult, op1=ALU.mult),
                      reads=[yo.key, ssq.key, rv.key], writes=[yo.key])
                S.next_stage()
                for k in range(4):
                    S.add("pe", lambda e, k=k: e.transpose(out=pT[:, k * 128:(k + 1) * 128], in_=yo[:, k * 128:(k + 1) * 128], identity=IDENT),
                          reads=[yo.key, cm.key], writes=[pT.key])
                S.add("act", lambda e: e.copy(out=yT[:].rearrange("p a b -> p (a b)"), in_=pT[:]), reads=[pT.key], writes=[yT.key])
                S.add("sp", lambda e: e.dma_start(out=MIXr[:, 4:8, n0:n0 + 128], in_=yT[:]), reads=[yT.key], dma=True, semkey="st_" + yT.key)

            run_staged(S, p2, seg, group=2)

        for seg in _hg_tiles(self, d):
            do_seg(seg)
        self.barrier()

    for d in range(2):
        do_dir(d)
    self.reset(m)


NAB_N = 5 * 4 * 5 * 128


def make_nabias(rpb, nlat):
    rows = 2 * nlat
    out = np.full((128, 5, 4, 5, 128), NEG, np.float32)
    its = [0, 1, 2, nlat - 2, nlat - 1]
    p = np.arange(128)
    q = np.arange(128)
    for v, it in enumerate(its):
        kt0 = min(max(it - 2, 0), nlat - 5)
        r = 2 * it + q // 64
        cq = q % 64
        r0 = np.clip(r - 4, 0, rows - 8)
        c0 = np.clip(cq - 8, 0, 48)
        for kt in range(5):
            rk = 2 * (kt0 + kt) + p // 64
            ck = p % 64
            inw = ((rk[:, None] >= r0[None, :]) & (rk[:, None] < r0[None, :] + 8)
                   & (ck[:, None] >= c0[None, :]) & (ck[:, None] < c0[None, :] + 16))
            dr = np.clip(rk[:, None] - r[None, :] + 7, 0, 14)
            dc = np.clip(ck[:, None] - cq[None, :], -15, 15) + 15
            for h in range(4):
                b = rpb[h][dr, dc]
                out[:, v, h, kt, :] = np.where(inw, b, NEG)
    return out.reshape(128, NAB_N)


def _na_io(self):
    nc = self.nc
    self.nabias = nc.dram_tensor("nabias", [DEPTH, 128, NAB_N], F32, kind="ExternalInput").ap()


def _phase_na(self, l):
    S = self.S
    m = self.mark()
    cm, fv, rv, ps = self.cm, self.fv, self.rv, self.ps
    NT, nlat, NTOK = self.NT, self.nlat, self.NTOK
    last = (l == DEPTH - 1)
    PFr = self.PF.rearrange("(f p) n -> p f n", p=128)
    MIXr = self.MIXT.rearrange("(f p) n -> p f n", p=128)
    IDENT = cm[:, CM_ID:CM_ID + 128]
    ro = l * RV_L
    NANrow = rv[:, ro + RV_NAN:ro + RV_NAN + 256]
    KT = self.sb("KT", [128, 2, NTOK], BF16)
    Vaug = self.sb("Vaug", [128, NT, 4, 65], BF16)
    BT = self.sb("BT", [128, 5, 4, 5, 128], BF16)
    IDb = self.sb("IDb", [128, 128], BF16)
    ID4 = self.sb("ID4", [128, 4, 128], F32)
    qf_R = [self.sb("qf", [128, 2, 128], F32) for _ in range(2)]
    QZ_R = [self.sb("QZ", [128, 2, 2, 128], BF16) for _ in range(2)]
    mx_R = [self.sb("mx", [128, 2], F32) for _ in range(4)]
    DG_R = [self.sb("DG", [128, 512], BF16) for _ in range(4)]
    PTs_R = [self.sb("PTs", [128, 896], BF16) for _ in range(4)]
    rc_R = [self.sb("rc", [128, 4], F32) for _ in range(2)]
    onat_R = [self.sb("onat", [128, 4, 64], F32) for _ in range(2)]
    junk_R = [self.sb("junk", [128, 256], F32) for _ in range(2)]
    ssq_R = [self.sb("nssq", [128, 2], F32) for _ in range(2)]
    oT_R = [self.sb("oT", [128, 2, 128], BF16) for _ in range(2)]
    hrm = self.sb("hrm", [128, 2], F32)
    SB, ST = self.psbig[0], self.psbig[1]
    pOV, pT = ps[4], ps[5]
    for pr in range(2):
        S.add("pool", lambda e, pr=pr: e.dma_start(out=KT[:, pr, :], in_=PFr[:, PF_KA + pr, :]), writes=[KT.key], dma=True)
    S.add("pool", lambda e: e.dma_start(out=BT[:].rearrange("p a b c d -> p (a b c d)"), in_=self.nabias[l]), writes=[BT.key], dma=True)
    S.add("dve", lambda e: e.tensor_copy(out=IDb[:], in_=IDENT), reads=[cm.key], writes=[IDb.key])
    S.add("dve", lambda e: e.tensor_copy(out=ID4[:], in_=IDENT.unsqueeze(1).to_broadcast([128, 4, 128])), reads=[cm.key], writes=[ID4.key])
    S.add("dve", lambda e: e.memset(Vaug[:, :, :, 64:65], 1.0), writes=[Vaug.key])
    for t in range(NT):
        S.add("pool", lambda e, t=t: e.dma_start(out=Vaug[:, t, :, 0:64],
                                                  in_=self.PT[t * 128:(t + 1) * 128, PT_VA:PT_VA + 256].rearrange("p (h q) -> p h q", h=4)),
              writes=[Vaug.key], dma=True)
    for h2 in range(2):
        S.add("dve", lambda e, h2=h2: e.tensor_copy(out=hrm[:, h2:h2 + 1], in_=cm[:, CM_BLK + h2 * 64:CM_BLK + h2 * 64 + 1]),
              reads=[cm.key], writes=[hrm.key])

    def q_tile(tile, keytiles, var):
        n0 = tile * 128
        nk = len(keytiles)
        qf, QZ, rc, onat, junk, ssq, oT = qf_R[tile % 2], QZ_R[tile % 2], rc_R[tile % 2], onat_R[tile % 2], junk_R[tile % 2], ssq_R[tile % 2], oT_R[tile % 2]
        nloc = 5 if var is not None else 0
        S.add("sp", lambda e: e.dma_start(out=qf[:], in_=PFr[:, PF_QA:PF_QA + 2, n0:n0 + 128]), writes=[qf.key], dma=True)
        S.add("dve", lambda e: e.tensor_tensor(out=QZ[:], in0=qf[:].unsqueeze(2).to_broadcast([128, 2, 2, 128]),
                                               in1=hrm[:].unsqueeze(1).unsqueeze(3).to_broadcast([128, 2, 2, 128]), op=ALU.mult),
              reads=[qf.key, hrm.key], writes=[QZ.key])
        def head_fn(hi, h):
            pr, h2 = h // 2, h % 2
            mx, DG, PTs = mx_R[h % 4], DG_R[h % 4], PTs_R[h % 4]
            SB, ST = (self.psbig[0], self.psbig[1]) if h % 2 == 0 else (self.psbig[3], self.psbig[1])
            col = 0
            runs = []
            i = 0
            while i < nk:
                j = i
                while j + 1 < nk and keytiles[j + 1] == keytiles[j] + 1 and (j + 1 - i) < 4 and ((col + (j + 1 - i) * 128) % 512 != 0):
                    j += 1
                runs.append((keytiles[i], j - i + 1, col))
                col += (j - i + 1) * 128
                i = j + 1
            for (kt_, cnt, c_) in runs:
                S.add("pe", lambda e, pr=pr, h2=h2, kt_=kt_, cnt=cnt, c_=c_: e.matmul(
                    SB[:, c_:c_ + cnt * 128], lhsT=QZ[:, pr, h2, :], rhs=KT[:, pr, kt_ * 128:(kt_ + cnt) * 128], start=True, stop=True),
                    reads=[QZ.key, KT.key], writes=[SB.key])
            S.add("dve", lambda e: e.reduce_max(out=mx[:, 0:1], in_=SB[:, 0:nk * 128], axis=AX.X), reads=[SB.key], writes=[mx.key])
            S.add("dve", lambda e: e.tensor_scalar_mul(out=mx[:, 1:2], in0=mx[:, 0:1], scalar1=-1.0), reads=[mx.key], writes=[mx.key])
            S.add("dve", lambda e: e.tensor_scalar_mul(out=DG[:], in0=ID4[:].rearrange("p a b -> p (a b)"), scalar1=mx[:, 1:2]),
                  reads=[mx.key, ID4.key], writes=[DG.key])
            S.next_stage()
            for b0 in (0, 4):
                kks = [kk for kk in range(nk) if b0 <= kk < b0 + 4]
                if not kks:
                    continue
                ncols = len(kks) * 128
                nbias = len([kk for kk in kks if kk < nloc])
                S.add("pe", lambda e, b0=b0, ncols=ncols: e.matmul(ST[:, b0 * 128:b0 * 128 + ncols], lhsT=self.ones_bf[:], rhs=DG[:, 0:ncols],
                                                                   start=True, stop=False, skip_group_check=True),
                      reads=[DG.key, self.ones_bf.key], writes=[ST.key])
                for kk in kks:
                    kt_ = keytiles[kk]
                    lastmm = (kk == kks[-1]) and nbias == 0
                    S.add("pe", lambda e, pr=pr, h2=h2, kt_=kt_, kk=kk, lastmm=lastmm: e.matmul(
                        ST[:, kk * 128:(kk + 1) * 128], lhsT=KT[:, pr, kt_ * 128:(kt_ + 1) * 128], rhs=QZ[:, pr, h2, :],
                        start=False, stop=lastmm, skip_group_check=True),
                        reads=[QZ.key, KT.key], writes=[ST.key])
                if nbias:
                    k0 = kks[0]
                    S.add("pe", lambda e, h=h, k0=k0, nbias=nbias: e.matmul(
                        ST[:, k0 * 128:(k0 + nbias) * 128], lhsT=IDb[:],
                        rhs=BT[:, var, h, k0:k0 + nbias, :].rearrange("p a b -> p (a b)"),
                        start=False, stop=True, skip_group_check=True),
                        reads=[BT.key, IDb.key], writes=[ST.key])
            S.add("act", lambda e: e.activation(out=PTs[:, 0:nk * 128], in_=ST[:, 0:nk * 128], func=AF.Exp), reads=[ST.key], writes=[PTs.key])
            S.next_stage()
            for kk, kt_ in enumerate(keytiles):
                S.add("pe", lambda e, kk=kk, kt_=kt_, h=h: e.matmul(pOV[:, h * 65:(h + 1) * 65], lhsT=PTs[:, kk * 128:(kk + 1) * 128],
                                                                    rhs=Vaug[:, kt_, h, :], start=(kk == 0), stop=(kk == nk - 1)),
                      reads=[PTs.key, Vaug.key], writes=[pOV.key])
        run_staged(S, head_fn, [0, 1, 2, 3], group=4)
        OVv = pOV[:, 0:260].rearrange("p (h c) -> p h c", h=4)
        S.add("dve", lambda e: e.reciprocal(out=rc[:], in_=OVv[:, :, 64]), reads=[pOV.key], writes=[rc.key])
        S.add("dve", lambda e: e.tensor_tensor(out=onat[:], in0=OVv[:, :, 0:64], in1=rc[:].unsqueeze(2).to_broadcast([128, 4, 64]), op=ALU.mult),
              reads=[pOV.key, rc.key], writes=[onat.key])
        o2 = onat[:].rearrange("p h c -> p (h c)")
        if "naraw" in self.dbg:
            S.add("sp", lambda e: e.dma_start(out=self.dbg_out["naraw%d" % l][n0:n0 + 128, :], in_=o2), reads=[onat.key], dma=True,
                  semkey="dbg_naraw")
        S.add("act", lambda e: e.activation(out=junk[:], in_=o2, func=AF.Square, accum_out=ssq[:, 0:1]), reads=[onat.key],
              writes=[junk.key, ssq.key])
        S.add("dve", lambda e: e.tensor_scalar(out=ssq[:, 1:2], in0=ssq[:, 0:1], scalar1=1.0 / 256, scalar2=EPS, op0=ALU.mult, op1=ALU.add),
              reads=[ssq.key], writes=[ssq.key])
        S.add("act", lambda e: e.activation(out=ssq[:, 1:2], in_=ssq[:, 1:2], func=AF.Ln), reads=[ssq.key], writes=[ssq.key])
        S.add("act", lambda e: e.activation(out=ssq[:, 1:2], in_=ssq[:, 1:2], func=AF.Exp, scale=-0.5), reads=[ssq.key], writes=[ssq.key])
        S.add("dve", lambda e: e.scalar_tensor_tensor(out=junk[:], in0=o2, scalar=ssq[:, 1:2], in1=NANrow, op0=ALU.mult, op1=ALU.mult),
              reads=[onat.key, ssq.key, rv.key, junk.key], writes=[junk.key])
        for k in range(2):
            S.add("pe", lambda e, k=k: e.transpose(out=pT[:, k * 128:(k + 1) * 128], in_=junk[:, k * 128:(k + 1) * 128], identity=IDENT),
                  reads=[junk.key, cm.key], writes=[pT.key])
        S.add("act", lambda e: e.copy(out=oT[:].rearrange("p a b -> p (a b)"), in_=pT[:, 0:256]), reads=[pT.key], writes=[oT.key])
        S.add("sp", lambda e: e.dma_start(out=MIXr[:, 2:4, n0:n0 + 128], in_=oT[:]), reads=[oT.key], dma=True, semkey="st_" + oT.key)

    if not last:
        for tile in range(2):
            q_tile(tile, [0, 1], None)
    for it in range(nlat):
        var = 0 if it == 0 else 1 if it == 1 else 3 if it == nlat - 2 else 4 if it == nlat - 1 else 2
        kt0 = min(max(it - 2, 0), nlat - 5)
        q_tile(it + 2, [kt0 + 2 + k for k in range(5)] + [0, 1], var)
    self.barrier()
    self.reset(m)


NLAT_FULL = 64
_CACHE = {}


def kernel(**inputs):
    inp = {k: np.asarray(v) for k, v in inputs.items()}
    nlat = NLAT_FULL
    B = inp["x"].shape[0]
    kb = KB(nlat=nlat)
    nc = kb.build()
    maps = make_in_maps(inp, nlat, list(range(B)))
    res = run_bass_kernel_spmd(nc, maps, core_ids=list(range(B)))
    out = np.stack([np.asarray(res.results[b]["outT"]).T for b in range(B)])
    return np.ascontiguousarray(out.astype(np.float32))
```
